# Optimizing a Trainium2 kernel written in Bass

```python
import jax, jax.numpy as jnp
from jax import lax
import numpy as np

D_MODEL = 1024
BATCH = 2
SEQ = 16384
DEPTH = 2

N_MIXERS = 2
D_FF = 2816
FFN_RES = 0.5
CONV_WIDTH = 31
CONV_PAD = (CONV_WIDTH - 1) // 2
GLU_WIDTH = 2 * D_MODEL
HGRN_HEAD_DIM = 128
HGRN_HEADS = D_MODEL // HGRN_HEAD_DIM
HGRN_N_PROJ = 5
CHUNK = 64
EPS = 1e-6
N_CONV_LAYERS = (DEPTH + 1) // 2
N_HGRN_LAYERS = DEPTH // 2

kernel_name = "conformer_hgrn2_interleaved_encoder"


def rms_norm(x, g):
    xf = x.astype(jnp.float32)
    y = xf * lax.rsqrt(jnp.mean(jnp.square(xf), axis=-1, keepdims=True) + EPS)
    return (y * g).astype(x.dtype)


def layer_norm(x, g, b):
    xf = x.astype(jnp.float32)
    mu = jnp.mean(xf, axis=-1, keepdims=True)
    var = jnp.mean(jnp.square(xf - mu), axis=-1, keepdims=True)
    return ((xf - mu) * lax.rsqrt(var + EPS) * g + b).astype(x.dtype)


def swiglu(h, w13, w2):
    gate, up = jnp.split(h @ w13, 2, axis=-1)
    return (jax.nn.silu(gate) * up) @ w2


def conformer_conv(h, w_pw1, b_pw1, w_dw, b_dw, ln_g, ln_b, w_pw2, b_pw2):
    a, gate = jnp.split(h @ w_pw1 + b_pw1, 2, axis=-1)
    u = a * jax.nn.sigmoid(gate)
    u = lax.conv_general_dilated(
        u, w_dw[:, None, :].astype(u.dtype), window_strides=(1,),
        padding=[(CONV_PAD, CONV_PAD)],
        dimension_numbers=("NWC", "WIO", "NWC"),
        feature_group_count=D_MODEL) + b_dw
    u = layer_norm(u, ln_g, ln_b)
    return jax.nn.silu(u) @ w_pw2 + b_pw2


def chunk_gated_linear_recurrence(q, k, v, logf):
    B, S, H, DK = q.shape
    DV = v.shape[-1]
    nc = S // CHUNK

    def to_chunks(t):
        return t.reshape(B, nc, CHUNK, H, t.shape[-1]).transpose(1, 0, 3, 2, 4)

    causal_in_chunk = jnp.tril(jnp.ones((CHUNK, CHUNK), dtype=bool))[:, :, None]

    def step(state, xs):
        qc, kc, vc, lf = xs
        b = jnp.cumsum(lf, axis=2)
        diff = b[:, :, :, None, :] - b[:, :, None, :, :]
        decay = jnp.exp(jnp.where(causal_in_chunk, diff, -jnp.inf))
        scores = jnp.einsum("bhtk,bhsk,bhtsk->bhts", qc, kc, decay)
        o = (jnp.einsum("bhts,bhsv->bhtv", scores, vc)
             + jnp.einsum("bhtk,bhkv->bhtv", qc * jnp.exp(b), state))
        b_last = b[:, :, -1:, :]
        state = (jnp.exp(b_last)[:, :, 0, :, None] * state
                 + jnp.einsum("bhsk,bhsv->bhkv", kc * jnp.exp(b_last - b), vc))
        return state, o

    s0 = jnp.zeros((B, H, DK, DV), jnp.float32)
    _, outs = lax.scan(step, s0, (to_chunks(q), to_chunks(k), to_chunks(v), to_chunks(logf)))
    return outs.transpose(1, 0, 3, 2, 4).reshape(B, S, H, DV)


def hgrn2_bidirectional(h, w_in, lb_fwd, lb_bwd, gn_g, w_out):
    B, S, _ = h.shape
    H, DK = HGRN_HEADS, HGRN_HEAD_DIM
    proj = h @ w_in
    q, i, zf_f, zf_b, g = jnp.split(proj, HGRN_N_PROJ, axis=-1)
    q = jax.nn.silu(q.astype(jnp.float32)).reshape(B, S, H, DK)
    v = i.astype(jnp.float32).reshape(B, S, H, DK)

    def gates(zf, lb):
        z = zf.astype(jnp.float32).reshape(B, S, H, DK)
        lbh = lb.reshape(H, DK)
        logf = jnp.log(lbh + (1.0 - lbh) * jax.nn.sigmoid(z))
        k = (1.0 - lbh) * jax.nn.sigmoid(-z)
        return logf, k

    logf_f, k_f = gates(zf_f, lb_fwd)
    logf_b, k_b = gates(zf_b, lb_bwd)
    o_fwd = chunk_gated_linear_recurrence(q, k_f, v, logf_f)
    o_bwd = jnp.flip(chunk_gated_linear_recurrence(
        jnp.flip(q, 1), jnp.flip(k_b, 1), jnp.flip(v, 1), jnp.flip(logf_b, 1)), 1)
    o = o_fwd + o_bwd
    o = o * lax.rsqrt(jnp.mean(jnp.square(o), axis=-1, keepdims=True) + EPS)
    o = o * gn_g.reshape(H, DK) * jax.nn.silu(g.astype(jnp.float32)).reshape(B, S, H, DK)
    return o.reshape(B, S, D_MODEL).astype(h.dtype) @ w_out


def setup_inputs(seed: int = 0) -> dict:
    key = jax.random.key(seed)
    ks = jax.random.split(key, 20)
    D, F = D_MODEL, D_FF
    nrm = jax.random.normal
    f32 = jnp.float32
    return {
        "x": nrm(ks[0], (BATCH, SEQ, D), f32),
        "norm_g": 1.0 + 0.02 * nrm(ks[1], (DEPTH, 3, D), f32),
        "ffn_w13": nrm(ks[2], (DEPTH, 2, D, 2 * F), f32) * D ** -0.5,
        "ffn_w2": nrm(ks[3], (DEPTH, 2, F, D), f32) * F ** -0.5,
        "conv_w_pw1": nrm(ks[4], (N_CONV_LAYERS, D, GLU_WIDTH), f32) * D ** -0.5,
        "conv_b_pw1": 0.02 * nrm(ks[5], (N_CONV_LAYERS, GLU_WIDTH), f32),
        "conv_w_dw": nrm(ks[6], (N_CONV_LAYERS, CONV_WIDTH, D), f32) * CONV_WIDTH ** -0.5,
        "conv_b_dw": 0.02 * nrm(ks[7], (N_CONV_LAYERS, D), f32),
        "conv_ln_g": 1.0 + 0.02 * nrm(ks[8], (N_CONV_LAYERS, D), f32),
        "conv_ln_b": 0.02 * nrm(ks[9], (N_CONV_LAYERS, D), f32),
        "conv_w_pw2": nrm(ks[10], (N_CONV_LAYERS, D, D), f32) * D ** -0.5,
        "conv_b_pw2": 0.02 * nrm(ks[11], (N_CONV_LAYERS, D), f32),
        "hgrn_w_in": nrm(ks[12], (N_HGRN_LAYERS, D, HGRN_N_PROJ * D), f32) * D ** -0.5,
        "hgrn_lb": 0.1 * nrm(ks[13], (2, DEPTH, D), f32),
        "hgrn_gn_g": 1.0 + 0.02 * nrm(ks[14], (N_HGRN_LAYERS, D), f32),
        "hgrn_w_out": nrm(ks[15], (N_HGRN_LAYERS, D, D), f32) * D ** -0.5,
        "final_g": 1.0 + 0.02 * nrm(ks[16], (D,), f32),
    }


def reference(x, norm_g, ffn_w13, ffn_w2, conv_w_pw1, conv_b_pw1, conv_w_dw, conv_b_dw,
              conv_ln_g, conv_ln_b, conv_w_pw2, conv_b_pw2, hgrn_w_in, hgrn_lb,
              hgrn_gn_g, hgrn_w_out, final_g):
    lb_table = jnp.cumsum(jax.nn.softmax(hgrn_lb.astype(jnp.float32), axis=1), axis=1)
    lb_table = lb_table - lb_table[:, :1]
    for layer in range(DEPTH):
        x = x + FFN_RES * swiglu(rms_norm(x, norm_g[layer, 0]), ffn_w13[layer, 0], ffn_w2[layer, 0])
        h = rms_norm(x, norm_g[layer, 1])
        j = layer // N_MIXERS
        if layer % N_MIXERS == 0:
            mix = conformer_conv(h, conv_w_pw1[j], conv_b_pw1[j], conv_w_dw[j], conv_b_dw[j],
                                 conv_ln_g[j], conv_ln_b[j], conv_w_pw2[j], conv_b_pw2[j])
        else:
            mix = hgrn2_bidirectional(h, hgrn_w_in[j], lb_table[0, layer], lb_table[1, layer],
                                      hgrn_gn_g[j], hgrn_w_out[j])
        x = x + mix
        x = x + FFN_RES * swiglu(rms_norm(x, norm_g[layer, 2]), ffn_w13[layer, 1], ffn_w2[layer, 1])
    return rms_norm(x, final_g)
```

```python
import contextlib
import numpy as np
import concourse.bass as bass
import concourse.mybir as mybir
from concourse.bass_utils import run_bass_kernel_spmd

F32 = mybir.dt.float32
BF16 = mybir.dt.bfloat16
AF = mybir.ActivationFunctionType
ALU = mybir.AluOpType

D = 1024
FF = 2816
NT = 4096
TT = 256
NTILE = NT // TT
HALO = 16
EPS = 1e-6
NBLK = 66
NCORES = 8

V_NG = 0
V_FG = 48
V_BPW1 = 56
V_WDW = 72
V_BDW = 320
V_LNG = 328
V_LNB = 336
V_BPW2 = 344
V_HLB = 352
V_GNG = 384
NV = 392
C_HM = 0
C_RM = 32
C_RM1 = 48
NCV = 64


class Buf:
    __slots__ = ("name", "w", "r", "dsem", "dcnt", "excl")

    def __init__(self, name, excl=False):
        self.name = name
        self.excl = excl
        self.w = None
        self.r = {}
        self.dsem = None
        self.dcnt = 0


class Sched:
    ENGS = ("pe", "act", "dve", "pool", "sp")
    ROLL = 30000

    def __init__(self, nc):
        self.nc = nc
        self.prog = {e: [] for e in self.ENGS}
        self.sem = {}
        self.cnt = {e: 0 for e in self.ENGS}
        self.seen = {e: {} for e in self.ENGS}
        self.nsem = 0
        self.pool_pending = []
        self.sp_out = {}
        for e in self.ENGS:
            self.sem[e] = self.new_sem("eng_" + e)

    def new_sem(self, name):
        self.nsem += 1
        return self.nc.alloc_semaphore("%s_%d" % (name, self.nsem))

    def _waits(self, eng, reads, writes, is_dma=False):
        own = self.sem[eng]
        need = {}

        def add(tok, kind):
            sem, val = tok
            if sem is own and not is_dma:
                if eng == "pe" or kind != "raw":
                    return
            k = id(sem)
            if self.seen[eng].get(k, 0) >= val:
                return
            if k not in need or need[k][1] < val:
                need[k] = (sem, val)

        for b in reads:
            if b.w is not None:
                add(b.w, "raw")
        for b in writes:
            if b.w is not None:
                add(b.w, "waw")
            for t in b.r.values():
                add(t, "war")
        out = []
        for k, (sem, val) in need.items():
            self.seen[eng][k] = val
            out.append((sem, val))
        return out

    def _mark(self, tok, reads, writes):
        k = id(tok[0])
        for b in reads:
            o = b.r.get(k)
            if o is None or o[1] < tok[1]:
                b.r[k] = tok
        for b in writes:
            b.w = tok
            b.r = {}

    def op(self, eng, fn, reads=(), writes=(), sig=True):
        ex = [b for b in reads if b.excl]
        if ex:
            reads = [b for b in reads if not b.excl]
            writes = list(writes) + ex
        waits = self._waits(eng, reads, writes)
        if eng == "pool" and self.pool_pending:
            for sem, val in self.pool_pending:
                if self.seen[eng].get(id(sem), 0) < val:
                    self.seen[eng][id(sem)] = val
                    waits.append((sem, val))
            self.pool_pending = []
        tok = (self.sem[eng], self.cnt[eng] + 1)
        self._mark(tok, reads, writes)
        inc = None
        if sig:
            self.cnt[eng] += 1
            inc = (self.sem[eng], 1)
        self.prog[eng].append((waits, fn, inc))
        if sig and self.cnt[eng] >= self.ROLL:
            self.sem[eng] = self.new_sem("eng_" + eng)
            self.cnt[eng] = 0

    def dma(self, q, out, in_, reads=(), writes=(), prim=None):
        waits = self._waits(q, reads, writes, is_dma=True)
        if prim is None:
            prim = writes[0] if writes else reads[0]
        if prim.dsem is None:
            prim.dsem = self.new_sem("d")
        if prim.dcnt >= self.ROLL:
            prim.dsem = self.new_sem("d")
            prim.dcnt = 0
        prim.dcnt += 16
        tok = (prim.dsem, prim.dcnt)
        self._mark(tok, reads, writes)
        if q == "sp":
            self.sp_out[id(prim.dsem)] = tok
        self.prog[q].append((waits, lambda e, o=out, i=in_: e.dma_start(out=o, in_=i), (prim.dsem, 16)))
        return tok

    def barrier(self):
        toks = [(self.sem[e], self.cnt[e]) for e in ("pe", "act", "dve", "pool") if self.cnt[e] > 0]
        toks += list(self.sp_out.values())
        self.sp_out = {}
        for e in ("pe", "act", "dve", "sp"):
            w = []
            for sem, val in toks:
                if sem is self.sem.get(e):
                    continue
                if self.seen[e].get(id(sem), 0) < val:
                    self.seen[e][id(sem)] = val
                    w.append((sem, val))
            if w:
                self.prog[e].append((w, None, None))
        self.pool_pending = [t for t in toks if t[0] is not self.sem["pool"]]

    def wait_tok(self, eng, tok):
        self.prog[eng].append(([tok], None, None))

    def emit(self, block):
        m = {"pe": "tensor", "act": "scalar", "dve": "vector", "pool": "gpsimd", "sp": "sync"}
        for e in self.ENGS:
            items = self.prog[e]

            def body(engine, items=items):
                for waits, fn, inc in items:
                    for sem, val in waits:
                        engine.wait_ge(sem, val)
                    if fn is not None:
                        ins = fn(engine)
                        if inc is not None:
                            ins.then_inc(inc[0], inc[1])

            getattr(block, m[e])(body)


def build(upto=99, cc_mode="fused"):
    nc = bass.Bass("TRN2", target_bir_lowering=False)

    def dram(name, shape, dt=F32, kind="Internal"):
        return nc.dram_tensor(name, list(shape), dt, kind=kind).ap()

    x_in = dram("x", [NT, D], kind="ExternalInput")
    xh_in = dram("xh", [2 * HALO, D], kind="ExternalInput")
    vecs_in = dram("vecs", [128, NV], kind="ExternalInput")
    cvec_in = dram("cvec", [128, NCV], kind="ExternalInput")
    need = needed_weights(upto)
    w13_d = {k: dram("w13_%d%d" % k, [D, 2 * FF], kind="ExternalInput") for k in [(0, 0), (0, 1), (1, 0), (1, 1)] if ("w13_%d%d" % k) in need}
    w2_d = {k: dram("w2_%d%d" % k, [FF, D], kind="ExternalInput") for k in [(0, 0), (0, 1), (1, 0), (1, 1)] if ("w2_%d%d" % k) in need}
    if "conv_w_pw1" in need:
        wpw1_in = dram("conv_w_pw1", [D, 2 * D], kind="ExternalInput")
        wpw2_in = dram("conv_w_pw2", [D, D], kind="ExternalInput")
    if "hgrn_w_in" in need:
        win_in = dram("hgrn_w_in", [D, 5 * D], kind="ExternalInput")
        wout_in = dram("hgrn_w_out", [D, D], kind="ExternalInput")
    out_d = dram("out", [NT, D], kind="ExternalOutput")

    xsA = dram("xsA", [D, NT])
    xsB = dram("xsB", [D, NT])
    xhA = dram("xhA", [D, 2 * HALO])
    xhB = dram("xhB", [D, 2 * HALO])
    ub_d = dram("ub_d", [8, NTILE, 128, 128])
    sb_d = dram("sb_d", [8, NTILE, 128, 128])
    cc_in = dram("cc_in", [17 * 128, 128], kind=("ExternalOutput" if cc_mode == "produce" else "Internal"))
    cc_out = dram("cc_out", [NCORES * 17 * 128, 128], kind=("ExternalInput" if cc_mode == "consume" else "Internal"))

    _uc = {"n": 0}

    def U(name):
        _uc["n"] += 1
        return "%s_u%d" % (name, _uc["n"])

    es = contextlib.ExitStack()
    with es:
        S = Sched(nc)

        def sb(name, shape, dt=F32):
            return es.enter_context(nc.sbuf_tensor("s_" + name, list(shape), dt))

        wa = sb("wa", [128, NBLK * 1024], BF16)
        WB = [Buf("wa%d" % i) for i in range(NBLK)]
        vecs = sb("vecs", [128, NV]); Bvecs = Buf("vecs")
        cvec = sb("cvec", [128, NCV]); Bcvec = Buf("cvec")
        identf = sb("identf", [128, 128]); Bidf = Buf("identf")
        identb = sb("identb", [128, 128], BF16); Bidb = Buf("identb")
        ones_d = sb("ones_d", [128, 128], BF16); Bones = Buf("ones")
        ones_h = sb("ones_h", [128, 128], BF16)
        Mf = sb("Mf", [128, 128]); Mb = sb("Mb", [128, 128]); BM = Buf("M")
        rmask = sb("rmask", [128, TT]); ones_t = sb("ones_t", [128, TT]); Brm = Buf("rmask")
        lbv = sb("lbv", [128, 2, 8]); oml = sb("oml", [128, 2, 8]); noml = sb("noml", [128, 2, 8]); Blb = Buf("lb")
        Dall = sb("Dall", [128, 8, NTILE]); BDall = Buf("Dall")
        xts = [sb("xt%d" % i, [128, 8, TT]) for i in range(2)]; Bxt = [Buf("xt%d" % i) for i in range(2)]
        hbuf = [sb("h%d" % i, [128, 8, TT], BF16) for i in range(2)]; Bh = [Buf("h%d" % i) for i in range(2)]
        sqb = sb("sqb", [128, 8, TT], BF16); Bsq = Buf("sq")
        rstd = [sb("rstd%d" % i, [128, TT]) for i in range(2)]; Brstd = [Buf("rstd%d" % i) for i in range(2)]

        pbank = [es.enter_context(nc.psum_tensor("pb%d" % i, [128, 512], F32)) for i in range(8)]
        ph = []
        for i in range(8):
            ph.append(pbank[i][:, 0:256])
            ph.append(pbank[i][:, 256:512])
        _PBK = [Buf("pbank%d" % i, excl=True) for i in range(8)]
        PB = [_PBK[i // 2] for i in range(16)]

        def mm(out, lhsT, rhs, start, stop, reads, writes, sig, sgc=False):
            if sgc:
                S.op("pe", lambda e: e.matmul(out, lhsT, rhs, start=start, stop=stop, skip_group_check=True), reads=reads, writes=writes, sig=sig)
            else:
                S.op("pe", lambda e: e.matmul(out, lhsT, rhs, start=start, stop=stop), reads=reads, writes=writes, sig=sig)

        def act(out, in_, func, reads, writes, bias=None, scale=None):
            kw = {}
            if bias is not None:
                kw["bias"] = bias
            if scale is not None:
                kw["scale"] = scale
            S.op("act", lambda e: e.activation(out=out, in_=in_, func=func, **kw), reads=reads, writes=writes)

        def tt(out, in0, in1, op, reads, writes, eng="dve"):
            S.op(eng, lambda e: e.tensor_tensor(out=out, in0=in0, in1=in1, op=op), reads=reads, writes=writes)

        def ts(out, in0, s1, s2, op0, op1, reads, writes, eng="dve"):
            if s2 is None:
                S.op(eng, lambda e: e.tensor_scalar(out=out, in0=in0, scalar1=s1, scalar2=None, op0=op0), reads=reads, writes=writes)
            else:
                S.op(eng, lambda e: e.tensor_scalar(out=out, in0=in0, scalar1=s1, scalar2=s2, op0=op0, op1=op1), reads=reads, writes=writes)

        def stt(out, in0, scalar, in1, op0, op1, reads, writes, eng="dve"):
            S.op(eng, lambda e: e.scalar_tensor_tensor(out=out, in0=in0, scalar=scalar, in1=in1, op0=op0, op1=op1), reads=reads, writes=writes)

        def cp(out, in_, reads, writes, eng="dve"):
            if eng == "act":
                act(out, in_, AF.Copy, reads, writes)
            else:
                S.op(eng, lambda e: e.tensor_copy(out, in_), reads=reads, writes=writes)

        ring = {"p": 0}

        def walloc(n):
            if ring["p"] + n > NBLK:
                ring["p"] = 0
            s = ring["p"]
            ring["p"] += n
            return s, WB[s:s + n]

        def wstage(nblocks):
            if ring["p"] + nblocks > NBLK:
                ring["p"] = 0

        wsemB = [Buf("wsem%d" % i) for i in range(16)]
        wcnt = {"k": 0}

        def wprim():
            b = wsemB[wcnt["k"] % 16]
            wcnt["k"] += 1
            return b

        def wload_cols(src2d, col0, ncols):
            nb = (8 * ncols + 1023) // 1024
            s, bufs = walloc(nb)
            view = wa[:, s * 1024: s * 1024 + 8 * ncols].rearrange("p (c n) -> p c n", c=8)
            pb_ = wprim()
            S.dma("pool", view, src2d[:, col0:col0 + ncols].rearrange("(c p) n -> p c n", p=128), writes=bufs + [pb_], prim=pb_)
            return view, bufs

        def wload_rows(src2d, row0, nrows):
            f = nrows // 128
            s, bufs = walloc(f)
            view = wa[:, s * 1024: (s + f) * 1024].rearrange("p (f n) -> p f n", f=f)
            pb_ = wprim()
            S.dma("pool", view, src2d[row0:row0 + nrows, :].rearrange("(f p) n -> p f n", p=128), writes=bufs + [pb_], prim=pb_)
            return view, bufs

        S.dma("sp", vecs[:], vecs_in, writes=[Bvecs])
        S.dma("sp", cvec[:], cvec_in, writes=[Bcvec])
        S.op("pool", lambda e: e.memset(identf[:], 0.0), writes=[Bidf])
        S.op("pool", lambda e: e.affine_select(out=identf[:], in_=identf[:], pattern=[[-1, 128]], compare_op=ALU.not_equal,
                                               fill=1.0, base=0, channel_multiplier=1), reads=[Bidf], writes=[Bidf])
        cp(identb[:], identf[:], [Bidf], [Bidb], eng="pool")
        S.op("pool", lambda e: e.memset(ones_d[:], 1.0 / 1024.0), writes=[Bones])
        S.op("pool", lambda e: e.memset(ones_h[:], 1.0 / 128.0), writes=[Bones])
        S.op("pool", lambda e: e.memset(Mf[:], 1.0), writes=[BM])
        S.op("pool", lambda e: e.affine_select(out=Mf[:], in_=Mf[:], pattern=[[1, 128]], compare_op=ALU.is_ge,
                                               fill=0.0, base=0, channel_multiplier=-1), reads=[BM], writes=[BM])
        S.op("pool", lambda e: e.memset(Mf[0:64, 64:128], 0.0), reads=[BM], writes=[BM])
        S.op("pool", lambda e: e.memset(Mb[:], 1.0), reads=[BM], writes=[BM])
        S.op("pool", lambda e: e.affine_select(out=Mb[:], in_=Mb[:], pattern=[[-1, 128]], compare_op=ALU.is_ge,
                                               fill=0.0, base=0, channel_multiplier=1), reads=[BM], writes=[BM])
        S.op("pool", lambda e: e.memset(Mb[64:128, 0:64], 0.0), reads=[BM], writes=[BM])
        S.op("pool", lambda e: e.memset(rmask[:], 1.0), writes=[Brm])
        S.op("pool", lambda e: e.memset(rmask[:].rearrange("p (c j) -> p c j", j=64)[:, :, 0:1], 0.0), reads=[Brm], writes=[Brm])
        S.op("pool", lambda e: e.memset(ones_t[:], 1.0), reads=[Brm], writes=[Brm])
        hl = vecs[:, V_HLB:V_HLB + 32].rearrange("p (d l h) -> p d l h", d=2, l=2)
        tt(lbv[:], hl[:, :, 1, :], hl[:, :, 0, :], ALU.subtract, [Bvecs], [Blb])
        act(lbv[:], lbv[:], AF.Sigmoid, [Blb], [Blb])
        ts(oml[:], lbv[:], -1.0, 1.0, ALU.mult, ALU.add, [Blb], [Blb])
        ts(noml[:], oml[:], -1.0, None, ALU.mult, None, [Blb], [Blb])

        epsc = sb("epsc", [128, 1]); Beps = Buf("epsc")
        S.op("pool", lambda e: e.memset(epsc[:], EPS), writes=[Beps])

        def rsqrt_eps(out, in_, reads, Bout):
            act(out, in_, AF.Sqrt, list(reads) + [Beps], [Bout], bias=epsc[:, 0:1])
            S.op("dve", lambda e: e.reciprocal(out, out), reads=[Bout], writes=[Bout])

        nstat = {"i": 0}

        def rmsnorm(xt, Bx, N, gcol, h, Bhh):
            i = nstat["i"] % 2
            nstat["i"] += 1
            st, Bst = ph[14 + i][:, :N], PB[14 + i]
            r, Br = rstd[i][:, :N], Brstd[i]
            tt(sqb[:, :, :N], xt[:, :, :N], xt[:, :, :N], ALU.mult, [Bx], [Bsq])
            for c in range(8):
                mm(st, ones_d[:], sqb[:, c, :N], c == 0, c == 7, [Bones, Bsq], [Bst], c == 7)
            rsqrt_eps(r, st, [Bst], Br)
            for c in range(8):
                stt(h[:, c, :N], xt[:, c, :N], vecs[:, gcol + c:gcol + c + 1], r, ALU.mult, ALU.mult, [Bx, Br, Bvecs], [Bhh])

        xcnt = {"i": 0}

        def next_xt():
            i = xcnt["i"] % 2
            xcnt["i"] += 1
            return xts[i], Bxt[i], hbuf[i], Bh[i]

        def dview(dr, off, N):
            return dr[:, off:off + N].rearrange("(c p) n -> p c n", p=128)

        Bx = {}

        def dbuf(name, t):
            k = (name, 0)
            if k not in Bx:
                Bx[k] = Buf("%s_%s" % (name, t))
            return Bx[k]

        def stage_in():
            with nc.sbuf_tensor(U("xin"), [128, 2, D], F32) as xin:
                Bxin = Buf("xin")
                jobs = [(x_in, t * TT, TT, xsA, t * TT, ("A", t)) for t in range(NTILE)] + [(xh_in, 0, 2 * HALO, xhA, 0, ("hA", 0))]
                for src, soff, N, dst, doff, key in jobs:
                    nb = (N + 127) // 128
                    rows = min(N, 128)
                    S.dma("sp", xin[:rows, :nb, :], src[soff:soff + N, :].rearrange("(b p) d -> p b d", p=rows), writes=[Bxin])
                    xt, Bxx, _, _ = next_xt()
                    for c2 in range(4):
                        for ci in range(2):
                            c = 2 * c2 + ci
                            for b in range(nb):
                                S.op("pe", lambda e, c=c, b=b, rows=rows: e.transpose(ph[2 + c][:, b * 128:b * 128 + rows], xin[:rows, b, c * 128:(c + 1) * 128], identf[:rows, :rows]),
                                     reads=[Bxin, Bidf], writes=[PB[2 + c]], sig=(b == nb - 1 and ci == 1))
                        cp(xt[:, 2 * c2:2 * c2 + 2, :N], pbank[1 + c2][:, :].rearrange("p (h n) -> p h n", h=2)[:, :, :N], [PB[2 + 2 * c2]], [Bxx], eng=("act" if c2 % 2 else "dve"))
                    S.dma("sp", dview(dst, doff, N), xt[:, :, :N], reads=[Bxx], writes=[dbuf(*key)])

        def stage_ffn(l, i, src, sname, dst, dname, halo=None):
            gcol = V_NG + (l * 3 + (0 if i == 0 else 2)) * 8
            S.barrier()
            w13 = w13_d[(l, i)]
            w2 = w2_d[(l, i)]
            units = {}
            wstage(66)

            def unit(g):
                if g not in units:
                    G = 4 if g < 5 else 2
                    gv, gb = wload_cols(w13, g * 512, G * 128)
                    uv, ub = wload_cols(w13, FF + g * 512, G * 128)
                    dv, db = wload_rows(w2, g * 512, G * 128)
                    units[g] = (gv, gb, uv, ub, dv, db)
                return units[g]

            with nc.sbuf_tensor(U("sg"), [128, 2, TT], F32) as sg, nc.sbuf_tensor(U("actb"), [128, 3, TT], BF16) as actb:
                Bsg = [Buf("sg0"), Buf("sg1")]
                Bact = [Buf("act%d" % k) for k in range(3)]
                jobs = [(src, sname, dst, dname, t, t * TT, TT) for t in range(NTILE)]
                if halo is not None:
                    jobs.append((halo[0], halo[1], halo[2], halo[3], 0, 0, 2 * HALO))
                def prep(job):
                    (s_d, s_n, d_d, d_n, t, off, N) = job
                    xt, Bxx, h, Bhh = next_xt()
                    S.dma("sp", xt[:, :, :N], dview(s_d, off, N), reads=[dbuf(s_n, t)], writes=[Bxx])
                    rmsnorm(xt, Bxx, N, gcol, h, Bhh)
                    return xt, Bxx, h, Bhh

                cur = prep(jobs[0])
                for idx, (s_d, s_n, d_d, d_n, t, off, N) in enumerate(jobs):
                    xt, Bxx, h, Bhh = cur
                    LAG = 1

                    def ymm(j):
                        g, jj = j // 4, j % 4
                        dv, db = unit(g)[4], unit(g)[5]
                        a = j % 3
                        for dc in range(8):
                            mm(ph[dc][:, :N], dv[:, jj, dc * 128:(dc + 1) * 128], actb[:, a, :N], (j == 0 and dc % 2 == 0), j == 21,
                               [Bact[a]] + db, [PB[dc]], (dc == 7) or (j == 21), sgc=True)

                    for j in range(22 + LAG):
                        if j == 8 and idx + 1 < len(jobs):
                            cur = prep(jobs[idx + 1])
                        if j < 22:
                            g, jj = j // 4, j % 4
                            gv, gb, uv, ub = unit(g)[:4]
                            s3 = j % 3
                            pg, pu = ph[8 + 2 * s3][:, :N], ph[9 + 2 * s3][:, :N]
                            Bpg, Bpu = PB[8 + 2 * s3], PB[9 + 2 * s3]
                            for c in range(8):
                                mm(pg, gv[:, c, jj * 128:(jj + 1) * 128], h[:, c, :N], c == 0, c == 7, [Bhh] + gb, [Bpg], c == 7)
                            for c in range(8):
                                mm(pu, uv[:, c, jj * 128:(jj + 1) * 128], h[:, c, :N], c == 0, c == 7, [Bhh] + ub, [Bpu], c == 7)
                            act(sg[:, j % 2, :N], pg, AF.Silu, [Bpg], [Bsg[j % 2]])
                            tt(actb[:, j % 3, :N], sg[:, j % 2, :N], pu, ALU.mult, [Bsg[j % 2], Bpu], [Bact[j % 3]])
                        if j >= LAG:
                            ymm(j - LAG)
                    for dc in range(8):
                        stt(xt[:, dc, :N], ph[dc][:, :N], 0.5, xt[:, dc, :N], ALU.mult, ALU.add, [PB[dc], Bxx], [Bxx])
                    S.dma("sp", dview(d_d, off, N), xt[:, :, :N], reads=[Bxx], writes=[dbuf(d_n, t)])

        def stage_conv(src, sname, hsrc, hname, dst, dname):
            gcol = V_NG + 1 * 8
            S.barrier()
            W = TT + 32
            with nc.sbuf_tensor(U("ubuf"), [128, 8, W], BF16) as ubuf, \
                    nc.sbuf_tensor(U("sgc"), [128, 2, TT], F32) as sgc, \
                    nc.sbuf_tensor(U("vbuf"), [128, 8, TT], F32) as vbuf, \
                    nc.sbuf_tensor(U("v16"), [128, 8, TT], BF16) as v16, \
                    nc.sbuf_tensor(U("sil"), [128, 8, TT], BF16) as sil, \
                    nc.sbuf_tensor(U("lnst"), [128, 4, TT], F32) as lnst, \
                    nc.sbuf_tensor(U("xw"), [128, 8, TT], F32) as xw:
                Bub, Bsgc = Buf("ubuf"), [Buf("sgc0"), Buf("sgc1")]
                Bvb, Bv16, Bq16, Bsil, Bln, Bxw = Buf("vbuf"), Buf("v16"), Bsq, Buf("sil"), Buf("lnst"), Buf("xw")
                q16 = sqb
                wstage(55)
                pw1 = [wload_cols(wpw1_in, g * 512, 512) for g in range(4)]
                pw2 = [wload_cols(wpw2_in, g * 512, 512) for g in range(2)]
                ds_, dbufs = walloc(31)
                diag = wa[:, ds_ * 1024:(ds_ + 31) * 1024]
                for c in range(8):
                    for j in range(31):
                        m = c * 31 + j
                        ts(diag[:, m * 128:(m + 1) * 128], identb[:], vecs[:, V_WDW + m:V_WDW + m + 1], None, ALU.mult, None,
                           [Bidb, Bvecs], dbufs)
                S.op("pool", lambda e: e.memset(ubuf[:], 0.0), writes=[Bub])

                def make_u(s_d, s_n, t, off, N, ucol, hmask_col=None):
                    xt, Bxx, h, Bhh = next_xt()
                    S.dma("sp", xt[:, :, :N], dview(s_d, off, N), reads=[dbuf(s_n, t)], writes=[Bxx])
                    rmsnorm(xt, Bxx, N, gcol, h, Bhh)
                    for j in range(8):
                        s3 = j % 2
                        pa, pg = ph[2 + 2 * s3][:, :N], ph[3 + 2 * s3][:, :N]
                        Bpa, Bpg = PB[2 + 2 * s3], PB[3 + 2 * s3]
                        av, ab = pw1[j // 4]
                        gv, gb = pw1[2 + j // 4]
                        jj = j % 4
                        for c in range(8):
                            mm(pa, av[:, c, jj * 128:(jj + 1) * 128], h[:, c, :N], c == 0, c == 7, [Bhh] + ab, [Bpa], c == 7)
                        for c in range(8):
                            mm(pg, gv[:, c, jj * 128:(jj + 1) * 128], h[:, c, :N], c == 0, c == 7, [Bhh] + gb, [Bpg], c == 7)
                        act(sgc[:, j % 2, :N], pg, AF.Sigmoid, [Bpg, Bvecs], [Bsgc[j % 2]], bias=vecs[:, V_BPW1 + 8 + j:V_BPW1 + 9 + j])
                        stt(ubuf[:, j, ucol:ucol + N], pa, vecs[:, V_BPW1 + j:V_BPW1 + j + 1], sgc[:, j % 2, :N], ALU.add, ALU.mult,
                            [Bpa, Bsgc[j % 2], Bvecs], [Bub])
                    if hmask_col is not None:
                        for j in range(8):
                            tt(ubuf[:, j, ucol:ucol + N], ubuf[:, j, ucol:ucol + N], cvec[:, hmask_col:hmask_col + N], ALU.mult, [Bub, Bcvec], [Bub])

                def window(tok0, Wn, col0):
                    S.dma("sp", xw[:, :, :Wn], dview(src, tok0, Wn), reads=[dbuf(sname, t) for t in range(NTILE)], writes=[Bxw])
                    for c in range(8):
                        pv, Bpv = ph[6 + 2 * (c % 2)][:, :Wn], PB[6 + 2 * (c % 2)]
                        for j in range(31):
                            m = c * 31 + j
                            mm(pv, diag[:, m * 128:(m + 1) * 128], ubuf[:, c, col0 - 15 + j: col0 - 15 + j + Wn], j == 0, j == 30,
                               [Bub] + dbufs, [Bpv], j == 30)
                        act(vbuf[:, c, :Wn], pv, AF.Identity, [Bpv, Bvecs], [Bvb], bias=vecs[:, V_BDW + c:V_BDW + c + 1])
                    cp(v16[:, :, :Wn], vbuf[:, :, :Wn], [Bvb], [Bv16])
                    tt(q16[:, :, :Wn], vbuf[:, :, :Wn], vbuf[:, :, :Wn], ALU.mult, [Bvb], [Bq16])
                    p1, p2 = ph[10][:, :Wn], ph[11][:, :Wn]
                    for c in range(8):
                        mm(p1, ones_d[:], v16[:, c, :Wn], c == 0, c == 7, [Bones, Bv16], [PB[10]], c == 7)
                    for c in range(8):
                        mm(p2, ones_d[:], q16[:, c, :Wn], c == 0, c == 7, [Bones, Bq16], [PB[11]], c == 7)
                    mean, msq, var, rs = lnst[:, 0, :Wn], lnst[:, 1, :Wn], lnst[:, 2, :Wn], lnst[:, 3, :Wn]
                    cp(mean, p1, [PB[10]], [Bln])
                    tt(msq, mean, mean, ALU.mult, [Bln], [Bln])
                    tt(var, p2, msq, ALU.subtract, [PB[11], Bln], [Bln])
                    rsqrt_eps(rs, var, [Bln], Bln)
                    for c in range(8):
                        tt(vbuf[:, c, :Wn], vbuf[:, c, :Wn], mean, ALU.subtract, [Bvb, Bln], [Bvb])
                        tt(vbuf[:, c, :Wn], vbuf[:, c, :Wn], rs, ALU.mult, [Bvb, Bln], [Bvb])
                        act(sil[:, c, :Wn], vbuf[:, c, :Wn], AF.Silu, [Bvb, Bvecs], [Bsil],
                            bias=vecs[:, V_LNB + c:V_LNB + c + 1], scale=vecs[:, V_LNG + c:V_LNG + c + 1])
                    for dc in range(8):
                        po, Bpo = (ph[12][:, :Wn], PB[12]) if dc % 2 == 0 else (ph[0][:, :Wn], PB[0])
                        wv, wb_ = pw2[dc // 4]
                        for c in range(8):
                            mm(po, wv[:, c, (dc % 4) * 128:(dc % 4 + 1) * 128], sil[:, c, :Wn], c == 0, c == 7, [Bsil] + wb_, [Bpo], c == 7)
                        stt(xw[:, dc, :Wn], po, vecs[:, V_BPW2 + dc:V_BPW2 + dc + 1], xw[:, dc, :Wn], ALU.add, ALU.add, [Bpo, Bxw, Bvecs], [Bxw])
                    S.dma("sp", dview(dst, tok0, Wn), xw[:, :, :Wn], reads=[Bxw], writes=[dbuf(dname, t) for t in range(NTILE)])

                make_u(hsrc, hname, 0, 0, HALO, 16, hmask_col=C_HM)
                for t in range(NTILE):
                    make_u(src, sname, t, t * TT, TT, 32)
                    if t == 0:
                        window(0, TT - 16, 32)
                    else:
                        window(t * TT - 16, TT, 16)
                    cp(ubuf[:, :, 0:32], ubuf[:, :, TT:TT + 32], [Bub], [Bub], eng="pool")
                make_u(hsrc, hname, 0, HALO, HALO, 32, hmask_col=C_HM + HALO)
                window(NT - 16, 16, 16)

        def stage_hgrn(src, sname, dst, dname):
            gcol = V_NG + (1 * 3 + 1) * 8
            NCH = TT // 64
            NSB = TT // 128
            TAIL0 = 48
            S.barrier()

            def run_lanes(gens):
                gens = list(gens)
                while gens:
                    nxt = []
                    for g in gens:
                        try:
                            next(g)
                            nxt.append(g)
                        except StopIteration:
                            pass
                    gens = nxt

            with contextlib.ExitStack() as hs_:
                def hsb(name, shape, dt=F32):
                    return hs_.enter_context(nc.sbuf_tensor(U(name), list(shape), dt))
                wstage(48)
                S.op("pool", lambda e: e.memset(wa[:, TAIL0 * 1024:TAIL0 * 1024 + 16], 0.0), writes=WB[TAIL0:NBLK])
                tail_tok = WB[TAIL0].w
                lane_bufs = []

                def LB(name):
                    b = Buf(name)
                    b.w = tail_tok
                    lane_bufs.append(b)
                    return b
                tail = {"o": TAIL0 * 1024}

                def tail_f32(n):
                    o = tail["o"]
                    tail["o"] += 2 * n
                    assert tail["o"] <= NBLK * 1024
                    return wa[:, o:o + 2 * n].bitcast(F32)

                def tail_bf(n):
                    o = tail["o"]
                    tail["o"] += n
                    assert tail["o"] <= NBLK * 1024
                    return wa[:, o:o + n]
                gA = [tail_f32(TT) for _ in range(4)]; BgA = [LB("gA%d" % i) for i in range(4)]
                gB = [tail_f32(TT) for _ in range(4)]; BgB = [LB("gB%d" % i) for i in range(4)]
                gC = [tail_f32(TT) for _ in range(4)]; BgC = [LB("gC%d" % i) for i in range(4)]
                gD = [tail_f32(TT) for _ in range(4)]; BgD_ = [LB("gD%d" % i) for i in range(4)]
                Qt = [tail_bf(TT) for _ in range(8)]; BQt = [LB("Qt%d" % i) for i in range(8)]
                Kt = [tail_bf(TT) for _ in range(8)]; BKt = [LB("Kt%d" % i) for i in range(8)]
                Kh = [tail_bf(TT) for _ in range(8)]; BKh = [LB("Kh%d" % i) for i in range(8)]
                Khtm_ = [tail_bf(TT) for _ in range(8)]; BKhtm = [LB("Khtm%d" % i) for i in range(8)]
                Khtm = [k.rearrange("p (s n) -> p s n", s=NSB) for k in Khtm_]
                erT = hsb("erT", [128, 8, NCH]); Ber = [Buf("er%d" % i) for i in range(8)]
                qbT = hsb("qbT", [128, 4, TT]); Bqb = [Buf("qb%d" % i) for i in range(4)]
                gsT = hsb("gsT", [128, 4, TT]); Bgs = [Buf("gs%d" % i) for i in range(4)]
                osbT = tail_f32(2 * TT).rearrange("p (s n) -> p s n", s=2); Bosb = [LB("osb%d" % i) for i in range(2)]
                osqT = hsb("osqT", [128, 2, TT], BF16); Bosq = [Buf("osq%d" % i) for i in range(2)]
                Vtm = hsb("Vtm", [128, NSB, D], BF16); BV = Buf("Vtm")
                Pm = hsb("Pm", [128, 2, 2, 128], BF16); BPm = [Buf("Pm0"), Buf("Pm1")]
                Sst = hsb("Sst", [128, 2, 128]); BSst = [Buf("Sst0"), Buf("Sst1")]
                Sbf = hsb("Sbf", [128, 2, 2, NCH + 1, 128], BF16); BSbf = [Buf("Sbf0"), Buf("Sbf1")]
                Sfr = hsb("Sfr", [128, 8, 128]); BSfr = [Buf("Sfr%d" % i) for i in range(8)]
                Sbt = hsb("Sbt", [128, 8, 128]); BSbt = [Buf("Sbt%d" % i) for i in range(8)]
                Cb = hsb("Cb", [128, 8]); BCb = Buf("Cb")
                Cf = hsb("Cf", [128, 8]); BCf = Buf("Cf")
                Usb = hsb("Usb", [128, 4, 128]); BUsb = [Buf("Usb%d" % i) for i in range(4)]
                on = hsb("on", [128, 8, TT], BF16); Bon = Buf("on")
                gU = hsb("gU", [128, 2, 128]); BgU = [Buf("gU0"), Buf("gU1")]
                gDD = hsb("gDD", [128, 128]); BgDD = Buf("gDD")
                rsl = tail_f32(2 * TT).rearrange("p (s n) -> p s n", s=2); Brsl = [LB("rsl0"), LB("rsl1")]

                win_u = {}

                def win(gi):
                    if gi not in win_u:
                        win_u[gi] = wload_cols(win_in, gi * 512, 512)
                    return win_u[gi]

                def rsqrt_le(out, in_, reads, Bout):
                    act(out, in_, AF.Ln, list(reads) + [Beps], [Bout], bias=epsc[:, 0:1])
                    act(out, out, AF.Exp, [Bout], [Bout], scale=-0.5)

                def load_norm(t):
                    xt, Bxx, h, Bhh = next_xt()
                    S.dma("sp", xt[:], dview(src, t * TT, TT), reads=[dbuf(sname, t)], writes=[Bxx])
                    i = nstat["i"] % 2
                    nstat["i"] += 1
                    st, Bst = ph[14 + i], PB[14 + i]
                    tt(sqb[:], xt[:], xt[:], ALU.mult, [Bxx], [Bsq])
                    for c in range(8):
                        mm(st, ones_d[:], sqb[:, c, :], c == 0, c == 7, [Bones, Bsq], [Bst], c == 7)
                    rsqrt_le(rstd[i][:], st, [Bst], Brstd[i])
                    for c in range(8):
                        stt(h[:, c, :], xt[:, c, :], vecs[:, gcol + c:gcol + c + 1], rstd[i][:], ALU.mult, ALU.mult, [Bxx, Brstd[i], Bvecs], [Bhh])
                    for sbk in range(NSB):
                        for cg in range(2):
                            wv, wb_ = win(2 + cg)
                            pv = pbank[1][:, :]
                            for c in range(8):
                                mm(pv, h[:, c, sbk * 128:(sbk + 1) * 128], wv[:, c, :], c == 0, c == 7, [Bhh] + wb_, [PB[2]], c == 7)
                            cp(Vtm[:, sbk, cg * 512:(cg + 1) * 512], pv, [PB[2]], [BV], eng=("act" if cg else "dve"))
                    return xt, Bxx, h, Bhh

                pslot = {"i": 0}

                def proj(sec, hd, h, Bhh):
                    s3 = pslot["i"] % 3
                    pslot["i"] += 1
                    pz, Bpz = ph[4 + 2 * s3], PB[4 + 2 * s3]
                    wv, wb_ = win(2 * sec + hd // 4)
                    for c in range(8):
                        mm(pz, wv[:, c, (hd % 4) * 128:(hd % 4 + 1) * 128], h[:, c, :], c == 0, c == 7, [Bhh] + wb_, [Bpz], c == 7)
                    return pz, Bpz

                def gates_gen(dr, hd, h, Bhh, li):
                    A, B_ = gA[li], gB[li]
                    pz, Bpz = proj(2 + dr, hd, h, Bhh)
                    act(A, pz, AF.Exp, [Bpz], [BgA[li]], scale=-1.0)
                    yield
                    ts(A, A, 1.0, None, ALU.add, None, [BgA[li]], [BgA[li]])
                    yield
                    S.op("dve", lambda e: e.reciprocal(A, A), reads=[BgA[li]], writes=[BgA[li]])
                    yield
                    ts(B_, A, oml[:, dr, hd:hd + 1], lbv[:, dr, hd:hd + 1], ALU.mult, ALU.add, [BgA[li], Blb], [BgB[li]])
                    yield
                    ts(A, B_, -1.0, 1.0, ALU.mult, ALU.add, [BgB[li]], [BgA[li]])
                    yield
                    act(B_, B_, AF.Ln, [BgB[li]], [BgB[li]])
                    yield

                def transp_gen(pi, li):
                    for sbk in range(NSB):
                        ptr = pbank[5][:, :].bitcast(BF16)[:, (li * NSB + sbk) * 128:(li * NSB + sbk + 1) * 128]
                        S.op("pe", lambda e, ptr=ptr, sbk=sbk: e.transpose(ptr, Kh[pi][:, sbk * 128:(sbk + 1) * 128], identb[:]),
                             reads=[BKh[pi], Bidb], writes=[PB[10]])
                        yield
                        cp(Khtm[pi][:, sbk, :], ptr, [PB[10]], [BKhtm[pi]], eng="act")
                        yield

                S.op("pool", lambda e: e.memset(Sfr[:], 0.0), writes=BSfr)
                S.op("pool", lambda e: e.memset(Sbt[:], 0.0), writes=BSbt)
                S.op("pool", lambda e: e.memset(Cb[:], 1.0), writes=[BCb])
                S.op("pool", lambda e: e.memset(Cf[:], 1.0), writes=[BCf])

                def laneA(t, hd, dr, li, pi, h, Bhh):
                    A, B_, C_, D_ = gA[li], gB[li], gC[li], gD[li]
                    yield from gates_gen(dr, hd, h, Bhh, li)
                    S.op("dve", lambda e: e.tensor_tensor_scan(out=C_, data0=ones_t[:], data1=B_, initial=0.0, op0=ALU.mult, op1=ALU.add),
                         reads=[BgB[li], Brm], writes=[BgC[li]])
                    yield
                    if dr == 0:
                        act(D_, C_, AF.Exp, [BgC[li]], [BgD_[li]], bias=C_[:, TT - 1:TT], scale=-1.0)
                        dcol, Bd = erT[:, pi, 0:1], Ber[pi]
                    else:
                        tt(D_, C_, B_, ALU.subtract, [BgC[li], BgB[li]], [BgD_[li]])
                        yield
                        act(D_, D_, AF.Exp, [BgD_[li]], [BgD_[li]])
                        dcol, Bd = Dall[:, hd, t:t + 1], BDall
                    yield
                    act(dcol, C_[:, TT - 1:TT], AF.Exp, [BgC[li]], [Bd])
                    yield
                    tt(Kh[pi], A, D_, ALU.mult, [BgA[li], BgD_[li]], [BKh[pi]])
                    yield
                    yield from transp_gen(pi, li)
                    pu, Bpu = ph[12 + li // 2][:, (li % 2) * 128:(li % 2 + 1) * 128], PB[12]
                    for sbk in range(NSB):
                        mm(pu, Khtm[pi][:, sbk, :], Vtm[:, sbk, hd * 128:(hd + 1) * 128], sbk == 0, sbk == NSB - 1, [BKhtm[pi], BV], [Bpu], sbk == NSB - 1)
                    yield
                    if dr == 0:
                        stt(Sfr[:, hd, :], Sfr[:, hd, :], erT[:, pi, 0:1], pu, ALU.mult, ALU.add, [BSfr[hd], Ber[pi], Bpu], [BSfr[hd]])
                        yield
                        tt(Cf[:, hd:hd + 1], Cf[:, hd:hd + 1], erT[:, pi, 0:1], ALU.mult, [BCf, Ber[pi]], [BCf])
                    else:
                        cp(Usb[:, li, :], pu, [Bpu], [BUsb[li]])
                        yield
                        S.dma("sp", ub_d[hd, t], Usb[:, li, :], reads=[BUsb[li]], writes=[dbuf("ub%d" % hd, 0)])
                        stt(Sbt[:, hd, :], pu, Cb[:, hd:hd + 1], Sbt[:, hd, :], ALU.mult, ALU.add, [Bpu, BCb, BSbt[hd]], [BSbt[hd]])
                        yield
                        tt(Cb[:, hd:hd + 1], Cb[:, hd:hd + 1], Dall[:, hd, t:t + 1], ALU.mult, [BCb, BDall], [BCb])
                    yield

                for t in range(NTILE):
                    xt, Bxx, h, Bhh = load_norm(t)
                    for gi in range(4):
                        lanes = []
                        for hh in range(2):
                            for dr in range(2):
                                li = hh * 2 + dr
                                lanes.append(laneA(t, 2 * gi + hh, dr, li, (gi % 2) * 4 + li, h, Bhh))
                        run_lanes(lanes)

                Bcc_in, Bcc_out = Buf("cc_in"), Buf("cc_out")
                ccv = cc_in.rearrange("(m p) v -> p m v", p=128)
                S.dma("sp", ccv[:, 0:8, :], Sfr[:], reads=BSfr, writes=[Bcc_in])
                S.dma("sp", ccv[:, 8:16, :], Sbt[:], reads=BSbt, writes=[Bcc_in])
                S.op("pool", lambda e: e.memset(gDD[:], 0.0), writes=[BgDD])
                cp(gDD[:, 0:8], Cf[:], [BCf, BgDD], [BgDD])
                cp(gDD[:, 8:16], Cb[:], [BCb, BgDD], [BgDD])
                S.dma("sp", ccv[:, 16, :], gDD[:], reads=[BgDD], writes=[Bcc_in])
                if cc_mode == "produce":
                    S.wait_tok("sp", Bcc_in.w)
                    return
                waits = S._waits("pool", [Bcc_in], [Bcc_out], is_dma=True)
                csem = S.new_sem("cc")
                tokc = (csem, 1)
                S._mark(tokc, [Bcc_in], [Bcc_out])
                if cc_mode == "fused":
                    S.prog["pool"].append((waits, lambda e: e.collective_compute("AllGather", ALU.bypass, replica_groups=[list(range(NCORES))],
                                                                                 ins=[cc_in], outs=[cc_out]), (csem, 1)))
                else:
                    Bcc_out.w = None
                S.op("pool", lambda e: e.memset(Sfr[:], 0.0), reads=BSfr, writes=BSfr)
                S.op("pool", lambda e: e.memset(Sbt[:], 0.0), reads=BSbt, writes=BSbt)
                for dr, order in ((0, range(NCORES)), (1, range(NCORES - 1, -1, -1))):
                    Sx, BSx = (Sfr, BSfr) if dr == 0 else (Sbt, BSbt)
                    for r in order:
                        base = r * 17 * 128
                        S.dma("sp", gDD[:], cc_out[base + 2048: base + 2176, :], reads=[Bcc_out], writes=[BgDD])
                        mcol = cvec[:, C_RM + dr * 8 + r: C_RM + dr * 8 + r + 1]
                        omcol = cvec[:, C_RM1 + dr * 8 + r: C_RM1 + dr * 8 + r + 1]
                        ts(gDD[:, 16:24], gDD[:, dr * 8: dr * 8 + 8], mcol, omcol, ALU.mult, ALU.add, [BgDD, Bcvec], [BgDD])
                        for hd in range(8):
                            row0 = base + (dr * 8 + hd) * 128
                            S.dma("sp", gU[:, hd % 2, :], cc_out[row0:row0 + 128, :], reads=[Bcc_out], writes=[BgU[hd % 2]])
                            ts(gU[:, hd % 2, :], gU[:, hd % 2, :], mcol, None, ALU.mult, None, [BgU[hd % 2], Bcvec], [BgU[hd % 2]])
                            stt(Sx[:, hd, :], Sx[:, hd, :], gDD[:, 16 + hd:17 + hd], gU[:, hd % 2, :], ALU.mult, ALU.add, [BSx[hd], BgDD, BgU[hd % 2]], [BSx[hd]])
                for t in range(NTILE - 1, -1, -1):
                    for hd in range(8):
                        if t < NTILE - 1:
                            S.dma("sp", Usb[:, hd % 4, :], ub_d[hd, t + 1], reads=[dbuf("ub%d" % hd, 0)], writes=[BUsb[hd % 4]])
                            stt(Sbt[:, hd, :], Sbt[:, hd, :], Dall[:, hd, t + 1:t + 2], Usb[:, hd % 4, :], ALU.mult, ALU.add, [BSbt[hd], BDall, BUsb[hd % 4]], [BSbt[hd]])
                        S.dma("sp", sb_d[hd, t], Sbt[:, hd, :], reads=[BSbt[hd]], writes=[dbuf("sb%d" % hd, 0)])

                wout_u = {}

                def wout(g):
                    if g not in wout_u:
                        wout_u[g] = wload_cols(wout_in, g * 512, 512)
                    return wout_u[g]

                Mm = (Mf, Mb)

                def headB(hd, hs, h, Bhh):
                    for sec, dstT, Bd in ((0, qbT, Bqb), (4, gsT, Bgs)):
                        pq, Bpq = proj(sec, hd, h, Bhh)
                        act(dstT[:, hs, :], pq, AF.Exp, [Bpq], [Bd[hs]], scale=-1.0)
                        ts(dstT[:, hs, :], dstT[:, hs, :], 1.0, None, ALU.add, None, [Bd[hs]], [Bd[hs]])
                        S.op("dve", lambda e, o=dstT[:, hs, :]: e.reciprocal(o, o), reads=[Bd[hs]], writes=[Bd[hs]])
                        tt(dstT[:, hs, :], pq, dstT[:, hs, :], ALU.mult, [Bpq, Bd[hs]], [Bd[hs]])
                        yield

                def laneB(t, hd, dr, li, pi, hs, h, Bhh):
                    A, B_, C_, D_ = gA[li], gB[li], gC[li], gD[li]
                    yield from gates_gen(dr, hd, h, Bhh, li)
                    S.op("dve", lambda e: e.tensor_tensor_scan(out=C_, data0=rmask[:], data1=B_, initial=0.0, op0=ALU.mult, op1=ALU.add),
                         reads=[BgB[li], Brm], writes=[BgC[li]])
                    yield
                    C3 = C_.rearrange("p (c j) -> p c j", j=64)
                    if dr == 1:
                        D3 = D_.rearrange("p (c j) -> p c j", j=64)
                        tt(D3, C3[:, :, 63:64].to_broadcast([128, NCH, 64]), C3, ALU.subtract, [BgC[li]], [BgD_[li]])
                        yield
                        tt(C_, D_, B_, ALU.add, [BgD_[li], BgB[li]], [BgC[li]])
                        yield
                        rcol = C3[:, :, 0]
                    else:
                        rcol = C3[:, :, 63]
                    act(erT[:, pi, :], rcol, AF.Exp, [BgC[li]], [Ber[pi]])
                    yield
                    act(D_, C_, AF.Exp, [BgC[li]], [BgD_[li]])
                    yield
                    tt(Qt[pi], qbT[:, hs, :], D_, ALU.mult, [Bqb[hs], BgD_[li]], [BQt[pi]])
                    yield
                    act(D_, C_, AF.Exp, [BgC[li]], [BgD_[li]], scale=-1.0)
                    yield
                    tt(A, A, D_, ALU.mult, [BgA[li], BgD_[li]], [BgA[li]])
                    yield
                    cp(Kt[pi], A, [BgA[li]], [BKt[pi]], eng="act")
                    yield
                    tt(Kh[pi].rearrange("p (c j) -> p c j", j=64), A.rearrange("p (c j) -> p c j", j=64),
                       erT[:, pi, :].unsqueeze(2).to_broadcast([128, NCH, 64]), ALU.mult, [BgA[li], Ber[pi]], [BKh[pi]])
                    yield
                    yield from transp_gen(pi, li)

                def postB(t, hd, pf, pb, hs, par, h, Bhh):
                    cp(Sbf[:, par, 0, 0, :], Sfr[:, hd, :], [BSfr[hd]], [BSbf[par]], eng="pool")
                    yield
                    for ch in range(NCH):
                        sbk, rows = ch // 2, slice((ch % 2) * 64, (ch % 2) * 64 + 64)
                        pu, Bpu = ph[12][:, (ch % 2) * 128:(ch % 2 + 1) * 128], PB[12]
                        mm(pu, Khtm[pf][rows, sbk, :], Vtm[rows, sbk, hd * 128:(hd + 1) * 128], True, True, [BKhtm[pf], BV], [Bpu], True)
                        yield
                        stt(Sfr[:, hd, :], Sfr[:, hd, :], erT[:, pf, ch:ch + 1], pu, ALU.mult, ALU.add, [BSfr[hd], Ber[pf], Bpu], [BSfr[hd]])
                        yield
                        if ch < NCH - 1:
                            cp(Sbf[:, par, 0, ch + 1, :], Sfr[:, hd, :], [BSfr[hd]], [BSbf[par]], eng="pool")
                            yield
                    S.dma("sp", Sst[:, par, :], sb_d[hd, t], reads=[dbuf("sb%d" % hd, 0)], writes=[BSst[par]])
                    cp(Sbf[:, par, 1, NCH, :], Sst[:, par, :], [BSst[par]], [BSbf[par]], eng="pool")
                    yield
                    for ch in range(NCH - 1, 0, -1):
                        sbk, rows = ch // 2, slice((ch % 2) * 64, (ch % 2) * 64 + 64)
                        pu, Bpu = ph[13][:, (ch % 2) * 128:(ch % 2 + 1) * 128], PB[13]
                        mm(pu, Khtm[pb][rows, sbk, :], Vtm[rows, sbk, hd * 128:(hd + 1) * 128], True, True, [BKhtm[pb], BV], [Bpu], True)
                        yield
                        stt(Sst[:, par, :], Sst[:, par, :], erT[:, pb, ch:ch + 1], pu, ALU.mult, ALU.add, [BSst[par], Ber[pb], Bpu], [BSst[par]])
                        yield
                        cp(Sbf[:, par, 1, ch, :], Sst[:, par, :], [BSst[par]], [BSbf[par]], eng="pool")
                        yield
                    po, Bpo = ph[0], PB[0]
                    pset = (pf, pb)
                    for sbk in range(NSB):
                        cols = slice(sbk * 128, (sbk + 1) * 128)
                        for dr in range(2):
                            psc, Bps = ph[3][:, dr * 128:(dr + 1) * 128], PB[3]
                            mm(psc, Kt[pset[dr]][:, cols], Qt[pset[dr]][:, cols], True, True, [BKt[pset[dr]], BQt[pset[dr]]], [Bps], True)
                            yield
                            tt(Pm[:, par, dr, :], psc, Mm[dr][:], ALU.mult, [Bps, BM], [BPm[par]])
                            yield
                        mm(po[:, cols], Vtm[:, sbk, hd * 128:(hd + 1) * 128], Pm[:, par, 0, :], True, False, [BV, BPm[par]], [Bpo], False)
                        mm(po[:, cols], Vtm[:, sbk, hd * 128:(hd + 1) * 128], Pm[:, par, 1, :], False, False, [BV, BPm[par]], [Bpo], False)
                        for cc in (2 * sbk, 2 * sbk + 1):
                            c64 = slice(cc * 64, cc * 64 + 64)
                            mm(po[:, c64], Sbf[:, par, 0, cc, :], Qt[pf][:, c64], False, False, [BSbf[par], BQt[pf]], [Bpo], False)
                            mm(po[:, c64], Sbf[:, par, 1, cc + 1, :], Qt[pb][:, c64], False, True, [BSbf[par], BQt[pb]], [Bpo], True)
                        yield
                    cp(osbT[:, par, :], po, [Bpo], [Bosb[par]], eng="act")
                    yield
                    tt(osqT[:, par, :], osbT[:, par, :], osbT[:, par, :], ALU.mult, [Bosb[par]], [Bosq[par]])
                    yield
                    pst, Bpst = ph[15], PB[15]
                    mm(pst, ones_h[:], osqT[:, par, :], True, True, [Bones, Bosq[par]], [Bpst], True)
                    yield
                    act(rsl[:, par, :], pst, AF.Ln, [Bpst, Beps], [Brsl[par]], bias=epsc[:, 0:1])
                    yield
                    act(rsl[:, par, :], rsl[:, par, :], AF.Exp, [Brsl[par]], [Brsl[par]], scale=-0.5)
                    yield
                    stt(osbT[:, par, :], osbT[:, par, :], vecs[:, V_GNG + hd:V_GNG + hd + 1], rsl[:, par, :], ALU.mult, ALU.mult, [Bosb[par], Bvecs, Brsl[par]], [Bosb[par]])
                    yield
                    tt(on[:, hd, :], osbT[:, par, :], gsT[:, hs, :], ALU.mult, [Bosb[par], Bgs[hs]], [Bon])
                    yield

                def postpair(t, gi, h, Bhh):
                    for hh in range(2):
                        hd = 2 * gi + hh
                        base = (gi % 2) * 4 + hh * 2
                        yield from postB(t, hd, base, base + 1, (gi % 2) * 2 + hh, hh, h, Bhh)

                for t in range(NTILE):
                    xt, Bxx, h, Bhh = load_norm(t)
                    pending = None
                    for gi in range(4):
                        gens = []
                        for hh in range(2):
                            gens.append(headB(2 * gi + hh, (gi % 2) * 2 + hh, h, Bhh))
                        for hh in range(2):
                            for dr in range(2):
                                li = hh * 2 + dr
                                gens.append(laneB(t, 2 * gi + hh, dr, li, (gi % 2) * 4 + li, (gi % 2) * 2 + hh, h, Bhh))
                        if pending is not None:
                            gens.append(pending)
                        run_lanes(gens)
                        pending = postpair(t, gi, h, Bhh)
                    run_lanes([pending])
                    for dc in range(8):
                        pw, Bpw = (ph[2], PB[2]) if dc % 2 == 0 else (ph[4], PB[4])
                        wv, wb_ = wout(dc // 4)
                        for c in range(8):
                            mm(pw, wv[:, c, (dc % 4) * 128:(dc % 4 + 1) * 128], on[:, c, :], c == 0, c == 7, [Bon] + wb_, [Bpw], c == 7)
                        tt(xt[:, dc, :], xt[:, dc, :], pw, ALU.add, [Bxx, Bpw], [Bxx])
                    S.dma("sp", dview(dst, t * TT, TT), xt[:], reads=[Bxx], writes=[dbuf(dname, t)])
                S.op("pool", lambda e: e.memset(wa[:, TAIL0 * 1024:TAIL0 * 1024 + 16], 0.0), reads=lane_bufs, writes=WB[TAIL0:NBLK])

        def stage_out(src, sname, do_norm=True):
            S.barrier()
            with nc.sbuf_tensor(U("xo"), [128, 8, TT], F32) as xo, nc.sbuf_tensor(U("ot"), [128, 2, D], F32) as ot:
                Bxo, Bot = Buf("xo"), Buf("ot")
                toks = []
                for t in range(NTILE):
                    xt, Bxx, h, Bhh = next_xt()
                    S.dma("sp", xt[:], dview(src, t * TT, TT), reads=[dbuf(sname, t)], writes=[Bxx])
                    if do_norm:
                        i = nstat["i"] % 2
                        nstat["i"] += 1
                        tt(sqb[:], xt[:], xt[:], ALU.mult, [Bxx], [Bsq])
                        for c in range(8):
                            mm(ph[14 + i], ones_d[:], sqb[:, c, :], c == 0, c == 7, [Bones, Bsq], [PB[14 + i]], c == 7)
                        rsqrt_eps(rstd[i][:], ph[14 + i], [PB[14 + i]], Brstd[i])
                        for c in range(8):
                            stt(xo[:, c, :], xt[:, c, :], vecs[:, V_FG + c:V_FG + c + 1], rstd[i][:], ALU.mult, ALU.mult, [Bxx, Brstd[i], Bvecs], [Bxo])
                        srcx, Bsrc = xo, Bxo
                    else:
                        srcx, Bsrc = xt, Bxx
                    for b in range(2):
                        for half in range(2):
                            pbk = pbank[1 + half]
                            for c4 in range(4):
                                c = half * 4 + c4
                                S.op("pe", lambda e, pbk=pbk, c4=c4, c=c, b=b, srcx=srcx: e.transpose(pbk[:, c4 * 128:(c4 + 1) * 128], srcx[:, c, b * 128:(b + 1) * 128], identf[:]),
                                     reads=[Bsrc, Bidf], writes=[PB[2 + 2 * half], PB[3 + 2 * half]], sig=(c4 == 3))
                            cp(ot[:, b, half * 512:(half + 1) * 512], pbk[:, :], [PB[2 + 2 * half], PB[3 + 2 * half]], [Bot], eng=("act" if half else "dve"))
                    toks.append(S.dma("sp", out_d[t * TT:(t + 1) * TT, :].rearrange("(b p) d -> p b d", p=128), ot[:], reads=[Bot], writes=[dbuf("out", t)]))
                for tk in toks:
                    S.wait_tok("sp", tk)

        stage_in()
        cur, cname, oth, oname = xsA, "A", xsB, "B"
        if upto >= 1:
            stage_ffn(0, 0, xsA, "A", xsB, "B", halo=(xhA, "hA", xhB, "hB"))
            cur, cname, oth, oname = xsB, "B", xsA, "A"
        if upto >= 2:
            stage_conv(xsB, "B", xhB, "hB", xsA, "A")
            cur, cname, oth, oname = xsA, "A", xsB, "B"
        if upto >= 3:
            stage_ffn(0, 1, xsA, "A", xsB, "B")
            cur, cname = xsB, "B"
        if upto >= 4:
            stage_ffn(1, 0, xsB, "B", xsA, "A")
            cur, cname = xsA, "A"
        if upto >= 5:
            stage_hgrn(xsA, "A", xsB, "B")
            cur, cname = xsB, "B"
        if cc_mode != "produce":
            if upto >= 6:
                stage_ffn(1, 1, xsB, "B", xsA, "A")
                cur, cname = xsA, "A"
            stage_out(cur, cname, do_norm=(upto >= 7))

        with nc.Block() as block:
            S.emit(block)
    return nc


def colpack(v):
    v = np.asarray(v, dtype=np.float32).reshape(-1, 128)
    return np.ascontiguousarray(v.T)


def needed_weights(upto):
    need = []
    if upto >= 1:
        need += ["w13_00", "w2_00"]
    if upto >= 2:
        need += ["conv_w_pw1", "conv_w_pw2"]
    if upto >= 3:
        need += ["w13_01", "w2_01"]
    if upto >= 4:
        need += ["w13_10", "w2_10"]
    if upto >= 5:
        need += ["hgrn_w_in", "hgrn_w_out"]
    if upto >= 6:
        need += ["w13_11", "w2_11"]
    return need


def make_inputs(upto, x, norm_g, ffn_w13, ffn_w2, conv_w_pw1, conv_b_pw1, conv_w_dw, conv_b_dw, conv_ln_g, conv_ln_b,
                conv_w_pw2, conv_b_pw2, hgrn_w_in, hgrn_lb, hgrn_gn_g, hgrn_w_out, final_g):
    f = lambda a: np.ascontiguousarray(np.asarray(a, dtype=np.float32))
    vecs = np.zeros((128, NV), np.float32)
    vecs[:, V_NG:V_NG + 48] = colpack(f(norm_g).reshape(-1))
    vecs[:, V_FG:V_FG + 8] = colpack(final_g)
    vecs[:, V_BPW1:V_BPW1 + 16] = colpack(f(conv_b_pw1)[0])
    wdw = f(conv_w_dw)[0]
    for c in range(8):
        vecs[:, V_WDW + c * 31:V_WDW + (c + 1) * 31] = wdw[:, c * 128:(c + 1) * 128].T
    vecs[:, V_BDW:V_BDW + 8] = colpack(f(conv_b_dw)[0])
    vecs[:, V_LNG:V_LNG + 8] = colpack(f(conv_ln_g)[0])
    vecs[:, V_LNB:V_LNB + 8] = colpack(f(conv_ln_b)[0])
    vecs[:, V_BPW2:V_BPW2 + 8] = colpack(f(conv_b_pw2)[0])
    vecs[:, V_HLB:V_HLB + 32] = colpack(f(hgrn_lb).reshape(-1))
    vecs[:, V_GNG:V_GNG + 8] = colpack(f(hgrn_gn_g)[0])
    x = f(x)
    allw = {"conv_w_pw1": f(conv_w_pw1)[0], "conv_w_pw2": f(conv_w_pw2)[0], "hgrn_w_in": f(hgrn_w_in)[0], "hgrn_w_out": f(hgrn_w_out)[0]}
    for l in range(2):
        for i in range(2):
            allw["w13_%d%d" % (l, i)] = f(np.asarray(ffn_w13)[l, i])
            allw["w2_%d%d" % (l, i)] = f(np.asarray(ffn_w2)[l, i])
    shared = {"vecs": vecs}
    for k in needed_weights(upto):
        shared[k] = allw[k]
    in_maps = []
    for c in range(NCORES):
        b, pos = c // 4, c % 4
        t0 = pos * NT
        xh = np.zeros((2 * HALO, D), np.float32)
        cvec = np.zeros((128, NCV), np.float32)
        if pos > 0:
            xh[:HALO] = x[b, t0 - HALO:t0]
            cvec[:, C_HM:C_HM + HALO] = 1.0
        if pos < 3:
            xh[HALO:] = x[b, t0 + NT:t0 + NT + HALO]
            cvec[:, C_HM + HALO:C_HM + 2 * HALO] = 1.0
        for r in range(NCORES):
            if r // 4 == b and r < c:
                cvec[:, C_RM + r] = 1.0
            if r // 4 == b and r > c:
                cvec[:, C_RM + 8 + r] = 1.0
        cvec[:, C_RM1:C_RM1 + 16] = 1.0 - cvec[:, C_RM:C_RM + 16]
        m = dict(shared)
        m["x"] = np.ascontiguousarray(x[b, t0:t0 + NT])
        m["xh"] = xh
        m["cvec"] = cvec
        in_maps.append(m)
    return in_maps


_NC_CACHE = {}
FUSED = True


def run(inputs, upto=99):
    in_maps = make_inputs(upto, **inputs)
    if FUSED or upto < 5:
        if upto not in _NC_CACHE:
            _NC_CACHE[upto] = build(upto)
        res = run_bass_kernel_spmd(_NC_CACHE[upto], in_maps, core_ids=list(range(NCORES)))
    else:
        ncA = build(upto, "produce")
        resA = run_bass_kernel_spmd(ncA, in_maps, core_ids=list(range(NCORES)))
        allcc = np.concatenate([resA.results[c]["cc_in"] for c in range(NCORES)], axis=0)
        for m in in_maps:
            m["cc_out"] = allcc
        ncB = build(upto, "consume")
        res = run_bass_kernel_spmd(ncB, in_maps, core_ids=list(range(NCORES)))
    out = np.zeros((2, 4 * NT, D), np.float32)
    for c in range(NCORES):
        out[c // 4, (c % 4) * NT:(c % 4 + 1) * NT] = res.results[c]["out"]
    return out


def kernel(**inputs):
    return run(inputs, 99)
```

```python
import contextlib
import numpy as np
import concourse.bass as bass
import concourse.mybir as mybir
from concourse.bass_utils import run_bass_kernel_spmd

F32 = mybir.dt.float32
BF16 = mybir.dt.bfloat16
AF = mybir.ActivationFunctionType
ALU = mybir.AluOpType

D = 1024
FF = 2816
NT = 4096
TT = 256
NTILE = NT // TT
HALO = 16
EPS = 1e-6
NBLK = 66
NCORES = 8

V_NG = 0
V_FG = 48
V_BPW1 = 56
V_WDW = 72
V_BDW = 320
V_LNG = 328
V_LNB = 336
V_BPW2 = 344
V_HLB = 352
V_GNG = 384
NV = 392
C_HM = 0
C_RM = 32
C_RM1 = 48
NCV = 64


class Buf:
    __slots__ = ("name", "w", "r", "dsem", "dcnt", "excl")

    def __init__(self, name, excl=False):
        self.name = name
        self.excl = excl
        self.w = None
        self.r = {}
        self.dsem = None
        self.dcnt = 0


class Sched:
    ENGS = ("pe", "act", "dve", "pool", "sp")
    ROLL = 30000

    def __init__(self, nc):
        self.nc = nc
        self.prog = {e: [] for e in self.ENGS}
        self.sem = {}
        self.cnt = {e: 0 for e in self.ENGS}
        self.seen = {e: {} for e in self.ENGS}
        self.nsem = 0
        self.pool_pending = []
        self.sp_out = {}
        for e in self.ENGS:
            self.sem[e] = self.new_sem("eng_" + e)

    def new_sem(self, name):
        self.nsem += 1
        return self.nc.alloc_semaphore("%s_%d" % (name, self.nsem))

    def _waits(self, eng, reads, writes, is_dma=False):
        own = self.sem[eng]
        need = {}

        def add(tok, kind):
            sem, val = tok
            if sem is own and not is_dma:
                if eng == "pe" or kind != "raw":
                    return
            k = id(sem)
            if self.seen[eng].get(k, 0) >= val:
                return
            if k not in need or need[k][1] < val:
                need[k] = (sem, val)

        for b in reads:
            if b.w is not None:
                add(b.w, "raw")
        for b in writes:
            if b.w is not None:
                add(b.w, "waw")
            for t in b.r.values():
                add(t, "war")
        out = []
        for k, (sem, val) in need.items():
            self.seen[eng][k] = val
            out.append((sem, val))
        return out

    def _mark(self, tok, reads, writes):
        k = id(tok[0])
        for b in reads:
            o = b.r.get(k)
            if o is None or o[1] < tok[1]:
                b.r[k] = tok
        for b in writes:
            b.w = tok
            b.r = {}

    def op(self, eng, fn, reads=(), writes=(), sig=True):
        ex = [b for b in reads if b.excl]
        if ex:
            reads = [b for b in reads if not b.excl]
            writes = list(writes) + ex
        waits = self._waits(eng, reads, writes)
        if eng == "pool" and self.pool_pending:
            for sem, val in self.pool_pending:
                if self.seen[eng].get(id(sem), 0) < val:
                    self.seen[eng][id(sem)] = val
                    waits.append((sem, val))
            self.pool_pending = []
        tok = (self.sem[eng], self.cnt[eng] + 1)
        self._mark(tok, reads, writes)
        inc = None
        if sig:
            self.cnt[eng] += 1
            inc = (self.sem[eng], 1)
        self.prog[eng].append((waits, fn, inc))
        if sig and self.cnt[eng] >= self.ROLL:
            self.sem[eng] = self.new_sem("eng_" + eng)
            self.cnt[eng] = 0

    def dma(self, q, out, in_, reads=(), writes=(), prim=None):
        waits = self._waits(q, reads, writes, is_dma=True)
        if prim is None:
            prim = writes[0] if writes else reads[0]
        if prim.dsem is None:
            prim.dsem = self.new_sem("d")
        if prim.dcnt >= self.ROLL:
            prim.dsem = self.new_sem("d")
            prim.dcnt = 0
        prim.dcnt += 16
        tok = (prim.dsem, prim.dcnt)
        self._mark(tok, reads, writes)
        if q == "sp":
            self.sp_out[id(prim.dsem)] = tok
        self.prog[q].append((waits, lambda e, o=out, i=in_: e.dma_start(out=o, in_=i), (prim.dsem, 16)))
        return tok

    def barrier(self):
        toks = [(self.sem[e], self.cnt[e]) for e in ("pe", "act", "dve", "pool") if self.cnt[e] > 0]
        toks += list(self.sp_out.values())
        self.sp_out = {}
        for e in ("pe", "act", "dve", "sp"):
            w = []
            for sem, val in toks:
                if sem is self.sem.get(e):
                    continue
                if self.seen[e].get(id(sem), 0) < val:
                    self.seen[e][id(sem)] = val
                    w.append((sem, val))
            if w:
                self.prog[e].append((w, None, None))
        self.pool_pending = [t for t in toks if t[0] is not self.sem["pool"]]

    def wait_tok(self, eng, tok):
        self.prog[eng].append(([tok], None, None))

    def emit(self, block):
        m = {"pe": "tensor", "act": "scalar", "dve": "vector", "pool": "gpsimd", "sp": "sync"}
        for e in self.ENGS:
            items = self.prog[e]

            def body(engine, items=items):
                for waits, fn, inc in items:
                    for sem, val in waits:
                        engine.wait_ge(sem, val)
                    if fn is not None:
                        ins = fn(engine)
                        if inc is not None:
                            ins.then_inc(inc[0], inc[1])

            getattr(block, m[e])(body)


def build(upto=99, cc_mode="fused"):
    nc = bass.Bass("TRN2", target_bir_lowering=False)

    def dram(name, shape, dt=F32, kind="Internal"):
        return nc.dram_tensor(name, list(shape), dt, kind=kind).ap()

    x_in = dram("x", [NT, D], kind="ExternalInput")
    xh_in = dram("xh", [2 * HALO, D], kind="ExternalInput")
    vecs_in = dram("vecs", [128, NV], kind="ExternalInput")
    cvec_in = dram("cvec", [128, NCV], kind="ExternalInput")
    need = needed_weights(upto)
    w13_d = {k: dram("w13_%d%d" % k, [D, 2 * FF], kind="ExternalInput") for k in [(0, 0), (0, 1), (1, 0), (1, 1)] if ("w13_%d%d" % k) in need}
    w2_d = {k: dram("w2_%d%d" % k, [FF, D], kind="ExternalInput") for k in [(0, 0), (0, 1), (1, 0), (1, 1)] if ("w2_%d%d" % k) in need}
    if "conv_w_pw1" in need:
        wpw1_in = dram("conv_w_pw1", [D, 2 * D], kind="ExternalInput")
        wpw2_in = dram("conv_w_pw2", [D, D], kind="ExternalInput")
    if "hgrn_w_in" in need:
        win_in = dram("hgrn_w_in", [D, 5 * D], kind="ExternalInput")
        wout_in = dram("hgrn_w_out", [D, D], kind="ExternalInput")
    out_d = dram("out", [NT, D], kind="ExternalOutput")

    xsA = dram("xsA", [D, NT])
    xsB = dram("xsB", [D, NT])
    xhA = dram("xhA", [D, 2 * HALO])
    xhB = dram("xhB", [D, 2 * HALO])
    ub_d = dram("ub_d", [8, NTILE, 128, 128])
    sb_d = dram("sb_d", [8, NTILE, 128, 128])
    cc_in = dram("cc_in", [17 * 128, 128], kind=("ExternalOutput" if cc_mode == "produce" else "Internal"))
    cc_out = dram("cc_out", [NCORES * 17 * 128, 128], kind=("ExternalInput" if cc_mode == "consume" else "Internal"))

    _uc = {"n": 0}

    def U(name):
        _uc["n"] += 1
        return "%s_u%d" % (name, _uc["n"])

    es = contextlib.ExitStack()
    with es:
        S = Sched(nc)

        def sb(name, shape, dt=F32):
            return es.enter_context(nc.sbuf_tensor("s_" + name, list(shape), dt))

        wa = sb("wa", [128, NBLK * 1024], BF16)
        WB = [Buf("wa%d" % i) for i in range(NBLK)]
        vecs = sb("vecs", [128, NV]); Bvecs = Buf("vecs")
        cvec = sb("cvec", [128, NCV]); Bcvec = Buf("cvec")
        identf = sb("identf", [128, 128]); Bidf = Buf("identf")
        identb = sb("identb", [128, 128], BF16); Bidb = Buf("identb")
        ones_d = sb("ones_d", [128, 128], BF16); Bones = Buf("ones")
        ones_h = sb("ones_h", [128, 128], BF16)
        Mf = sb("Mf", [128, 128]); Mb = sb("Mb", [128, 128]); BM = Buf("M")
        rmask = sb("rmask", [128, TT]); ones_t = sb("ones_t", [128, TT]); Brm = Buf("rmask")
        lbv = sb("lbv", [128, 2, 8]); oml = sb("oml", [128, 2, 8]); noml = sb("noml", [128, 2, 8]); Blb = Buf("lb")
        Dall = sb("Dall", [128, 8, NTILE]); BDall = Buf("Dall")
        xts = [sb("xt%d" % i, [128, 8, TT]) for i in range(2)]; Bxt = [Buf("xt%d" % i) for i in range(2)]
        hbuf = [sb("h%d" % i, [128, 8, TT], BF16) for i in range(2)]; Bh = [Buf("h%d" % i) for i in range(2)]
        sqb = sb("sqb", [128, 8, TT], BF16); Bsq = Buf("sq")
        rstd = [sb("rstd%d" % i, [128, TT]) for i in range(2)]; Brstd = [Buf("rstd%d" % i) for i in range(2)]

        pbank = [es.enter_context(nc.psum_tensor("pb%d" % i, [128, 512], F32)) for i in range(8)]
        ph = []
        for i in range(8):
            ph.append(pbank[i][:, 0:256])
            ph.append(pbank[i][:, 256:512])
        _PBK = [Buf("pbank%d" % i, excl=True) for i in range(8)]
        PB = [_PBK[i // 2] for i in range(16)]

        def mm(out, lhsT, rhs, start, stop, reads, writes, sig, sgc=False):
            if sgc:
                S.op("pe", lambda e: e.matmul(out, lhsT, rhs, start=start, stop=stop, skip_group_check=True), reads=reads, writes=writes, sig=sig)
            else:
                S.op("pe", lambda e: e.matmul(out, lhsT, rhs, start=start, stop=stop), reads=reads, writes=writes, sig=sig)

        def act(out, in_, func, reads, writes, bias=None, scale=None):
            kw = {}
            if bias is not None:
                kw["bias"] = bias
            if scale is not None:
                kw["scale"] = scale
            S.op("act", lambda e: e.activation(out=out, in_=in_, func=func, **kw), reads=reads, writes=writes)

        def tt(out, in0, in1, op, reads, writes, eng="dve"):
            S.op(eng, lambda e: e.tensor_tensor(out=out, in0=in0, in1=in1, op=op), reads=reads, writes=writes)

        def ts(out, in0, s1, s2, op0, op1, reads, writes, eng="dve"):
            if s2 is None:
                S.op(eng, lambda e: e.tensor_scalar(out=out, in0=in0, scalar1=s1, scalar2=None, op0=op0), reads=reads, writes=writes)
            else:
                S.op(eng, lambda e: e.tensor_scalar(out=out, in0=in0, scalar1=s1, scalar2=s2, op0=op0, op1=op1), reads=reads, writes=writes)

        def stt(out, in0, scalar, in1, op0, op1, reads, writes, eng="dve"):
            S.op(eng, lambda e: e.scalar_tensor_tensor(out=out, in0=in0, scalar=scalar, in1=in1, op0=op0, op1=op1), reads=reads, writes=writes)

        def cp(out, in_, reads, writes, eng="dve"):
            if eng == "act":
                act(out, in_, AF.Copy, reads, writes)
            else:
                S.op(eng, lambda e: e.tensor_copy(out, in_), reads=reads, writes=writes)

        ring = {"p": 0}

        def walloc(n):
            if ring["p"] + n > NBLK:
                ring["p"] = 0
            s = ring["p"]
            ring["p"] += n
            return s, WB[s:s + n]

        def wstage(nblocks):
            if ring["p"] + nblocks > NBLK:
                ring["p"] = 0

        wsemB = [Buf("wsem%d" % i) for i in range(16)]
        wcnt = {"k": 0}

        def wprim():
            b = wsemB[wcnt["k"] % 16]
            wcnt["k"] += 1
            return b

        def wload_cols(src2d, col0, ncols):
            nb = (8 * ncols + 1023) // 1024
            s, bufs = walloc(nb)
            view = wa[:, s * 1024: s * 1024 + 8 * ncols].rearrange("p (c n) -> p c n", c=8)
            pb_ = wprim()
            S.dma("pool", view, src2d[:, col0:col0 + ncols].rearrange("(c p) n -> p c n", p=128), writes=bufs + [pb_], prim=pb_)
            return view, bufs

        def wload_rows(src2d, row0, nrows):
            f = nrows // 128
            s, bufs = walloc(f)
            view = wa[:, s * 1024: (s + f) * 1024].rearrange("p (f n) -> p f n", f=f)
            pb_ = wprim()
            S.dma("pool", view, src2d[row0:row0 + nrows, :].rearrange("(f p) n -> p f n", p=128), writes=bufs + [pb_], prim=pb_)
            return view, bufs

        S.dma("sp", vecs[:], vecs_in, writes=[Bvecs])
        S.dma("sp", cvec[:], cvec_in, writes=[Bcvec])
        S.op("pool", lambda e: e.memset(identf[:], 0.0), writes=[Bidf])
        S.op("pool", lambda e: e.affine_select(out=identf[:], in_=identf[:], pattern=[[-1, 128]], compare_op=ALU.not_equal,
                                               fill=1.0, base=0, channel_multiplier=1), reads=[Bidf], writes=[Bidf])
        cp(identb[:], identf[:], [Bidf], [Bidb], eng="pool")
        S.op("pool", lambda e: e.memset(ones_d[:], 1.0 / 1024.0), writes=[Bones])
        S.op("pool", lambda e: e.memset(ones_h[:], 1.0 / 128.0), writes=[Bones])
        S.op("pool", lambda e: e.memset(Mf[:], 1.0), writes=[BM])
        S.op("pool", lambda e: e.affine_select(out=Mf[:], in_=Mf[:], pattern=[[1, 128]], compare_op=ALU.is_ge,
                                               fill=0.0, base=0, channel_multiplier=-1), reads=[BM], writes=[BM])
        S.op("pool", lambda e: e.memset(Mf[0:64, 64:128], 0.0), reads=[BM], writes=[BM])
        S.op("pool", lambda e: e.memset(Mb[:], 1.0), reads=[BM], writes=[BM])
        S.op("pool", lambda e: e.affine_select(out=Mb[:], in_=Mb[:], pattern=[[-1, 128]], compare_op=ALU.is_ge,
                                               fill=0.0, base=0, channel_multiplier=1), reads=[BM], writes=[BM])
        S.op("pool", lambda e: e.memset(Mb[64:128, 0:64], 0.0), reads=[BM], writes=[BM])
        S.op("pool", lambda e: e.memset(rmask[:], 1.0), writes=[Brm])
        S.op("pool", lambda e: e.memset(rmask[:].rearrange("p (c j) -> p c j", j=64)[:, :, 0:1], 0.0), reads=[Brm], writes=[Brm])
        S.op("pool", lambda e: e.memset(ones_t[:], 1.0), reads=[Brm], writes=[Brm])
        hl = vecs[:, V_HLB:V_HLB + 32].rearrange("p (d l h) -> p d l h", d=2, l=2)
        tt(lbv[:], hl[:, :, 1, :], hl[:, :, 0, :], ALU.subtract, [Bvecs], [Blb])
        act(lbv[:], lbv[:], AF.Sigmoid, [Blb], [Blb])
        ts(oml[:], lbv[:], -1.0, 1.0, ALU.mult, ALU.add, [Blb], [Blb])
        ts(noml[:], oml[:], -1.0, None, ALU.mult, None, [Blb], [Blb])
        homl = sb("homl", [128, 2, 8]); c1v = sb("c1v", [128, 2, 8])
        ts(homl[:], oml[:], 0.5, None, ALU.mult, None, [Blb], [Blb])
        tt(c1v[:], homl[:], lbv[:], ALU.add, [Blb], [Blb])

        epsc = sb("epsc", [128, 1]); Beps = Buf("epsc")
        S.op("pool", lambda e: e.memset(epsc[:], EPS), writes=[Beps])

        def rsqrt_eps(out, in_, reads, Bout):
            act(out, in_, AF.Sqrt, list(reads) + [Beps], [Bout], bias=epsc[:, 0:1])
            S.op("dve", lambda e: e.reciprocal(out, out), reads=[Bout], writes=[Bout])

        nstat = {"i": 0}

        def rmsnorm(xt, Bx, N, gcol, h, Bhh):
            i = nstat["i"] % 2
            nstat["i"] += 1
            st, Bst = ph[14 + i][:, :N], PB[14 + i]
            r, Br = rstd[i][:, :N], Brstd[i]
            tt(sqb[:, :, :N], xt[:, :, :N], xt[:, :, :N], ALU.mult, [Bx], [Bsq])
            for c in range(8):
                mm(st, ones_d[:], sqb[:, c, :N], c == 0, c == 7, [Bones, Bsq], [Bst], c == 7)
            rsqrt_eps(r, st, [Bst], Br)
            for c in range(8):
                stt(h[:, c, :N], xt[:, c, :N], vecs[:, gcol + c:gcol + c + 1], r, ALU.mult, ALU.mult, [Bx, Br, Bvecs], [Bhh])

        xcnt = {"i": 0}

        def next_xt():
            i = xcnt["i"] % 2
            xcnt["i"] += 1
            return xts[i], Bxt[i], hbuf[i], Bh[i]

        def dview(dr, off, N):
            return dr[:, off:off + N].rearrange("(c p) n -> p c n", p=128)

        Bx = {}

        def dbuf(name, t):
            k = (name, 0)
            if k not in Bx:
                Bx[k] = Buf("%s_%s" % (name, t))
            return Bx[k]

        def stage_in():
            with nc.sbuf_tensor(U("xin"), [128, 2, D], F32) as xin:
                Bxin = Buf("xin")
                jobs = [(x_in, t * TT, TT, xsA, t * TT, ("A", t)) for t in range(NTILE)] + [(xh_in, 0, 2 * HALO, xhA, 0, ("hA", 0))]
                for src, soff, N, dst, doff, key in jobs:
                    nb = (N + 127) // 128
                    rows = min(N, 128)
                    S.dma("sp", xin[:rows, :nb, :], src[soff:soff + N, :].rearrange("(b p) d -> p b d", p=rows), writes=[Bxin])
                    xt, Bxx, _, _ = next_xt()
                    for c2 in range(4):
                        for ci in range(2):
                            c = 2 * c2 + ci
                            for b in range(nb):
                                S.op("pe", lambda e, c=c, b=b, rows=rows: e.transpose(ph[2 + c][:, b * 128:b * 128 + rows], xin[:rows, b, c * 128:(c + 1) * 128], identf[:rows, :rows]),
                                     reads=[Bxin, Bidf], writes=[PB[2 + c]], sig=(b == nb - 1 and ci == 1))
                        cp(xt[:, 2 * c2:2 * c2 + 2, :N], pbank[1 + c2][:, :].rearrange("p (h n) -> p h n", h=2)[:, :, :N], [PB[2 + 2 * c2]], [Bxx], eng=("act" if c2 % 2 else "dve"))
                    S.dma("sp", dview(dst, doff, N), xt[:, :, :N], reads=[Bxx], writes=[dbuf(*key)])

        def stage_ffn(l, i, src, sname, dst, dname, halo=None):
            gcol = V_NG + (l * 3 + (0 if i == 0 else 2)) * 8
            S.barrier()
            w13 = w13_d[(l, i)]
            w2 = w2_d[(l, i)]
            units = {}
            wstage(66)

            def unit(g):
                if g not in units:
                    G = 4 if g < 5 else 2
                    gv, gb = wload_cols(w13, g * 512, G * 128)
                    uv, ub = wload_cols(w13, FF + g * 512, G * 128)
                    dv, db = wload_rows(w2, g * 512, G * 128)
                    units[g] = (gv, gb, uv, ub, dv, db)
                return units[g]

            with nc.sbuf_tensor(U("sg"), [128, 2, TT], F32) as sg, nc.sbuf_tensor(U("actb"), [128, 3, TT], BF16) as actb:
                Bsg = [Buf("sg0"), Buf("sg1")]
                Bact = [Buf("act%d" % k) for k in range(3)]
                jobs = [(src, sname, dst, dname, t, t * TT, TT) for t in range(NTILE)]
                if halo is not None:
                    jobs.append((halo[0], halo[1], halo[2], halo[3], 0, 0, 2 * HALO))
                def prep(job):
                    (s_d, s_n, d_d, d_n, t, off, N) = job
                    xt, Bxx, h, Bhh = next_xt()
                    S.dma("sp", xt[:, :, :N], dview(s_d, off, N), reads=[dbuf(s_n, t)], writes=[Bxx])
                    rmsnorm(xt, Bxx, N, gcol, h, Bhh)
                    return xt, Bxx, h, Bhh

                cur = prep(jobs[0])
                for idx, (s_d, s_n, d_d, d_n, t, off, N) in enumerate(jobs):
                    xt, Bxx, h, Bhh = cur
                    LAG = 1

                    def ymm(j):
                        g, jj = j // 4, j % 4
                        dv, db = unit(g)[4], unit(g)[5]
                        a = j % 3
                        for dc in range(8):
                            mm(ph[dc][:, :N], dv[:, jj, dc * 128:(dc + 1) * 128], actb[:, a, :N], (j == 0 and dc % 2 == 0), j == 21,
                               [Bact[a]] + db, [PB[dc]], (dc == 7) or (j == 21), sgc=True)

                    for j in range(22 + LAG):
                        if j == 8 and idx + 1 < len(jobs):
                            cur = prep(jobs[idx + 1])
                        if j < 22:
                            g, jj = j // 4, j % 4
                            gv, gb, uv, ub = unit(g)[:4]
                            s3 = j % 3
                            pg, pu = ph[8 + 2 * s3][:, :N], ph[9 + 2 * s3][:, :N]
                            Bpg, Bpu = PB[8 + 2 * s3], PB[9 + 2 * s3]
                            for c in range(8):
                                mm(pg, gv[:, c, jj * 128:(jj + 1) * 128], h[:, c, :N], c == 0, c == 7, [Bhh] + gb, [Bpg], c == 7)
                            for c in range(8):
                                mm(pu, uv[:, c, jj * 128:(jj + 1) * 128], h[:, c, :N], c == 0, c == 7, [Bhh] + ub, [Bpu], c == 7)
                            act(sg[:, j % 2, :N], pg, AF.Silu, [Bpg], [Bsg[j % 2]])
                            tt(actb[:, j % 3, :N], sg[:, j % 2, :N], pu, ALU.mult, [Bsg[j % 2], Bpu], [Bact[j % 3]])
                        if j >= LAG:
                            ymm(j - LAG)
                    for dc in range(8):
                        stt(xt[:, dc, :N], ph[dc][:, :N], 0.5, xt[:, dc, :N], ALU.mult, ALU.add, [PB[dc], Bxx], [Bxx])
                    S.dma("sp", dview(d_d, off, N), xt[:, :, :N], reads=[Bxx], writes=[dbuf(d_n, t)])

        def stage_conv(src, sname, hsrc, hname, dst, dname):
            gcol = V_NG + 1 * 8
            S.barrier()
            W = TT + 32
            with nc.sbuf_tensor(U("ubuf"), [128, 8, W], BF16) as ubuf, \
                    nc.sbuf_tensor(U("sgc"), [128, 2, TT], F32) as sgc, \
                    nc.sbuf_tensor(U("vbuf"), [128, 8, TT], F32) as vbuf, \
                    nc.sbuf_tensor(U("v16"), [128, 8, TT], BF16) as v16, \
                    nc.sbuf_tensor(U("sil"), [128, 8, TT], BF16) as sil, \
                    nc.sbuf_tensor(U("lnst"), [128, 4, TT], F32) as lnst, \
                    nc.sbuf_tensor(U("xw"), [128, 8, TT], F32) as xw:
                Bub, Bsgc = Buf("ubuf"), [Buf("sgc0"), Buf("sgc1")]
                Bvb, Bv16, Bq16, Bsil, Bln, Bxw = Buf("vbuf"), Buf("v16"), Bsq, Buf("sil"), Buf("lnst"), Buf("xw")
                q16 = sqb
                wstage(55)
                pw1 = [wload_cols(wpw1_in, g * 512, 512) for g in range(4)]
                pw2 = [wload_cols(wpw2_in, g * 512, 512) for g in range(2)]
                ds_, dbufs = walloc(31)
                diag = wa[:, ds_ * 1024:(ds_ + 31) * 1024]
                for c in range(8):
                    for j in range(31):
                        m = c * 31 + j
                        ts(diag[:, m * 128:(m + 1) * 128], identb[:], vecs[:, V_WDW + m:V_WDW + m + 1], None, ALU.mult, None,
                           [Bidb, Bvecs], dbufs)
                S.op("pool", lambda e: e.memset(ubuf[:], 0.0), writes=[Bub])

                def make_u(s_d, s_n, t, off, N, ucol, hmask_col=None):
                    xt, Bxx, h, Bhh = next_xt()
                    S.dma("sp", xt[:, :, :N], dview(s_d, off, N), reads=[dbuf(s_n, t)], writes=[Bxx])
                    rmsnorm(xt, Bxx, N, gcol, h, Bhh)
                    for j in range(8):
                        s3 = j % 2
                        pa, pg = ph[2 + 2 * s3][:, :N], ph[3 + 2 * s3][:, :N]
                        Bpa, Bpg = PB[2 + 2 * s3], PB[3 + 2 * s3]
                        av, ab = pw1[j // 4]
                        gv, gb = pw1[2 + j // 4]
                        jj = j % 4
                        for c in range(8):
                            mm(pa, av[:, c, jj * 128:(jj + 1) * 128], h[:, c, :N], c == 0, c == 7, [Bhh] + ab, [Bpa], c == 7)
                        for c in range(8):
                            mm(pg, gv[:, c, jj * 128:(jj + 1) * 128], h[:, c, :N], c == 0, c == 7, [Bhh] + gb, [Bpg], c == 7)
                        act(sgc[:, j % 2, :N], pg, AF.Sigmoid, [Bpg, Bvecs], [Bsgc[j % 2]], bias=vecs[:, V_BPW1 + 8 + j:V_BPW1 + 9 + j])
                        stt(ubuf[:, j, ucol:ucol + N], pa, vecs[:, V_BPW1 + j:V_BPW1 + j + 1], sgc[:, j % 2, :N], ALU.add, ALU.mult,
                            [Bpa, Bsgc[j % 2], Bvecs], [Bub])
                    if hmask_col is not None:
                        for j in range(8):
                            tt(ubuf[:, j, ucol:ucol + N], ubuf[:, j, ucol:ucol + N], cvec[:, hmask_col:hmask_col + N], ALU.mult, [Bub, Bcvec], [Bub])

                def window1(tok0, Wn, col0):
                    S.dma("sp", xw[:, :, :Wn], dview(src, tok0, Wn), reads=[dbuf(sname, t) for t in range(NTILE)], writes=[Bxw])
                    for c in range(8):
                        pv, Bpv = ph[6 + 2 * (c % 2)][:, :Wn], PB[6 + 2 * (c % 2)]
                        for j in range(31):
                            m = c * 31 + j
                            mm(pv, diag[:, m * 128:(m + 1) * 128], ubuf[:, c, col0 - 15 + j: col0 - 15 + j + Wn], j == 0, j == 30,
                               [Bub] + dbufs, [Bpv], j == 30)
                        act(vbuf[:, c, :Wn], pv, AF.Identity, [Bpv, Bvecs], [Bvb], bias=vecs[:, V_BDW + c:V_BDW + c + 1])

                def window2(tok0, Wn, col0):
                    cp(v16[:, :, :Wn], vbuf[:, :, :Wn], [Bvb], [Bv16])
                    tt(q16[:, :, :Wn], vbuf[:, :, :Wn], vbuf[:, :, :Wn], ALU.mult, [Bvb], [Bq16])
                    p1, p2 = ph[10][:, :Wn], ph[11][:, :Wn]
                    for c in range(8):
                        mm(p1, ones_d[:], v16[:, c, :Wn], c == 0, c == 7, [Bones, Bv16], [PB[10]], c == 7)
                    for c in range(8):
                        mm(p2, ones_d[:], q16[:, c, :Wn], c == 0, c == 7, [Bones, Bq16], [PB[11]], c == 7)
                    mean, msq, var, rs = lnst[:, 0, :Wn], lnst[:, 1, :Wn], lnst[:, 2, :Wn], lnst[:, 3, :Wn]
                    cp(mean, p1, [PB[10]], [Bln])
                    tt(msq, mean, mean, ALU.mult, [Bln], [Bln])
                    tt(var, p2, msq, ALU.subtract, [PB[11], Bln], [Bln])
                    rsqrt_eps(rs, var, [Bln], Bln)
                    for c in range(8):
                        tt(vbuf[:, c, :Wn], vbuf[:, c, :Wn], mean, ALU.subtract, [Bvb, Bln], [Bvb])
                        tt(vbuf[:, c, :Wn], vbuf[:, c, :Wn], rs, ALU.mult, [Bvb, Bln], [Bvb])
                        act(sil[:, c, :Wn], vbuf[:, c, :Wn], AF.Silu, [Bvb, Bvecs], [Bsil],
                            bias=vecs[:, V_LNB + c:V_LNB + c + 1], scale=vecs[:, V_LNG + c:V_LNG + c + 1])
                    for dc in range(8):
                        po, Bpo = (ph[12][:, :Wn], PB[12]) if dc % 2 == 0 else (ph[0][:, :Wn], PB[0])
                        wv, wb_ = pw2[dc // 4]
                        for c in range(8):
                            mm(po, wv[:, c, (dc % 4) * 128:(dc % 4 + 1) * 128], sil[:, c, :Wn], c == 0, c == 7, [Bsil] + wb_, [Bpo], c == 7)
                        stt(xw[:, dc, :Wn], po, vecs[:, V_BPW2 + dc:V_BPW2 + dc + 1], xw[:, dc, :Wn], ALU.add, ALU.add, [Bpo, Bxw, Bvecs], [Bxw])
                    S.dma("sp", dview(dst, tok0, Wn), xw[:, :, :Wn], reads=[Bxw], writes=[dbuf(dname, t) for t in range(NTILE)])

                make_u(hsrc, hname, 0, 0, HALO, 16, hmask_col=C_HM)
                make_u(src, sname, 0, 0, TT, 32)
                for t in range(NTILE):
                    wargs = (0, TT - 16, 32) if t == 0 else (t * TT - 16, TT, 16)
                    window1(*wargs)
                    cp(ubuf[:, :, 0:32], ubuf[:, :, TT:TT + 32], [Bub], [Bub], eng="pool")
                    if t + 1 < NTILE:
                        make_u(src, sname, t + 1, (t + 1) * TT, TT, 32)
                    else:
                        make_u(hsrc, hname, 0, HALO, HALO, 32, hmask_col=C_HM + HALO)
                    window2(*wargs)
                window1(NT - 16, 16, 16)
                window2(NT - 16, 16, 16)

        def stage_hgrn(src, sname, dst, dname):
            gcol = V_NG + (1 * 3 + 1) * 8
            NCH = TT // 64
            NSB = TT // 128
            TAIL0 = 48
            S.barrier()

            def run_lanes(gens):
                gens = list(gens)
                while gens:
                    nxt = []
                    for g in gens:
                        try:
                            next(g)
                            nxt.append(g)
                        except StopIteration:
                            pass
                    gens = nxt

            with contextlib.ExitStack() as hs_:
                def hsb(name, shape, dt=F32):
                    return hs_.enter_context(nc.sbuf_tensor(U(name), list(shape), dt))
                wstage(48)
                S.op("pool", lambda e: e.memset(wa[:, TAIL0 * 1024:TAIL0 * 1024 + 16], 0.0), writes=WB[TAIL0:NBLK])
                tail_tok = WB[TAIL0].w
                lane_bufs = []

                def LB(name):
                    b = Buf(name)
                    b.w = tail_tok
                    lane_bufs.append(b)
                    return b
                tail = {"o": TAIL0 * 1024}

                def tail_f32(n):
                    o = tail["o"]
                    tail["o"] += 2 * n
                    assert tail["o"] <= NBLK * 1024
                    return wa[:, o:o + 2 * n].bitcast(F32)

                def tail_bf(n):
                    o = tail["o"]
                    tail["o"] += n
                    assert tail["o"] <= NBLK * 1024
                    return wa[:, o:o + n]
                gA = [tail_f32(TT) for _ in range(4)]; BgA = [LB("gA%d" % i) for i in range(4)]
                gB = [tail_f32(TT) for _ in range(4)]; BgB = [LB("gB%d" % i) for i in range(4)]
                gC = [tail_f32(TT) for _ in range(4)]; BgC = [LB("gC%d" % i) for i in range(4)]
                gD = [tail_f32(TT) for _ in range(4)]; BgD_ = [LB("gD%d" % i) for i in range(4)]
                Qt = [tail_bf(TT) for _ in range(8)]; BQt = [LB("Qt%d" % i) for i in range(8)]
                Kt = [tail_bf(TT) for _ in range(8)]; BKt = [LB("Kt%d" % i) for i in range(8)]
                Kh = [tail_bf(TT) for _ in range(8)]; BKh = [LB("Kh%d" % i) for i in range(8)]
                Khtm_ = [tail_bf(TT) for _ in range(8)]; BKhtm = [LB("Khtm%d" % i) for i in range(8)]
                Khtm = [k.rearrange("p (s n) -> p s n", s=NSB) for k in Khtm_]
                erT = hsb("erT", [128, 8, NCH]); Ber = [Buf("er%d" % i) for i in range(8)]
                qbT = hsb("qbT", [128, 4, TT]); Bqb = [Buf("qb%d" % i) for i in range(4)]
                gsT = hsb("gsT", [128, 4, TT]); Bgs = [Buf("gs%d" % i) for i in range(4)]
                osbT = tail_f32(2 * TT).rearrange("p (s n) -> p s n", s=2); Bosb = [LB("osb%d" % i) for i in range(2)]
                osqT = hsb("osqT", [128, 2, TT], BF16); Bosq = [Buf("osq%d" % i) for i in range(2)]
                Vtm = hsb("Vtm", [128, NSB, D], BF16); BV = Buf("Vtm")
                Pm = hsb("Pm", [128, 2, 2, 128], BF16); BPm = [Buf("Pm0"), Buf("Pm1")]
                Sst = hsb("Sst", [128, 2, 128]); BSst = [Buf("Sst0"), Buf("Sst1")]
                Sbf = hsb("Sbf", [128, 2, 2, NCH + 1, 128], BF16); BSbf = [Buf("Sbf0"), Buf("Sbf1")]
                Sfr = hsb("Sfr", [128, 8, 128]); BSfr = [Buf("Sfr%d" % i) for i in range(8)]
                Sbt = hsb("Sbt", [128, 8, 128]); BSbt = [Buf("Sbt%d" % i) for i in range(8)]
                Cb = hsb("Cb", [128, 8]); BCb = Buf("Cb")
                Cf = hsb("Cf", [128, 8]); BCf = Buf("Cf")
                Usb = hsb("Usb", [128, 4, 128]); BUsb = [Buf("Usb%d" % i) for i in range(4)]
                on = hsb("on", [128, 8, TT], BF16); Bon = Buf("on")
                gU = hsb("gU", [128, 2, 128]); BgU = [Buf("gU0"), Buf("gU1")]
                gDD = hsb("gDD", [128, 128]); BgDD = Buf("gDD")
                rsl = tail_f32(2 * TT).rearrange("p (s n) -> p s n", s=2); Brsl = [LB("rsl0"), LB("rsl1")]
                hx = hsb("hx", [128, 2, TT]); Bhx = [Buf("hx0"), Buf("hx1")]

                win_u = {}

                def win(gi):
                    if gi not in win_u:
                        win_u[gi] = wload_cols(win_in, gi * 512, 512)
                    return win_u[gi]

                def rsqrt_le(out, in_, reads, Bout):
                    act(out, in_, AF.Ln, list(reads) + [Beps], [Bout], bias=epsc[:, 0:1])
                    act(out, out, AF.Exp, [Bout], [Bout], scale=-0.5)

                def load_norm(t):
                    xt, Bxx, h, Bhh = next_xt()
                    S.dma("sp", xt[:], dview(src, t * TT, TT), reads=[dbuf(sname, t)], writes=[Bxx])
                    i = nstat["i"] % 2
                    nstat["i"] += 1
                    st, Bst = ph[14 + i], PB[14 + i]
                    tt(sqb[:], xt[:], xt[:], ALU.mult, [Bxx], [Bsq])
                    for c in range(8):
                        mm(st, ones_d[:], sqb[:, c, :], c == 0, c == 7, [Bones, Bsq], [Bst], c == 7)
                    rsqrt_le(rstd[i][:], st, [Bst], Brstd[i])
                    for c in range(8):
                        stt(h[:, c, :], xt[:, c, :], vecs[:, gcol + c:gcol + c + 1], rstd[i][:], ALU.mult, ALU.mult, [Bxx, Brstd[i], Bvecs], [Bhh])
                    for sbk in range(NSB):
                        for cg in range(2):
                            wv, wb_ = win(2 + cg)
                            pv = pbank[1][:, :]
                            for c in range(8):
                                mm(pv, h[:, c, sbk * 128:(sbk + 1) * 128], wv[:, c, :], c == 0, c == 7, [Bhh] + wb_, [PB[2]], c == 7)
                            cp(Vtm[:, sbk, cg * 512:(cg + 1) * 512], pv, [PB[2]], [BV], eng=("act" if cg else "dve"))
                    return xt, Bxx, h, Bhh

                pslot = {"i": 0}

                def proj(sec, hd, h, Bhh):
                    s3 = pslot["i"] % 3
                    pslot["i"] += 1
                    pz, Bpz = ph[4 + 2 * s3], PB[4 + 2 * s3]
                    wv, wb_ = win(2 * sec + hd // 4)
                    for c in range(8):
                        mm(pz, wv[:, c, (hd % 4) * 128:(hd % 4 + 1) * 128], h[:, c, :], c == 0, c == 7, [Bhh] + wb_, [Bpz], c == 7)
                    return pz, Bpz

                def gates_gen(dr, hd, h, Bhh, li):
                    A, B_ = gA[li], gB[li]
                    pz, Bpz = proj(2 + dr, hd, h, Bhh)
                    act(A, pz, AF.Tanh, [Bpz], [BgA[li]], scale=0.5)
                    yield
                    ts(B_, A, homl[:, dr, hd:hd + 1], c1v[:, dr, hd:hd + 1], ALU.mult, ALU.add, [BgA[li], Blb], [BgB[li]])
                    yield
                    ts(A, B_, -1.0, 1.0, ALU.mult, ALU.add, [BgB[li]], [BgA[li]])
                    yield
                    act(B_, B_, AF.Ln, [BgB[li]], [BgB[li]])
                    yield

                def transp_gen(pi, li):
                    for sbk in range(NSB):
                        ptr = pbank[5][:, :].bitcast(BF16)[:, (li * NSB + sbk) * 128:(li * NSB + sbk + 1) * 128]
                        S.op("pe", lambda e, ptr=ptr, sbk=sbk: e.transpose(ptr, Kh[pi][:, sbk * 128:(sbk + 1) * 128], identb[:]),
                             reads=[BKh[pi], Bidb], writes=[PB[10]])
                        yield
                        cp(Khtm[pi][:, sbk, :], ptr, [PB[10]], [BKhtm[pi]], eng="act")
                        yield

                S.op("pool", lambda e: e.memset(Sfr[:], 0.0), writes=BSfr)
                S.op("pool", lambda e: e.memset(Sbt[:], 0.0), writes=BSbt)
                S.op("pool", lambda e: e.memset(Cb[:], 1.0), writes=[BCb])
                S.op("pool", lambda e: e.memset(Cf[:], 1.0), writes=[BCf])

                def laneA(t, hd, dr, li, pi, h, Bhh):
                    A, B_, C_, D_ = gA[li], gB[li], gC[li], gD[li]
                    yield from gates_gen(dr, hd, h, Bhh, li)
                    S.op("dve", lambda e: e.tensor_tensor_scan(out=C_, data0=ones_t[:], data1=B_, initial=0.0, op0=ALU.mult, op1=ALU.add),
                         reads=[BgB[li], Brm], writes=[BgC[li]])
                    yield
                    if dr == 0:
                        act(D_, C_, AF.Exp, [BgC[li]], [BgD_[li]], bias=C_[:, TT - 1:TT], scale=-1.0)
                        dcol, Bd = erT[:, pi, 0:1], Ber[pi]
                    else:
                        tt(D_, C_, B_, ALU.subtract, [BgC[li], BgB[li]], [BgD_[li]])
                        yield
                        act(D_, D_, AF.Exp, [BgD_[li]], [BgD_[li]])
                        dcol, Bd = Dall[:, hd, t:t + 1], BDall
                    yield
                    act(dcol, C_[:, TT - 1:TT], AF.Exp, [BgC[li]], [Bd])
                    yield
                    tt(Kh[pi], A, D_, ALU.mult, [BgA[li], BgD_[li]], [BKh[pi]])
                    yield
                    yield from transp_gen(pi, li)
                    pu, Bpu = ph[12 + li // 2][:, (li % 2) * 128:(li % 2 + 1) * 128], PB[12]
                    for sbk in range(NSB):
                        mm(pu, Khtm[pi][:, sbk, :], Vtm[:, sbk, hd * 128:(hd + 1) * 128], sbk == 0, sbk == NSB - 1, [BKhtm[pi], BV], [Bpu], sbk == NSB - 1)
                    yield
                    if dr == 0:
                        stt(Sfr[:, hd, :], Sfr[:, hd, :], erT[:, pi, 0:1], pu, ALU.mult, ALU.add, [BSfr[hd], Ber[pi], Bpu], [BSfr[hd]])
                        yield
                        tt(Cf[:, hd:hd + 1], Cf[:, hd:hd + 1], erT[:, pi, 0:1], ALU.mult, [BCf, Ber[pi]], [BCf])
                    else:
                        cp(Usb[:, li, :], pu, [Bpu], [BUsb[li]])
                        yield
                        S.dma("sp", ub_d[hd, t], Usb[:, li, :], reads=[BUsb[li]], writes=[dbuf("ub%d" % hd, 0)])
                        stt(Sbt[:, hd, :], pu, Cb[:, hd:hd + 1], Sbt[:, hd, :], ALU.mult, ALU.add, [Bpu, BCb, BSbt[hd]], [BSbt[hd]])
                        yield
                        tt(Cb[:, hd:hd + 1], Cb[:, hd:hd + 1], Dall[:, hd, t:t + 1], ALU.mult, [BCb, BDall], [BCb])
                    yield

                for t in range(NTILE):
                    xt, Bxx, h, Bhh = load_norm(t)
                    for gi in range(4):
                        lanes = []
                        for hh in range(2):
                            for dr in range(2):
                                li = hh * 2 + dr
                                lanes.append(laneA(t, 2 * gi + hh, dr, li, (gi % 2) * 4 + li, h, Bhh))
                        run_lanes(lanes)

                Bcc_in, Bcc_out = Buf("cc_in"), Buf("cc_out")
                ccv = cc_in.rearrange("(m p) v -> p m v", p=128)
                S.dma("sp", ccv[:, 0:8, :], Sfr[:], reads=BSfr, writes=[Bcc_in])
                S.dma("sp", ccv[:, 8:16, :], Sbt[:], reads=BSbt, writes=[Bcc_in])
                S.op("pool", lambda e: e.memset(gDD[:], 0.0), writes=[BgDD])
                cp(gDD[:, 0:8], Cf[:], [BCf, BgDD], [BgDD])
                cp(gDD[:, 8:16], Cb[:], [BCb, BgDD], [BgDD])
                S.dma("sp", ccv[:, 16, :], gDD[:], reads=[BgDD], writes=[Bcc_in])
                if cc_mode == "produce":
                    S.wait_tok("sp", Bcc_in.w)
                    return
                waits = S._waits("pool", [Bcc_in], [Bcc_out], is_dma=True)
                csem = S.new_sem("cc")
                tokc = (csem, 1)
                S._mark(tokc, [Bcc_in], [Bcc_out])
                if cc_mode == "fused":
                    S.prog["pool"].append((waits, lambda e: e.collective_compute("AllGather", ALU.bypass, replica_groups=[list(range(NCORES))],
                                                                                 ins=[cc_in], outs=[cc_out]), (csem, 1)))
                else:
                    Bcc_out.w = None
                S.op("pool", lambda e: e.memset(Sfr[:], 0.0), reads=BSfr, writes=BSfr)
                S.op("pool", lambda e: e.memset(Sbt[:], 0.0), reads=BSbt, writes=BSbt)
                for dr, order in ((0, range(NCORES)), (1, range(NCORES - 1, -1, -1))):
                    Sx, BSx = (Sfr, BSfr) if dr == 0 else (Sbt, BSbt)
                    for r in order:
                        base = r * 17 * 128
                        S.dma("sp", gDD[:], cc_out[base + 2048: base + 2176, :], reads=[Bcc_out], writes=[BgDD])
                        mcol = cvec[:, C_RM + dr * 8 + r: C_RM + dr * 8 + r + 1]
                        omcol = cvec[:, C_RM1 + dr * 8 + r: C_RM1 + dr * 8 + r + 1]
                        ts(gDD[:, 16:24], gDD[:, dr * 8: dr * 8 + 8], mcol, omcol, ALU.mult, ALU.add, [BgDD, Bcvec], [BgDD])
                        for hd in range(8):
                            row0 = base + (dr * 8 + hd) * 128
                            S.dma("sp", gU[:, hd % 2, :], cc_out[row0:row0 + 128, :], reads=[Bcc_out], writes=[BgU[hd % 2]])
                            ts(gU[:, hd % 2, :], gU[:, hd % 2, :], mcol, None, ALU.mult, None, [BgU[hd % 2], Bcvec], [BgU[hd % 2]])
                            stt(Sx[:, hd, :], Sx[:, hd, :], gDD[:, 16 + hd:17 + hd], gU[:, hd % 2, :], ALU.mult, ALU.add, [BSx[hd], BgDD, BgU[hd % 2]], [BSx[hd]])
                for t in range(NTILE - 1, -1, -1):
                    for hd in range(8):
                        if t < NTILE - 1:
                            S.dma("sp", Usb[:, hd % 4, :], ub_d[hd, t + 1], reads=[dbuf("ub%d" % hd, 0)], writes=[BUsb[hd % 4]])
                            stt(Sbt[:, hd, :], Sbt[:, hd, :], Dall[:, hd, t + 1:t + 2], Usb[:, hd % 4, :], ALU.mult, ALU.add, [BSbt[hd], BDall, BUsb[hd % 4]], [BSbt[hd]])
                        S.dma("sp", sb_d[hd, t], Sbt[:, hd, :], reads=[BSbt[hd]], writes=[dbuf("sb%d" % hd, 0)])

                wout_u = {}

                def wout(g):
                    if g not in wout_u:
                        wout_u[g] = wload_cols(wout_in, g * 512, 512)
                    return wout_u[g]

                Mm = (Mf, Mb)

                def headB(hd, hs, h, Bhh):
                    for sec, dstT, Bd in ((0, qbT, Bqb), (4, gsT, Bgs)):
                        pq, Bpq = proj(sec, hd, h, Bhh)
                        act(dstT[:, hs, :], pq, AF.Tanh, [Bpq], [Bd[hs]], scale=0.5)
                        act(hx[:, hs % 2, :], pq, AF.Copy, [Bpq], [Bhx[hs % 2]], scale=0.5)
                        yield
                        stt(dstT[:, hs, :], dstT[:, hs, :], 1.0, hx[:, hs % 2, :], ALU.add, ALU.mult, [Bd[hs], Bhx[hs % 2]], [Bd[hs]])
                        yield

                def laneB(t, hd, dr, li, pi, hs, h, Bhh):
                    A, B_, C_, D_ = gA[li], gB[li], gC[li], gD[li]
                    yield from gates_gen(dr, hd, h, Bhh, li)
                    S.op("dve", lambda e: e.tensor_tensor_scan(out=C_, data0=rmask[:], data1=B_, initial=0.0, op0=ALU.mult, op1=ALU.add),
                         reads=[BgB[li], Brm], writes=[BgC[li]])
                    yield
                    C3 = C_.rearrange("p (c j) -> p c j", j=64)
                    if dr == 1:
                        D3 = D_.rearrange("p (c j) -> p c j", j=64)
                        tt(D3, C3[:, :, 63:64].to_broadcast([128, NCH, 64]), C3, ALU.subtract, [BgC[li]], [BgD_[li]])
                        yield
                        tt(C_, D_, B_, ALU.add, [BgD_[li], BgB[li]], [BgC[li]])
                        yield
                        rcol = C3[:, :, 0]
                    else:
                        rcol = C3[:, :, 63]
                    act(erT[:, pi, :], rcol, AF.Exp, [BgC[li]], [Ber[pi]])
                    yield
                    act(D_, C_, AF.Exp, [BgC[li]], [BgD_[li]])
                    yield
                    tt(Qt[pi], qbT[:, hs, :], D_, ALU.mult, [Bqb[hs], BgD_[li]], [BQt[pi]])
                    yield
                    act(D_, C_, AF.Exp, [BgC[li]], [BgD_[li]], scale=-1.0)
                    yield
                    tt(A, A, D_, ALU.mult, [BgA[li], BgD_[li]], [BgA[li]])
                    yield
                    cp(Kt[pi], A, [BgA[li]], [BKt[pi]], eng="act")
                    yield
                    tt(Kh[pi].rearrange("p (c j) -> p c j", j=64), A.rearrange("p (c j) -> p c j", j=64),
                       erT[:, pi, :].unsqueeze(2).to_broadcast([128, NCH, 64]), ALU.mult, [BgA[li], Ber[pi]], [BKh[pi]])
                    yield
                    yield from transp_gen(pi, li)

                def postB(t, hd, pf, pb, hs, par, h, Bhh):
                    cp(Sbf[:, par, 0, 0, :], Sfr[:, hd, :], [BSfr[hd]], [BSbf[par]], eng="pool")
                    yield
                    for ch in range(NCH):
                        sbk, rows = ch // 2, slice((ch % 2) * 64, (ch % 2) * 64 + 64)
                        pu, Bpu = ph[12][:, (ch % 2) * 128:(ch % 2 + 1) * 128], PB[12]
                        mm(pu, Khtm[pf][rows, sbk, :], Vtm[rows, sbk, hd * 128:(hd + 1) * 128], True, True, [BKhtm[pf], BV], [Bpu], True)
                        yield
                        stt(Sfr[:, hd, :], Sfr[:, hd, :], erT[:, pf, ch:ch + 1], pu, ALU.mult, ALU.add, [BSfr[hd], Ber[pf], Bpu], [BSfr[hd]])
                        yield
                        if ch < NCH - 1:
                            cp(Sbf[:, par, 0, ch + 1, :], Sfr[:, hd, :], [BSfr[hd]], [BSbf[par]], eng="pool")
                            yield
                    S.dma("sp", Sst[:, par, :], sb_d[hd, t], reads=[dbuf("sb%d" % hd, 0)], writes=[BSst[par]])
                    cp(Sbf[:, par, 1, NCH, :], Sst[:, par, :], [BSst[par]], [BSbf[par]], eng="pool")
                    yield
                    for ch in range(NCH - 1, 0, -1):
                        sbk, rows = ch // 2, slice((ch % 2) * 64, (ch % 2) * 64 + 64)
                        pu, Bpu = ph[13][:, (ch % 2) * 128:(ch % 2 + 1) * 128], PB[13]
                        mm(pu, Khtm[pb][rows, sbk, :], Vtm[rows, sbk, hd * 128:(hd + 1) * 128], True, True, [BKhtm[pb], BV], [Bpu], True)
                        yield
                        stt(Sst[:, par, :], Sst[:, par, :], erT[:, pb, ch:ch + 1], pu, ALU.mult, ALU.add, [BSst[par], Ber[pb], Bpu], [BSst[par]])
                        yield
                        cp(Sbf[:, par, 1, ch, :], Sst[:, par, :], [BSst[par]], [BSbf[par]], eng="pool")
                        yield
                    po, Bpo = ph[0], PB[0]
                    pset = (pf, pb)
                    for sbk in range(NSB):
                        cols = slice(sbk * 128, (sbk + 1) * 128)
                        for dr in range(2):
                            psc, Bps = ph[3][:, dr * 128:(dr + 1) * 128], PB[3]
                            mm(psc, Kt[pset[dr]][:, cols], Qt[pset[dr]][:, cols], True, True, [BKt[pset[dr]], BQt[pset[dr]]], [Bps], True)
                            yield
                            tt(Pm[:, par, dr, :], psc, Mm[dr][:], ALU.mult, [Bps, BM], [BPm[par]])
                            yield
                        mm(po[:, cols], Vtm[:, sbk, hd * 128:(hd + 1) * 128], Pm[:, par, 0, :], True, False, [BV, BPm[par]], [Bpo], False)
                        mm(po[:, cols], Vtm[:, sbk, hd * 128:(hd + 1) * 128], Pm[:, par, 1, :], False, False, [BV, BPm[par]], [Bpo], False)
                        for cc in (2 * sbk, 2 * sbk + 1):
                            c64 = slice(cc * 64, cc * 64 + 64)
                            mm(po[:, c64], Sbf[:, par, 0, cc, :], Qt[pf][:, c64], False, False, [BSbf[par], BQt[pf]], [Bpo], False)
                            mm(po[:, c64], Sbf[:, par, 1, cc + 1, :], Qt[pb][:, c64], False, True, [BSbf[par], BQt[pb]], [Bpo], True)
                        yield
                    cp(osbT[:, par, :], po, [Bpo], [Bosb[par]], eng="act")
                    yield
                    tt(osqT[:, par, :], osbT[:, par, :], osbT[:, par, :], ALU.mult, [Bosb[par]], [Bosq[par]])
                    yield
                    pst, Bpst = ph[15], PB[15]
                    mm(pst, ones_h[:], osqT[:, par, :], True, True, [Bones, Bosq[par]], [Bpst], True)
                    yield
                    act(rsl[:, par, :], pst, AF.Ln, [Bpst, Beps], [Brsl[par]], bias=epsc[:, 0:1])
                    yield
                    act(rsl[:, par, :], rsl[:, par, :], AF.Exp, [Brsl[par]], [Brsl[par]], scale=-0.5)
                    yield
                    stt(osbT[:, par, :], osbT[:, par, :], vecs[:, V_GNG + hd:V_GNG + hd + 1], rsl[:, par, :], ALU.mult, ALU.mult, [Bosb[par], Bvecs, Brsl[par]], [Bosb[par]])
                    yield
                    tt(on[:, hd, :], osbT[:, par, :], gsT[:, hs, :], ALU.mult, [Bosb[par], Bgs[hs]], [Bon])
                    yield

                def postpair(t, gi, h, Bhh):
                    for hh in range(2):
                        hd = 2 * gi + hh
                        base = (gi % 2) * 4 + hh * 2
                        yield from postB(t, hd, base, base + 1, (gi % 2) * 2 + hh, hh, h, Bhh)

                for t in range(NTILE):
                    xt, Bxx, h, Bhh = load_norm(t)
                    pending = None
                    for gi in range(4):
                        gens = []
                        for hh in range(2):
                            gens.append(headB(2 * gi + hh, (gi % 2) * 2 + hh, h, Bhh))
                        for hh in range(2):
                            for dr in range(2):
                                li = hh * 2 + dr
                                gens.append(laneB(t, 2 * gi + hh, dr, li, (gi % 2) * 4 + li, (gi % 2) * 2 + hh, h, Bhh))
                        if pending is not None:
                            gens.append(pending)
                        run_lanes(gens)
                        pending = postpair(t, gi, h, Bhh)
                    run_lanes([pending])
                    for dc in range(8):
                        pw, Bpw = (ph[2], PB[2]) if dc % 2 == 0 else (ph[4], PB[4])
                        wv, wb_ = wout(dc // 4)
                        for c in range(8):
                            mm(pw, wv[:, c, (dc % 4) * 128:(dc % 4 + 1) * 128], on[:, c, :], c == 0, c == 7, [Bon] + wb_, [Bpw], c == 7)
                        tt(xt[:, dc, :], xt[:, dc, :], pw, ALU.add, [Bxx, Bpw], [Bxx])
                    S.dma("sp", dview(dst, t * TT, TT), xt[:], reads=[Bxx], writes=[dbuf(dname, t)])
                S.op("pool", lambda e: e.memset(wa[:, TAIL0 * 1024:TAIL0 * 1024 + 16], 0.0), reads=lane_bufs, writes=WB[TAIL0:NBLK])

        def stage_out(src, sname, do_norm=True):
            S.barrier()
            with nc.sbuf_tensor(U("xo"), [128, 8, TT], F32) as xo, nc.sbuf_tensor(U("ot"), [128, 2, D], F32) as ot:
                Bxo, Bot = Buf("xo"), Buf("ot")
                toks = []
                for t in range(NTILE):
                    xt, Bxx, h, Bhh = next_xt()
                    S.dma("sp", xt[:], dview(src, t * TT, TT), reads=[dbuf(sname, t)], writes=[Bxx])
                    if do_norm:
                        i = nstat["i"] % 2
                        nstat["i"] += 1
                        tt(sqb[:], xt[:], xt[:], ALU.mult, [Bxx], [Bsq])
                        for c in range(8):
                            mm(ph[14 + i], ones_d[:], sqb[:, c, :], c == 0, c == 7, [Bones, Bsq], [PB[14 + i]], c == 7)
                        rsqrt_eps(rstd[i][:], ph[14 + i], [PB[14 + i]], Brstd[i])
                        for c in range(8):
                            stt(xo[:, c, :], xt[:, c, :], vecs[:, V_FG + c:V_FG + c + 1], rstd[i][:], ALU.mult, ALU.mult, [Bxx, Brstd[i], Bvecs], [Bxo])
                        srcx, Bsrc = xo, Bxo
                    else:
                        srcx, Bsrc = xt, Bxx
                    for b in range(2):
                        for half in range(2):
                            pbk = pbank[1 + half]
                            for c4 in range(4):
                                c = half * 4 + c4
                                S.op("pe", lambda e, pbk=pbk, c4=c4, c=c, b=b, srcx=srcx: e.transpose(pbk[:, c4 * 128:(c4 + 1) * 128], srcx[:, c, b * 128:(b + 1) * 128], identf[:]),
                                     reads=[Bsrc, Bidf], writes=[PB[2 + 2 * half], PB[3 + 2 * half]], sig=(c4 == 3))
                            cp(ot[:, b, half * 512:(half + 1) * 512], pbk[:, :], [PB[2 + 2 * half], PB[3 + 2 * half]], [Bot], eng=("act" if half else "dve"))
                    toks.append(S.dma("sp", out_d[t * TT:(t + 1) * TT, :].rearrange("(b p) d -> p b d", p=128), ot[:], reads=[Bot], writes=[dbuf("out", t)]))
                for tk in toks:
                    S.wait_tok("sp", tk)

        stage_in()
        cur, cname, oth, oname = xsA, "A", xsB, "B"
        if upto >= 1:
            stage_ffn(0, 0, xsA, "A", xsB, "B", halo=(xhA, "hA", xhB, "hB"))
            cur, cname, oth, oname = xsB, "B", xsA, "A"
        if upto >= 2:
            stage_conv(xsB, "B", xhB, "hB", xsA, "A")
            cur, cname, oth, oname = xsA, "A", xsB, "B"
        if upto >= 3:
            stage_ffn(0, 1, xsA, "A", xsB, "B")
            cur, cname = xsB, "B"
        if upto >= 4:
            stage_ffn(1, 0, xsB, "B", xsA, "A")
            cur, cname = xsA, "A"
        if upto >= 5:
            stage_hgrn(xsA, "A", xsB, "B")
            cur, cname = xsB, "B"
        if cc_mode != "produce":
            if upto >= 6:
                stage_ffn(1, 1, xsB, "B", xsA, "A")
                cur, cname = xsA, "A"
            stage_out(cur, cname, do_norm=(upto >= 7))

        with nc.Block() as block:
            S.emit(block)
    return nc


def colpack(v):
    v = np.asarray(v, dtype=np.float32).reshape(-1, 128)
    return np.ascontiguousarray(v.T)


def needed_weights(upto):
    need = []
    if upto >= 1:
        need += ["w13_00", "w2_00"]
    if upto >= 2:
        need += ["conv_w_pw1", "conv_w_pw2"]
    if upto >= 3:
        need += ["w13_01", "w2_01"]
    if upto >= 4:
        need += ["w13_10", "w2_10"]
    if upto >= 5:
        need += ["hgrn_w_in", "hgrn_w_out"]
    if upto >= 6:
        need += ["w13_11", "w2_11"]
    return need


def make_inputs(upto, x, norm_g, ffn_w13, ffn_w2, conv_w_pw1, conv_b_pw1, conv_w_dw, conv_b_dw, conv_ln_g, conv_ln_b,
                conv_w_pw2, conv_b_pw2, hgrn_w_in, hgrn_lb, hgrn_gn_g, hgrn_w_out, final_g):
    f = lambda a: np.ascontiguousarray(np.asarray(a, dtype=np.float32))
    vecs = np.zeros((128, NV), np.float32)
    vecs[:, V_NG:V_NG + 48] = colpack(f(norm_g).reshape(-1))
    vecs[:, V_FG:V_FG + 8] = colpack(final_g)
    vecs[:, V_BPW1:V_BPW1 + 16] = colpack(f(conv_b_pw1)[0])
    wdw = f(conv_w_dw)[0]
    for c in range(8):
        vecs[:, V_WDW + c * 31:V_WDW + (c + 1) * 31] = wdw[:, c * 128:(c + 1) * 128].T
    vecs[:, V_BDW:V_BDW + 8] = colpack(f(conv_b_dw)[0])
    vecs[:, V_LNG:V_LNG + 8] = colpack(f(conv_ln_g)[0])
    vecs[:, V_LNB:V_LNB + 8] = colpack(f(conv_ln_b)[0])
    vecs[:, V_BPW2:V_BPW2 + 8] = colpack(f(conv_b_pw2)[0])
    vecs[:, V_HLB:V_HLB + 32] = colpack(f(hgrn_lb).reshape(-1))
    vecs[:, V_GNG:V_GNG + 8] = colpack(f(hgrn_gn_g)[0])
    x = f(x)
    allw = {"conv_w_pw1": f(conv_w_pw1)[0], "conv_w_pw2": f(conv_w_pw2)[0], "hgrn_w_in": f(hgrn_w_in)[0], "hgrn_w_out": f(hgrn_w_out)[0]}
    for l in range(2):
        for i in range(2):
            allw["w13_%d%d" % (l, i)] = f(np.asarray(ffn_w13)[l, i])
            allw["w2_%d%d" % (l, i)] = f(np.asarray(ffn_w2)[l, i])
    shared = {"vecs": vecs}
    for k in needed_weights(upto):
        shared[k] = allw[k]
    in_maps = []
    for c in range(NCORES):
        b, pos = c // 4, c % 4
        t0 = pos * NT
        xh = np.zeros((2 * HALO, D), np.float32)
        cvec = np.zeros((128, NCV), np.float32)
        if pos > 0:
            xh[:HALO] = x[b, t0 - HALO:t0]
            cvec[:, C_HM:C_HM + HALO] = 1.0
        if pos < 3:
            xh[HALO:] = x[b, t0 + NT:t0 + NT + HALO]
            cvec[:, C_HM + HALO:C_HM + 2 * HALO] = 1.0
        for r in range(NCORES):
            if r // 4 == b and r < c:
                cvec[:, C_RM + r] = 1.0
            if r // 4 == b and r > c:
                cvec[:, C_RM + 8 + r] = 1.0
        cvec[:, C_RM1:C_RM1 + 16] = 1.0 - cvec[:, C_RM:C_RM + 16]
        m = dict(shared)
        m["x"] = np.ascontiguousarray(x[b, t0:t0 + NT])
        m["xh"] = xh
        m["cvec"] = cvec
        in_maps.append(m)
    return in_maps


_NC_CACHE = {}
FUSED = True


def run(inputs, upto=99):
    in_maps = make_inputs(upto, **inputs)
    if FUSED or upto < 5:
        if upto not in _NC_CACHE:
            _NC_CACHE[upto] = build(upto)
        res = run_bass_kernel_spmd(_NC_CACHE[upto], in_maps, core_ids=list(range(NCORES)))
    else:
        ncA = build(upto, "produce")
        resA = run_bass_kernel_spmd(ncA, in_maps, core_ids=list(range(NCORES)))
        allcc = np.concatenate([resA.results[c]["cc_in"] for c in range(NCORES)], axis=0)
        for m in in_maps:
            m["cc_out"] = allcc
        ncB = build(upto, "consume")
        res = run_bass_kernel_spmd(ncB, in_maps, core_ids=list(range(NCORES)))
    out = np.zeros((2, 4 * NT, D), np.float32)
    for c in range(NCORES):
        out[c // 4, (c % 4) * NT:(c % 4 + 1) * NT] = res.results[c]["out"]
    return out


def kernel(**inputs):
    return run(inputs, 99)
```

```python
import contextlib
import numpy as np
import concourse.bass as bass
import concourse.mybir as mybir
from concourse.bass_utils import run_bass_kernel_spmd

F32 = mybir.dt.float32
BF16 = mybir.dt.bfloat16
AF = mybir.ActivationFunctionType
ALU = mybir.AluOpType

D = 1024
FF = 2816
NT = 4096
TT = 256
NTILE = NT // TT
HALO = 16
EPS = 1e-6
NBLK = 66
NCORES = 8

V_NG = 0
V_FG = 48
V_BPW1 = 56
V_WDW = 72
V_BDW = 320
V_LNG = 328
V_LNB = 336
V_BPW2 = 344
V_HLB = 352
V_GNG = 384
NV = 392
C_HM = 0
C_RM = 32
C_RM1 = 48
NCV = 64


class Buf:
    __slots__ = ("name", "w", "r", "dsem", "dcnt", "excl")

    def __init__(self, name, excl=False):
        self.name = name
        self.excl = excl
        self.w = None
        self.r = {}
        self.dsem = None
        self.dcnt = 0


class Sched:
    ENGS = ("pe", "act", "dve", "pool", "sp")
    ROLL = 30000

    def __init__(self, nc):
        self.nc = nc
        self.prog = {e: [] for e in self.ENGS}
        self.sem = {}
        self.cnt = {e: 0 for e in self.ENGS}
        self.seen = {e: {} for e in self.ENGS}
        self.nsem = 0
        self.pool_pending = []
        self.sp_out = {}
        for e in self.ENGS:
            self.sem[e] = self.new_sem("eng_" + e)

    def new_sem(self, name):
        self.nsem += 1
        return self.nc.alloc_semaphore("%s_%d" % (name, self.nsem))

    def _waits(self, eng, reads, writes, is_dma=False):
        own = self.sem[eng]
        need = {}

        def add(tok, kind):
            sem, val = tok
            if sem is own and not is_dma:
                if eng == "pe" or kind != "raw":
                    return
            k = id(sem)
            if self.seen[eng].get(k, 0) >= val:
                return
            if k not in need or need[k][1] < val:
                need[k] = (sem, val)

        for b in reads:
            if b.w is not None:
                add(b.w, "raw")
        for b in writes:
            if b.w is not None:
                add(b.w, "waw")
            for t in b.r.values():
                add(t, "war")
        out = []
        for k, (sem, val) in need.items():
            self.seen[eng][k] = val
            out.append((sem, val))
        return out

    def _mark(self, tok, reads, writes):
        k = id(tok[0])
        for b in reads:
            o = b.r.get(k)
            if o is None or o[1] < tok[1]:
                b.r[k] = tok
        for b in writes:
            b.w = tok
            b.r = {}

    def op(self, eng, fn, reads=(), writes=(), sig=True):
        ex = [b for b in reads if b.excl]
        if ex:
            reads = [b for b in reads if not b.excl]
            writes = list(writes) + ex
        waits = self._waits(eng, reads, writes)
        if eng == "pool" and self.pool_pending:
            for sem, val in self.pool_pending:
                if self.seen[eng].get(id(sem), 0) < val:
                    self.seen[eng][id(sem)] = val
                    waits.append((sem, val))
            self.pool_pending = []
        tok = (self.sem[eng], self.cnt[eng] + 1)
        self._mark(tok, reads, writes)
        inc = None
        if sig:
            self.cnt[eng] += 1
            inc = (self.sem[eng], 1)
        self.prog[eng].append((waits, fn, inc))
        if sig and self.cnt[eng] >= self.ROLL:
            self.sem[eng] = self.new_sem("eng_" + eng)
            self.cnt[eng] = 0

    def dma(self, q, out, in_, reads=(), writes=(), prim=None):
        waits = self._waits(q, reads, writes, is_dma=True)
        if prim is None:
            prim = writes[0] if writes else reads[0]
        if prim.dsem is None:
            prim.dsem = self.new_sem("d")
        if prim.dcnt >= self.ROLL:
            prim.dsem = self.new_sem("d")
            prim.dcnt = 0
        prim.dcnt += 16
        tok = (prim.dsem, prim.dcnt)
        self._mark(tok, reads, writes)
        if q == "sp":
            self.sp_out[id(prim.dsem)] = tok
        self.prog[q].append((waits, lambda e, o=out, i=in_: e.dma_start(out=o, in_=i), (prim.dsem, 16)))
        return tok

    def barrier(self):
        toks = [(self.sem[e], self.cnt[e]) for e in ("pe", "act", "dve", "pool") if self.cnt[e] > 0]
        toks += list(self.sp_out.values())
        self.sp_out = {}
        for e in ("pe", "act", "dve", "sp"):
            w = []
            for sem, val in toks:
                if sem is self.sem.get(e):
                    continue
                if self.seen[e].get(id(sem), 0) < val:
                    self.seen[e][id(sem)] = val
                    w.append((sem, val))
            if w:
                self.prog[e].append((w, None, None))
        self.pool_pending = [t for t in toks if t[0] is not self.sem["pool"]]

    def wait_tok(self, eng, tok):
        self.prog[eng].append(([tok], None, None))

    def emit(self, block):
        m = {"pe": "tensor", "act": "scalar", "dve": "vector", "pool": "gpsimd", "sp": "sync"}
        for e in self.ENGS:
            items = self.prog[e]

            def body(engine, items=items):
                for waits, fn, inc in items:
                    for sem, val in waits:
                        engine.wait_ge(sem, val)
                    if fn is not None:
                        ins = fn(engine)
                        if inc is not None:
                            ins.then_inc(inc[0], inc[1])

            getattr(block, m[e])(body)


def build(upto=99, cc_mode="fused"):
    nc = bass.Bass("TRN2", target_bir_lowering=False)

    def dram(name, shape, dt=F32, kind="Internal"):
        return nc.dram_tensor(name, list(shape), dt, kind=kind).ap()

    x_in = dram("x", [NT, D], kind="ExternalInput")
    xh_in = dram("xh", [2 * HALO, D], kind="ExternalInput")
    vecs_in = dram("vecs", [128, NV], kind="ExternalInput")
    cvec_in = dram("cvec", [128, NCV], kind="ExternalInput")
    need = needed_weights(upto)
    w13_d = {k: dram("w13_%d%d" % k, [D, 2 * FF], kind="ExternalInput") for k in [(0, 0), (0, 1), (1, 0), (1, 1)] if ("w13_%d%d" % k) in need}
    w2_d = {k: dram("w2_%d%d" % k, [FF, D], kind="ExternalInput") for k in [(0, 0), (0, 1), (1, 0), (1, 1)] if ("w2_%d%d" % k) in need}
    if "conv_w_pw1" in need:
        wpw1_in = dram("conv_w_pw1", [D, 2 * D], kind="ExternalInput")
        wpw2_in = dram("conv_w_pw2", [D, D], kind="ExternalInput")
    if "hgrn_w_in" in need:
        win_in = dram("hgrn_w_in", [D, 5 * D], kind="ExternalInput")
        wout_in = dram("hgrn_w_out", [D, D], kind="ExternalInput")
    out_d = dram("out", [NT, D], kind="ExternalOutput")

    xsA = dram("xsA", [D, NT])
    xsB = dram("xsB", [D, NT])
    xhA = dram("xhA", [D, 2 * HALO])
    xhB = dram("xhB", [D, 2 * HALO])
    ub_d = dram("ub_d", [8, NTILE, 128, 128])
    sb_d = dram("sb_d", [8, NTILE, 128, 128])
    cc_in = dram("cc_in", [17 * 128, 128], kind=("ExternalOutput" if cc_mode == "produce" else "Internal"))
    cc_out = dram("cc_out", [NCORES * 17 * 128, 128], kind=("ExternalInput" if cc_mode == "consume" else "Internal"))

    _uc = {"n": 0}

    def U(name):
        _uc["n"] += 1
        return "%s_u%d" % (name, _uc["n"])

    es = contextlib.ExitStack()
    with es:
        S = Sched(nc)

        def sb(name, shape, dt=F32):
            return es.enter_context(nc.sbuf_tensor("s_" + name, list(shape), dt))

        wa = sb("wa", [128, NBLK * 1024], BF16)
        WB = [Buf("wa%d" % i) for i in range(NBLK)]
        vecs = sb("vecs", [128, NV]); Bvecs = Buf("vecs")
        cvec = sb("cvec", [128, NCV]); Bcvec = Buf("cvec")
        identf = sb("identf", [128, 128]); Bidf = Buf("identf")
        identb = sb("identb", [128, 128], BF16); Bidb = Buf("identb")
        ones_d = sb("ones_d", [128, 128], BF16); Bones = Buf("ones")
        ones_h = sb("ones_h", [128, 128], BF16)
        Mf = sb("Mf", [128, 128]); Mb = sb("Mb", [128, 128]); BM = Buf("M")
        rmask = sb("rmask", [128, TT]); ones_t = sb("ones_t", [128, TT]); Brm = Buf("rmask")
        lbv = sb("lbv", [128, 2, 8]); oml = sb("oml", [128, 2, 8]); noml = sb("noml", [128, 2, 8]); Blb = Buf("lb")
        Dall = sb("Dall", [128, 8, NTILE]); BDall = Buf("Dall")
        xts = [sb("xt%d" % i, [128, 8, TT]) for i in range(2)]; Bxt = [Buf("xt%d" % i) for i in range(2)]
        hbuf = [sb("h%d" % i, [128, 8, TT], BF16) for i in range(2)]; Bh = [Buf("h%d" % i) for i in range(2)]
        sqb = sb("sqb", [128, 8, TT], BF16); Bsq = Buf("sq")
        rstd = [sb("rstd%d" % i, [128, TT]) for i in range(2)]; Brstd = [Buf("rstd%d" % i) for i in range(2)]

        pbank = [es.enter_context(nc.psum_tensor("pb%d" % i, [128, 512], F32)) for i in range(8)]
        ph = []
        for i in range(8):
            ph.append(pbank[i][:, 0:256])
            ph.append(pbank[i][:, 256:512])
        _PBK = [Buf("pbank%d" % i, excl=True) for i in range(8)]
        PB = [_PBK[i // 2] for i in range(16)]

        def mm(out, lhsT, rhs, start, stop, reads, writes, sig, sgc=False):
            if sgc:
                S.op("pe", lambda e: e.matmul(out, lhsT, rhs, start=start, stop=stop, skip_group_check=True), reads=reads, writes=writes, sig=sig)
            else:
                S.op("pe", lambda e: e.matmul(out, lhsT, rhs, start=start, stop=stop), reads=reads, writes=writes, sig=sig)

        def act(out, in_, func, reads, writes, bias=None, scale=None):
            kw = {}
            if bias is not None:
                kw["bias"] = bias
            if scale is not None:
                kw["scale"] = scale
            S.op("act", lambda e: e.activation(out=out, in_=in_, func=func, **kw), reads=reads, writes=writes)

        def tt(out, in0, in1, op, reads, writes, eng="dve"):
            S.op(eng, lambda e: e.tensor_tensor(out=out, in0=in0, in1=in1, op=op), reads=reads, writes=writes)

        def ts(out, in0, s1, s2, op0, op1, reads, writes, eng="dve"):
            if s2 is None:
                S.op(eng, lambda e: e.tensor_scalar(out=out, in0=in0, scalar1=s1, scalar2=None, op0=op0), reads=reads, writes=writes)
            else:
                S.op(eng, lambda e: e.tensor_scalar(out=out, in0=in0, scalar1=s1, scalar2=s2, op0=op0, op1=op1), reads=reads, writes=writes)

        def stt(out, in0, scalar, in1, op0, op1, reads, writes, eng="dve"):
            S.op(eng, lambda e: e.scalar_tensor_tensor(out=out, in0=in0, scalar=scalar, in1=in1, op0=op0, op1=op1), reads=reads, writes=writes)

        def cp(out, in_, reads, writes, eng="dve"):
            if eng == "act":
                act(out, in_, AF.Copy, reads, writes)
            else:
                S.op(eng, lambda e: e.tensor_copy(out, in_), reads=reads, writes=writes)

        ring = {"p": 0}

        def walloc(n):
            if ring["p"] + n > NBLK:
                ring["p"] = 0
            s = ring["p"]
            ring["p"] += n
            return s, WB[s:s + n]

        def wstage(nblocks):
            if ring["p"] + nblocks > NBLK:
                ring["p"] = 0

        wsemB = [Buf("wsem%d" % i) for i in range(16)]
        wcnt = {"k": 0}

        def wprim():
            b = wsemB[wcnt["k"] % 16]
            wcnt["k"] += 1
            return b

        def wload_cols(src2d, col0, ncols):
            nb = (8 * ncols + 1023) // 1024
            s, bufs = walloc(nb)
            view = wa[:, s * 1024: s * 1024 + 8 * ncols].rearrange("p (c n) -> p c n", c=8)
            pb_ = wprim()
            S.dma("pool", view, src2d[:, col0:col0 + ncols].rearrange("(c p) n -> p c n", p=128), writes=bufs + [pb_], prim=pb_)
            return view, bufs

        def wload_rows(src2d, row0, nrows):
            f = nrows // 128
            s, bufs = walloc(f)
            view = wa[:, s * 1024: (s + f) * 1024].rearrange("p (f n) -> p f n", f=f)
            pb_ = wprim()
            S.dma("pool", view, src2d[row0:row0 + nrows, :].rearrange("(f p) n -> p f n", p=128), writes=bufs + [pb_], prim=pb_)
            return view, bufs

        S.dma("sp", vecs[:], vecs_in, writes=[Bvecs])
        S.dma("sp", cvec[:], cvec_in, writes=[Bcvec])
        S.op("pool", lambda e: e.memset(identf[:], 0.0), writes=[Bidf])
        S.op("pool", lambda e: e.affine_select(out=identf[:], in_=identf[:], pattern=[[-1, 128]], compare_op=ALU.not_equal,
                                               fill=1.0, base=0, channel_multiplier=1), reads=[Bidf], writes=[Bidf])
        cp(identb[:], identf[:], [Bidf], [Bidb], eng="pool")
        S.op("pool", lambda e: e.memset(ones_d[:], 1.0 / 1024.0), writes=[Bones])
        S.op("pool", lambda e: e.memset(ones_h[:], 1.0 / 128.0), writes=[Bones])
        S.op("pool", lambda e: e.memset(Mf[:], 1.0), writes=[BM])
        S.op("pool", lambda e: e.affine_select(out=Mf[:], in_=Mf[:], pattern=[[1, 128]], compare_op=ALU.is_ge,
                                               fill=0.0, base=0, channel_multiplier=-1), reads=[BM], writes=[BM])
        S.op("pool", lambda e: e.memset(Mf[0:64, 64:128], 0.0), reads=[BM], writes=[BM])
        S.op("pool", lambda e: e.memset(Mb[:], 1.0), reads=[BM], writes=[BM])
        S.op("pool", lambda e: e.affine_select(out=Mb[:], in_=Mb[:], pattern=[[-1, 128]], compare_op=ALU.is_ge,
                                               fill=0.0, base=0, channel_multiplier=1), reads=[BM], writes=[BM])
        S.op("pool", lambda e: e.memset(Mb[64:128, 0:64], 0.0), reads=[BM], writes=[BM])
        S.op("pool", lambda e: e.memset(rmask[:], 1.0), writes=[Brm])
        S.op("pool", lambda e: e.memset(rmask[:].rearrange("p (c j) -> p c j", j=64)[:, :, 0:1], 0.0), reads=[Brm], writes=[Brm])
        S.op("pool", lambda e: e.memset(ones_t[:], 1.0), reads=[Brm], writes=[Brm])
        epsc = sb("epsc", [128, 1]); Beps = Buf("epsc")
        S.op("pool", lambda e: e.memset(epsc[:], EPS), writes=[Beps])
        S.barrier()
        hl = vecs[:, V_HLB:V_HLB + 32].rearrange("p (d l h) -> p d l h", d=2, l=2)
        tt(lbv[:], hl[:, :, 1, :], hl[:, :, 0, :], ALU.subtract, [Bvecs], [Blb])
        act(lbv[:], lbv[:], AF.Sigmoid, [Blb], [Blb])
        ts(oml[:], lbv[:], -1.0, 1.0, ALU.mult, ALU.add, [Blb], [Blb])
        ts(noml[:], oml[:], -1.0, None, ALU.mult, None, [Blb], [Blb])
        homl = sb("homl", [128, 2, 8]); c1v = sb("c1v", [128, 2, 8])
        ts(homl[:], oml[:], 0.5, None, ALU.mult, None, [Blb], [Blb])
        tt(c1v[:], homl[:], lbv[:], ALU.add, [Blb], [Blb])

        def rsqrt_eps(out, in_, reads, Bout):
            act(out, in_, AF.Sqrt, list(reads) + [Beps], [Bout], bias=epsc[:, 0:1])
            S.op("dve", lambda e: e.reciprocal(out, out), reads=[Bout], writes=[Bout])

        nstat = {"i": 0}

        def rmsnorm(xt, Bx, N, gcol, h, Bhh):
            i = nstat["i"] % 2
            nstat["i"] += 1
            st, Bst = ph[14 + i][:, :N], PB[14 + i]
            r, Br = rstd[i][:, :N], Brstd[i]
            tt(sqb[:, :, :N], xt[:, :, :N], xt[:, :, :N], ALU.mult, [Bx], [Bsq])
            for c in range(8):
                mm(st, ones_d[:], sqb[:, c, :N], c == 0, c == 7, [Bones, Bsq], [Bst], c == 7)
            rsqrt_eps(r, st, [Bst], Br)
            for c in range(8):
                stt(h[:, c, :N], xt[:, c, :N], vecs[:, gcol + c:gcol + c + 1], r, ALU.mult, ALU.mult, [Bx, Br, Bvecs], [Bhh])

        xcnt = {"i": 0}

        def next_xt():
            i = xcnt["i"] % 2
            xcnt["i"] += 1
            return xts[i], Bxt[i], hbuf[i], Bh[i]

        def dview(dr, off, N):
            return dr[:, off:off + N].rearrange("(c p) n -> p c n", p=128)

        Bx = {}

        def dbuf(name, t):
            k = (name, 0)
            if k not in Bx:
                Bx[k] = Buf("%s_%s" % (name, t))
            return Bx[k]

        def stage_in():
            with nc.sbuf_tensor(U("xin"), [128, 2, D], F32) as xin:
                Bxin = Buf("xin")
                jobs = [(x_in, t * TT, TT, xsA, t * TT, ("A", t)) for t in range(NTILE)] + [(xh_in, 0, 2 * HALO, xhA, 0, ("hA", 0))]
                for src, soff, N, dst, doff, key in jobs:
                    nb = (N + 127) // 128
                    rows = min(N, 128)
                    S.dma("sp", xin[:rows, :nb, :], src[soff:soff + N, :].rearrange("(b p) d -> p b d", p=rows), writes=[Bxin])
                    xt, Bxx, _, _ = next_xt()
                    for c2 in range(4):
                        for ci in range(2):
                            c = 2 * c2 + ci
                            for b in range(nb):
                                S.op("pe", lambda e, c=c, b=b, rows=rows: e.transpose(ph[2 + c][:, b * 128:b * 128 + rows], xin[:rows, b, c * 128:(c + 1) * 128], identf[:rows, :rows]),
                                     reads=[Bxin, Bidf], writes=[PB[2 + c]], sig=(b == nb - 1 and ci == 1))
                        cp(xt[:, 2 * c2:2 * c2 + 2, :N], pbank[1 + c2][:, :].rearrange("p (h n) -> p h n", h=2)[:, :, :N], [PB[2 + 2 * c2]], [Bxx], eng=("act" if c2 % 2 else "dve"))
                    S.dma("sp", dview(dst, doff, N), xt[:, :, :N], reads=[Bxx], writes=[dbuf(*key)])

        def stage_ffn(l, i, src, sname, dst, dname, halo=None):
            gcol = V_NG + (l * 3 + (0 if i == 0 else 2)) * 8
            S.barrier()
            w13 = w13_d[(l, i)]
            w2 = w2_d[(l, i)]
            units = {}
            wstage(66)

            def unit(g):
                if g not in units:
                    G = 4 if g < 5 else 2
                    gv, gb = wload_cols(w13, g * 512, G * 128)
                    uv, ub = wload_cols(w13, FF + g * 512, G * 128)
                    dv, db = wload_rows(w2, g * 512, G * 128)
                    units[g] = (gv, gb, uv, ub, dv, db)
                return units[g]

            with nc.sbuf_tensor(U("sg"), [128, 2, TT], F32) as sg, nc.sbuf_tensor(U("actb"), [128, 3, TT], BF16) as actb:
                Bsg = [Buf("sg0"), Buf("sg1")]
                Bact = [Buf("act%d" % k) for k in range(3)]
                jobs = [(src, sname, dst, dname, t, t * TT, TT) for t in range(NTILE)]
                if halo is not None:
                    jobs.append((halo[0], halo[1], halo[2], halo[3], 0, 0, 2 * HALO))
                def prep(job):
                    (s_d, s_n, d_d, d_n, t, off, N) = job
                    xt, Bxx, h, Bhh = next_xt()
                    S.dma("sp", xt[:, :, :N], dview(s_d, off, N), reads=[dbuf(s_n, t)], writes=[Bxx])
                    rmsnorm(xt, Bxx, N, gcol, h, Bhh)
                    return xt, Bxx, h, Bhh

                cur = prep(jobs[0])
                for idx, (s_d, s_n, d_d, d_n, t, off, N) in enumerate(jobs):
                    xt, Bxx, h, Bhh = cur
                    LAG = 1

                    def ymm(j):
                        g, jj = j // 4, j % 4
                        dv, db = unit(g)[4], unit(g)[5]
                        a = j % 3
                        for dc in range(8):
                            mm(ph[dc][:, :N], dv[:, jj, dc * 128:(dc + 1) * 128], actb[:, a, :N], (j == 0 and dc % 2 == 0), j == 21,
                               [Bact[a]] + db, [PB[dc]], (dc == 7) or (j == 21), sgc=True)

                    for j in range(22 + LAG):
                        if j == 8 and idx + 1 < len(jobs):
                            cur = prep(jobs[idx + 1])
                        if j < 22:
                            g, jj = j // 4, j % 4
                            gv, gb, uv, ub = unit(g)[:4]
                            s3 = j % 3
                            pg, pu = ph[8 + 2 * s3][:, :N], ph[9 + 2 * s3][:, :N]
                            Bpg, Bpu = PB[8 + 2 * s3], PB[9 + 2 * s3]
                            for c in range(8):
                                mm(pg, gv[:, c, jj * 128:(jj + 1) * 128], h[:, c, :N], c == 0, c == 7, [Bhh] + gb, [Bpg], c == 7)
                            for c in range(8):
                                mm(pu, uv[:, c, jj * 128:(jj + 1) * 128], h[:, c, :N], c == 0, c == 7, [Bhh] + ub, [Bpu], c == 7)
                            act(sg[:, j % 2, :N], pg, AF.Silu, [Bpg], [Bsg[j % 2]])
                            tt(actb[:, j % 3, :N], sg[:, j % 2, :N], pu, ALU.mult, [Bsg[j % 2], Bpu], [Bact[j % 3]])
                        if j >= LAG:
                            ymm(j - LAG)
                    for dc in range(8):
                        stt(xt[:, dc, :N], ph[dc][:, :N], 0.5, xt[:, dc, :N], ALU.mult, ALU.add, [PB[dc], Bxx], [Bxx])
                    S.dma("sp", dview(d_d, off, N), xt[:, :, :N], reads=[Bxx], writes=[dbuf(d_n, t)])

        def stage_conv(src, sname, hsrc, hname, dst, dname):
            gcol = V_NG + 1 * 8
            S.barrier()
            W = TT + 32
            with nc.sbuf_tensor(U("ubuf"), [128, 8, W], BF16) as ubuf, \
                    nc.sbuf_tensor(U("sgc"), [128, 2, TT], F32) as sgc, \
                    nc.sbuf_tensor(U("vbuf"), [128, 8, TT], F32) as vbuf, \
                    nc.sbuf_tensor(U("v16"), [128, 8, TT], BF16) as v16, \
                    nc.sbuf_tensor(U("sil"), [128, 8, TT], BF16) as sil, \
                    nc.sbuf_tensor(U("lnst"), [128, 4, TT], F32) as lnst, \
                    nc.sbuf_tensor(U("xw"), [128, 8, TT], F32) as xw:
                Bub, Bsgc = Buf("ubuf"), [Buf("sgc0"), Buf("sgc1")]
                Bvb, Bv16, Bq16, Bsil, Bln, Bxw = Buf("vbuf"), Buf("v16"), Bsq, Buf("sil"), Buf("lnst"), Buf("xw")
                q16 = sqb
                wstage(55)
                pw1 = [wload_cols(wpw1_in, g * 512, 512) for g in range(4)]
                pw2 = [wload_cols(wpw2_in, g * 512, 512) for g in range(2)]
                ds_, dbufs = walloc(31)
                diag = wa[:, ds_ * 1024:(ds_ + 31) * 1024]
                for c in range(8):
                    for j in range(31):
                        m = c * 31 + j
                        ts(diag[:, m * 128:(m + 1) * 128], identb[:], vecs[:, V_WDW + m:V_WDW + m + 1], None, ALU.mult, None,
                           [Bidb, Bvecs], dbufs)
                S.op("dve", lambda e: e.memset(ubuf[:], 0.0), writes=[Bub])

                def make_u(s_d, s_n, t, off, N, ucol, hmask_col=None):
                    xt, Bxx, h, Bhh = next_xt()
                    S.dma("sp", xt[:, :, :N], dview(s_d, off, N), reads=[dbuf(s_n, t)], writes=[Bxx])
                    rmsnorm(xt, Bxx, N, gcol, h, Bhh)
                    for j in range(8):
                        s3 = j % 2
                        pa, pg = ph[2 + 2 * s3][:, :N], ph[3 + 2 * s3][:, :N]
                        Bpa, Bpg = PB[2 + 2 * s3], PB[3 + 2 * s3]
                        av, ab = pw1[j // 4]
                        gv, gb = pw1[2 + j // 4]
                        jj = j % 4
                        for c in range(8):
                            mm(pa, av[:, c, jj * 128:(jj + 1) * 128], h[:, c, :N], c == 0, c == 7, [Bhh] + ab, [Bpa], c == 7)
                        for c in range(8):
                            mm(pg, gv[:, c, jj * 128:(jj + 1) * 128], h[:, c, :N], c == 0, c == 7, [Bhh] + gb, [Bpg], c == 7)
                        act(sgc[:, j % 2, :N], pg, AF.Sigmoid, [Bpg, Bvecs], [Bsgc[j % 2]], bias=vecs[:, V_BPW1 + 8 + j:V_BPW1 + 9 + j])
                        stt(ubuf[:, j, ucol:ucol + N], pa, vecs[:, V_BPW1 + j:V_BPW1 + j + 1], sgc[:, j % 2, :N], ALU.add, ALU.mult,
                            [Bpa, Bsgc[j % 2], Bvecs], [Bub])
                    if hmask_col is not None:
                        for j in range(8):
                            tt(ubuf[:, j, ucol:ucol + N], ubuf[:, j, ucol:ucol + N], cvec[:, hmask_col:hmask_col + N], ALU.mult, [Bub, Bcvec], [Bub])

                def window1(tok0, Wn, col0):
                    S.dma("sp", xw[:, :, :Wn], dview(src, tok0, Wn), reads=[dbuf(sname, t) for t in range(NTILE)], writes=[Bxw])
                    for c in range(8):
                        pv, Bpv = ph[6 + 2 * (c % 2)][:, :Wn], PB[6 + 2 * (c % 2)]
                        for j in range(31):
                            m = c * 31 + j
                            mm(pv, diag[:, m * 128:(m + 1) * 128], ubuf[:, c, col0 - 15 + j: col0 - 15 + j + Wn], j == 0, j == 30,
                               [Bub] + dbufs, [Bpv], j == 30)
                        act(vbuf[:, c, :Wn], pv, AF.Identity, [Bpv, Bvecs], [Bvb], bias=vecs[:, V_BDW + c:V_BDW + c + 1])

                def window2(tok0, Wn, col0):
                    cp(v16[:, :, :Wn], vbuf[:, :, :Wn], [Bvb], [Bv16])
                    tt(q16[:, :, :Wn], vbuf[:, :, :Wn], vbuf[:, :, :Wn], ALU.mult, [Bvb], [Bq16])
                    p1, p2 = ph[10][:, :Wn], ph[11][:, :Wn]
                    for c in range(8):
                        mm(p1, ones_d[:], v16[:, c, :Wn], c == 0, c == 7, [Bones, Bv16], [PB[10]], c == 7)
                    for c in range(8):
                        mm(p2, ones_d[:], q16[:, c, :Wn], c == 0, c == 7, [Bones, Bq16], [PB[11]], c == 7)
                    mean, msq, var, rs = lnst[:, 0, :Wn], lnst[:, 1, :Wn], lnst[:, 2, :Wn], lnst[:, 3, :Wn]
                    cp(mean, p1, [PB[10]], [Bln])
                    tt(msq, mean, mean, ALU.mult, [Bln], [Bln])
                    tt(var, p2, msq, ALU.subtract, [PB[11], Bln], [Bln])
                    rsqrt_eps(rs, var, [Bln], Bln)
                    for c in range(8):
                        tt(vbuf[:, c, :Wn], vbuf[:, c, :Wn], mean, ALU.subtract, [Bvb, Bln], [Bvb])
                        tt(vbuf[:, c, :Wn], vbuf[:, c, :Wn], rs, ALU.mult, [Bvb, Bln], [Bvb])
                        act(sil[:, c, :Wn], vbuf[:, c, :Wn], AF.Silu, [Bvb, Bvecs], [Bsil],
                            bias=vecs[:, V_LNB + c:V_LNB + c + 1], scale=vecs[:, V_LNG + c:V_LNG + c + 1])
                    for dc in range(8):
                        po, Bpo = (ph[12][:, :Wn], PB[12]) if dc % 2 == 0 else (ph[0][:, :Wn], PB[0])
                        wv, wb_ = pw2[dc // 4]
                        for c in range(8):
                            mm(po, wv[:, c, (dc % 4) * 128:(dc % 4 + 1) * 128], sil[:, c, :Wn], c == 0, c == 7, [Bsil] + wb_, [Bpo], c == 7)
                        stt(xw[:, dc, :Wn], po, vecs[:, V_BPW2 + dc:V_BPW2 + dc + 1], xw[:, dc, :Wn], ALU.add, ALU.add, [Bpo, Bxw, Bvecs], [Bxw])
                    S.dma("sp", dview(dst, tok0, Wn), xw[:, :, :Wn], reads=[Bxw], writes=[dbuf(dname, t) for t in range(NTILE)])

                make_u(hsrc, hname, 0, 0, HALO, 16, hmask_col=C_HM)
                make_u(src, sname, 0, 0, TT, 32)
                for t in range(NTILE):
                    wargs = (0, TT - 16, 32) if t == 0 else (t * TT - 16, TT, 16)
                    window1(*wargs)
                    cp(ubuf[:, :, 0:32], ubuf[:, :, TT:TT + 32], [Bub], [Bub], eng="act")
                    if t + 1 < NTILE:
                        make_u(src, sname, t + 1, (t + 1) * TT, TT, 32)
                    else:
                        make_u(hsrc, hname, 0, HALO, HALO, 32, hmask_col=C_HM + HALO)
                    window2(*wargs)
                window1(NT - 16, 16, 16)
                window2(NT - 16, 16, 16)

        def stage_hgrn(src, sname, dst, dname):
            gcol = V_NG + (1 * 3 + 1) * 8
            NCH = TT // 64
            NSB = TT // 128
            TAIL0 = 48
            S.barrier()

            def run_lanes(gens):
                gens = list(gens)
                while gens:
                    nxt = []
                    for g in gens:
                        try:
                            next(g)
                            nxt.append(g)
                        except StopIteration:
                            pass
                    gens = nxt

            with contextlib.ExitStack() as hs_:
                def hsb(name, shape, dt=F32):
                    return hs_.enter_context(nc.sbuf_tensor(U(name), list(shape), dt))
                wstage(48)
                S.op("dve", lambda e: e.memset(wa[:, TAIL0 * 1024:TAIL0 * 1024 + 16], 0.0), writes=WB[TAIL0:NBLK])
                tail_tok = WB[TAIL0].w
                lane_bufs = []

                def LB(name):
                    b = Buf(name)
                    b.w = tail_tok
                    lane_bufs.append(b)
                    return b
                tail = {"o": TAIL0 * 1024}

                def tail_f32(n):
                    o = tail["o"]
                    tail["o"] += 2 * n
                    assert tail["o"] <= NBLK * 1024
                    return wa[:, o:o + 2 * n].bitcast(F32)

                def tail_bf(n):
                    o = tail["o"]
                    tail["o"] += n
                    assert tail["o"] <= NBLK * 1024
                    return wa[:, o:o + n]
                gA = [tail_f32(TT) for _ in range(4)]; BgA = [LB("gA%d" % i) for i in range(4)]
                gB = [tail_f32(TT) for _ in range(4)]; BgB = [LB("gB%d" % i) for i in range(4)]
                gC = [tail_f32(TT) for _ in range(4)]; BgC = [LB("gC%d" % i) for i in range(4)]
                gD = [tail_f32(TT) for _ in range(4)]; BgD_ = [LB("gD%d" % i) for i in range(4)]
                Qt = [tail_bf(TT) for _ in range(8)]; BQt = [LB("Qt%d" % i) for i in range(8)]
                Kt = [tail_bf(TT) for _ in range(8)]; BKt = [LB("Kt%d" % i) for i in range(8)]
                Kh = [tail_bf(TT) for _ in range(8)]; BKh = [LB("Kh%d" % i) for i in range(8)]
                Khtm_ = [tail_bf(TT) for _ in range(8)]; BKhtm = [LB("Khtm%d" % i) for i in range(8)]
                Khtm = [k.rearrange("p (s n) -> p s n", s=NSB) for k in Khtm_]
                erT = hsb("erT", [128, 8, NCH]); Ber = [Buf("er%d" % i) for i in range(8)]
                qbT = hsb("qbT", [128, 4, TT]); Bqb = [Buf("qb%d" % i) for i in range(4)]
                gsT = hsb("gsT", [128, 4, TT]); Bgs = [Buf("gs%d" % i) for i in range(4)]
                osbT = tail_f32(2 * TT).rearrange("p (s n) -> p s n", s=2); Bosb = [LB("osb%d" % i) for i in range(2)]
                osqT = hsb("osqT", [128, 2, TT], BF16); Bosq = [Buf("osq%d" % i) for i in range(2)]
                Vtm = hsb("Vtm", [128, NSB, D], BF16); BV = Buf("Vtm")
                Pm = hsb("Pm", [128, 2, 2, 128], BF16); BPm = [Buf("Pm0"), Buf("Pm1")]
                Sst = hsb("Sst", [128, 2, 128]); BSst = [Buf("Sst0"), Buf("Sst1")]
                Sbf = hsb("Sbf", [128, 2, 2, NCH + 1, 128], BF16); BSbf = [Buf("Sbf0"), Buf("Sbf1")]; BSbf2 = [Buf("Sbf2"), Buf("Sbf3")]
                Sfr = hsb("Sfr", [128, 8, 128]); BSfr = [Buf("Sfr%d" % i) for i in range(8)]
                Sbt = hsb("Sbt", [128, 8, 128]); BSbt = [Buf("Sbt%d" % i) for i in range(8)]
                Cb = hsb("Cb", [128, 8]); BCb = Buf("Cb")
                Cf = hsb("Cf", [128, 8]); BCf = Buf("Cf")
                Usb = hsb("Usb", [128, 4, 128]); BUsb = [Buf("Usb%d" % i) for i in range(4)]
                on = hsb("on", [128, 8, TT], BF16); Bon = Buf("on")
                gU = hsb("gU", [128, 2, 128]); BgU = [Buf("gU0"), Buf("gU1")]
                gDD = hsb("gDD", [128, 128]); BgDD = Buf("gDD")
                rsl = tail_f32(2 * TT).rearrange("p (s n) -> p s n", s=2); Brsl = [LB("rsl0"), LB("rsl1")]
                hx = hsb("hx", [128, 2, TT]); Bhx = [Buf("hx0"), Buf("hx1")]

                win_u = {}

                def win(gi):
                    if gi not in win_u:
                        win_u[gi] = wload_cols(win_in, gi * 512, 512)
                    return win_u[gi]

                def rsqrt_le(out, in_, reads, Bout):
                    act(out, in_, AF.Ln, list(reads) + [Beps], [Bout], bias=epsc[:, 0:1])
                    act(out, out, AF.Exp, [Bout], [Bout], scale=-0.5)

                def load_norm(t):
                    xt, Bxx, h, Bhh = next_xt()
                    S.dma("sp", xt[:], dview(src, t * TT, TT), reads=[dbuf(sname, t)], writes=[Bxx])
                    i = nstat["i"] % 2
                    nstat["i"] += 1
                    st, Bst = ph[14 + i], PB[14 + i]
                    tt(sqb[:], xt[:], xt[:], ALU.mult, [Bxx], [Bsq])
                    for c in range(8):
                        mm(st, ones_d[:], sqb[:, c, :], c == 0, c == 7, [Bones, Bsq], [Bst], c == 7)
                    rsqrt_le(rstd[i][:], st, [Bst], Brstd[i])
                    for c in range(8):
                        stt(h[:, c, :], xt[:, c, :], vecs[:, gcol + c:gcol + c + 1], rstd[i][:], ALU.mult, ALU.mult, [Bxx, Brstd[i], Bvecs], [Bhh])
                    for sbk in range(NSB):
                        for cg in range(2):
                            wv, wb_ = win(2 + cg)
                            pv = pbank[1][:, :]
                            for c in range(8):
                                mm(pv, h[:, c, sbk * 128:(sbk + 1) * 128], wv[:, c, :], c == 0, c == 7, [Bhh] + wb_, [PB[2]], c == 7)
                            cp(Vtm[:, sbk, cg * 512:(cg + 1) * 512], pv, [PB[2]], [BV], eng=("act" if cg else "dve"))
                    return xt, Bxx, h, Bhh

                pslot = {"i": 0}

                def proj(sec, hd, h, Bhh):
                    s3 = pslot["i"] % 3
                    pslot["i"] += 1
                    pz, Bpz = ph[4 + 2 * s3], PB[4 + 2 * s3]
                    wv, wb_ = win(2 * sec + hd // 4)
                    for c in range(8):
                        mm(pz, wv[:, c, (hd % 4) * 128:(hd % 4 + 1) * 128], h[:, c, :], c == 0, c == 7, [Bhh] + wb_, [Bpz], c == 7)
                    return pz, Bpz

                def gates_gen(dr, hd, h, Bhh, li):
                    A, B_ = gA[li], gB[li]
                    pz, Bpz = proj(2 + dr, hd, h, Bhh)
                    act(A, pz, AF.Tanh, [Bpz], [BgA[li]], scale=0.5)
                    yield
                    ts(B_, A, homl[:, dr, hd:hd + 1], c1v[:, dr, hd:hd + 1], ALU.mult, ALU.add, [BgA[li], Blb], [BgB[li]])
                    yield
                    ts(A, B_, -1.0, 1.0, ALU.mult, ALU.add, [BgB[li]], [BgA[li]])
                    yield
                    act(B_, B_, AF.Ln, [BgB[li]], [BgB[li]])
                    yield

                def transp_gen(pi, li):
                    for sbk in range(NSB):
                        ptr = pbank[5][:, :].bitcast(BF16)[:, (li * NSB + sbk) * 128:(li * NSB + sbk + 1) * 128]
                        S.op("pe", lambda e, ptr=ptr, sbk=sbk: e.transpose(ptr, Kh[pi][:, sbk * 128:(sbk + 1) * 128], identb[:]),
                             reads=[BKh[pi], Bidb], writes=[PB[10]])
                        yield
                        cp(Khtm[pi][:, sbk, :], ptr, [PB[10]], [BKhtm[pi]], eng="act")
                        yield

                S.op("dve", lambda e: e.memset(Sfr[:], 0.0), writes=BSfr)
                S.op("dve", lambda e: e.memset(Sbt[:], 0.0), writes=BSbt)
                S.op("dve", lambda e: e.memset(Cb[:], 1.0), writes=[BCb])
                S.op("dve", lambda e: e.memset(Cf[:], 1.0), writes=[BCf])

                def laneA(t, hd, dr, li, pi, h, Bhh):
                    A, B_, C_, D_ = gA[li], gB[li], gC[li], gD[li]
                    yield from gates_gen(dr, hd, h, Bhh, li)
                    S.op("dve", lambda e: e.tensor_tensor_scan(out=C_, data0=ones_t[:], data1=B_, initial=0.0, op0=ALU.mult, op1=ALU.add),
                         reads=[BgB[li], Brm], writes=[BgC[li]])
                    yield
                    if dr == 0:
                        act(D_, C_, AF.Exp, [BgC[li]], [BgD_[li]], bias=C_[:, TT - 1:TT], scale=-1.0)
                        dcol, Bd = erT[:, pi, 0:1], Ber[pi]
                    else:
                        tt(D_, C_, B_, ALU.subtract, [BgC[li], BgB[li]], [BgD_[li]])
                        yield
                        act(D_, D_, AF.Exp, [BgD_[li]], [BgD_[li]])
                        dcol, Bd = Dall[:, hd, t:t + 1], BDall
                    yield
                    act(dcol, C_[:, TT - 1:TT], AF.Exp, [BgC[li]], [Bd])
                    yield
                    tt(Kh[pi], A, D_, ALU.mult, [BgA[li], BgD_[li]], [BKh[pi]])
                    yield
                    yield from transp_gen(pi, li)
                    pu, Bpu = ph[12 + li // 2][:, (li % 2) * 128:(li % 2 + 1) * 128], PB[12]
                    for sbk in range(NSB):
                        mm(pu, Khtm[pi][:, sbk, :], Vtm[:, sbk, hd * 128:(hd + 1) * 128], sbk == 0, sbk == NSB - 1, [BKhtm[pi], BV], [Bpu], sbk == NSB - 1)
                    yield
                    if dr == 0:
                        stt(Sfr[:, hd, :], Sfr[:, hd, :], erT[:, pi, 0:1], pu, ALU.mult, ALU.add, [BSfr[hd], Ber[pi], Bpu], [BSfr[hd]])
                        yield
                        tt(Cf[:, hd:hd + 1], Cf[:, hd:hd + 1], erT[:, pi, 0:1], ALU.mult, [BCf, Ber[pi]], [BCf])
                    else:
                        cp(Usb[:, li, :], pu, [Bpu], [BUsb[li]])
                        yield
                        S.dma("sp", ub_d[hd, t], Usb[:, li, :], reads=[BUsb[li]], writes=[dbuf("ub%d" % hd, 0)])
                        stt(Sbt[:, hd, :], pu, Cb[:, hd:hd + 1], Sbt[:, hd, :], ALU.mult, ALU.add, [Bpu, BCb, BSbt[hd]], [BSbt[hd]])
                        yield
                        tt(Cb[:, hd:hd + 1], Cb[:, hd:hd + 1], Dall[:, hd, t:t + 1], ALU.mult, [BCb, BDall], [BCb])
                    yield

                for t in range(NTILE):
                    xt, Bxx, h, Bhh = load_norm(t)
                    for gi in range(4):
                        lanes = []
                        for hh in range(2):
                            for dr in range(2):
                                li = hh * 2 + dr
                                lanes.append(laneA(t, 2 * gi + hh, dr, li, (gi % 2) * 4 + li, h, Bhh))
                        run_lanes(lanes)

                Bcc_in, Bcc_out = Buf("cc_in"), Buf("cc_out")
                ccv = cc_in.rearrange("(m p) v -> p m v", p=128)
                S.dma("sp", ccv[:, 0:8, :], Sfr[:], reads=BSfr, writes=[Bcc_in])
                S.dma("sp", ccv[:, 8:16, :], Sbt[:], reads=BSbt, writes=[Bcc_in])
                S.op("dve", lambda e: e.memset(gDD[:], 0.0), writes=[BgDD])
                cp(gDD[:, 0:8], Cf[:], [BCf, BgDD], [BgDD])
                cp(gDD[:, 8:16], Cb[:], [BCb, BgDD], [BgDD])
                S.dma("sp", ccv[:, 16, :], gDD[:], reads=[BgDD], writes=[Bcc_in])
                if cc_mode == "produce":
                    S.wait_tok("sp", Bcc_in.w)
                    return
                waits = S._waits("pool", [Bcc_in], [Bcc_out], is_dma=True)
                csem = S.new_sem("cc")
                tokc = (csem, 1)
                S._mark(tokc, [Bcc_in], [Bcc_out])
                if cc_mode == "fused":
                    S.prog["pool"].append((waits, lambda e: e.collective_compute("AllGather", ALU.bypass, replica_groups=[list(range(NCORES))],
                                                                                 ins=[cc_in], outs=[cc_out]), (csem, 1)))
                else:
                    Bcc_out.w = None
                S.op("dve", lambda e: e.memset(Sfr[:], 0.0), reads=BSfr, writes=BSfr)
                S.op("dve", lambda e: e.memset(Sbt[:], 0.0), reads=BSbt, writes=BSbt)
                for dr, order in ((0, range(NCORES)), (1, range(NCORES - 1, -1, -1))):
                    Sx, BSx = (Sfr, BSfr) if dr == 0 else (Sbt, BSbt)
                    for r in order:
                        base = r * 17 * 128
                        S.dma("sp", gDD[:], cc_out[base + 2048: base + 2176, :], reads=[Bcc_out], writes=[BgDD])
                        mcol = cvec[:, C_RM + dr * 8 + r: C_RM + dr * 8 + r + 1]
                        omcol = cvec[:, C_RM1 + dr * 8 + r: C_RM1 + dr * 8 + r + 1]
                        ts(gDD[:, 16:24], gDD[:, dr * 8: dr * 8 + 8], mcol, omcol, ALU.mult, ALU.add, [BgDD, Bcvec], [BgDD])
                        for hd in range(8):
                            row0 = base + (dr * 8 + hd) * 128
                            S.dma("sp", gU[:, hd % 2, :], cc_out[row0:row0 + 128, :], reads=[Bcc_out], writes=[BgU[hd % 2]])
                            ts(gU[:, hd % 2, :], gU[:, hd % 2, :], mcol, None, ALU.mult, None, [BgU[hd % 2], Bcvec], [BgU[hd % 2]])
                            stt(Sx[:, hd, :], Sx[:, hd, :], gDD[:, 16 + hd:17 + hd], gU[:, hd % 2, :], ALU.mult, ALU.add, [BSx[hd], BgDD, BgU[hd % 2]], [BSx[hd]])
                for t in range(NTILE - 1, -1, -1):
                    for hd in range(8):
                        if t < NTILE - 1:
                            S.dma("sp", Usb[:, hd % 4, :], ub_d[hd, t + 1], reads=[dbuf("ub%d" % hd, 0)], writes=[BUsb[hd % 4]])
                            stt(Sbt[:, hd, :], Sbt[:, hd, :], Dall[:, hd, t + 1:t + 2], Usb[:, hd % 4, :], ALU.mult, ALU.add, [BSbt[hd], BDall, BUsb[hd % 4]], [BSbt[hd]])
                        S.dma("sp", sb_d[hd, t], Sbt[:, hd, :], reads=[BSbt[hd]], writes=[dbuf("sb%d" % hd, 0)])

                wout_u = {}

                def wout(g):
                    if g not in wout_u:
                        wout_u[g] = wload_cols(wout_in, g * 512, 512)
                    return wout_u[g]

                Mm = (Mf, Mb)

                def headB(hd, hs, h, Bhh):
                    for sec, dstT, Bd in ((0, qbT, Bqb), (4, gsT, Bgs)):
                        pq, Bpq = proj(sec, hd, h, Bhh)
                        act(dstT[:, hs, :], pq, AF.Tanh, [Bpq], [Bd[hs]], scale=0.5)
                        act(hx[:, hs % 2, :], pq, AF.Copy, [Bpq], [Bhx[hs % 2]], scale=0.5)
                        yield
                        stt(dstT[:, hs, :], dstT[:, hs, :], 1.0, hx[:, hs % 2, :], ALU.add, ALU.mult, [Bd[hs], Bhx[hs % 2]], [Bd[hs]])
                        yield

                def laneB(t, hd, dr, li, pi, hs, h, Bhh):
                    A, B_, C_, D_ = gA[li], gB[li], gC[li], gD[li]
                    yield from gates_gen(dr, hd, h, Bhh, li)
                    S.op("dve", lambda e: e.tensor_tensor_scan(out=C_, data0=rmask[:], data1=B_, initial=0.0, op0=ALU.mult, op1=ALU.add),
                         reads=[BgB[li], Brm], writes=[BgC[li]])
                    yield
                    C3 = C_.rearrange("p (c j) -> p c j", j=64)
                    if dr == 1:
                        D3 = D_.rearrange("p (c j) -> p c j", j=64)
                        tt(D3, C3[:, :, 63:64].to_broadcast([128, NCH, 64]), C3, ALU.subtract, [BgC[li]], [BgD_[li]])
                        yield
                        tt(C_, D_, B_, ALU.add, [BgD_[li], BgB[li]], [BgC[li]])
                        yield
                        rcol = C3[:, :, 0]
                    else:
                        rcol = C3[:, :, 63]
                    act(erT[:, pi, :], rcol, AF.Exp, [BgC[li]], [Ber[pi]])
                    yield
                    act(D_, C_, AF.Exp, [BgC[li]], [BgD_[li]])
                    yield
                    tt(Qt[pi], qbT[:, hs, :], D_, ALU.mult, [Bqb[hs], BgD_[li]], [BQt[pi]])
                    yield
                    act(D_, C_, AF.Exp, [BgC[li]], [BgD_[li]], scale=-1.0)
                    yield
                    tt(A, A, D_, ALU.mult, [BgA[li], BgD_[li]], [BgA[li]])
                    yield
                    cp(Kt[pi], A, [BgA[li]], [BKt[pi]], eng="act")
                    yield
                    tt(Kh[pi].rearrange("p (c j) -> p c j", j=64), A.rearrange("p (c j) -> p c j", j=64),
                       erT[:, pi, :].unsqueeze(2).to_broadcast([128, NCH, 64]), ALU.mult, [BgA[li], Ber[pi]], [BKh[pi]])
                    yield
                    yield from transp_gen(pi, li)

                def par_gens(*gens):
                    gens = list(gens)
                    while gens:
                        nxt = []
                        for g in gens:
                            try:
                                next(g)
                                nxt.append(g)
                            except StopIteration:
                                pass
                        gens = nxt
                        yield

                Pm5 = sqb[:, 0:4, :].rearrange("p a n -> p (a n)").rearrange("p (a s d n) -> p a s d n", a=2, s=2, d=2)

                def fwd_chain(t, hd, pf, par):
                    cp(Sbf[:, par, 0, 0, :], Sfr[:, hd, :], [BSfr[hd]], [BSbf[par]], eng="act")
                    yield
                    for ch in range(NCH):
                        sbk, rows = ch // 2, slice((ch % 2) * 64, (ch % 2) * 64 + 64)
                        pu, Bpu = ph[12][:, par * 128:(par + 1) * 128], PB[12]
                        mm(pu, Khtm[pf][rows, sbk, :], Vtm[rows, sbk, hd * 128:(hd + 1) * 128], True, True, [BKhtm[pf], BV], [Bpu], True)
                        yield
                        stt(Sfr[:, hd, :], Sfr[:, hd, :], erT[:, pf, ch:ch + 1], pu, ALU.mult, ALU.add, [BSfr[hd], Ber[pf], Bpu], [BSfr[hd]])
                        yield
                        if ch < NCH - 1:
                            cp(Sbf[:, par, 0, ch + 1, :], Sfr[:, hd, :], [BSfr[hd]], [BSbf[par]], eng="act")
                            yield

                def bwd_chain(t, hd, pb, par):
                    S.dma("sp", Sst[:, par, :], sb_d[hd, t], reads=[dbuf("sb%d" % hd, 0)], writes=[BSst[par]])
                    cp(Sbf[:, par, 1, NCH, :], Sst[:, par, :], [BSst[par]], [BSbf2[par]], eng="act")
                    yield
                    for ch in range(NCH - 1, 0, -1):
                        sbk, rows = ch // 2, slice((ch % 2) * 64, (ch % 2) * 64 + 64)
                        pu, Bpu = ph[13][:, par * 128:(par + 1) * 128], PB[13]
                        mm(pu, Khtm[pb][rows, sbk, :], Vtm[rows, sbk, hd * 128:(hd + 1) * 128], True, True, [BKhtm[pb], BV], [Bpu], True)
                        yield
                        stt(Sst[:, par, :], Sst[:, par, :], erT[:, pb, ch:ch + 1], pu, ALU.mult, ALU.add, [BSst[par], Ber[pb], Bpu], [BSst[par]])
                        yield
                        cp(Sbf[:, par, 1, ch, :], Sst[:, par, :], [BSst[par]], [BSbf2[par]], eng="act")
                        yield

                def scores_gen(hd, pf, pb, par):
                    pset = (pf, pb)
                    for sbk in range(NSB):
                        cols = slice(sbk * 128, (sbk + 1) * 128)
                        for dr in range(2):
                            psc, Bps = ph[3 - par][:, dr * 128:(dr + 1) * 128], PB[3]
                            mm(psc, Kt[pset[dr]][:, cols], Qt[pset[dr]][:, cols], True, True, [BKt[pset[dr]], BQt[pset[dr]]], [Bps], True)
                            yield
                            tt(Pm5[:, par, sbk, dr, :], psc, Mm[dr][:], ALU.mult, [Bps, BM], [Bsq])
                            yield

                def postB(t, hd, pf, pb, hs, par, h, Bhh):
                    yield from fwd_chain(t, hd, pf, par)
                    yield from bwd_chain(t, hd, pb, par)
                    yield from scores_gen(hd, pf, pb, par)
                    po, Bpo = ph[par], PB[0]
                    for sbk in range(NSB):
                        cols = slice(sbk * 128, (sbk + 1) * 128)
                        mm(po[:, cols], Vtm[:, sbk, hd * 128:(hd + 1) * 128], Pm5[:, par, sbk, 0, :], True, False, [BV, Bsq], [Bpo], False)
                        mm(po[:, cols], Vtm[:, sbk, hd * 128:(hd + 1) * 128], Pm5[:, par, sbk, 1, :], False, False, [BV, Bsq], [Bpo], False)
                        for cc in (2 * sbk, 2 * sbk + 1):
                            c64 = slice(cc * 64, cc * 64 + 64)
                            mm(po[:, c64], Sbf[:, par, 0, cc, :], Qt[pf][:, c64], False, False, [BSbf[par], BQt[pf]], [Bpo], False)
                            mm(po[:, c64], Sbf[:, par, 1, cc + 1, :], Qt[pb][:, c64], False, True, [BSbf2[par], BQt[pb]], [Bpo], True)
                        yield
                    cp(osbT[:, par, :], po, [Bpo], [Bosb[par]], eng="act")
                    yield
                    tt(osqT[:, par, :], osbT[:, par, :], osbT[:, par, :], ALU.mult, [Bosb[par]], [Bosq[par]])
                    yield
                    pst, Bpst = ph[14 + par], PB[15]
                    mm(pst, ones_h[:], osqT[:, par, :], True, True, [Bones, Bosq[par]], [Bpst], True)
                    yield
                    act(rsl[:, par, :], pst, AF.Ln, [Bpst, Beps], [Brsl[par]], bias=epsc[:, 0:1])
                    yield
                    act(rsl[:, par, :], rsl[:, par, :], AF.Exp, [Brsl[par]], [Brsl[par]], scale=-0.5)
                    yield
                    stt(osbT[:, par, :], osbT[:, par, :], vecs[:, V_GNG + hd:V_GNG + hd + 1], rsl[:, par, :], ALU.mult, ALU.mult, [Bosb[par], Bvecs, Brsl[par]], [Bosb[par]])
                    yield
                    tt(on[:, hd, :], osbT[:, par, :], gsT[:, hs, :], ALU.mult, [Bosb[par], Bgs[hs]], [Bon])
                    yield

                def postpair(t, gi, h, Bhh):
                    gens = []
                    for hh in range(2):
                        hd = 2 * gi + hh
                        base = (gi % 2) * 4 + hh * 2
                        gens.append(postB(t, hd, base, base + 1, (gi % 2) * 2 + hh, hh, h, Bhh))
                    for g in gens:
                        yield from g

                for t in range(NTILE):
                    xt, Bxx, h, Bhh = load_norm(t)
                    pending = None
                    for gi in range(4):
                        gens = []
                        for hh in range(2):
                            gens.append(headB(2 * gi + hh, (gi % 2) * 2 + hh, h, Bhh))
                        for hh in range(2):
                            for dr in range(2):
                                li = hh * 2 + dr
                                gens.append(laneB(t, 2 * gi + hh, dr, li, (gi % 2) * 4 + li, (gi % 2) * 2 + hh, h, Bhh))
                        if pending is not None:
                            gens.append(pending)
                        run_lanes(gens)
                        pending = postpair(t, gi, h, Bhh)
                    run_lanes([pending])
                    for dc in range(8):
                        pw, Bpw = (ph[2], PB[2]) if dc % 2 == 0 else (ph[4], PB[4])
                        wv, wb_ = wout(dc // 4)
                        for c in range(8):
                            mm(pw, wv[:, c, (dc % 4) * 128:(dc % 4 + 1) * 128], on[:, c, :], c == 0, c == 7, [Bon] + wb_, [Bpw], c == 7)
                        tt(xt[:, dc, :], xt[:, dc, :], pw, ALU.add, [Bxx, Bpw], [Bxx])
                    S.dma("sp", dview(dst, t * TT, TT), xt[:], reads=[Bxx], writes=[dbuf(dname, t)])
                S.op("dve", lambda e: e.memset(wa[:, TAIL0 * 1024:TAIL0 * 1024 + 16], 0.0), reads=lane_bufs, writes=WB[TAIL0:NBLK])

        def stage_out(src, sname, do_norm=True):
            S.barrier()
            with nc.sbuf_tensor(U("xo"), [128, 8, TT], F32) as xo, nc.sbuf_tensor(U("ot"), [128, 2, D], F32) as ot:
                Bxo, Bot = Buf("xo"), Buf("ot")
                toks = []
                for t in range(NTILE):
                    xt, Bxx, h, Bhh = next_xt()
                    S.dma("sp", xt[:], dview(src, t * TT, TT), reads=[dbuf(sname, t)], writes=[Bxx])
                    if do_norm:
                        i = nstat["i"] % 2
                        nstat["i"] += 1
                        tt(sqb[:], xt[:], xt[:], ALU.mult, [Bxx], [Bsq])
                        for c in range(8):
                            mm(ph[14 + i], ones_d[:], sqb[:, c, :], c == 0, c == 7, [Bones, Bsq], [PB[14 + i]], c == 7)
                        rsqrt_eps(rstd[i][:], ph[14 + i], [PB[14 + i]], Brstd[i])
                        for c in range(8):
                            stt(xo[:, c, :], xt[:, c, :], vecs[:, V_FG + c:V_FG + c + 1], rstd[i][:], ALU.mult, ALU.mult, [Bxx, Brstd[i], Bvecs], [Bxo])
                        srcx, Bsrc = xo, Bxo
                    else:
                        srcx, Bsrc = xt, Bxx
                    for b in range(2):
                        for half in range(2):
                            pbk = pbank[1 + half]
                            for c4 in range(4):
                                c = half * 4 + c4
                                S.op("pe", lambda e, pbk=pbk, c4=c4, c=c, b=b, srcx=srcx: e.transpose(pbk[:, c4 * 128:(c4 + 1) * 128], srcx[:, c, b * 128:(b + 1) * 128], identf[:]),
                                     reads=[Bsrc, Bidf], writes=[PB[2 + 2 * half], PB[3 + 2 * half]], sig=(c4 == 3))
                            cp(ot[:, b, half * 512:(half + 1) * 512], pbk[:, :], [PB[2 + 2 * half], PB[3 + 2 * half]], [Bot], eng=("act" if half else "dve"))
                    toks.append(S.dma("sp", out_d[t * TT:(t + 1) * TT, :].rearrange("(b p) d -> p b d", p=128), ot[:], reads=[Bot], writes=[dbuf("out", t)]))
                for tk in toks:
                    S.wait_tok("sp", tk)

        stage_in()
        cur, cname, oth, oname = xsA, "A", xsB, "B"
        if upto >= 1:
            stage_ffn(0, 0, xsA, "A", xsB, "B", halo=(xhA, "hA", xhB, "hB"))
            cur, cname, oth, oname = xsB, "B", xsA, "A"
        if upto >= 2:
            stage_conv(xsB, "B", xhB, "hB", xsA, "A")
            cur, cname, oth, oname = xsA, "A", xsB, "B"
        if upto >= 3:
            stage_ffn(0, 1, xsA, "A", xsB, "B")
            cur, cname = xsB, "B"
        if upto >= 4:
            stage_ffn(1, 0, xsB, "B", xsA, "A")
            cur, cname = xsA, "A"
        if upto >= 5:
            stage_hgrn(xsA, "A", xsB, "B")
            cur, cname = xsB, "B"
        if cc_mode != "produce":
            if upto >= 6:
                stage_ffn(1, 1, xsB, "B", xsA, "A")
                cur, cname = xsA, "A"
            stage_out(cur, cname, do_norm=(upto >= 7))

        with nc.Block() as block:
            S.emit(block)
    return nc


def colpack(v):
    v = np.asarray(v, dtype=np.float32).reshape(-1, 128)
    return np.ascontiguousarray(v.T)


def needed_weights(upto):
    need = []
    if upto >= 1:
        need += ["w13_00", "w2_00"]
    if upto >= 2:
        need += ["conv_w_pw1", "conv_w_pw2"]
    if upto >= 3:
        need += ["w13_01", "w2_01"]
    if upto >= 4:
        need += ["w13_10", "w2_10"]
    if upto >= 5:
        need += ["hgrn_w_in", "hgrn_w_out"]
    if upto >= 6:
        need += ["w13_11", "w2_11"]
    return need


def make_inputs(upto, x, norm_g, ffn_w13, ffn_w2, conv_w_pw1, conv_b_pw1, conv_w_dw, conv_b_dw, conv_ln_g, conv_ln_b,
                conv_w_pw2, conv_b_pw2, hgrn_w_in, hgrn_lb, hgrn_gn_g, hgrn_w_out, final_g):
    f = lambda a: np.ascontiguousarray(np.asarray(a, dtype=np.float32))
    vecs = np.zeros((128, NV), np.float32)
    vecs[:, V_NG:V_NG + 48] = colpack(f(norm_g).reshape(-1))
    vecs[:, V_FG:V_FG + 8] = colpack(final_g)
    vecs[:, V_BPW1:V_BPW1 + 16] = colpack(f(conv_b_pw1)[0])
    wdw = f(conv_w_dw)[0]
    for c in range(8):
        vecs[:, V_WDW + c * 31:V_WDW + (c + 1) * 31] = wdw[:, c * 128:(c + 1) * 128].T
    vecs[:, V_BDW:V_BDW + 8] = colpack(f(conv_b_dw)[0])
    vecs[:, V_LNG:V_LNG + 8] = colpack(f(conv_ln_g)[0])
    vecs[:, V_LNB:V_LNB + 8] = colpack(f(conv_ln_b)[0])
    vecs[:, V_BPW2:V_BPW2 + 8] = colpack(f(conv_b_pw2)[0])
    vecs[:, V_HLB:V_HLB + 32] = colpack(f(hgrn_lb).reshape(-1))
    vecs[:, V_GNG:V_GNG + 8] = colpack(f(hgrn_gn_g)[0])
    x = f(x)
    allw = {"conv_w_pw1": f(conv_w_pw1)[0], "conv_w_pw2": f(conv_w_pw2)[0], "hgrn_w_in": f(hgrn_w_in)[0], "hgrn_w_out": f(hgrn_w_out)[0]}
    for l in range(2):
        for i in range(2):
            allw["w13_%d%d" % (l, i)] = f(np.asarray(ffn_w13)[l, i])
            allw["w2_%d%d" % (l, i)] = f(np.asarray(ffn_w2)[l, i])
    shared = {"vecs": vecs}
    for k in needed_weights(upto):
        shared[k] = allw[k]
    in_maps = []
    for c in range(NCORES):
        b, pos = c // 4, c % 4
        t0 = pos * NT
        xh = np.zeros((2 * HALO, D), np.float32)
        cvec = np.zeros((128, NCV), np.float32)
        if pos > 0:
            xh[:HALO] = x[b, t0 - HALO:t0]
            cvec[:, C_HM:C_HM + HALO] = 1.0
        if pos < 3:
            xh[HALO:] = x[b, t0 + NT:t0 + NT + HALO]
            cvec[:, C_HM + HALO:C_HM + 2 * HALO] = 1.0
        for r in range(NCORES):
            if r // 4 == b and r < c:
                cvec[:, C_RM + r] = 1.0
            if r // 4 == b and r > c:
                cvec[:, C_RM + 8 + r] = 1.0
        cvec[:, C_RM1:C_RM1 + 16] = 1.0 - cvec[:, C_RM:C_RM + 16]
        m = dict(shared)
        m["x"] = np.ascontiguousarray(x[b, t0:t0 + NT])
        m["xh"] = xh
        m["cvec"] = cvec
        in_maps.append(m)
    return in_maps


_NC_CACHE = {}
FUSED = True


def run(inputs, upto=99):
    in_maps = make_inputs(upto, **inputs)
    if FUSED or upto < 5:
        if upto not in _NC_CACHE:
            _NC_CACHE[upto] = build(upto)
        res = run_bass_kernel_spmd(_NC_CACHE[upto], in_maps, core_ids=list(range(NCORES)))
    else:
        ncA = build(upto, "produce")
        resA = run_bass_kernel_spmd(ncA, in_maps, core_ids=list(range(NCORES)))
        allcc = np.concatenate([resA.results[c]["cc_in"] for c in range(NCORES)], axis=0)
        for m in in_maps:
            m["cc_out"] = allcc
        ncB = build(upto, "consume")
        res = run_bass_kernel_spmd(ncB, in_maps, core_ids=list(range(NCORES)))
    out = np.zeros((2, 4 * NT, D), np.float32)
    for c in range(NCORES):
        out[c // 4, (c % 4) * NT:(c % 4 + 1) * NT] = res.results[c]["out"]
    return out


def kernel(**inputs):
    return run(inputs, 99)
```

```python
import contextlib
import numpy as np
import concourse.bass as bass
import concourse.mybir as mybir
from concourse.bass_utils import run_bass_kernel_spmd

F32 = mybir.dt.float32
BF16 = mybir.dt.bfloat16
AF = mybir.ActivationFunctionType
ALU = mybir.AluOpType

D = 1024
FF = 2816
NT = 4096
TT = 256
NTILE = NT // TT
HALO = 16
EPS = 1e-6
NBLK = 66
NCORES = 8

V_NG = 0
V_FG = 48
V_BPW1 = 56
V_WDW = 72
V_BDW = 320
V_LNG = 328
V_LNB = 336
V_BPW2 = 344
V_HLB = 352
V_GNG = 384
NV = 392
C_HM = 0
C_RM = 32
C_RM1 = 48
NCV = 64


class Buf:
    __slots__ = ("name", "w", "r", "dsem", "dcnt", "excl")

    def __init__(self, name, excl=False):
        self.name = name
        self.excl = excl
        self.w = None
        self.r = {}
        self.dsem = None
        self.dcnt = 0


class Sched:
    ENGS = ("pe", "act", "dve", "pool", "sp")
    ROLL = 30000

    def __init__(self, nc):
        self.nc = nc
        self.prog = {e: [] for e in self.ENGS}
        self.sem = {}
        self.cnt = {e: 0 for e in self.ENGS}
        self.seen = {e: {} for e in self.ENGS}
        self.nsem = 0
        self.pool_pending = []
        self.sp_out = {}
        for e in self.ENGS:
            self.sem[e] = self.new_sem("eng_" + e)

    def new_sem(self, name):
        self.nsem += 1
        return self.nc.alloc_semaphore("%s_%d" % (name, self.nsem))

    def _waits(self, eng, reads, writes, is_dma=False):
        own = self.sem[eng]
        need = {}

        def add(tok, kind):
            sem, val = tok
            if sem is own and not is_dma:
                if eng == "pe" or kind != "raw":
                    return
            k = id(sem)
            if self.seen[eng].get(k, 0) >= val:
                return
            if k not in need or need[k][1] < val:
                need[k] = (sem, val)

        for b in reads:
            if b.w is not None:
                add(b.w, "raw")
        for b in writes:
            if b.w is not None:
                add(b.w, "waw")
            for t in b.r.values():
                add(t, "war")
        out = []
        for k, (sem, val) in need.items():
            self.seen[eng][k] = val
            out.append((sem, val))
        return out

    def _mark(self, tok, reads, writes):
        k = id(tok[0])
        for b in reads:
            o = b.r.get(k)
            if o is None or o[1] < tok[1]:
                b.r[k] = tok
        for b in writes:
            b.w = tok
            b.r = {}

    def op(self, eng, fn, reads=(), writes=(), sig=True):
        ex = [b for b in reads if b.excl]
        if ex:
            reads = [b for b in reads if not b.excl]
            writes = list(writes) + ex
        waits = self._waits(eng, reads, writes)
        if eng == "pool" and self.pool_pending:
            for sem, val in self.pool_pending:
                if self.seen[eng].get(id(sem), 0) < val:
                    self.seen[eng][id(sem)] = val
                    waits.append((sem, val))
            self.pool_pending = []
        tok = (self.sem[eng], self.cnt[eng] + 1)
        self._mark(tok, reads, writes)
        inc = None
        if sig:
            self.cnt[eng] += 1
            inc = (self.sem[eng], 1)
        self.prog[eng].append((waits, fn, inc))
        if sig and self.cnt[eng] >= self.ROLL:
            self.sem[eng] = self.new_sem("eng_" + eng)
            self.cnt[eng] = 0

    def dma(self, q, out, in_, reads=(), writes=(), prim=None):
        waits = self._waits(q, reads, writes, is_dma=True)
        if prim is None:
            prim = writes[0] if writes else reads[0]
        if prim.dsem is None:
            prim.dsem = self.new_sem("d")
        if prim.dcnt >= self.ROLL:
            prim.dsem = self.new_sem("d")
            prim.dcnt = 0
        prim.dcnt += 16
        tok = (prim.dsem, prim.dcnt)
        self._mark(tok, reads, writes)
        if q == "sp":
            self.sp_out[id(prim.dsem)] = tok
        self.prog[q].append((waits, lambda e, o=out, i=in_: e.dma_start(out=o, in_=i), (prim.dsem, 16)))
        return tok

    def barrier(self):
        toks = [(self.sem[e], self.cnt[e]) for e in ("pe", "act", "dve", "pool") if self.cnt[e] > 0]
        toks += list(self.sp_out.values())
        self.sp_out = {}
        for e in ("pe", "act", "dve", "sp"):
            w = []
            for sem, val in toks:
                if sem is self.sem.get(e):
                    continue
                if self.seen[e].get(id(sem), 0) < val:
                    self.seen[e][id(sem)] = val
                    w.append((sem, val))
            if w:
                self.prog[e].append((w, None, None))
        self.pool_pending = [t for t in toks if t[0] is not self.sem["pool"]]

    def wait_tok(self, eng, tok):
        self.prog[eng].append(([tok], None, None))

    def emit(self, block):
        m = {"pe": "tensor", "act": "scalar", "dve": "vector", "pool": "gpsimd", "sp": "sync"}
        for e in self.ENGS:
            items = self.prog[e]

            def body(engine, items=items):
                for waits, fn, inc in items:
                    for sem, val in waits:
                        engine.wait_ge(sem, val)
                    if fn is not None:
                        ins = fn(engine)
                        if inc is not None:
                            ins.then_inc(inc[0], inc[1])

            getattr(block, m[e])(body)


def build(upto=99, cc_mode="fused"):
    nc = bass.Bass("TRN2", target_bir_lowering=False)

    def dram(name, shape, dt=F32, kind="Internal"):
        return nc.dram_tensor(name, list(shape), dt, kind=kind).ap()

    x_in = dram("x", [NT, D], kind="ExternalInput")
    xh_in = dram("xh", [2 * HALO, D], kind="ExternalInput")
    vecs_in = dram("vecs", [128, NV], kind="ExternalInput")
    cvec_in = dram("cvec", [128, NCV], kind="ExternalInput")
    need = needed_weights(upto)
    w13_d = {k: dram("w13_%d%d" % k, [D, 2 * FF], kind="ExternalInput") for k in [(0, 0), (0, 1), (1, 0), (1, 1)] if ("w13_%d%d" % k) in need}
    w2_d = {k: dram("w2_%d%d" % k, [FF, D], kind="ExternalInput") for k in [(0, 0), (0, 1), (1, 0), (1, 1)] if ("w2_%d%d" % k) in need}
    if "conv_w_pw1" in need:
        wpw1_in = dram("conv_w_pw1", [D, 2 * D], kind="ExternalInput")
        wpw2_in = dram("conv_w_pw2", [D, D], kind="ExternalInput")
    if "hgrn_w_in" in need:
        win_in = dram("hgrn_w_in", [D, 5 * D], kind="ExternalInput")
        wout_in = dram("hgrn_w_out", [D, D], kind="ExternalInput")
    out_d = dram("out", [NT, D], kind="ExternalOutput")

    xsA = dram("xsA", [D, NT])
    xsB = dram("xsB", [D, NT])
    xhA = dram("xhA", [D, 2 * HALO])
    xhB = dram("xhB", [D, 2 * HALO])
    ub_d = dram("ub_d", [8, NTILE, 128, 128])
    sb_d = dram("sb_d", [8, NTILE, 128, 128])
    cc_in = dram("cc_in", [17 * 128, 128], kind=("ExternalOutput" if cc_mode == "produce" else "Internal"))
    cc_out = dram("cc_out", [NCORES * 17 * 128, 128], kind=("ExternalInput" if cc_mode == "consume" else "Internal"))

    _uc = {"n": 0}

    def U(name):
        _uc["n"] += 1
        return "%s_u%d" % (name, _uc["n"])

    es = contextlib.ExitStack()
    with es:
        S = Sched(nc)

        def sb(name, shape, dt=F32):
            return es.enter_context(nc.sbuf_tensor("s_" + name, list(shape), dt))

        wa = sb("wa", [128, NBLK * 1024], BF16)
        WB = [Buf("wa%d" % i) for i in range(NBLK)]
        vecs = sb("vecs", [128, NV]); Bvecs = Buf("vecs")
        cvec = sb("cvec", [128, NCV]); Bcvec = Buf("cvec")
        identf = sb("identf", [128, 128]); Bidf = Buf("identf")
        identb = sb("identb", [128, 128], BF16); Bidb = Buf("identb")
        ones_d = sb("ones_d", [128, 128], BF16); Bones = Buf("ones")
        ones_h = sb("ones_h", [128, 128], BF16)
        Mf = sb("Mf", [128, 128]); Mb = sb("Mb", [128, 128]); BM = Buf("M")
        rmask = sb("rmask", [128, TT]); ones_t = sb("ones_t", [128, TT]); Brm = Buf("rmask")
        lbv = sb("lbv", [128, 2, 8]); oml = sb("oml", [128, 2, 8]); noml = sb("noml", [128, 2, 8]); Blb = Buf("lb")
        Dall = sb("Dall", [128, 8, NTILE]); BDall = Buf("Dall")
        xts = [sb("xt%d" % i, [128, 8, TT]) for i in range(2)]; Bxt = [Buf("xt%d" % i) for i in range(2)]
        hbuf = [sb("h%d" % i, [128, 8, TT], BF16) for i in range(2)]; Bh = [Buf("h%d" % i) for i in range(2)]
        sqb = sb("sqb", [128, 8, TT], BF16); Bsq = Buf("sq")
        rstd = [sb("rstd%d" % i, [128, TT]) for i in range(2)]; Brstd = [Buf("rstd%d" % i) for i in range(2)]

        pbank = [es.enter_context(nc.psum_tensor("pb%d" % i, [128, 512], F32)) for i in range(8)]
        ph = []
        for i in range(8):
            ph.append(pbank[i][:, 0:256])
            ph.append(pbank[i][:, 256:512])
        _PBK = [Buf("pbank%d" % i, excl=True) for i in range(8)]
        PB = [_PBK[i // 2] for i in range(16)]

        def mm(out, lhsT, rhs, start, stop, reads, writes, sig, sgc=False):
            if sgc:
                S.op("pe", lambda e: e.matmul(out, lhsT, rhs, start=start, stop=stop, skip_group_check=True), reads=reads, writes=writes, sig=sig)
            else:
                S.op("pe", lambda e: e.matmul(out, lhsT, rhs, start=start, stop=stop), reads=reads, writes=writes, sig=sig)

        def act(out, in_, func, reads, writes, bias=None, scale=None):
            kw = {}
            if bias is not None:
                kw["bias"] = bias
            if scale is not None:
                kw["scale"] = scale
            S.op("act", lambda e: e.activation(out=out, in_=in_, func=func, **kw), reads=reads, writes=writes)

        def tt(out, in0, in1, op, reads, writes, eng="dve"):
            S.op(eng, lambda e: e.tensor_tensor(out=out, in0=in0, in1=in1, op=op), reads=reads, writes=writes)

        def ts(out, in0, s1, s2, op0, op1, reads, writes, eng="dve"):
            if s2 is None:
                S.op(eng, lambda e: e.tensor_scalar(out=out, in0=in0, scalar1=s1, scalar2=None, op0=op0), reads=reads, writes=writes)
            else:
                S.op(eng, lambda e: e.tensor_scalar(out=out, in0=in0, scalar1=s1, scalar2=s2, op0=op0, op1=op1), reads=reads, writes=writes)

        def stt(out, in0, scalar, in1, op0, op1, reads, writes, eng="dve"):
            S.op(eng, lambda e: e.scalar_tensor_tensor(out=out, in0=in0, scalar=scalar, in1=in1, op0=op0, op1=op1), reads=reads, writes=writes)

        def cp(out, in_, reads, writes, eng="dve"):
            if eng == "act":
                act(out, in_, AF.Copy, reads, writes)
            else:
                S.op(eng, lambda e: e.tensor_copy(out, in_), reads=reads, writes=writes)

        ring = {"p": 0}

        def walloc(n):
            if ring["p"] + n > NBLK:
                ring["p"] = 0
            s = ring["p"]
            ring["p"] += n
            return s, WB[s:s + n]

        def wstage(nblocks):
            if ring["p"] + nblocks > NBLK:
                ring["p"] = 0

        wsemB = [Buf("wsem%d" % i) for i in range(16)]
        wcnt = {"k": 0}

        def wprim():
            b = wsemB[wcnt["k"] % 16]
            wcnt["k"] += 1
            return b

        def wload_cols(src2d, col0, ncols):
            nb = (8 * ncols + 1023) // 1024
            s, bufs = walloc(nb)
            view = wa[:, s * 1024: s * 1024 + 8 * ncols].rearrange("p (c n) -> p c n", c=8)
            pb_ = wprim()
            S.dma("pool", view, src2d[:, col0:col0 + ncols].rearrange("(c p) n -> p c n", p=128), writes=bufs + [pb_], prim=pb_)
            return view, bufs

        def wload_rows(src2d, row0, nrows):
            f = nrows // 128
            s, bufs = walloc(f)
            view = wa[:, s * 1024: (s + f) * 1024].rearrange("p (f n) -> p f n", f=f)
            pb_ = wprim()
            S.dma("pool", view, src2d[row0:row0 + nrows, :].rearrange("(f p) n -> p f n", p=128), writes=bufs + [pb_], prim=pb_)
            return view, bufs

        S.dma("sp", vecs[:], vecs_in, writes=[Bvecs])
        S.dma("sp", cvec[:], cvec_in, writes=[Bcvec])
        S.op("pool", lambda e: e.memset(identf[:], 0.0), writes=[Bidf])
        S.op("pool", lambda e: e.affine_select(out=identf[:], in_=identf[:], pattern=[[-1, 128]], compare_op=ALU.not_equal,
                                               fill=1.0, base=0, channel_multiplier=1), reads=[Bidf], writes=[Bidf])
        cp(identb[:], identf[:], [Bidf], [Bidb], eng="pool")
        S.op("pool", lambda e: e.memset(ones_d[:], 1.0 / 1024.0), writes=[Bones])
        S.op("pool", lambda e: e.memset(ones_h[:], 1.0 / 128.0), writes=[Bones])
        S.op("pool", lambda e: e.memset(Mf[:], 1.0), writes=[BM])
        S.op("pool", lambda e: e.affine_select(out=Mf[:], in_=Mf[:], pattern=[[1, 128]], compare_op=ALU.is_ge,
                                               fill=0.0, base=0, channel_multiplier=-1), reads=[BM], writes=[BM])
        S.op("pool", lambda e: e.memset(Mf[0:64, 64:128], 0.0), reads=[BM], writes=[BM])
        S.op("pool", lambda e: e.memset(Mb[:], 1.0), reads=[BM], writes=[BM])
        S.op("pool", lambda e: e.affine_select(out=Mb[:], in_=Mb[:], pattern=[[-1, 128]], compare_op=ALU.is_ge,
                                               fill=0.0, base=0, channel_multiplier=1), reads=[BM], writes=[BM])
        S.op("pool", lambda e: e.memset(Mb[64:128, 0:64], 0.0), reads=[BM], writes=[BM])
        S.op("pool", lambda e: e.memset(rmask[:], 1.0), writes=[Brm])
        S.op("pool", lambda e: e.memset(rmask[:].rearrange("p (c j) -> p c j", j=64)[:, :, 0:1], 0.0), reads=[Brm], writes=[Brm])
        S.op("pool", lambda e: e.memset(ones_t[:], 1.0), reads=[Brm], writes=[Brm])
        epsc = sb("epsc", [128, 1]); Beps = Buf("epsc")
        S.op("pool", lambda e: e.memset(epsc[:], EPS), writes=[Beps])
        S.barrier()
        hl = vecs[:, V_HLB:V_HLB + 32].rearrange("p (d l h) -> p d l h", d=2, l=2)
        tt(lbv[:], hl[:, :, 1, :], hl[:, :, 0, :], ALU.subtract, [Bvecs], [Blb])
        act(lbv[:], lbv[:], AF.Sigmoid, [Blb], [Blb])
        ts(oml[:], lbv[:], -1.0, 1.0, ALU.mult, ALU.add, [Blb], [Blb])
        ts(noml[:], oml[:], -1.0, None, ALU.mult, None, [Blb], [Blb])
        homl = sb("homl", [128, 2, 8]); c1v = sb("c1v", [128, 2, 8])
        ts(homl[:], oml[:], 0.5, None, ALU.mult, None, [Blb], [Blb])
        tt(c1v[:], homl[:], lbv[:], ALU.add, [Blb], [Blb])

        def rsqrt_eps(out, in_, reads, Bout):
            act(out, in_, AF.Sqrt, list(reads) + [Beps], [Bout], bias=epsc[:, 0:1])
            S.op("dve", lambda e: e.reciprocal(out, out), reads=[Bout], writes=[Bout])

        nstat = {"i": 0}

        def rmsnorm(xt, Bx, N, gcol, h, Bhh):
            i = nstat["i"] % 2
            nstat["i"] += 1
            st, Bst = ph[14 + i][:, :N], PB[14 + i]
            r, Br = rstd[i][:, :N], Brstd[i]
            tt(sqb[:, :, :N], xt[:, :, :N], xt[:, :, :N], ALU.mult, [Bx], [Bsq])
            for c in range(8):
                mm(st, ones_d[:], sqb[:, c, :N], c == 0, c == 7, [Bones, Bsq], [Bst], c == 7)
            rsqrt_eps(r, st, [Bst], Br)
            for c in range(8):
                stt(h[:, c, :N], xt[:, c, :N], vecs[:, gcol + c:gcol + c + 1], r, ALU.mult, ALU.mult, [Bx, Br, Bvecs], [Bhh])

        xcnt = {"i": 0}

        def next_xt():
            i = xcnt["i"] % 2
            xcnt["i"] += 1
            return xts[i], Bxt[i], hbuf[i], Bh[i]

        def dview(dr, off, N):
            return dr[:, off:off + N].rearrange("(c p) n -> p c n", p=128)

        Bx = {}

        def dbuf(name, t):
            k = (name, 0)
            if k not in Bx:
                Bx[k] = Buf("%s_%s" % (name, t))
            return Bx[k]

        def stage_in():
            with nc.sbuf_tensor(U("xin"), [128, 2, D], F32) as xin:
                Bxin = Buf("xin")
                jobs = [(x_in, t * TT, TT, xsA, t * TT, ("A", t)) for t in range(NTILE)] + [(xh_in, 0, 2 * HALO, xhA, 0, ("hA", 0))]
                for src, soff, N, dst, doff, key in jobs:
                    nb = (N + 127) // 128
                    rows = min(N, 128)
                    S.dma("sp", xin[:rows, :nb, :], src[soff:soff + N, :].rearrange("(b p) d -> p b d", p=rows), writes=[Bxin])
                    xt, Bxx, _, _ = next_xt()
                    for c2 in range(4):
                        for ci in range(2):
                            c = 2 * c2 + ci
                            for b in range(nb):
                                S.op("pe", lambda e, c=c, b=b, rows=rows: e.transpose(ph[2 + c][:, b * 128:b * 128 + rows], xin[:rows, b, c * 128:(c + 1) * 128], identf[:rows, :rows]),
                                     reads=[Bxin, Bidf], writes=[PB[2 + c]], sig=(b == nb - 1 and ci == 1))
                        cp(xt[:, 2 * c2:2 * c2 + 2, :N], pbank[1 + c2][:, :].rearrange("p (h n) -> p h n", h=2)[:, :, :N], [PB[2 + 2 * c2]], [Bxx], eng=("act" if c2 % 2 else "dve"))
                    S.dma("sp", dview(dst, doff, N), xt[:, :, :N], reads=[Bxx], writes=[dbuf(*key)])

        def stage_ffn(l, i, src, sname, dst, dname, halo=None):
            gcol = V_NG + (l * 3 + (0 if i == 0 else 2)) * 8
            S.barrier()
            w13 = w13_d[(l, i)]
            w2 = w2_d[(l, i)]
            units = {}
            wstage(66)

            def unit(g):
                if g not in units:
                    G = 4 if g < 5 else 2
                    gv, gb = wload_cols(w13, g * 512, G * 128)
                    uv, ub = wload_cols(w13, FF + g * 512, G * 128)
                    dv, db = wload_rows(w2, g * 512, G * 128)
                    units[g] = (gv, gb, uv, ub, dv, db)
                return units[g]

            with nc.sbuf_tensor(U("sg"), [128, 2, TT], F32) as sg, nc.sbuf_tensor(U("actb"), [128, 3, TT], BF16) as actb:
                Bsg = [Buf("sg0"), Buf("sg1")]
                Bact = [Buf("act%d" % k) for k in range(3)]
                jobs = [(src, sname, dst, dname, t, t * TT, TT) for t in range(NTILE)]
                if halo is not None:
                    jobs.append((halo[0], halo[1], halo[2], halo[3], 0, 0, 2 * HALO))
                def prep(job):
                    (s_d, s_n, d_d, d_n, t, off, N) = job
                    xt, Bxx, h, Bhh = next_xt()
                    S.dma("sp", xt[:, :, :N], dview(s_d, off, N), reads=[dbuf(s_n, t)], writes=[Bxx])
                    rmsnorm(xt, Bxx, N, gcol, h, Bhh)
                    return xt, Bxx, h, Bhh

                cur = prep(jobs[0])
                for idx, (s_d, s_n, d_d, d_n, t, off, N) in enumerate(jobs):
                    xt, Bxx, h, Bhh = cur
                    LAG = 1

                    def ymm(j):
                        g, jj = j // 4, j % 4
                        dv, db = unit(g)[4], unit(g)[5]
                        a = j % 3
                        for dc in range(8):
                            mm(ph[dc][:, :N], dv[:, jj, dc * 128:(dc + 1) * 128], actb[:, a, :N], (j == 0 and dc % 2 == 0), j == 21,
                               [Bact[a]] + db, [PB[dc]], (dc == 7) or (j == 21), sgc=True)

                    for j in range(22 + LAG):
                        if j == 8 and idx + 1 < len(jobs):
                            cur = prep(jobs[idx + 1])
                        if j < 22:
                            g, jj = j // 4, j % 4
                            gv, gb, uv, ub = unit(g)[:4]
                            s3 = j % 3
                            pg, pu = ph[8 + 2 * s3][:, :N], ph[9 + 2 * s3][:, :N]
                            Bpg, Bpu = PB[8 + 2 * s3], PB[9 + 2 * s3]
                            for c in range(8):
                                mm(pg, gv[:, c, jj * 128:(jj + 1) * 128], h[:, c, :N], c == 0, c == 7, [Bhh] + gb, [Bpg], c == 7)
                            for c in range(8):
                                mm(pu, uv[:, c, jj * 128:(jj + 1) * 128], h[:, c, :N], c == 0, c == 7, [Bhh] + ub, [Bpu], c == 7)
                            act(sg[:, j % 2, :N], pg, AF.Silu, [Bpg], [Bsg[j % 2]])
                            tt(actb[:, j % 3, :N], sg[:, j % 2, :N], pu, ALU.mult, [Bsg[j % 2], Bpu], [Bact[j % 3]])
                        if j >= LAG:
                            ymm(j - LAG)
                    for dc in range(8):
                        stt(xt[:, dc, :N], ph[dc][:, :N], 0.5, xt[:, dc, :N], ALU.mult, ALU.add, [PB[dc], Bxx], [Bxx])
                    S.dma("sp", dview(d_d, off, N), xt[:, :, :N], reads=[Bxx], writes=[dbuf(d_n, t)])

        def stage_conv(src, sname, hsrc, hname, dst, dname):
            gcol = V_NG + 1 * 8
            S.barrier()
            W = TT + 32
            with nc.sbuf_tensor(U("ubuf"), [128, 8, W], BF16) as ubuf, \
                    nc.sbuf_tensor(U("sgc"), [128, 2, TT], F32) as sgc, \
                    nc.sbuf_tensor(U("vbuf"), [128, 8, TT], F32) as vbuf, \
                    nc.sbuf_tensor(U("v16"), [128, 8, TT], BF16) as v16, \
                    nc.sbuf_tensor(U("sil"), [128, 8, TT], BF16) as sil, \
                    nc.sbuf_tensor(U("lnst"), [128, 4, TT], F32) as lnst, \
                    nc.sbuf_tensor(U("xw"), [128, 8, TT], F32) as xw:
                Bub, Bsgc = Buf("ubuf"), [Buf("sgc0"), Buf("sgc1")]
                Bvb, Bv16, Bq16, Bsil, Bln, Bxw = Buf("vbuf"), Buf("v16"), Bsq, Buf("sil"), Buf("lnst"), Buf("xw")
                q16 = sqb
                wstage(55)
                pw1 = [wload_cols(wpw1_in, g * 512, 512) for g in range(4)]
                pw2 = [wload_cols(wpw2_in, g * 512, 512) for g in range(2)]
                ds_, dbufs = walloc(31)
                diag = wa[:, ds_ * 1024:(ds_ + 31) * 1024]
                for c in range(8):
                    for j in range(31):
                        m = c * 31 + j
                        ts(diag[:, m * 128:(m + 1) * 128], identb[:], vecs[:, V_WDW + m:V_WDW + m + 1], None, ALU.mult, None,
                           [Bidb, Bvecs], dbufs)
                S.op("dve", lambda e: e.memset(ubuf[:], 0.0), writes=[Bub])

                def make_u(s_d, s_n, t, off, N, ucol, hmask_col=None):
                    xt, Bxx, h, Bhh = next_xt()
                    S.dma("sp", xt[:, :, :N], dview(s_d, off, N), reads=[dbuf(s_n, t)], writes=[Bxx])
                    rmsnorm(xt, Bxx, N, gcol, h, Bhh)
                    for j in range(8):
                        s3 = j % 2
                        pa, pg = ph[2 + 2 * s3][:, :N], ph[3 + 2 * s3][:, :N]
                        Bpa, Bpg = PB[2 + 2 * s3], PB[3 + 2 * s3]
                        av, ab = pw1[j // 4]
                        gv, gb = pw1[2 + j // 4]
                        jj = j % 4
                        for c in range(8):
                            mm(pa, av[:, c, jj * 128:(jj + 1) * 128], h[:, c, :N], c == 0, c == 7, [Bhh] + ab, [Bpa], c == 7)
                        for c in range(8):
                            mm(pg, gv[:, c, jj * 128:(jj + 1) * 128], h[:, c, :N], c == 0, c == 7, [Bhh] + gb, [Bpg], c == 7)
                        act(sgc[:, j % 2, :N], pg, AF.Sigmoid, [Bpg, Bvecs], [Bsgc[j % 2]], bias=vecs[:, V_BPW1 + 8 + j:V_BPW1 + 9 + j])
                        stt(ubuf[:, j, ucol:ucol + N], pa, vecs[:, V_BPW1 + j:V_BPW1 + j + 1], sgc[:, j % 2, :N], ALU.add, ALU.mult,
                            [Bpa, Bsgc[j % 2], Bvecs], [Bub])
                    if hmask_col is not None:
                        for j in range(8):
                            tt(ubuf[:, j, ucol:ucol + N], ubuf[:, j, ucol:ucol + N], cvec[:, hmask_col:hmask_col + N], ALU.mult, [Bub, Bcvec], [Bub])

                def window1(tok0, Wn, col0):
                    S.dma("sp", xw[:, :, :Wn], dview(src, tok0, Wn), reads=[dbuf(sname, t) for t in range(NTILE)], writes=[Bxw])
                    for c in range(8):
                        pv, Bpv = ph[6 + 2 * (c % 2)][:, :Wn], PB[6 + 2 * (c % 2)]
                        for j in range(31):
                            m = c * 31 + j
                            mm(pv, diag[:, m * 128:(m + 1) * 128], ubuf[:, c, col0 - 15 + j: col0 - 15 + j + Wn], j == 0, j == 30,
                               [Bub] + dbufs, [Bpv], j == 30)
                        act(vbuf[:, c, :Wn], pv, AF.Identity, [Bpv, Bvecs], [Bvb], bias=vecs[:, V_BDW + c:V_BDW + c + 1])

                def window2(tok0, Wn, col0):
                    cp(v16[:, :, :Wn], vbuf[:, :, :Wn], [Bvb], [Bv16])
                    tt(q16[:, :, :Wn], vbuf[:, :, :Wn], vbuf[:, :, :Wn], ALU.mult, [Bvb], [Bq16])
                    p1, p2 = ph[10][:, :Wn], ph[11][:, :Wn]
                    for c in range(8):
                        mm(p1, ones_d[:], v16[:, c, :Wn], c == 0, c == 7, [Bones, Bv16], [PB[10]], c == 7)
                    for c in range(8):
                        mm(p2, ones_d[:], q16[:, c, :Wn], c == 0, c == 7, [Bones, Bq16], [PB[11]], c == 7)
                    mean, msq, var, rs = lnst[:, 0, :Wn], lnst[:, 1, :Wn], lnst[:, 2, :Wn], lnst[:, 3, :Wn]
                    cp(mean, p1, [PB[10]], [Bln])
                    tt(msq, mean, mean, ALU.mult, [Bln], [Bln])
                    tt(var, p2, msq, ALU.subtract, [PB[11], Bln], [Bln])
                    rsqrt_eps(rs, var, [Bln], Bln)
                    for c in range(8):
                        tt(vbuf[:, c, :Wn], vbuf[:, c, :Wn], mean, ALU.subtract, [Bvb, Bln], [Bvb])
                        tt(vbuf[:, c, :Wn], vbuf[:, c, :Wn], rs, ALU.mult, [Bvb, Bln], [Bvb])
                        act(sil[:, c, :Wn], vbuf[:, c, :Wn], AF.Silu, [Bvb, Bvecs], [Bsil],
                            bias=vecs[:, V_LNB + c:V_LNB + c + 1], scale=vecs[:, V_LNG + c:V_LNG + c + 1])
                    for dc in range(8):
                        po, Bpo = (ph[12][:, :Wn], PB[12]) if dc % 2 == 0 else (ph[0][:, :Wn], PB[0])
                        wv, wb_ = pw2[dc // 4]
                        for c in range(8):
                            mm(po, wv[:, c, (dc % 4) * 128:(dc % 4 + 1) * 128], sil[:, c, :Wn], c == 0, c == 7, [Bsil] + wb_, [Bpo], c == 7)
                        stt(xw[:, dc, :Wn], po, vecs[:, V_BPW2 + dc:V_BPW2 + dc + 1], xw[:, dc, :Wn], ALU.add, ALU.add, [Bpo, Bxw, Bvecs], [Bxw])
                    S.dma("sp", dview(dst, tok0, Wn), xw[:, :, :Wn], reads=[Bxw], writes=[dbuf(dname, t) for t in range(NTILE)])

                make_u(hsrc, hname, 0, 0, HALO, 16, hmask_col=C_HM)
                make_u(src, sname, 0, 0, TT, 32)
                for t in range(NTILE):
                    wargs = (0, TT - 16, 32) if t == 0 else (t * TT - 16, TT, 16)
                    window1(*wargs)
                    cp(ubuf[:, :, 0:32], ubuf[:, :, TT:TT + 32], [Bub], [Bub], eng="act")
                    if t + 1 < NTILE:
                        make_u(src, sname, t + 1, (t + 1) * TT, TT, 32)
                    else:
                        make_u(hsrc, hname, 0, HALO, HALO, 32, hmask_col=C_HM + HALO)
                    window2(*wargs)
                window1(NT - 16, 16, 16)
                window2(NT - 16, 16, 16)

        def stage_hgrn(src, sname, dst, dname):
            gcol = V_NG + (1 * 3 + 1) * 8
            NCH = TT // 64
            NSB = TT // 128
            TAIL0 = 48
            S.barrier()

            def run_lanes(gens):
                gens = list(gens)
                while gens:
                    nxt = []
                    for g in gens:
                        try:
                            next(g)
                            nxt.append(g)
                        except StopIteration:
                            pass
                    gens = nxt

            with contextlib.ExitStack() as hs_:
                def hsb(name, shape, dt=F32):
                    return hs_.enter_context(nc.sbuf_tensor(U(name), list(shape), dt))
                wstage(48)
                S.op("dve", lambda e: e.memset(wa[:, TAIL0 * 1024:TAIL0 * 1024 + 16], 0.0), writes=WB[TAIL0:NBLK])
                tail_tok = WB[TAIL0].w
                lane_bufs = []

                def LB(name):
                    b = Buf(name)
                    b.w = tail_tok
                    lane_bufs.append(b)
                    return b
                tail = {"o": TAIL0 * 1024}

                def tail_f32(n):
                    o = tail["o"]
                    tail["o"] += 2 * n
                    assert tail["o"] <= NBLK * 1024
                    return wa[:, o:o + 2 * n].bitcast(F32)

                def tail_bf(n):
                    o = tail["o"]
                    tail["o"] += n
                    assert tail["o"] <= NBLK * 1024
                    return wa[:, o:o + n]
                gA = [tail_f32(TT) for _ in range(4)]; BgA = [LB("gA%d" % i) for i in range(4)]
                gB = [tail_f32(TT) for _ in range(4)]; BgB = [LB("gB%d" % i) for i in range(4)]
                gC = [tail_f32(TT) for _ in range(4)]; BgC = [LB("gC%d" % i) for i in range(4)]
                gD = [tail_f32(TT) for _ in range(4)]; BgD_ = [LB("gD%d" % i) for i in range(4)]
                Qt = [tail_bf(TT) for _ in range(8)]; BQt = [LB("Qt%d" % i) for i in range(8)]
                Kt = [tail_bf(TT) for _ in range(8)]; BKt = [LB("Kt%d" % i) for i in range(8)]
                Kh = [tail_bf(TT) for _ in range(8)]; BKh = [LB("Kh%d" % i) for i in range(8)]
                Khtm_ = [tail_bf(TT) for _ in range(8)]; BKhtm = [LB("Khtm%d" % i) for i in range(8)]
                Khtm = [k.rearrange("p (s n) -> p s n", s=NSB) for k in Khtm_]
                erT = hsb("erT", [128, 8, NCH]); Ber = [Buf("er%d" % i) for i in range(8)]
                qbT = hsb("qbT", [128, 4, TT]); Bqb = [Buf("qb%d" % i) for i in range(4)]
                gsT = hsb("gsT", [128, 4, TT]); Bgs = [Buf("gs%d" % i) for i in range(4)]
                osbT = tail_f32(2 * TT).rearrange("p (s n) -> p s n", s=2); Bosb = [LB("osb%d" % i) for i in range(2)]
                osqT = hsb("osqT", [128, 2, TT], BF16); Bosq = [Buf("osq%d" % i) for i in range(2)]
                Vtm = hsb("Vtm", [128, NSB, D], BF16); BV = Buf("Vtm")
                Pm = hsb("Pm", [128, 2, 2, 128], BF16); BPm = [Buf("Pm0"), Buf("Pm1")]
                Sst = hsb("Sst", [128, 2, 128]); BSst = [Buf("Sst0"), Buf("Sst1")]
                Sbf = hsb("Sbf", [128, 2, 2, NCH + 1, 128], BF16); BSbf = [Buf("Sbf0"), Buf("Sbf1")]; BSbf2 = [Buf("Sbf2"), Buf("Sbf3")]
                Sfr = hsb("Sfr", [128, 8, 128]); BSfr = [Buf("Sfr%d" % i) for i in range(8)]
                Sbt = hsb("Sbt", [128, 8, 128]); BSbt = [Buf("Sbt%d" % i) for i in range(8)]
                Cb = hsb("Cb", [128, 8]); BCb = Buf("Cb")
                Cf = hsb("Cf", [128, 8]); BCf = Buf("Cf")
                Usb = hsb("Usb", [128, 4, 128]); BUsb = [Buf("Usb%d" % i) for i in range(4)]
                on = hsb("on", [128, 8, TT], BF16); Bon = Buf("on")
                gU = hsb("gU", [128, 2, 128]); BgU = [Buf("gU0"), Buf("gU1")]
                gDD = hsb("gDD", [128, 128]); BgDD = Buf("gDD")
                rsl = tail_f32(2 * TT).rearrange("p (s n) -> p s n", s=2); Brsl = [LB("rsl0"), LB("rsl1")]
                hx = hsb("hx", [128, 2, TT]); Bhx = [Buf("hx0"), Buf("hx1")]

                win_u = {}

                def win(gi):
                    if gi not in win_u:
                        win_u[gi] = wload_cols(win_in, gi * 512, 512)
                    return win_u[gi]

                def rsqrt_le(out, in_, reads, Bout):
                    act(out, in_, AF.Ln, list(reads) + [Beps], [Bout], bias=epsc[:, 0:1])
                    act(out, out, AF.Exp, [Bout], [Bout], scale=-0.5)

                def load_norm(t):
                    xt, Bxx, h, Bhh = next_xt()
                    S.dma("sp", xt[:], dview(src, t * TT, TT), reads=[dbuf(sname, t)], writes=[Bxx])
                    i = nstat["i"] % 2
                    nstat["i"] += 1
                    st, Bst = ph[14 + i], PB[14 + i]
                    tt(sqb[:], xt[:], xt[:], ALU.mult, [Bxx], [Bsq])
                    for c in range(8):
                        mm(st, ones_d[:], sqb[:, c, :], c == 0, c == 7, [Bones, Bsq], [Bst], c == 7)
                    rsqrt_le(rstd[i][:], st, [Bst], Brstd[i])
                    for c in range(8):
                        stt(h[:, c, :], xt[:, c, :], vecs[:, gcol + c:gcol + c + 1], rstd[i][:], ALU.mult, ALU.mult, [Bxx, Brstd[i], Bvecs], [Bhh])
                    for sbk in range(NSB):
                        for cg in range(2):
                            wv, wb_ = win(2 + cg)
                            pv = pbank[1][:, :]
                            for c in range(8):
                                mm(pv, h[:, c, sbk * 128:(sbk + 1) * 128], wv[:, c, :], c == 0, c == 7, [Bhh] + wb_, [PB[2]], c == 7)
                            cp(Vtm[:, sbk, cg * 512:(cg + 1) * 512], pv, [PB[2]], [BV], eng=("act" if cg else "dve"))
                    return xt, Bxx, h, Bhh

                pslot = {"i": 0}

                def proj(sec, hd, h, Bhh):
                    s3 = pslot["i"] % 3
                    pslot["i"] += 1
                    pz, Bpz = ph[4 + 2 * s3], PB[4 + 2 * s3]
                    wv, wb_ = win(2 * sec + hd // 4)
                    for c in range(8):
                        mm(pz, wv[:, c, (hd % 4) * 128:(hd % 4 + 1) * 128], h[:, c, :], c == 0, c == 7, [Bhh] + wb_, [Bpz], c == 7)
                    return pz, Bpz

                def gates_gen(dr, hd, h, Bhh, li):
                    A, B_ = gA[li], gB[li]
                    pz, Bpz = proj(2 + dr, hd, h, Bhh)
                    act(A, pz, AF.Tanh, [Bpz], [BgA[li]], scale=0.5)
                    yield
                    ts(B_, A, homl[:, dr, hd:hd + 1], c1v[:, dr, hd:hd + 1], ALU.mult, ALU.add, [BgA[li], Blb], [BgB[li]])
                    yield
                    ts(A, B_, -1.0, 1.0, ALU.mult, ALU.add, [BgB[li]], [BgA[li]])
                    yield
                    act(B_, B_, AF.Ln, [BgB[li]], [BgB[li]])
                    yield

                def transp_gen(pi, li):
                    for sbk in range(NSB):
                        ptr = pbank[5][:, :].bitcast(BF16)[:, (li * NSB + sbk) * 128:(li * NSB + sbk + 1) * 128]
                        S.op("pe", lambda e, ptr=ptr, sbk=sbk: e.transpose(ptr, Kh[pi][:, sbk * 128:(sbk + 1) * 128], identb[:]),
                             reads=[BKh[pi], Bidb], writes=[PB[10]])
                        yield
                        cp(Khtm[pi][:, sbk, :], ptr, [PB[10]], [BKhtm[pi]], eng="act")
                        yield

                S.op("dve", lambda e: e.memset(Sfr[:], 0.0), writes=BSfr)
                S.op("dve", lambda e: e.memset(Sbt[:], 0.0), writes=BSbt)
                S.op("dve", lambda e: e.memset(Cb[:], 1.0), writes=[BCb])
                S.op("dve", lambda e: e.memset(Cf[:], 1.0), writes=[BCf])

                def laneA(t, hd, dr, li, pi, h, Bhh):
                    A, B_, C_, D_ = gA[li], gB[li], gC[li], gD[li]
                    yield from gates_gen(dr, hd, h, Bhh, li)
                    S.op("dve", lambda e: e.tensor_tensor_scan(out=C_, data0=ones_t[:], data1=B_, initial=0.0, op0=ALU.mult, op1=ALU.add),
                         reads=[BgB[li], Brm], writes=[BgC[li]])
                    yield
                    if dr == 0:
                        act(D_, C_, AF.Exp, [BgC[li]], [BgD_[li]], bias=C_[:, TT - 1:TT], scale=-1.0)
                        dcol, Bd = erT[:, pi, 0:1], Ber[pi]
                    else:
                        tt(D_, C_, B_, ALU.subtract, [BgC[li], BgB[li]], [BgD_[li]])
                        yield
                        act(D_, D_, AF.Exp, [BgD_[li]], [BgD_[li]])
                        dcol, Bd = Dall[:, hd, t:t + 1], BDall
                    yield
                    act(dcol, C_[:, TT - 1:TT], AF.Exp, [BgC[li]], [Bd])
                    yield
                    tt(Kh[pi], A, D_, ALU.mult, [BgA[li], BgD_[li]], [BKh[pi]])
                    yield
                    yield from transp_gen(pi, li)
                    pu, Bpu = ph[12 + li // 2][:, (li % 2) * 128:(li % 2 + 1) * 128], PB[12]
                    for sbk in range(NSB):
                        mm(pu, Khtm[pi][:, sbk, :], Vtm[:, sbk, hd * 128:(hd + 1) * 128], sbk == 0, sbk == NSB - 1, [BKhtm[pi], BV], [Bpu], sbk == NSB - 1)
                    yield
                    if dr == 0:
                        stt(Sfr[:, hd, :], Sfr[:, hd, :], erT[:, pi, 0:1], pu, ALU.mult, ALU.add, [BSfr[hd], Ber[pi], Bpu], [BSfr[hd]])
                        yield
                        tt(Cf[:, hd:hd + 1], Cf[:, hd:hd + 1], erT[:, pi, 0:1], ALU.mult, [BCf, Ber[pi]], [BCf])
                    else:
                        cp(Usb[:, li, :], pu, [Bpu], [BUsb[li]])
                        yield
                        S.dma("sp", ub_d[hd, t], Usb[:, li, :], reads=[BUsb[li]], writes=[dbuf("ub%d" % hd, 0)])
                        stt(Sbt[:, hd, :], pu, Cb[:, hd:hd + 1], Sbt[:, hd, :], ALU.mult, ALU.add, [Bpu, BCb, BSbt[hd]], [BSbt[hd]])
                        yield
                        tt(Cb[:, hd:hd + 1], Cb[:, hd:hd + 1], Dall[:, hd, t:t + 1], ALU.mult, [BCb, BDall], [BCb])
                    yield

                for t in range(NTILE):
                    xt, Bxx, h, Bhh = load_norm(t)
                    for gi in range(4):
                        lanes = []
                        for hh in range(2):
                            for dr in range(2):
                                li = hh * 2 + dr
                                lanes.append(laneA(t, 2 * gi + hh, dr, li, (gi % 2) * 4 + li, h, Bhh))
                        run_lanes(lanes)

                Bcc_in, Bcc_out = Buf("cc_in"), Buf("cc_out")
                ccv = cc_in.rearrange("(m p) v -> p m v", p=128)
                S.dma("sp", ccv[:, 0:8, :], Sfr[:], reads=BSfr, writes=[Bcc_in])
                S.dma("sp", ccv[:, 8:16, :], Sbt[:], reads=BSbt, writes=[Bcc_in])
                S.op("dve", lambda e: e.memset(gDD[:], 0.0), writes=[BgDD])
                cp(gDD[:, 0:8], Cf[:], [BCf, BgDD], [BgDD])
                cp(gDD[:, 8:16], Cb[:], [BCb, BgDD], [BgDD])
                S.dma("sp", ccv[:, 16, :], gDD[:], reads=[BgDD], writes=[Bcc_in])
                if cc_mode == "produce":
                    S.wait_tok("sp", Bcc_in.w)
                    return
                waits = S._waits("pool", [Bcc_in], [Bcc_out], is_dma=True)
                csem = S.new_sem("cc")
                tokc = (csem, 1)
                S._mark(tokc, [Bcc_in], [Bcc_out])
                if cc_mode == "fused":
                    S.prog["pool"].append((waits, lambda e: e.collective_compute("AllGather", ALU.bypass, replica_groups=[list(range(NCORES))],
                                                                                 ins=[cc_in], outs=[cc_out]), (csem, 1)))
                else:
                    Bcc_out.w = None
                S.op("dve", lambda e: e.memset(Sfr[:], 0.0), reads=BSfr, writes=BSfr)
                S.op("dve", lambda e: e.memset(Sbt[:], 0.0), reads=BSbt, writes=BSbt)
                for dr, order in ((0, range(NCORES)), (1, range(NCORES - 1, -1, -1))):
                    Sx, BSx = (Sfr, BSfr) if dr == 0 else (Sbt, BSbt)
                    for r in order:
                        base = r * 17 * 128
                        S.dma("sp", gDD[:], cc_out[base + 2048: base + 2176, :], reads=[Bcc_out], writes=[BgDD])
                        mcol = cvec[:, C_RM + dr * 8 + r: C_RM + dr * 8 + r + 1]
                        omcol = cvec[:, C_RM1 + dr * 8 + r: C_RM1 + dr * 8 + r + 1]
                        ts(gDD[:, 16:24], gDD[:, dr * 8: dr * 8 + 8], mcol, omcol, ALU.mult, ALU.add, [BgDD, Bcvec], [BgDD])
                        for hd in range(8):
                            row0 = base + (dr * 8 + hd) * 128
                            S.dma("sp", gU[:, hd % 2, :], cc_out[row0:row0 + 128, :], reads=[Bcc_out], writes=[BgU[hd % 2]])
                            ts(gU[:, hd % 2, :], gU[:, hd % 2, :], mcol, None, ALU.mult, None, [BgU[hd % 2], Bcvec], [BgU[hd % 2]])
                            stt(Sx[:, hd, :], Sx[:, hd, :], gDD[:, 16 + hd:17 + hd], gU[:, hd % 2, :], ALU.mult, ALU.add, [BSx[hd], BgDD, BgU[hd % 2]], [BSx[hd]])
                for t in range(NTILE - 1, -1, -1):
                    for hd in range(8):
                        if t < NTILE - 1:
                            S.dma("sp", Usb[:, hd % 4, :], ub_d[hd, t + 1], reads=[dbuf("ub%d" % hd, 0)], writes=[BUsb[hd % 4]])
                            stt(Sbt[:, hd, :], Sbt[:, hd, :], Dall[:, hd, t + 1:t + 2], Usb[:, hd % 4, :], ALU.mult, ALU.add, [BSbt[hd], BDall, BUsb[hd % 4]], [BSbt[hd]])
                        S.dma("sp", sb_d[hd, t], Sbt[:, hd, :], reads=[BSbt[hd]], writes=[dbuf("sb%d" % hd, 0)])

                wout_u = {}

                def wout(g):
                    if g not in wout_u:
                        wout_u[g] = wload_cols(wout_in, g * 512, 512)
                    return wout_u[g]

                Mm = (Mf, Mb)

                def headB(hd, hs, h, Bhh):
                    for sec, dstT, Bd in ((0, qbT, Bqb), (4, gsT, Bgs)):
                        pq, Bpq = proj(sec, hd, h, Bhh)
                        act(dstT[:, hs, :], pq, AF.Tanh, [Bpq], [Bd[hs]], scale=0.5)
                        act(hx[:, hs % 2, :], pq, AF.Copy, [Bpq], [Bhx[hs % 2]], scale=0.5)
                        yield
                        stt(dstT[:, hs, :], dstT[:, hs, :], 1.0, hx[:, hs % 2, :], ALU.add, ALU.mult, [Bd[hs], Bhx[hs % 2]], [Bd[hs]])
                        yield

                def laneB(t, hd, dr, li, pi, hs, h, Bhh):
                    A, B_, C_, D_ = gA[li], gB[li], gC[li], gD[li]
                    yield from gates_gen(dr, hd, h, Bhh, li)
                    S.op("dve", lambda e: e.tensor_tensor_scan(out=C_, data0=rmask[:], data1=B_, initial=0.0, op0=ALU.mult, op1=ALU.add),
                         reads=[BgB[li], Brm], writes=[BgC[li]])
                    yield
                    C3 = C_.rearrange("p (c j) -> p c j", j=64)
                    if dr == 1:
                        D3 = D_.rearrange("p (c j) -> p c j", j=64)
                        tt(D3, C3[:, :, 63:64].to_broadcast([128, NCH, 64]), C3, ALU.subtract, [BgC[li]], [BgD_[li]])
                        yield
                        tt(C_, D_, B_, ALU.add, [BgD_[li], BgB[li]], [BgC[li]])
                        yield
                        rcol = C3[:, :, 0]
                    else:
                        rcol = C3[:, :, 63]
                    act(erT[:, pi, :], rcol, AF.Exp, [BgC[li]], [Ber[pi]])
                    yield
                    act(D_, C_, AF.Exp, [BgC[li]], [BgD_[li]])
                    yield
                    tt(Qt[pi], qbT[:, hs, :], D_, ALU.mult, [Bqb[hs], BgD_[li]], [BQt[pi]])
                    yield
                    act(D_, C_, AF.Exp, [BgC[li]], [BgD_[li]], scale=-1.0)
                    yield
                    tt(A, A, D_, ALU.mult, [BgA[li], BgD_[li]], [BgA[li]])
                    yield
                    cp(Kt[pi], A, [BgA[li]], [BKt[pi]], eng="act")
                    yield
                    tt(Kh[pi].rearrange("p (c j) -> p c j", j=64), A.rearrange("p (c j) -> p c j", j=64),
                       erT[:, pi, :].unsqueeze(2).to_broadcast([128, NCH, 64]), ALU.mult, [BgA[li], Ber[pi]], [BKh[pi]])
                    yield
                    yield from transp_gen(pi, li)

                def par_gens(*gens):
                    gens = list(gens)
                    while gens:
                        nxt = []
                        for g in gens:
                            try:
                                next(g)
                                nxt.append(g)
                            except StopIteration:
                                pass
                        gens = nxt
                        yield

                Pm5 = sqb[:, 0:4, :].rearrange("p a n -> p (a n)").rearrange("p (a s d n) -> p a s d n", a=2, s=2, d=2)

                def fwd_chain(t, hd, pf, par):
                    cp(Sbf[:, par, 0, 0, :], Sfr[:, hd, :], [BSfr[hd]], [BSbf[par]], eng="act")
                    yield
                    for ch in range(NCH):
                        sbk, rows = ch // 2, slice((ch % 2) * 64, (ch % 2) * 64 + 64)
                        pu, Bpu = (ph[12][:, 0:128], PB[12]) if par == 0 else (ph[7][:, 0:128], PB[7])
                        mm(pu, Khtm[pf][rows, sbk, :], Vtm[rows, sbk, hd * 128:(hd + 1) * 128], True, True, [BKhtm[pf], BV], [Bpu], True)
                        yield
                        stt(Sfr[:, hd, :], Sfr[:, hd, :], erT[:, pf, ch:ch + 1], pu, ALU.mult, ALU.add, [BSfr[hd], Ber[pf], Bpu], [BSfr[hd]])
                        yield
                        if ch < NCH - 1:
                            cp(Sbf[:, par, 0, ch + 1, :], Sfr[:, hd, :], [BSfr[hd]], [BSbf[par]], eng="act")
                            yield

                def bwd_chain(t, hd, pb, par):
                    S.dma("sp", Sst[:, par, :], sb_d[hd, t], reads=[dbuf("sb%d" % hd, 0)], writes=[BSst[par]])
                    cp(Sbf[:, par, 1, NCH, :], Sst[:, par, :], [BSst[par]], [BSbf2[par]], eng="act")
                    yield
                    for ch in range(NCH - 1, 0, -1):
                        sbk, rows = ch // 2, slice((ch % 2) * 64, (ch % 2) * 64 + 64)
                        pu, Bpu = (ph[5][:, 0:128], PB[5]) if par == 0 else (ph[9][:, 0:128], PB[9])
                        mm(pu, Khtm[pb][rows, sbk, :], Vtm[rows, sbk, hd * 128:(hd + 1) * 128], True, True, [BKhtm[pb], BV], [Bpu], True)
                        yield
                        stt(Sst[:, par, :], Sst[:, par, :], erT[:, pb, ch:ch + 1], pu, ALU.mult, ALU.add, [BSst[par], Ber[pb], Bpu], [BSst[par]])
                        yield
                        cp(Sbf[:, par, 1, ch, :], Sst[:, par, :], [BSst[par]], [BSbf2[par]], eng="act")
                        yield

                def scores_gen(hd, pf, pb, par):
                    pset = (pf, pb)
                    for sbk in range(NSB):
                        cols = slice(sbk * 128, (sbk + 1) * 128)
                        for dr in range(2):
                            psc, Bps = ph[3 - par][:, dr * 128:(dr + 1) * 128], PB[3]
                            mm(psc, Kt[pset[dr]][:, cols], Qt[pset[dr]][:, cols], True, True, [BKt[pset[dr]], BQt[pset[dr]]], [Bps], True)
                            yield
                            tt(Pm5[:, par, sbk, dr, :], psc, Mm[dr][:], ALU.mult, [Bps, BM], [Bsq])
                            yield

                def postB(t, hd, pf, pb, hs, par, h, Bhh):
                    yield from par_gens(fwd_chain(t, hd, pf, par), bwd_chain(t, hd, pb, par), scores_gen(hd, pf, pb, par))
                    po, Bpo = ph[par], PB[0]
                    for sbk in range(NSB):
                        cols = slice(sbk * 128, (sbk + 1) * 128)
                        mm(po[:, cols], Vtm[:, sbk, hd * 128:(hd + 1) * 128], Pm5[:, par, sbk, 0, :], True, False, [BV, Bsq], [Bpo], False)
                        mm(po[:, cols], Vtm[:, sbk, hd * 128:(hd + 1) * 128], Pm5[:, par, sbk, 1, :], False, False, [BV, Bsq], [Bpo], False)
                        for cc in (2 * sbk, 2 * sbk + 1):
                            c64 = slice(cc * 64, cc * 64 + 64)
                            mm(po[:, c64], Sbf[:, par, 0, cc, :], Qt[pf][:, c64], False, False, [BSbf[par], BQt[pf]], [Bpo], False)
                            mm(po[:, c64], Sbf[:, par, 1, cc + 1, :], Qt[pb][:, c64], False, True, [BSbf2[par], BQt[pb]], [Bpo], True)
                        yield
                    cp(osbT[:, par, :], po, [Bpo], [Bosb[par]], eng="act")
                    yield
                    tt(osqT[:, par, :], osbT[:, par, :], osbT[:, par, :], ALU.mult, [Bosb[par]], [Bosq[par]])
                    yield
                    pst, Bpst = ph[14 + par], PB[15]
                    mm(pst, ones_h[:], osqT[:, par, :], True, True, [Bones, Bosq[par]], [Bpst], True)
                    yield
                    act(rsl[:, par, :], pst, AF.Ln, [Bpst, Beps], [Brsl[par]], bias=epsc[:, 0:1])
                    yield
                    act(rsl[:, par, :], rsl[:, par, :], AF.Exp, [Brsl[par]], [Brsl[par]], scale=-0.5)
                    yield
                    stt(osbT[:, par, :], osbT[:, par, :], vecs[:, V_GNG + hd:V_GNG + hd + 1], rsl[:, par, :], ALU.mult, ALU.mult, [Bosb[par], Bvecs, Brsl[par]], [Bosb[par]])
                    yield
                    tt(on[:, hd, :], osbT[:, par, :], gsT[:, hs, :], ALU.mult, [Bosb[par], Bgs[hs]], [Bon])
                    yield

                def postpair(t, gi, h, Bhh):
                    gens = []
                    for hh in range(2):
                        hd = 2 * gi + hh
                        base = (gi % 2) * 4 + hh * 2
                        gens.append(postB(t, hd, base, base + 1, (gi % 2) * 2 + hh, hh, h, Bhh))
                    yield from par_gens(*gens)

                for t in range(NTILE):
                    xt, Bxx, h, Bhh = load_norm(t)
                    pending = None
                    for gi in range(4):
                        gens = []
                        for hh in range(2):
                            gens.append(headB(2 * gi + hh, (gi % 2) * 2 + hh, h, Bhh))
                        for hh in range(2):
                            for dr in range(2):
                                li = hh * 2 + dr
                                gens.append(laneB(t, 2 * gi + hh, dr, li, (gi % 2) * 4 + li, (gi % 2) * 2 + hh, h, Bhh))
                        if pending is not None:
                            gens.append(pending)
                        run_lanes(gens)
                        pending = postpair(t, gi, h, Bhh)
                    run_lanes([pending])
                    for dc in range(8):
                        pw, Bpw = (ph[2], PB[2]) if dc % 2 == 0 else (ph[4], PB[4])
                        wv, wb_ = wout(dc // 4)
                        for c in range(8):
                            mm(pw, wv[:, c, (dc % 4) * 128:(dc % 4 + 1) * 128], on[:, c, :], c == 0, c == 7, [Bon] + wb_, [Bpw], c == 7)
                        tt(xt[:, dc, :], xt[:, dc, :], pw, ALU.add, [Bxx, Bpw], [Bxx])
                    S.dma("sp", dview(dst, t * TT, TT), xt[:], reads=[Bxx], writes=[dbuf(dname, t)])
                S.op("dve", lambda e: e.memset(wa[:, TAIL0 * 1024:TAIL0 * 1024 + 16], 0.0), reads=lane_bufs, writes=WB[TAIL0:NBLK])

        def stage_out(src, sname, do_norm=True):
            S.barrier()
            with nc.sbuf_tensor(U("xo"), [128, 8, TT], F32) as xo, nc.sbuf_tensor(U("ot"), [128, 2, D], F32) as ot:
                Bxo, Bot = Buf("xo"), Buf("ot")
                toks = []
                for t in range(NTILE):
                    xt, Bxx, h, Bhh = next_xt()
                    S.dma("sp", xt[:], dview(src, t * TT, TT), reads=[dbuf(sname, t)], writes=[Bxx])
                    if do_norm:
                        i = nstat["i"] % 2
                        nstat["i"] += 1
                        tt(sqb[:], xt[:], xt[:], ALU.mult, [Bxx], [Bsq])
                        for c in range(8):
                            mm(ph[14 + i], ones_d[:], sqb[:, c, :], c == 0, c == 7, [Bones, Bsq], [PB[14 + i]], c == 7)
                        rsqrt_eps(rstd[i][:], ph[14 + i], [PB[14 + i]], Brstd[i])
                        for c in range(8):
                            stt(xo[:, c, :], xt[:, c, :], vecs[:, V_FG + c:V_FG + c + 1], rstd[i][:], ALU.mult, ALU.mult, [Bxx, Brstd[i], Bvecs], [Bxo])
                        srcx, Bsrc = xo, Bxo
                    else:
                        srcx, Bsrc = xt, Bxx
                    for b in range(2):
                        for half in range(2):
                            pbk = pbank[1 + half]
                            for c4 in range(4):
                                c = half * 4 + c4
                                S.op("pe", lambda e, pbk=pbk, c4=c4, c=c, b=b, srcx=srcx: e.transpose(pbk[:, c4 * 128:(c4 + 1) * 128], srcx[:, c, b * 128:(b + 1) * 128], identf[:]),
                                     reads=[Bsrc, Bidf], writes=[PB[2 + 2 * half], PB[3 + 2 * half]], sig=(c4 == 3))
                            cp(ot[:, b, half * 512:(half + 1) * 512], pbk[:, :], [PB[2 + 2 * half], PB[3 + 2 * half]], [Bot], eng=("act" if half else "dve"))
                    toks.append(S.dma("sp", out_d[t * TT:(t + 1) * TT, :].rearrange("(b p) d -> p b d", p=128), ot[:], reads=[Bot], writes=[dbuf("out", t)]))
                for tk in toks:
                    S.wait_tok("sp", tk)

        stage_in()
        cur, cname, oth, oname = xsA, "A", xsB, "B"
        if upto >= 1:
            stage_ffn(0, 0, xsA, "A", xsB, "B", halo=(xhA, "hA", xhB, "hB"))
            cur, cname, oth, oname = xsB, "B", xsA, "A"
        if upto >= 2:
            stage_conv(xsB, "B", xhB, "hB", xsA, "A")
            cur, cname, oth, oname = xsA, "A", xsB, "B"
        if upto >= 3:
            stage_ffn(0, 1, xsA, "A", xsB, "B")
            cur, cname = xsB, "B"
        if upto >= 4:
            stage_ffn(1, 0, xsB, "B", xsA, "A")
            cur, cname = xsA, "A"
        if upto >= 5:
            stage_hgrn(xsA, "A", xsB, "B")
            cur, cname = xsB, "B"
        if cc_mode != "produce":
            if upto >= 6:
                stage_ffn(1, 1, xsB, "B", xsA, "A")
                cur, cname = xsA, "A"
            stage_out(cur, cname, do_norm=(upto >= 7))

        with nc.Block() as block:
            S.emit(block)
    return nc


def colpack(v):
    v = np.asarray(v, dtype=np.float32).reshape(-1, 128)
    return np.ascontiguousarray(v.T)


def needed_weights(upto):
    need = []
    if upto >= 1:
        need += ["w13_00", "w2_00"]
    if upto >= 2:
        need += ["conv_w_pw1", "conv_w_pw2"]
    if upto >= 3:
        need += ["w13_01", "w2_01"]
    if upto >= 4:
        need += ["w13_10", "w2_10"]
    if upto >= 5:
        need += ["hgrn_w_in", "hgrn_w_out"]
    if upto >= 6:
        need += ["w13_11", "w2_11"]
    return need


def make_inputs(upto, x, norm_g, ffn_w13, ffn_w2, conv_w_pw1, conv_b_pw1, conv_w_dw, conv_b_dw, conv_ln_g, conv_ln_b,
                conv_w_pw2, conv_b_pw2, hgrn_w_in, hgrn_lb, hgrn_gn_g, hgrn_w_out, final_g):
    f = lambda a: np.ascontiguousarray(np.asarray(a, dtype=np.float32))
    vecs = np.zeros((128, NV), np.float32)
    vecs[:, V_NG:V_NG + 48] = colpack(f(norm_g).reshape(-1))
    vecs[:, V_FG:V_FG + 8] = colpack(final_g)
    vecs[:, V_BPW1:V_BPW1 + 16] = colpack(f(conv_b_pw1)[0])
    wdw = f(conv_w_dw)[0]
    for c in range(8):
        vecs[:, V_WDW + c * 31:V_WDW + (c + 1) * 31] = wdw[:, c * 128:(c + 1) * 128].T
    vecs[:, V_BDW:V_BDW + 8] = colpack(f(conv_b_dw)[0])
    vecs[:, V_LNG:V_LNG + 8] = colpack(f(conv_ln_g)[0])
    vecs[:, V_LNB:V_LNB + 8] = colpack(f(conv_ln_b)[0])
    vecs[:, V_BPW2:V_BPW2 + 8] = colpack(f(conv_b_pw2)[0])
    vecs[:, V_HLB:V_HLB + 32] = colpack(f(hgrn_lb).reshape(-1))
    vecs[:, V_GNG:V_GNG + 8] = colpack(f(hgrn_gn_g)[0])
    x = f(x)
    allw = {"conv_w_pw1": f(conv_w_pw1)[0], "conv_w_pw2": f(conv_w_pw2)[0], "hgrn_w_in": f(hgrn_w_in)[0], "hgrn_w_out": f(hgrn_w_out)[0]}
    for l in range(2):
        for i in range(2):
            allw["w13_%d%d" % (l, i)] = f(np.asarray(ffn_w13)[l, i])
            allw["w2_%d%d" % (l, i)] = f(np.asarray(ffn_w2)[l, i])
    shared = {"vecs": vecs}
    for k in needed_weights(upto):
        shared[k] = allw[k]
    in_maps = []
    for c in range(NCORES):
        b, pos = c // 4, c % 4
        t0 = pos * NT
        xh = np.zeros((2 * HALO, D), np.float32)
        cvec = np.zeros((128, NCV), np.float32)
        if pos > 0:
            xh[:HALO] = x[b, t0 - HALO:t0]
            cvec[:, C_HM:C_HM + HALO] = 1.0
        if pos < 3:
            xh[HALO:] = x[b, t0 + NT:t0 + NT + HALO]
            cvec[:, C_HM + HALO:C_HM + 2 * HALO] = 1.0
        for r in range(NCORES):
            if r // 4 == b and r < c:
                cvec[:, C_RM + r] = 1.0
            if r // 4 == b and r > c:
                cvec[:, C_RM + 8 + r] = 1.0
        cvec[:, C_RM1:C_RM1 + 16] = 1.0 - cvec[:, C_RM:C_RM + 16]
        m = dict(shared)
        m["x"] = np.ascontiguousarray(x[b, t0:t0 + NT])
        m["xh"] = xh
        m["cvec"] = cvec
        in_maps.append(m)
    return in_maps


_NC_CACHE = {}
FUSED = True


def run(inputs, upto=99):
    in_maps = make_inputs(upto, **inputs)
    if FUSED or upto < 5:
        if upto not in _NC_CACHE:
            _NC_CACHE[upto] = build(upto)
        res = run_bass_kernel_spmd(_NC_CACHE[upto], in_maps, core_ids=list(range(NCORES)))
    else:
        ncA = build(upto, "produce")
        resA = run_bass_kernel_spmd(ncA, in_maps, core_ids=list(range(NCORES)))
        allcc = np.concatenate([resA.results[c]["cc_in"] for c in range(NCORES)], axis=0)
        for m in in_maps:
            m["cc_out"] = allcc
        ncB = build(upto, "consume")
        res = run_bass_kernel_spmd(ncB, in_maps, core_ids=list(range(NCORES)))
    out = np.zeros((2, 4 * NT, D), np.float32)
    for c in range(NCORES):
        out[c // 4, (c % 4) * NT:(c % 4 + 1) * NT] = res.results[c]["out"]
    return out


def kernel(**inputs):
    return run(inputs, 99)
```

```python
import contextlib
import numpy as np
import concourse.bass as bass
import concourse.mybir as mybir
from concourse.bass_utils import run_bass_kernel_spmd

F32 = mybir.dt.float32
BF16 = mybir.dt.bfloat16
AF = mybir.ActivationFunctionType
ALU = mybir.AluOpType

D = 1024
FF = 2816
NT = 4096
TT = 256
NTILE = NT // TT
HALO = 16
EPS = 1e-6
NBLK = 66
NCORES = 8

V_NG = 0
V_FG = 48
V_BPW1 = 56
V_WDW = 72
V_BDW = 320
V_LNG = 328
V_LNB = 336
V_BPW2 = 344
V_HLB = 352
V_GNG = 384
NV = 392
C_HM = 0
C_RM = 32
C_RM1 = 48
NCV = 64


class Buf:
    __slots__ = ("name", "w", "r", "dsem", "dcnt", "excl")

    def __init__(self, name, excl=False):
        self.name = name
        self.excl = excl
        self.w = None
        self.r = {}
        self.dsem = None
        self.dcnt = 0


class Sched:
    ENGS = ("pe", "act", "dve", "pool", "sp")
    ROLL = 30000

    def __init__(self, nc):
        self.nc = nc
        self.prog = {e: [] for e in self.ENGS}
        self.sem = {}
        self.cnt = {e: 0 for e in self.ENGS}
        self.seen = {e: {} for e in self.ENGS}
        self.nsem = 0
        self.pool_pending = []
        self.sp_out = {}
        for e in self.ENGS:
            self.sem[e] = self.new_sem("eng_" + e)

    def new_sem(self, name):
        self.nsem += 1
        return self.nc.alloc_semaphore("%s_%d" % (name, self.nsem))

    def _waits(self, eng, reads, writes, is_dma=False):
        own = self.sem[eng]
        need = {}

        def add(tok, kind):
            sem, val = tok
            if sem is own and not is_dma:
                if eng == "pe" or kind != "raw":
                    return
            k = id(sem)
            if self.seen[eng].get(k, 0) >= val:
                return
            if k not in need or need[k][1] < val:
                need[k] = (sem, val)

        for b in reads:
            if b.w is not None:
                add(b.w, "raw")
        for b in writes:
            if b.w is not None:
                add(b.w, "waw")
            for t in b.r.values():
                add(t, "war")
        out = []
        for k, (sem, val) in need.items():
            self.seen[eng][k] = val
            out.append((sem, val))
        return out

    def _mark(self, tok, reads, writes):
        k = id(tok[0])
        for b in reads:
            o = b.r.get(k)
            if o is None or o[1] < tok[1]:
                b.r[k] = tok
        for b in writes:
            b.w = tok
            b.r = {}

    def op(self, eng, fn, reads=(), writes=(), sig=True):
        ex = [b for b in reads if b.excl]
        if ex:
            reads = [b for b in reads if not b.excl]
            writes = list(writes) + ex
        waits = self._waits(eng, reads, writes)
        if eng == "pool" and self.pool_pending:
            for sem, val in self.pool_pending:
                if self.seen[eng].get(id(sem), 0) < val:
                    self.seen[eng][id(sem)] = val
                    waits.append((sem, val))
            self.pool_pending = []
        tok = (self.sem[eng], self.cnt[eng] + 1)
        self._mark(tok, reads, writes)
        inc = None
        if sig:
            self.cnt[eng] += 1
            inc = (self.sem[eng], 1)
        self.prog[eng].append((waits, fn, inc))
        if sig and self.cnt[eng] >= self.ROLL:
            self.sem[eng] = self.new_sem("eng_" + eng)
            self.cnt[eng] = 0

    def dma(self, q, out, in_, reads=(), writes=(), prim=None):
        waits = self._waits(q, reads, writes, is_dma=True)
        if prim is None:
            prim = writes[0] if writes else reads[0]
        if prim.dsem is None:
            prim.dsem = self.new_sem("d")
        if prim.dcnt >= self.ROLL:
            prim.dsem = self.new_sem("d")
            prim.dcnt = 0
        prim.dcnt += 16
        tok = (prim.dsem, prim.dcnt)
        self._mark(tok, reads, writes)
        if q == "sp":
            self.sp_out[id(prim.dsem)] = tok
        self.prog[q].append((waits, lambda e, o=out, i=in_: e.dma_start(out=o, in_=i), (prim.dsem, 16)))
        return tok

    def barrier(self):
        toks = [(self.sem[e], self.cnt[e]) for e in ("pe", "act", "dve", "pool") if self.cnt[e] > 0]
        toks += list(self.sp_out.values())
        self.sp_out = {}
        for e in ("pe", "act", "dve", "sp"):
            w = []
            for sem, val in toks:
                if sem is self.sem.get(e):
                    continue
                if self.seen[e].get(id(sem), 0) < val:
                    self.seen[e][id(sem)] = val
                    w.append((sem, val))
            if w:
                self.prog[e].append((w, None, None))
        self.pool_pending = [t for t in toks if t[0] is not self.sem["pool"]]

    def wait_tok(self, eng, tok):
        self.prog[eng].append(([tok], None, None))

    def emit(self, block):
        m = {"pe": "tensor", "act": "scalar", "dve": "vector", "pool": "gpsimd", "sp": "sync"}
        for e in self.ENGS:
            items = self.prog[e]

            def body(engine, items=items):
                for waits, fn, inc in items:
                    for sem, val in waits:
                        engine.wait_ge(sem, val)
                    if fn is not None:
                        ins = fn(engine)
                        if inc is not None:
                            ins.then_inc(inc[0], inc[1])

            getattr(block, m[e])(body)


def build(upto=99, cc_mode="fused"):
    nc = bass.Bass("TRN2", target_bir_lowering=False)

    def dram(name, shape, dt=F32, kind="Internal"):
        return nc.dram_tensor(name, list(shape), dt, kind=kind).ap()

    x_in = dram("x", [NT, D], kind="ExternalInput")
    xh_in = dram("xh", [2 * HALO, D], kind="ExternalInput")
    vecs_in = dram("vecs", [128, NV], kind="ExternalInput")
    cvec_in = dram("cvec", [128, NCV], kind="ExternalInput")
    need = needed_weights(upto)
    w13_d = {k: dram("w13_%d%d" % k, [D, 2 * FF], kind="ExternalInput") for k in [(0, 0), (0, 1), (1, 0), (1, 1)] if ("w13_%d%d" % k) in need}
    w2_d = {k: dram("w2_%d%d" % k, [FF, D], kind="ExternalInput") for k in [(0, 0), (0, 1), (1, 0), (1, 1)] if ("w2_%d%d" % k) in need}
    if "conv_w_pw1" in need:
        wpw1_in = dram("conv_w_pw1", [D, 2 * D], kind="ExternalInput")
        wpw2_in = dram("conv_w_pw2", [D, D], kind="ExternalInput")
    if "hgrn_w_in" in need:
        win_in = dram("hgrn_w_in", [D, 5 * D], kind="ExternalInput")
        wout_in = dram("hgrn_w_out", [D, D], kind="ExternalInput")
    out_d = dram("out", [NT, D], kind="ExternalOutput")

    xsA = dram("xsA", [D, NT])
    xsB = dram("xsB", [D, NT])
    xhA = dram("xhA", [D, 2 * HALO])
    xhB = dram("xhB", [D, 2 * HALO])
    ub_d = dram("ub_d", [8, NTILE, 128, 128])
    sb_d = dram("sb_d", [8, NTILE, 128, 128])
    cc_in = dram("cc_in", [17 * 128, 128], kind=("ExternalOutput" if cc_mode == "produce" else "Internal"))
    cc_out = dram("cc_out", [NCORES * 17 * 128, 128], kind=("ExternalInput" if cc_mode == "consume" else "Internal"))

    _uc = {"n": 0}

    def U(name):
        _uc["n"] += 1
        return "%s_u%d" % (name, _uc["n"])

    es = contextlib.ExitStack()
    with es:
        S = Sched(nc)

        def sb(name, shape, dt=F32):
            return es.enter_context(nc.sbuf_tensor("s_" + name, list(shape), dt))

        wa = sb("wa", [128, NBLK * 1024], BF16)
        WB = [Buf("wa%d" % i) for i in range(NBLK)]
        vecs = sb("vecs", [128, NV]); Bvecs = Buf("vecs")
        cvec = sb("cvec", [128, NCV]); Bcvec = Buf("cvec")
        identf = sb("identf", [128, 128]); Bidf = Buf("identf")
        identb = sb("identb", [128, 128], BF16); Bidb = Buf("identb")
        ones_d = sb("ones_d", [128, 128], BF16); Bones = Buf("ones")
        ones_h = sb("ones_h", [128, 128], BF16)
        Mf = sb("Mf", [128, 128]); Mb = sb("Mb", [128, 128]); BM = Buf("M")
        rmask = sb("rmask", [128, TT]); ones_t = sb("ones_t", [128, TT]); Brm = Buf("rmask")
        lbv = sb("lbv", [128, 2, 8]); oml = sb("oml", [128, 2, 8]); noml = sb("noml", [128, 2, 8]); Blb = Buf("lb")
        Dall = sb("Dall", [128, 8, NTILE]); BDall = Buf("Dall")
        xts = [sb("xt%d" % i, [128, 8, TT]) for i in range(2)]; Bxt = [Buf("xt%d" % i) for i in range(2)]
        hbuf = [sb("h%d" % i, [128, 8, TT], BF16) for i in range(2)]; Bh = [Buf("h%d" % i) for i in range(2)]
        sqb = sb("sqb", [128, 8, TT], BF16); Bsq = Buf("sq")
        rstd = [sb("rstd%d" % i, [128, TT]) for i in range(2)]; Brstd = [Buf("rstd%d" % i) for i in range(2)]

        pbank = [es.enter_context(nc.psum_tensor("pb%d" % i, [128, 512], F32)) for i in range(8)]
        ph = []
        for i in range(8):
            ph.append(pbank[i][:, 0:256])
            ph.append(pbank[i][:, 256:512])
        _PBK = [Buf("pbank%d" % i, excl=True) for i in range(8)]
        PB = [_PBK[i // 2] for i in range(16)]

        def mm(out, lhsT, rhs, start, stop, reads, writes, sig, sgc=False):
            if sgc:
                S.op("pe", lambda e: e.matmul(out, lhsT, rhs, start=start, stop=stop, skip_group_check=True), reads=reads, writes=writes, sig=sig)
            else:
                S.op("pe", lambda e: e.matmul(out, lhsT, rhs, start=start, stop=stop), reads=reads, writes=writes, sig=sig)

        def act(out, in_, func, reads, writes, bias=None, scale=None):
            kw = {}
            if bias is not None:
                kw["bias"] = bias
            if scale is not None:
                kw["scale"] = scale
            S.op("act", lambda e: e.activation(out=out, in_=in_, func=func, **kw), reads=reads, writes=writes)

        def tt(out, in0, in1, op, reads, writes, eng="dve"):
            S.op(eng, lambda e: e.tensor_tensor(out=out, in0=in0, in1=in1, op=op), reads=reads, writes=writes)

        def ts(out, in0, s1, s2, op0, op1, reads, writes, eng="dve"):
            if s2 is None:
                S.op(eng, lambda e: e.tensor_scalar(out=out, in0=in0, scalar1=s1, scalar2=None, op0=op0), reads=reads, writes=writes)
            else:
                S.op(eng, lambda e: e.tensor_scalar(out=out, in0=in0, scalar1=s1, scalar2=s2, op0=op0, op1=op1), reads=reads, writes=writes)

        def stt(out, in0, scalar, in1, op0, op1, reads, writes, eng="dve"):
            S.op(eng, lambda e: e.scalar_tensor_tensor(out=out, in0=in0, scalar=scalar, in1=in1, op0=op0, op1=op1), reads=reads, writes=writes)

        def cp(out, in_, reads, writes, eng="dve"):
            if eng == "act":
                act(out, in_, AF.Copy, reads, writes)
            else:
                S.op(eng, lambda e: e.tensor_copy(out, in_), reads=reads, writes=writes)

        ring = {"p": 0}

        def walloc(n):
            if ring["p"] + n > NBLK:
                ring["p"] = 0
            s = ring["p"]
            ring["p"] += n
            return s, WB[s:s + n]

        def wstage(nblocks):
            if ring["p"] + nblocks > NBLK:
                ring["p"] = 0

        wsemB = [Buf("wsem%d" % i) for i in range(16)]
        wcnt = {"k": 0}

        def wprim():
            b = wsemB[wcnt["k"] % 16]
            wcnt["k"] += 1
            return b

        def wload_cols(src2d, col0, ncols):
            nb = (8 * ncols + 1023) // 1024
            s, bufs = walloc(nb)
            view = wa[:, s * 1024: s * 1024 + 8 * ncols].rearrange("p (c n) -> p c n", c=8)
            pb_ = wprim()
            S.dma("pool", view, src2d[:, col0:col0 + ncols].rearrange("(c p) n -> p c n", p=128), writes=bufs + [pb_], prim=pb_)
            return view, bufs

        def wload_rows(src2d, row0, nrows):
            f = nrows // 128
            s, bufs = walloc(f)
            view = wa[:, s * 1024: (s + f) * 1024].rearrange("p (f n) -> p f n", f=f)
            pb_ = wprim()
            S.dma("pool", view, src2d[row0:row0 + nrows, :].rearrange("(f p) n -> p f n", p=128), writes=bufs + [pb_], prim=pb_)
            return view, bufs

        S.dma("sp", vecs[:], vecs_in, writes=[Bvecs])
        S.dma("sp", cvec[:], cvec_in, writes=[Bcvec])
        S.op("pool", lambda e: e.memset(identf[:], 0.0), writes=[Bidf])
        S.op("pool", lambda e: e.affine_select(out=identf[:], in_=identf[:], pattern=[[-1, 128]], compare_op=ALU.not_equal,
                                               fill=1.0, base=0, channel_multiplier=1), reads=[Bidf], writes=[Bidf])
        cp(identb[:], identf[:], [Bidf], [Bidb], eng="pool")
        S.op("pool", lambda e: e.memset(ones_d[:], 1.0 / 1024.0), writes=[Bones])
        S.op("pool", lambda e: e.memset(ones_h[:], 1.0 / 128.0), writes=[Bones])
        S.op("pool", lambda e: e.memset(Mf[:], 1.0), writes=[BM])
        S.op("pool", lambda e: e.affine_select(out=Mf[:], in_=Mf[:], pattern=[[1, 128]], compare_op=ALU.is_ge,
                                               fill=0.0, base=0, channel_multiplier=-1), reads=[BM], writes=[BM])
        S.op("pool", lambda e: e.memset(Mf[0:64, 64:128], 0.0), reads=[BM], writes=[BM])
        S.op("pool", lambda e: e.memset(Mb[:], 1.0), reads=[BM], writes=[BM])
        S.op("pool", lambda e: e.affine_select(out=Mb[:], in_=Mb[:], pattern=[[-1, 128]], compare_op=ALU.is_ge,
                                               fill=0.0, base=0, channel_multiplier=1), reads=[BM], writes=[BM])
        S.op("pool", lambda e: e.memset(Mb[64:128, 0:64], 0.0), reads=[BM], writes=[BM])
        S.op("pool", lambda e: e.memset(rmask[:], 1.0), writes=[Brm])
        S.op("pool", lambda e: e.memset(rmask[:].rearrange("p (c j) -> p c j", j=64)[:, :, 0:1], 0.0), reads=[Brm], writes=[Brm])
        S.op("pool", lambda e: e.memset(ones_t[:], 1.0), reads=[Brm], writes=[Brm])
        epsc = sb("epsc", [128, 1]); Beps = Buf("epsc")
        S.op("pool", lambda e: e.memset(epsc[:], EPS), writes=[Beps])
        S.barrier()
        hl = vecs[:, V_HLB:V_HLB + 32].rearrange("p (d l h) -> p d l h", d=2, l=2)
        tt(lbv[:], hl[:, :, 1, :], hl[:, :, 0, :], ALU.subtract, [Bvecs], [Blb])
        act(lbv[:], lbv[:], AF.Sigmoid, [Blb], [Blb])
        ts(oml[:], lbv[:], -1.0, 1.0, ALU.mult, ALU.add, [Blb], [Blb])
        ts(noml[:], oml[:], -1.0, None, ALU.mult, None, [Blb], [Blb])
        homl = sb("homl", [128, 2, 8]); c1v = sb("c1v", [128, 2, 8])
        ts(homl[:], oml[:], 0.5, None, ALU.mult, None, [Blb], [Blb])
        tt(c1v[:], homl[:], lbv[:], ALU.add, [Blb], [Blb])

        def rsqrt_eps(out, in_, reads, Bout):
            act(out, in_, AF.Sqrt, list(reads) + [Beps], [Bout], bias=epsc[:, 0:1])
            S.op("dve", lambda e: e.reciprocal(out, out), reads=[Bout], writes=[Bout])

        nstat = {"i": 0}

        def rmsnorm(xt, Bx, N, gcol, h, Bhh):
            i = nstat["i"] % 2
            nstat["i"] += 1
            st, Bst = ph[14 + i][:, :N], PB[14 + i]
            r, Br = rstd[i][:, :N], Brstd[i]
            tt(sqb[:, :, :N], xt[:, :, :N], xt[:, :, :N], ALU.mult, [Bx], [Bsq])
            for c in range(8):
                mm(st, ones_d[:], sqb[:, c, :N], c == 0, c == 7, [Bones, Bsq], [Bst], c == 7)
            rsqrt_eps(r, st, [Bst], Br)
            for c in range(8):
                stt(h[:, c, :N], xt[:, c, :N], vecs[:, gcol + c:gcol + c + 1], r, ALU.mult, ALU.mult, [Bx, Br, Bvecs], [Bhh])

        xcnt = {"i": 0}

        def next_xt():
            i = xcnt["i"] % 2
            xcnt["i"] += 1
            return xts[i], Bxt[i], hbuf[i], Bh[i]

        def dview(dr, off, N):
            return dr[:, off:off + N].rearrange("(c p) n -> p c n", p=128)

        Bx = {}

        def dbuf(name, t):
            k = (name, 0)
            if k not in Bx:
                Bx[k] = Buf("%s_%s" % (name, t))
            return Bx[k]

        def stage_in():
            with nc.sbuf_tensor(U("xin"), [128, 2, D], F32) as xin:
                Bxin = Buf("xin")
                jobs = [(x_in, t * TT, TT, xsA, t * TT, ("A", t)) for t in range(NTILE)] + [(xh_in, 0, 2 * HALO, xhA, 0, ("hA", 0))]
                for src, soff, N, dst, doff, key in jobs:
                    nb = (N + 127) // 128
                    rows = min(N, 128)
                    S.dma("sp", xin[:rows, :nb, :], src[soff:soff + N, :].rearrange("(b p) d -> p b d", p=rows), writes=[Bxin])
                    xt, Bxx, _, _ = next_xt()
                    for c2 in range(4):
                        for ci in range(2):
                            c = 2 * c2 + ci
                            for b in range(nb):
                                S.op("pe", lambda e, c=c, b=b, rows=rows: e.transpose(ph[2 + c][:, b * 128:b * 128 + rows], xin[:rows, b, c * 128:(c + 1) * 128], identf[:rows, :rows]),
                                     reads=[Bxin, Bidf], writes=[PB[2 + c]], sig=(b == nb - 1 and ci == 1))
                        cp(xt[:, 2 * c2:2 * c2 + 2, :N], pbank[1 + c2][:, :].rearrange("p (h n) -> p h n", h=2)[:, :, :N], [PB[2 + 2 * c2]], [Bxx], eng=("act" if c2 % 2 else "dve"))
                    S.dma("sp", dview(dst, doff, N), xt[:, :, :N], reads=[Bxx], writes=[dbuf(*key)])

        def stage_ffn(l, i, src, sname, dst, dname, halo=None):
            gcol = V_NG + (l * 3 + (0 if i == 0 else 2)) * 8
            S.barrier()
            w13 = w13_d[(l, i)]
            w2 = w2_d[(l, i)]
            units = {}
            wstage(66)

            def unit(g):
                if g not in units:
                    G = 4 if g < 5 else 2
                    gv, gb = wload_cols(w13, g * 512, G * 128)
                    uv, ub = wload_cols(w13, FF + g * 512, G * 128)
                    dv, db = wload_rows(w2, g * 512, G * 128)
                    units[g] = (gv, gb, uv, ub, dv, db)
                return units[g]

            with nc.sbuf_tensor(U("sg"), [128, 2, TT], F32) as sg, nc.sbuf_tensor(U("actb"), [128, 3, TT], BF16) as actb:
                Bsg = [Buf("sg0"), Buf("sg1")]
                Bact = [Buf("act%d" % k) for k in range(3)]
                jobs = [(src, sname, dst, dname, t, t * TT, TT) for t in range(NTILE)]
                if halo is not None:
                    jobs.append((halo[0], halo[1], halo[2], halo[3], 0, 0, 2 * HALO))
                def prep(job):
                    (s_d, s_n, d_d, d_n, t, off, N) = job
                    xt, Bxx, h, Bhh = next_xt()
                    S.dma("sp", xt[:, :, :N], dview(s_d, off, N), reads=[dbuf(s_n, t)], writes=[Bxx])
                    rmsnorm(xt, Bxx, N, gcol, h, Bhh)
                    return xt, Bxx, h, Bhh

                cur = prep(jobs[0])
                for idx, (s_d, s_n, d_d, d_n, t, off, N) in enumerate(jobs):
                    xt, Bxx, h, Bhh = cur
                    LAG = 1

                    def ymm(j):
                        g, jj = j // 4, j % 4
                        dv, db = unit(g)[4], unit(g)[5]
                        a = j % 3
                        for dc in range(8):
                            mm(ph[dc][:, :N], dv[:, jj, dc * 128:(dc + 1) * 128], actb[:, a, :N], (j == 0 and dc % 2 == 0), j == 21,
                               [Bact[a]] + db, [PB[dc]], (dc == 7) or (j == 21), sgc=True)

                    for j in range(22 + LAG):
                        if j == 8 and idx + 1 < len(jobs):
                            cur = prep(jobs[idx + 1])
                        if j < 22:
                            g, jj = j // 4, j % 4
                            gv, gb, uv, ub = unit(g)[:4]
                            s3 = j % 3
                            pg, pu = ph[8 + 2 * s3][:, :N], ph[9 + 2 * s3][:, :N]
                            Bpg, Bpu = PB[8 + 2 * s3], PB[9 + 2 * s3]
                            for c in range(8):
                                mm(pg, gv[:, c, jj * 128:(jj + 1) * 128], h[:, c, :N], c == 0, c == 7, [Bhh] + gb, [Bpg], c == 7)
                            for c in range(8):
                                mm(pu, uv[:, c, jj * 128:(jj + 1) * 128], h[:, c, :N], c == 0, c == 7, [Bhh] + ub, [Bpu], c == 7)
                            act(sg[:, j % 2, :N], pg, AF.Silu, [Bpg], [Bsg[j % 2]])
                            tt(actb[:, j % 3, :N], sg[:, j % 2, :N], pu, ALU.mult, [Bsg[j % 2], Bpu], [Bact[j % 3]])
                        if j >= LAG:
                            ymm(j - LAG)
                    for dc in range(8):
                        stt(xt[:, dc, :N], ph[dc][:, :N], 0.5, xt[:, dc, :N], ALU.mult, ALU.add, [PB[dc], Bxx], [Bxx])
                    S.dma("sp", dview(d_d, off, N), xt[:, :, :N], reads=[Bxx], writes=[dbuf(d_n, t)])

        def stage_conv(src, sname, hsrc, hname, dst, dname):
            gcol = V_NG + 1 * 8
            S.barrier()
            W = TT + 32
            with nc.sbuf_tensor(U("ubuf"), [128, 8, W], BF16) as ubuf, \
                    nc.sbuf_tensor(U("sgc"), [128, 2, TT], F32) as sgc, \
                    nc.sbuf_tensor(U("vbuf"), [128, 8, TT], F32) as vbuf, \
                    nc.sbuf_tensor(U("v16"), [128, 8, TT], BF16) as v16, \
                    nc.sbuf_tensor(U("sil"), [128, 8, TT], BF16) as sil, \
                    nc.sbuf_tensor(U("lnst"), [128, 4, TT], F32) as lnst, \
                    nc.sbuf_tensor(U("xw"), [128, 8, TT], F32) as xw:
                Bub, Bsgc = Buf("ubuf"), [Buf("sgc0"), Buf("sgc1")]
                Bvb, Bv16, Bq16, Bsil, Bln, Bxw = Buf("vbuf"), Buf("v16"), Bsq, Buf("sil"), Buf("lnst"), Buf("xw")
                q16 = sqb
                wstage(55)
                pw1 = [wload_cols(wpw1_in, g * 512, 512) for g in range(4)]
                pw2 = [wload_cols(wpw2_in, g * 512, 512) for g in range(2)]
                ds_, dbufs = walloc(31)
                diag = wa[:, ds_ * 1024:(ds_ + 31) * 1024]
                for c in range(8):
                    for j in range(31):
                        m = c * 31 + j
                        ts(diag[:, m * 128:(m + 1) * 128], identb[:], vecs[:, V_WDW + m:V_WDW + m + 1], None, ALU.mult, None,
                           [Bidb, Bvecs], dbufs)
                S.op("dve", lambda e: e.memset(ubuf[:], 0.0), writes=[Bub])

                def make_u(s_d, s_n, t, off, N, ucol, hmask_col=None):
                    xt, Bxx, h, Bhh = next_xt()
                    S.dma("sp", xt[:, :, :N], dview(s_d, off, N), reads=[dbuf(s_n, t)], writes=[Bxx])
                    rmsnorm(xt, Bxx, N, gcol, h, Bhh)
                    for j in range(8):
                        s3 = j % 2
                        pa, pg = ph[2 + 2 * s3][:, :N], ph[3 + 2 * s3][:, :N]
                        Bpa, Bpg = PB[2 + 2 * s3], PB[3 + 2 * s3]
                        av, ab = pw1[j // 4]
                        gv, gb = pw1[2 + j // 4]
                        jj = j % 4
                        for c in range(8):
                            mm(pa, av[:, c, jj * 128:(jj + 1) * 128], h[:, c, :N], c == 0, c == 7, [Bhh] + ab, [Bpa], c == 7)
                        for c in range(8):
                            mm(pg, gv[:, c, jj * 128:(jj + 1) * 128], h[:, c, :N], c == 0, c == 7, [Bhh] + gb, [Bpg], c == 7)
                        act(sgc[:, j % 2, :N], pg, AF.Sigmoid, [Bpg, Bvecs], [Bsgc[j % 2]], bias=vecs[:, V_BPW1 + 8 + j:V_BPW1 + 9 + j])
                        stt(ubuf[:, j, ucol:ucol + N], pa, vecs[:, V_BPW1 + j:V_BPW1 + j + 1], sgc[:, j % 2, :N], ALU.add, ALU.mult,
                            [Bpa, Bsgc[j % 2], Bvecs], [Bub])
                    if hmask_col is not None:
                        for j in range(8):
                            tt(ubuf[:, j, ucol:ucol + N], ubuf[:, j, ucol:ucol + N], cvec[:, hmask_col:hmask_col + N], ALU.mult, [Bub, Bcvec], [Bub])

                def window1(tok0, Wn, col0):
                    S.dma("sp", xw[:, :, :Wn], dview(src, tok0, Wn), reads=[dbuf(sname, t) for t in range(NTILE)], writes=[Bxw])
                    for c in range(8):
                        pv, Bpv = ph[6 + 2 * (c % 2)][:, :Wn], PB[6 + 2 * (c % 2)]
                        for j in range(31):
                            m = c * 31 + j
                            mm(pv, diag[:, m * 128:(m + 1) * 128], ubuf[:, c, col0 - 15 + j: col0 - 15 + j + Wn], j == 0, j == 30,
                               [Bub] + dbufs, [Bpv], j == 30)
                        act(vbuf[:, c, :Wn], pv, AF.Identity, [Bpv, Bvecs], [Bvb], bias=vecs[:, V_BDW + c:V_BDW + c + 1])

                def window2(tok0, Wn, col0):
                    cp(v16[:, :, :Wn], vbuf[:, :, :Wn], [Bvb], [Bv16])
                    tt(q16[:, :, :Wn], vbuf[:, :, :Wn], vbuf[:, :, :Wn], ALU.mult, [Bvb], [Bq16])
                    p1, p2 = ph[10][:, :Wn], ph[11][:, :Wn]
                    for c in range(8):
                        mm(p1, ones_d[:], v16[:, c, :Wn], c == 0, c == 7, [Bones, Bv16], [PB[10]], c == 7)
                    for c in range(8):
                        mm(p2, ones_d[:], q16[:, c, :Wn], c == 0, c == 7, [Bones, Bq16], [PB[11]], c == 7)
                    mean, msq, var, rs = lnst[:, 0, :Wn], lnst[:, 1, :Wn], lnst[:, 2, :Wn], lnst[:, 3, :Wn]
                    cp(mean, p1, [PB[10]], [Bln])
                    tt(msq, mean, mean, ALU.mult, [Bln], [Bln])
                    tt(var, p2, msq, ALU.subtract, [PB[11], Bln], [Bln])
                    rsqrt_eps(rs, var, [Bln], Bln)
                    for c in range(8):
                        tt(vbuf[:, c, :Wn], vbuf[:, c, :Wn], mean, ALU.subtract, [Bvb, Bln], [Bvb])
                        tt(vbuf[:, c, :Wn], vbuf[:, c, :Wn], rs, ALU.mult, [Bvb, Bln], [Bvb])
                        act(sil[:, c, :Wn], vbuf[:, c, :Wn], AF.Silu, [Bvb, Bvecs], [Bsil],
                            bias=vecs[:, V_LNB + c:V_LNB + c + 1], scale=vecs[:, V_LNG + c:V_LNG + c + 1])
                    for dc in range(8):
                        po, Bpo = (ph[12][:, :Wn], PB[12]) if dc % 2 == 0 else (ph[0][:, :Wn], PB[0])
                        wv, wb_ = pw2[dc // 4]
                        for c in range(8):
                            mm(po, wv[:, c, (dc % 4) * 128:(dc % 4 + 1) * 128], sil[:, c, :Wn], c == 0, c == 7, [Bsil] + wb_, [Bpo], c == 7)
                        stt(xw[:, dc, :Wn], po, vecs[:, V_BPW2 + dc:V_BPW2 + dc + 1], xw[:, dc, :Wn], ALU.add, ALU.add, [Bpo, Bxw, Bvecs], [Bxw])
                    S.dma("sp", dview(dst, tok0, Wn), xw[:, :, :Wn], reads=[Bxw], writes=[dbuf(dname, t) for t in range(NTILE)])

                make_u(hsrc, hname, 0, 0, HALO, 16, hmask_col=C_HM)
                make_u(src, sname, 0, 0, TT, 32)
                for t in range(NTILE):
                    wargs = (0, TT - 16, 32) if t == 0 else (t * TT - 16, TT, 16)
                    window1(*wargs)
                    cp(ubuf[:, :, 0:32], ubuf[:, :, TT:TT + 32], [Bub], [Bub], eng="act")
                    if t + 1 < NTILE:
                        make_u(src, sname, t + 1, (t + 1) * TT, TT, 32)
                    else:
                        make_u(hsrc, hname, 0, HALO, HALO, 32, hmask_col=C_HM + HALO)
                    window2(*wargs)
                window1(NT - 16, 16, 16)
                window2(NT - 16, 16, 16)

        def stage_hgrn(src, sname, dst, dname):
            gcol = V_NG + (1 * 3 + 1) * 8
            NCH = TT // 64
            NSB = TT // 128
            TAIL0 = 48
            S.barrier()

            def run_lanes(gens):
                gens = list(gens)
                while gens:
                    nxt = []
                    for g in gens:
                        try:
                            next(g)
                            nxt.append(g)
                        except StopIteration:
                            pass
                    gens = nxt

            with contextlib.ExitStack() as hs_:
                def hsb(name, shape, dt=F32):
                    return hs_.enter_context(nc.sbuf_tensor(U(name), list(shape), dt))
                wstage(48)
                S.op("dve", lambda e: e.memset(wa[:, TAIL0 * 1024:TAIL0 * 1024 + 16], 0.0), writes=WB[TAIL0:NBLK])
                tail_tok = WB[TAIL0].w
                lane_bufs = []

                def LB(name):
                    b = Buf(name)
                    b.w = tail_tok
                    lane_bufs.append(b)
                    return b
                tail = {"o": TAIL0 * 1024}

                def tail_f32(n):
                    o = tail["o"]
                    tail["o"] += 2 * n
                    assert tail["o"] <= NBLK * 1024
                    return wa[:, o:o + 2 * n].bitcast(F32)

                def tail_bf(n):
                    o = tail["o"]
                    tail["o"] += n
                    assert tail["o"] <= NBLK * 1024
                    return wa[:, o:o + n]
                gA = [tail_f32(TT) for _ in range(4)]; BgA = [LB("gA%d" % i) for i in range(4)]
                gB = [tail_f32(TT) for _ in range(4)]; BgB = [LB("gB%d" % i) for i in range(4)]
                gC = [tail_f32(TT) for _ in range(4)]; BgC = [LB("gC%d" % i) for i in range(4)]
                gD = [tail_f32(TT) for _ in range(4)]; BgD_ = [LB("gD%d" % i) for i in range(4)]
                Qt = [tail_bf(TT) for _ in range(8)]; BQt = [LB("Qt%d" % i) for i in range(8)]
                Kt = [tail_bf(TT) for _ in range(8)]; BKt = [LB("Kt%d" % i) for i in range(8)]
                Kh = [tail_bf(TT) for _ in range(8)]; BKh = [LB("Kh%d" % i) for i in range(8)]
                Khtm_ = [tail_bf(TT) for _ in range(8)]; BKhtm = [LB("Khtm%d" % i) for i in range(8)]
                Khtm = [k.rearrange("p (s n) -> p s n", s=NSB) for k in Khtm_]
                erT = hsb("erT", [128, 8, NCH]); Ber = [Buf("er%d" % i) for i in range(8)]
                qbT = hsb("qbT", [128, 4, TT]); Bqb = [Buf("qb%d" % i) for i in range(4)]
                gsT = hsb("gsT", [128, 4, TT]); Bgs = [Buf("gs%d" % i) for i in range(4)]
                osbT = tail_f32(2 * TT).rearrange("p (s n) -> p s n", s=2); Bosb = [LB("osb%d" % i) for i in range(2)]
                osqT = hsb("osqT", [128, 2, TT], BF16); Bosq = [Buf("osq%d" % i) for i in range(2)]
                Vtm = hsb("Vtm", [128, NSB, D], BF16); BV = Buf("Vtm")
                Pm = hsb("Pm", [128, 2, 2, 128], BF16); BPm = [Buf("Pm0"), Buf("Pm1")]
                Sst = hsb("Sst", [128, 2, 128]); BSst = [Buf("Sst0"), Buf("Sst1")]
                Sbf = hsb("Sbf", [128, 2, 2, NCH + 1, 128], BF16); BSbf = [Buf("Sbf0"), Buf("Sbf1")]; BSbf2 = [Buf("Sbf2"), Buf("Sbf3")]
                Sfr = hsb("Sfr", [128, 8, 128]); BSfr = [Buf("Sfr%d" % i) for i in range(8)]
                Sbt = hsb("Sbt", [128, 8, 128]); BSbt = [Buf("Sbt%d" % i) for i in range(8)]
                Cb = hsb("Cb", [128, 8]); BCb = Buf("Cb")
                Cf = hsb("Cf", [128, 8]); BCf = Buf("Cf")
                Usb = hsb("Usb", [128, 4, 128]); BUsb = [Buf("Usb%d" % i) for i in range(4)]
                on = hsb("on", [128, 8, TT], BF16); Bon = Buf("on")
                gU = hsb("gU", [128, 2, 128]); BgU = [Buf("gU0"), Buf("gU1")]
                gDD = hsb("gDD", [128, 128]); BgDD = Buf("gDD")
                rsl = tail_f32(2 * TT).rearrange("p (s n) -> p s n", s=2); Brsl = [LB("rsl0"), LB("rsl1")]
                hx = hsb("hx", [128, 2, TT]); Bhx = [Buf("hx0"), Buf("hx1")]

                win_u = {}

                def win(gi):
                    if gi not in win_u:
                        win_u[gi] = wload_cols(win_in, gi * 512, 512)
                    return win_u[gi]

                def rsqrt_le(out, in_, reads, Bout):
                    act(out, in_, AF.Ln, list(reads) + [Beps], [Bout], bias=epsc[:, 0:1])
                    act(out, out, AF.Exp, [Bout], [Bout], scale=-0.5)

                def load_norm(t):
                    xt, Bxx, h, Bhh = next_xt()
                    S.dma("sp", xt[:], dview(src, t * TT, TT), reads=[dbuf(sname, t)], writes=[Bxx])
                    i = nstat["i"] % 2
                    nstat["i"] += 1
                    st, Bst = ph[14 + i], PB[14 + i]
                    tt(sqb[:], xt[:], xt[:], ALU.mult, [Bxx], [Bsq])
                    for c in range(8):
                        mm(st, ones_d[:], sqb[:, c, :], c == 0, c == 7, [Bones, Bsq], [Bst], c == 7)
                    rsqrt_le(rstd[i][:], st, [Bst], Brstd[i])
                    for c in range(8):
                        stt(h[:, c, :], xt[:, c, :], vecs[:, gcol + c:gcol + c + 1], rstd[i][:], ALU.mult, ALU.mult, [Bxx, Brstd[i], Bvecs], [Bhh])
                    for sbk in range(NSB):
                        for cg in range(2):
                            wv, wb_ = win(2 + cg)
                            pv = pbank[1][:, :]
                            for c in range(8):
                                mm(pv, h[:, c, sbk * 128:(sbk + 1) * 128], wv[:, c, :], c == 0, c == 7, [Bhh] + wb_, [PB[2]], c == 7)
                            cp(Vtm[:, sbk, cg * 512:(cg + 1) * 512], pv, [PB[2]], [BV], eng=("act" if cg else "dve"))
                    return xt, Bxx, h, Bhh

                pslot = {"i": 0}

                def proj(sec, hd, h, Bhh):
                    s3 = pslot["i"] % 3
                    pslot["i"] += 1
                    pz, Bpz = ph[4 + 2 * s3], PB[4 + 2 * s3]
                    wv, wb_ = win(2 * sec + hd // 4)
                    for c in range(8):
                        mm(pz, wv[:, c, (hd % 4) * 128:(hd % 4 + 1) * 128], h[:, c, :], c == 0, c == 7, [Bhh] + wb_, [Bpz], c == 7)
                    return pz, Bpz

                def gates_gen(dr, hd, h, Bhh, li):
                    A, B_ = gA[li], gB[li]
                    pz, Bpz = proj(2 + dr, hd, h, Bhh)
                    act(A, pz, AF.Tanh, [Bpz], [BgA[li]], scale=0.5)
                    yield
                    ts(B_, A, homl[:, dr, hd:hd + 1], c1v[:, dr, hd:hd + 1], ALU.mult, ALU.add, [BgA[li], Blb], [BgB[li]])
                    yield
                    ts(A, B_, -1.0, 1.0, ALU.mult, ALU.add, [BgB[li]], [BgA[li]])
                    yield
                    act(B_, B_, AF.Ln, [BgB[li]], [BgB[li]])
                    yield

                def transp_gen(pi, li):
                    for sbk in range(NSB):
                        ptr = pbank[5][:, :].bitcast(BF16)[:, (li * NSB + sbk) * 128:(li * NSB + sbk + 1) * 128]
                        S.op("pe", lambda e, ptr=ptr, sbk=sbk: e.transpose(ptr, Kh[pi][:, sbk * 128:(sbk + 1) * 128], identb[:]),
                             reads=[BKh[pi], Bidb], writes=[PB[10]])
                        yield
                        cp(Khtm[pi][:, sbk, :], ptr, [PB[10]], [BKhtm[pi]], eng="act")
                        yield

                S.op("dve", lambda e: e.memset(Sfr[:], 0.0), writes=BSfr)
                S.op("dve", lambda e: e.memset(Sbt[:], 0.0), writes=BSbt)
                S.op("dve", lambda e: e.memset(Cb[:], 1.0), writes=[BCb])
                S.op("dve", lambda e: e.memset(Cf[:], 1.0), writes=[BCf])

                def laneA(t, hd, dr, li, pi, h, Bhh):
                    A, B_, C_, D_ = gA[li], gB[li], gC[li], gD[li]
                    yield from gates_gen(dr, hd, h, Bhh, li)
                    S.op("dve", lambda e: e.tensor_tensor_scan(out=C_, data0=ones_t[:], data1=B_, initial=0.0, op0=ALU.mult, op1=ALU.add),
                         reads=[BgB[li], Brm], writes=[BgC[li]])
                    yield
                    if dr == 0:
                        act(D_, C_, AF.Exp, [BgC[li]], [BgD_[li]], bias=C_[:, TT - 1:TT], scale=-1.0)
                        dcol, Bd = erT[:, pi, 0:1], Ber[pi]
                    else:
                        tt(D_, C_, B_, ALU.subtract, [BgC[li], BgB[li]], [BgD_[li]])
                        yield
                        act(D_, D_, AF.Exp, [BgD_[li]], [BgD_[li]])
                        dcol, Bd = Dall[:, hd, t:t + 1], BDall
                    yield
                    act(dcol, C_[:, TT - 1:TT], AF.Exp, [BgC[li]], [Bd])
                    yield
                    tt(Kh[pi], A, D_, ALU.mult, [BgA[li], BgD_[li]], [BKh[pi]])
                    yield
                    yield from transp_gen(pi, li)
                    pu, Bpu = ph[12 + li // 2][:, (li % 2) * 128:(li % 2 + 1) * 128], PB[12]
                    for sbk in range(NSB):
                        mm(pu, Khtm[pi][:, sbk, :], Vtm[:, sbk, hd * 128:(hd + 1) * 128], sbk == 0, sbk == NSB - 1, [BKhtm[pi], BV], [Bpu], sbk == NSB - 1)
                    yield
                    if dr == 0:
                        stt(Sfr[:, hd, :], Sfr[:, hd, :], erT[:, pi, 0:1], pu, ALU.mult, ALU.add, [BSfr[hd], Ber[pi], Bpu], [BSfr[hd]])
                        yield
                        tt(Cf[:, hd:hd + 1], Cf[:, hd:hd + 1], erT[:, pi, 0:1], ALU.mult, [BCf, Ber[pi]], [BCf])
                    else:
                        cp(Usb[:, li, :], pu, [Bpu], [BUsb[li]])
                        yield
                        S.dma("sp", ub_d[hd, t], Usb[:, li, :], reads=[BUsb[li]], writes=[dbuf("ub%d" % hd, 0)])
                        stt(Sbt[:, hd, :], pu, Cb[:, hd:hd + 1], Sbt[:, hd, :], ALU.mult, ALU.add, [Bpu, BCb, BSbt[hd]], [BSbt[hd]])
                        yield
                        tt(Cb[:, hd:hd + 1], Cb[:, hd:hd + 1], Dall[:, hd, t:t + 1], ALU.mult, [BCb, BDall], [BCb])
                    yield

                for t in range(NTILE):
                    xt, Bxx, h, Bhh = load_norm(t)
                    for gi in range(4):
                        lanes = []
                        for hh in range(2):
                            for dr in range(2):
                                li = hh * 2 + dr
                                lanes.append(laneA(t, 2 * gi + hh, dr, li, (gi % 2) * 4 + li, h, Bhh))
                        run_lanes(lanes)

                Bcc_in, Bcc_out = Buf("cc_in"), Buf("cc_out")
                ccv = cc_in.rearrange("(m p) v -> p m v", p=128)
                S.dma("sp", ccv[:, 0:8, :], Sfr[:], reads=BSfr, writes=[Bcc_in])
                S.dma("sp", ccv[:, 8:16, :], Sbt[:], reads=BSbt, writes=[Bcc_in])
                S.op("dve", lambda e: e.memset(gDD[:], 0.0), writes=[BgDD])
                cp(gDD[:, 0:8], Cf[:], [BCf, BgDD], [BgDD])
                cp(gDD[:, 8:16], Cb[:], [BCb, BgDD], [BgDD])
                S.dma("sp", ccv[:, 16, :], gDD[:], reads=[BgDD], writes=[Bcc_in])
                if cc_mode == "produce":
                    S.wait_tok("sp", Bcc_in.w)
                    return
                waits = S._waits("pool", [Bcc_in], [Bcc_out], is_dma=True)
                csem = S.new_sem("cc")
                tokc = (csem, 1)
                S._mark(tokc, [Bcc_in], [Bcc_out])
                if cc_mode == "fused":
                    S.prog["pool"].append((waits, lambda e: e.collective_compute("AllGather", ALU.bypass, replica_groups=[list(range(NCORES))],
                                                                                 ins=[cc_in], outs=[cc_out]), (csem, 1)))
                else:
                    Bcc_out.w = None
                S.op("dve", lambda e: e.memset(Sfr[:], 0.0), reads=BSfr, writes=BSfr)
                S.op("dve", lambda e: e.memset(Sbt[:], 0.0), reads=BSbt, writes=BSbt)
                fold_buf = [wa[:, TAIL0 * 1024 + k * 8 * TT: TAIL0 * 1024 + (k + 1) * 8 * TT].bitcast(F32).rearrange("p (m v) -> p m v", m=8) for k in range(2)]
                fold_B = [BgA, BgB]
                gD2 = [gDD[:, 0:32], gDD[:, 32:64]]
                BgD2 = [Buf("gD2a"), Buf("gD2b")]
                kk_ = 0
                for dr, order in ((0, range(NCORES)), (1, range(NCORES - 1, -1, -1))):
                    Sx, BSx = (Sfr, BSfr) if dr == 0 else (Sbt, BSbt)
                    for r in order:
                        base = r * 17 * 128
                        fb, fB, gd, Bgd = fold_buf[kk_ % 2], fold_B[kk_ % 2], gD2[kk_ % 2], BgD2[kk_ % 2]
                        kk_ += 1
                        S.dma("sp", gd[:, 0:16], cc_out[base + 2048: base + 2176, 0:16], reads=[Bcc_out], writes=[Bgd])
                        S.dma("sp", fb, cc_out[base + dr * 1024: base + dr * 1024 + 1024, :].rearrange("(m p) v -> p m v", p=128),
                              reads=[Bcc_out], writes=list(fB))
                        mcol = cvec[:, C_RM + dr * 8 + r: C_RM + dr * 8 + r + 1]
                        omcol = cvec[:, C_RM1 + dr * 8 + r: C_RM1 + dr * 8 + r + 1]
                        ts(gd[:, 16:24], gd[:, dr * 8: dr * 8 + 8], mcol, omcol, ALU.mult, ALU.add, [Bgd, Bcvec], [Bgd])
                        for hd in range(8):
                            ts(fb[:, hd, :], fb[:, hd, :], mcol, None, ALU.mult, None, list(fB) + [Bcvec], list(fB))
                            stt(Sx[:, hd, :], Sx[:, hd, :], gd[:, 16 + hd:17 + hd], fb[:, hd, :], ALU.mult, ALU.add, [BSx[hd], Bgd] + list(fB), [BSx[hd]])
                for t in range(NTILE - 1, -1, -1):
                    for hd in range(8):
                        if t < NTILE - 1:
                            S.dma("sp", Usb[:, hd % 4, :], ub_d[hd, t + 1], reads=[dbuf("ub%d" % hd, 0)], writes=[BUsb[hd % 4]])
                            stt(Sbt[:, hd, :], Sbt[:, hd, :], Dall[:, hd, t + 1:t + 2], Usb[:, hd % 4, :], ALU.mult, ALU.add, [BSbt[hd], BDall, BUsb[hd % 4]], [BSbt[hd]])
                        S.dma("sp", sb_d[hd, t], Sbt[:, hd, :], reads=[BSbt[hd]], writes=[dbuf("sb%d" % hd, 0)])

                wout_u = {}

                def wout(g):
                    if g not in wout_u:
                        wout_u[g] = wload_cols(wout_in, g * 512, 512)
                    return wout_u[g]

                Mm = (Mf, Mb)

                def headB(hd, hs, h, Bhh):
                    for sec, dstT, Bd in ((0, qbT, Bqb), (4, gsT, Bgs)):
                        pq, Bpq = proj(sec, hd, h, Bhh)
                        act(dstT[:, hs, :], pq, AF.Tanh, [Bpq], [Bd[hs]], scale=0.5)
                        act(hx[:, hs % 2, :], pq, AF.Copy, [Bpq], [Bhx[hs % 2]], scale=0.5)
                        yield
                        stt(dstT[:, hs, :], dstT[:, hs, :], 1.0, hx[:, hs % 2, :], ALU.add, ALU.mult, [Bd[hs], Bhx[hs % 2]], [Bd[hs]])
                        yield

                def laneB(t, hd, dr, li, pi, hs, h, Bhh):
                    A, B_, C_, D_ = gA[li], gB[li], gC[li], gD[li]
                    yield from gates_gen(dr, hd, h, Bhh, li)
                    S.op("dve", lambda e: e.tensor_tensor_scan(out=C_, data0=rmask[:], data1=B_, initial=0.0, op0=ALU.mult, op1=ALU.add),
                         reads=[BgB[li], Brm], writes=[BgC[li]])
                    yield
                    C3 = C_.rearrange("p (c j) -> p c j", j=64)
                    if dr == 1:
                        D3 = D_.rearrange("p (c j) -> p c j", j=64)
                        tt(D3, C3[:, :, 63:64].to_broadcast([128, NCH, 64]), C3, ALU.subtract, [BgC[li]], [BgD_[li]])
                        yield
                        tt(C_, D_, B_, ALU.add, [BgD_[li], BgB[li]], [BgC[li]])
                        yield
                        rcol = C3[:, :, 0]
                    else:
                        rcol = C3[:, :, 63]
                    act(erT[:, pi, :], rcol, AF.Exp, [BgC[li]], [Ber[pi]])
                    yield
                    act(D_, C_, AF.Exp, [BgC[li]], [BgD_[li]])
                    yield
                    tt(Qt[pi], qbT[:, hs, :], D_, ALU.mult, [Bqb[hs], BgD_[li]], [BQt[pi]])
                    yield
                    act(D_, C_, AF.Exp, [BgC[li]], [BgD_[li]], scale=-1.0)
                    yield
                    tt(A, A, D_, ALU.mult, [BgA[li], BgD_[li]], [BgA[li]])
                    yield
                    cp(Kt[pi], A, [BgA[li]], [BKt[pi]], eng="act")
                    yield
                    tt(Kh[pi].rearrange("p (c j) -> p c j", j=64), A.rearrange("p (c j) -> p c j", j=64),
                       erT[:, pi, :].unsqueeze(2).to_broadcast([128, NCH, 64]), ALU.mult, [BgA[li], Ber[pi]], [BKh[pi]])
                    yield
                    yield from transp_gen(pi, li)

                def par_gens(*gens):
                    gens = list(gens)
                    while gens:
                        nxt = []
                        for g in gens:
                            try:
                                next(g)
                                nxt.append(g)
                            except StopIteration:
                                pass
                        gens = nxt
                        yield

                Pm5 = sqb[:, 0:4, :].rearrange("p a n -> p (a n)").rearrange("p (a s d n) -> p a s d n", a=2, s=2, d=2)

                def fwd_chain(t, hd, pf, par):
                    cp(Sbf[:, par, 0, 0, :], Sfr[:, hd, :], [BSfr[hd]], [BSbf[par]], eng="act")
                    yield
                    for ch in range(NCH):
                        sbk, rows = ch // 2, slice((ch % 2) * 64, (ch % 2) * 64 + 64)
                        pu, Bpu = (ph[12][:, 0:128], PB[12]) if par == 0 else (ph[7][:, 0:128], PB[7])
                        mm(pu, Khtm[pf][rows, sbk, :], Vtm[rows, sbk, hd * 128:(hd + 1) * 128], True, True, [BKhtm[pf], BV], [Bpu], True)
                        yield
                        stt(Sfr[:, hd, :], Sfr[:, hd, :], erT[:, pf, ch:ch + 1], pu, ALU.mult, ALU.add, [BSfr[hd], Ber[pf], Bpu], [BSfr[hd]])
                        yield
                        if ch < NCH - 1:
                            cp(Sbf[:, par, 0, ch + 1, :], Sfr[:, hd, :], [BSfr[hd]], [BSbf[par]], eng="act")
                            yield

                def bwd_chain(t, hd, pb, par):
                    S.dma("sp", Sst[:, par, :], sb_d[hd, t], reads=[dbuf("sb%d" % hd, 0)], writes=[BSst[par]])
                    cp(Sbf[:, par, 1, NCH, :], Sst[:, par, :], [BSst[par]], [BSbf2[par]], eng="act")
                    yield
                    for ch in range(NCH - 1, 0, -1):
                        sbk, rows = ch // 2, slice((ch % 2) * 64, (ch % 2) * 64 + 64)
                        pu, Bpu = (ph[5][:, 0:128], PB[5]) if par == 0 else (ph[9][:, 0:128], PB[9])
                        mm(pu, Khtm[pb][rows, sbk, :], Vtm[rows, sbk, hd * 128:(hd + 1) * 128], True, True, [BKhtm[pb], BV], [Bpu], True)
                        yield
                        stt(Sst[:, par, :], Sst[:, par, :], erT[:, pb, ch:ch + 1], pu, ALU.mult, ALU.add, [BSst[par], Ber[pb], Bpu], [BSst[par]])
                        yield
                        cp(Sbf[:, par, 1, ch, :], Sst[:, par, :], [BSst[par]], [BSbf2[par]], eng="act")
                        yield

                def scores_gen(hd, pf, pb, par):
                    pset = (pf, pb)
                    for sbk in range(NSB):
                        cols = slice(sbk * 128, (sbk + 1) * 128)
                        for dr in range(2):
                            psc, Bps = ph[3 - par][:, dr * 128:(dr + 1) * 128], PB[3]
                            mm(psc, Kt[pset[dr]][:, cols], Qt[pset[dr]][:, cols], True, True, [BKt[pset[dr]], BQt[pset[dr]]], [Bps], True)
                            yield
                            tt(Pm5[:, par, sbk, dr, :], psc, Mm[dr][:], ALU.mult, [Bps, BM], [Bsq])
                            yield

                def postB(t, hd, pf, pb, hs, par, h, Bhh):
                    yield from par_gens(fwd_chain(t, hd, pf, par), bwd_chain(t, hd, pb, par), scores_gen(hd, pf, pb, par))
                    po, Bpo = ph[par], PB[0]
                    for sbk in range(NSB):
                        cols = slice(sbk * 128, (sbk + 1) * 128)
                        mm(po[:, cols], Vtm[:, sbk, hd * 128:(hd + 1) * 128], Pm5[:, par, sbk, 0, :], True, False, [BV, Bsq], [Bpo], False)
                        mm(po[:, cols], Vtm[:, sbk, hd * 128:(hd + 1) * 128], Pm5[:, par, sbk, 1, :], False, False, [BV, Bsq], [Bpo], False)
                        for cc in (2 * sbk, 2 * sbk + 1):
                            c64 = slice(cc * 64, cc * 64 + 64)
                            mm(po[:, c64], Sbf[:, par, 0, cc, :], Qt[pf][:, c64], False, False, [BSbf[par], BQt[pf]], [Bpo], False)
                            mm(po[:, c64], Sbf[:, par, 1, cc + 1, :], Qt[pb][:, c64], False, True, [BSbf2[par], BQt[pb]], [Bpo], True)
                        yield
                    cp(osbT[:, par, :], po, [Bpo], [Bosb[par]], eng="act")
                    yield
                    tt(osqT[:, par, :], osbT[:, par, :], osbT[:, par, :], ALU.mult, [Bosb[par]], [Bosq[par]])
                    yield
                    pst, Bpst = ph[14 + par], PB[15]
                    mm(pst, ones_h[:], osqT[:, par, :], True, True, [Bones, Bosq[par]], [Bpst], True)
                    yield
                    act(rsl[:, par, :], pst, AF.Ln, [Bpst, Beps], [Brsl[par]], bias=epsc[:, 0:1])
                    yield
                    act(rsl[:, par, :], rsl[:, par, :], AF.Exp, [Brsl[par]], [Brsl[par]], scale=-0.5)
                    yield
                    stt(osbT[:, par, :], osbT[:, par, :], vecs[:, V_GNG + hd:V_GNG + hd + 1], rsl[:, par, :], ALU.mult, ALU.mult, [Bosb[par], Bvecs, Brsl[par]], [Bosb[par]])
                    yield
                    tt(on[:, hd, :], osbT[:, par, :], gsT[:, hs, :], ALU.mult, [Bosb[par], Bgs[hs]], [Bon])
                    yield

                def postpair(t, gi, h, Bhh):
                    gens = []
                    for hh in range(2):
                        hd = 2 * gi + hh
                        base = (gi % 2) * 4 + hh * 2
                        gens.append(postB(t, hd, base, base + 1, (gi % 2) * 2 + hh, hh, h, Bhh))
                    yield from par_gens(*gens)

                for t in range(NTILE):
                    xt, Bxx, h, Bhh = load_norm(t)
                    pending = None
                    for gi in range(4):
                        gens = []
                        for hh in range(2):
                            gens.append(headB(2 * gi + hh, (gi % 2) * 2 + hh, h, Bhh))
                        for hh in range(2):
                            for dr in range(2):
                                li = hh * 2 + dr
                                gens.append(laneB(t, 2 * gi + hh, dr, li, (gi % 2) * 4 + li, (gi % 2) * 2 + hh, h, Bhh))
                        if pending is not None:
                            gens.append(pending)
                        run_lanes(gens)
                        pending = postpair(t, gi, h, Bhh)
                    run_lanes([pending])
                    for dc in range(8):
                        pw, Bpw = (ph[2], PB[2]) if dc % 2 == 0 else (ph[4], PB[4])
                        wv, wb_ = wout(dc // 4)
                        for c in range(8):
                            mm(pw, wv[:, c, (dc % 4) * 128:(dc % 4 + 1) * 128], on[:, c, :], c == 0, c == 7, [Bon] + wb_, [Bpw], c == 7)
                        tt(xt[:, dc, :], xt[:, dc, :], pw, ALU.add, [Bxx, Bpw], [Bxx])
                    S.dma("sp", dview(dst, t * TT, TT), xt[:], reads=[Bxx], writes=[dbuf(dname, t)])
                S.op("dve", lambda e: e.memset(wa[:, TAIL0 * 1024:TAIL0 * 1024 + 16], 0.0), reads=lane_bufs, writes=WB[TAIL0:NBLK])

        def stage_out(src, sname, do_norm=True):
            S.barrier()
            with nc.sbuf_tensor(U("xo"), [128, 8, TT], F32) as xo, nc.sbuf_tensor(U("ot"), [128, 2, D], F32) as ot:
                Bxo, Bot = Buf("xo"), Buf("ot")
                toks = []
                for t in range(NTILE):
                    xt, Bxx, h, Bhh = next_xt()
                    S.dma("sp", xt[:], dview(src, t * TT, TT), reads=[dbuf(sname, t)], writes=[Bxx])
                    if do_norm:
                        i = nstat["i"] % 2
                        nstat["i"] += 1
                        tt(sqb[:], xt[:], xt[:], ALU.mult, [Bxx], [Bsq])
                        for c in range(8):
                            mm(ph[14 + i], ones_d[:], sqb[:, c, :], c == 0, c == 7, [Bones, Bsq], [PB[14 + i]], c == 7)
                        rsqrt_eps(rstd[i][:], ph[14 + i], [PB[14 + i]], Brstd[i])
                        for c in range(8):
                            stt(xo[:, c, :], xt[:, c, :], vecs[:, V_FG + c:V_FG + c + 1], rstd[i][:], ALU.mult, ALU.mult, [Bxx, Brstd[i], Bvecs], [Bxo])
                        srcx, Bsrc = xo, Bxo
                    else:
                        srcx, Bsrc = xt, Bxx
                    for b in range(2):
                        for half in range(2):
                            pbk = pbank[1 + half]
                            for c4 in range(4):
                                c = half * 4 + c4
                                S.op("pe", lambda e, pbk=pbk, c4=c4, c=c, b=b, srcx=srcx: e.transpose(pbk[:, c4 * 128:(c4 + 1) * 128], srcx[:, c, b * 128:(b + 1) * 128], identf[:]),
                                     reads=[Bsrc, Bidf], writes=[PB[2 + 2 * half], PB[3 + 2 * half]], sig=(c4 == 3))
                            cp(ot[:, b, half * 512:(half + 1) * 512], pbk[:, :], [PB[2 + 2 * half], PB[3 + 2 * half]], [Bot], eng=("act" if half else "dve"))
                    toks.append(S.dma("sp", out_d[t * TT:(t + 1) * TT, :].rearrange("(b p) d -> p b d", p=128), ot[:], reads=[Bot], writes=[dbuf("out", t)]))
                for tk in toks:
                    S.wait_tok("sp", tk)

        stage_in()
        cur, cname, oth, oname = xsA, "A", xsB, "B"
        if upto >= 1:
            stage_ffn(0, 0, xsA, "A", xsB, "B", halo=(xhA, "hA", xhB, "hB"))
            cur, cname, oth, oname = xsB, "B", xsA, "A"
        if upto >= 2:
            stage_conv(xsB, "B", xhB, "hB", xsA, "A")
            cur, cname, oth, oname = xsA, "A", xsB, "B"
        if upto >= 3:
            stage_ffn(0, 1, xsA, "A", xsB, "B")
            cur, cname = xsB, "B"
        if upto >= 4:
            stage_ffn(1, 0, xsB, "B", xsA, "A")
            cur, cname = xsA, "A"
        if upto >= 5:
            stage_hgrn(xsA, "A", xsB, "B")
            cur, cname = xsB, "B"
        if cc_mode != "produce":
            if upto >= 6:
                stage_ffn(1, 1, xsB, "B", xsA, "A")
                cur, cname = xsA, "A"
            stage_out(cur, cname, do_norm=(upto >= 7))

        with nc.Block() as block:
            S.emit(block)
    return nc


def colpack(v):
    v = np.asarray(v, dtype=np.float32).reshape(-1, 128)
    return np.ascontiguousarray(v.T)


def needed_weights(upto):
    need = []
    if upto >= 1:
        need += ["w13_00", "w2_00"]
    if upto >= 2:
        need += ["conv_w_pw1", "conv_w_pw2"]
    if upto >= 3:
        need += ["w13_01", "w2_01"]
    if upto >= 4:
        need += ["w13_10", "w2_10"]
    if upto >= 5:
        need += ["hgrn_w_in", "hgrn_w_out"]
    if upto >= 6:
        need += ["w13_11", "w2_11"]
    return need


def make_inputs(upto, x, norm_g, ffn_w13, ffn_w2, conv_w_pw1, conv_b_pw1, conv_w_dw, conv_b_dw, conv_ln_g, conv_ln_b,
                conv_w_pw2, conv_b_pw2, hgrn_w_in, hgrn_lb, hgrn_gn_g, hgrn_w_out, final_g):
    f = lambda a: np.ascontiguousarray(np.asarray(a, dtype=np.float32))
    vecs = np.zeros((128, NV), np.float32)
    vecs[:, V_NG:V_NG + 48] = colpack(f(norm_g).reshape(-1))
    vecs[:, V_FG:V_FG + 8] = colpack(final_g)
    vecs[:, V_BPW1:V_BPW1 + 16] = colpack(f(conv_b_pw1)[0])
    wdw = f(conv_w_dw)[0]
    for c in range(8):
        vecs[:, V_WDW + c * 31:V_WDW + (c + 1) * 31] = wdw[:, c * 128:(c + 1) * 128].T
    vecs[:, V_BDW:V_BDW + 8] = colpack(f(conv_b_dw)[0])
    vecs[:, V_LNG:V_LNG + 8] = colpack(f(conv_ln_g)[0])
    vecs[:, V_LNB:V_LNB + 8] = colpack(f(conv_ln_b)[0])
    vecs[:, V_BPW2:V_BPW2 + 8] = colpack(f(conv_b_pw2)[0])
    vecs[:, V_HLB:V_HLB + 32] = colpack(f(hgrn_lb).reshape(-1))
    vecs[:, V_GNG:V_GNG + 8] = colpack(f(hgrn_gn_g)[0])
    x = f(x)
    allw = {"conv_w_pw1": f(conv_w_pw1)[0], "conv_w_pw2": f(conv_w_pw2)[0], "hgrn_w_in": f(hgrn_w_in)[0], "hgrn_w_out": f(hgrn_w_out)[0]}
    for l in range(2):
        for i in range(2):
            allw["w13_%d%d" % (l, i)] = f(np.asarray(ffn_w13)[l, i])
            allw["w2_%d%d" % (l, i)] = f(np.asarray(ffn_w2)[l, i])
    shared = {"vecs": vecs}
    for k in needed_weights(upto):
        shared[k] = allw[k]
    in_maps = []
    for c in range(NCORES):
        b, pos = c // 4, c % 4
        t0 = pos * NT
        xh = np.zeros((2 * HALO, D), np.float32)
        cvec = np.zeros((128, NCV), np.float32)
        if pos > 0:
            xh[:HALO] = x[b, t0 - HALO:t0]
            cvec[:, C_HM:C_HM + HALO] = 1.0
        if pos < 3:
            xh[HALO:] = x[b, t0 + NT:t0 + NT + HALO]
            cvec[:, C_HM + HALO:C_HM + 2 * HALO] = 1.0
        for r in range(NCORES):
            if r // 4 == b and r < c:
                cvec[:, C_RM + r] = 1.0
            if r // 4 == b and r > c:
                cvec[:, C_RM + 8 + r] = 1.0
        cvec[:, C_RM1:C_RM1 + 16] = 1.0 - cvec[:, C_RM:C_RM + 16]
        m = dict(shared)
        m["x"] = np.ascontiguousarray(x[b, t0:t0 + NT])
        m["xh"] = xh
        m["cvec"] = cvec
        in_maps.append(m)
    return in_maps


_NC_CACHE = {}
FUSED = True


def run(inputs, upto=99):
    in_maps = make_inputs(upto, **inputs)
    if FUSED or upto < 5:
        if upto not in _NC_CACHE:
            _NC_CACHE[upto] = build(upto)
        res = run_bass_kernel_spmd(_NC_CACHE[upto], in_maps, core_ids=list(range(NCORES)))
    else:
        ncA = build(upto, "produce")
        resA = run_bass_kernel_spmd(ncA, in_maps, core_ids=list(range(NCORES)))
        allcc = np.concatenate([resA.results[c]["cc_in"] for c in range(NCORES)], axis=0)
        for m in in_maps:
            m["cc_out"] = allcc
        ncB = build(upto, "consume")
        res = run_bass_kernel_spmd(ncB, in_maps, core_ids=list(range(NCORES)))
    out = np.zeros((2, 4 * NT, D), np.float32)
    for c in range(NCORES):
        out[c // 4, (c % 4) * NT:(c % 4 + 1) * NT] = res.results[c]["out"]
    return out


def kernel(**inputs):
    return run(inputs, 99)
```

```python
import contextlib
import numpy as np
import concourse.bass as bass
import concourse.mybir as mybir
from concourse.bass_utils import run_bass_kernel_spmd

F32 = mybir.dt.float32
BF16 = mybir.dt.bfloat16
AF = mybir.ActivationFunctionType
ALU = mybir.AluOpType

D = 1024
FF = 2816
NT = 4096
TT = 256
NTILE = NT // TT
HALO = 16
EPS = 1e-6
NBLK = 66
NCORES = 8

V_NG = 0
V_FG = 48
V_BPW1 = 56
V_WDW = 72
V_BDW = 320
V_LNG = 328
V_LNB = 336
V_BPW2 = 344
V_HLB = 352
V_GNG = 384
NV = 392
C_HM = 0
C_RM = 32
C_RM1 = 48
NCV = 64


class Buf:
    __slots__ = ("name", "w", "r", "dsem", "dcnt", "excl")

    def __init__(self, name, excl=False):
        self.name = name
        self.excl = excl
        self.w = None
        self.r = {}
        self.dsem = None
        self.dcnt = 0


class Sched:
    ENGS = ("pe", "act", "dve", "pool", "sp")
    ROLL = 30000

    def __init__(self, nc):
        self.nc = nc
        self.prog = {e: [] for e in self.ENGS}
        self.sem = {}
        self.cnt = {e: 0 for e in self.ENGS}
        self.seen = {e: {} for e in self.ENGS}
        self.nsem = 0
        self.pool_pending = []
        self.pe_fence = None
        self.sp_out = {}
        for e in self.ENGS:
            self.sem[e] = self.new_sem("eng_" + e)

    def new_sem(self, name):
        self.nsem += 1
        return self.nc.alloc_semaphore("%s_%d" % (name, self.nsem))

    def _waits(self, eng, reads, writes, is_dma=False):
        own = self.sem[eng]
        need = {}

        def add(tok, kind):
            sem, val = tok
            if sem is own and not is_dma:
                if eng == "pe" or kind != "raw":
                    return
            k = id(sem)
            if self.seen[eng].get(k, 0) >= val:
                return
            if k not in need or need[k][1] < val:
                need[k] = (sem, val)

        for b in reads:
            if b.w is not None:
                add(b.w, "raw")
        for b in writes:
            if b.w is not None:
                add(b.w, "waw")
            for t in b.r.values():
                add(t, "war")
        out = []
        for k, (sem, val) in need.items():
            self.seen[eng][k] = val
            out.append((sem, val))
        return out

    def _mark(self, tok, reads, writes):
        k = id(tok[0])
        for b in reads:
            o = b.r.get(k)
            if o is None or o[1] < tok[1]:
                b.r[k] = tok
        for b in writes:
            b.w = tok
            b.r = {}

    def op(self, eng, fn, reads=(), writes=(), sig=True, fence=False):
        ex = [b for b in reads if b.excl]
        if ex:
            reads = [b for b in reads if not b.excl]
            writes = list(writes) + ex
        waits = self._waits(eng, reads, writes)
        if eng == "pool" and self.pool_pending:
            for sem, val in self.pool_pending:
                if self.seen[eng].get(id(sem), 0) < val:
                    self.seen[eng][id(sem)] = val
                    waits.append((sem, val))
            self.pool_pending = []
        if eng == "pe":
            if fence and self.cnt[eng] > 0 and self.seen[eng].get(id(self.sem[eng]), 0) < self.cnt[eng]:
                self.seen[eng][id(self.sem[eng])] = self.cnt[eng]
                waits.append((self.sem[eng], self.cnt[eng]))
            if self.pe_fence is not None:
                fs, fv = self.pe_fence
                if self.seen[eng].get(id(fs), 0) < fv:
                    self.seen[eng][id(fs)] = fv
                    waits.append((fs, fv))
                self.pe_fence = None
        tok = (self.sem[eng], self.cnt[eng] + 1)
        self._mark(tok, reads, writes)
        if eng == "pe" and fence:
            assert sig
            self.pe_fence = tok
        inc = None
        if sig:
            self.cnt[eng] += 1
            inc = (self.sem[eng], 1)
        self.prog[eng].append((waits, fn, inc))
        if sig and self.cnt[eng] >= self.ROLL:
            self.sem[eng] = self.new_sem("eng_" + eng)
            self.cnt[eng] = 0

    def dma(self, q, out, in_, reads=(), writes=(), prim=None):
        waits = self._waits(q, reads, writes, is_dma=True)
        if prim is None:
            prim = writes[0] if writes else reads[0]
        if prim.dsem is None:
            prim.dsem = self.new_sem("d")
        if prim.dcnt >= self.ROLL:
            prim.dsem = self.new_sem("d")
            prim.dcnt = 0
        prim.dcnt += 16
        tok = (prim.dsem, prim.dcnt)
        self._mark(tok, reads, writes)
        if q == "sp":
            self.sp_out[id(prim.dsem)] = tok
        self.prog[q].append((waits, lambda e, o=out, i=in_: e.dma_start(out=o, in_=i), (prim.dsem, 16)))
        return tok

    def barrier(self):
        toks = [(self.sem[e], self.cnt[e]) for e in ("pe", "act", "dve", "pool") if self.cnt[e] > 0]
        toks += list(self.sp_out.values())
        self.sp_out = {}
        for e in ("pe", "act", "dve", "sp"):
            w = []
            for sem, val in toks:
                if sem is self.sem.get(e):
                    continue
                if self.seen[e].get(id(sem), 0) < val:
                    self.seen[e][id(sem)] = val
                    w.append((sem, val))
            if w:
                self.prog[e].append((w, None, None))
        self.pool_pending = [t for t in toks if t[0] is not self.sem["pool"]]

    def wait_tok(self, eng, tok):
        self.prog[eng].append(([tok], None, None))

    def emit(self, block):
        m = {"pe": "tensor", "act": "scalar", "dve": "vector", "pool": "gpsimd", "sp": "sync"}
        for e in self.ENGS:
            items = self.prog[e]

            def body(engine, items=items):
                for waits, fn, inc in items:
                    for sem, val in waits:
                        engine.wait_ge(sem, val)
                    if fn is not None:
                        ins = fn(engine)
                        if inc is not None:
                            ins.then_inc(inc[0], inc[1])

            getattr(block, m[e])(body)


def build(upto=99, cc_mode="fused"):
    nc = bass.Bass("TRN2", target_bir_lowering=False)

    def dram(name, shape, dt=F32, kind="Internal"):
        return nc.dram_tensor(name, list(shape), dt, kind=kind).ap()

    x_in = dram("x", [NT, D], kind="ExternalInput")
    xh_in = dram("xh", [2 * HALO, D], kind="ExternalInput")
    vecs_in = dram("vecs", [128, NV], kind="ExternalInput")
    cvec_in = dram("cvec", [128, NCV], kind="ExternalInput")
    need = needed_weights(upto)
    w13_d = {k: dram("w13_%d%d" % k, [D, 2 * FF], kind="ExternalInput") for k in [(0, 0), (0, 1), (1, 0), (1, 1)] if ("w13_%d%d" % k) in need}
    w2_d = {k: dram("w2_%d%d" % k, [FF, D], kind="ExternalInput") for k in [(0, 0), (0, 1), (1, 0), (1, 1)] if ("w2_%d%d" % k) in need}
    if "conv_w_pw1" in need:
        wpw1_in = dram("conv_w_pw1", [D, 2 * D], kind="ExternalInput")
        wpw2_in = dram("conv_w_pw2", [D, D], kind="ExternalInput")
    if "hgrn_w_in" in need:
        win_in = dram("hgrn_w_in", [D, 5 * D], kind="ExternalInput")
        wout_in = dram("hgrn_w_out", [D, D], kind="ExternalInput")
    out_d = dram("out", [NT, D], kind="ExternalOutput")

    xsA = dram("xsA", [D, NT])
    xsB = dram("xsB", [D, NT])
    xhA = dram("xhA", [D, 2 * HALO])
    xhB = dram("xhB", [D, 2 * HALO])
    ub_d = dram("ub_d", [8, NTILE, 128, 128])
    sb_d = dram("sb_d", [8, NTILE, 128, 128])
    cc_in = dram("cc_in", [17 * 128, 128], kind=("ExternalOutput" if cc_mode == "produce" else "Internal"))
    cc_out = dram("cc_out", [NCORES * 17 * 128, 128], kind=("ExternalInput" if cc_mode == "consume" else "Internal"))

    _uc = {"n": 0}

    def U(name):
        _uc["n"] += 1
        return "%s_u%d" % (name, _uc["n"])

    es = contextlib.ExitStack()
    with es:
        S = Sched(nc)

        def sb(name, shape, dt=F32):
            return es.enter_context(nc.sbuf_tensor("s_" + name, list(shape), dt))

        wa = sb("wa", [128, NBLK * 1024], BF16)
        WB = [Buf("wa%d" % i) for i in range(NBLK)]
        vecs = sb("vecs", [128, NV]); Bvecs = Buf("vecs")
        cvec = sb("cvec", [128, NCV]); Bcvec = Buf("cvec")
        identf = sb("identf", [128, 128]); Bidf = Buf("identf")
        identb = sb("identb", [128, 128], BF16); Bidb = Buf("identb")
        ones_d = sb("ones_d", [128, 128], BF16); Bones = Buf("ones")
        ones_h = sb("ones_h", [128, 128], BF16)
        Mf = sb("Mf", [128, 128]); Mb = sb("Mb", [128, 128]); BM = Buf("M")
        rmask = sb("rmask", [128, TT]); ones_t = sb("ones_t", [128, TT]); Brm = Buf("rmask")
        lbv = sb("lbv", [128, 2, 8]); oml = sb("oml", [128, 2, 8]); noml = sb("noml", [128, 2, 8]); Blb = Buf("lb")
        Dall = sb("Dall", [128, 8, NTILE]); BDall = Buf("Dall")
        xts = [sb("xt%d" % i, [128, 8, TT]) for i in range(2)]; Bxt = [Buf("xt%d" % i) for i in range(2)]
        hbuf = [sb("h%d" % i, [128, 8, TT], BF16) for i in range(2)]; Bh = [Buf("h%d" % i) for i in range(2)]
        sqb = sb("sqb", [128, 8, TT], BF16); Bsq = Buf("sq")
        rstd = [sb("rstd%d" % i, [128, TT]) for i in range(2)]; Brstd = [Buf("rstd%d" % i) for i in range(2)]

        pbank = [es.enter_context(nc.psum_tensor("pb%d" % i, [128, 512], F32)) for i in range(8)]
        ph = []
        for i in range(8):
            ph.append(pbank[i][:, 0:256])
            ph.append(pbank[i][:, 256:512])
        _PBK = [Buf("pbank%d" % i, excl=True) for i in range(8)]
        PB = [_PBK[i // 2] for i in range(16)]

        def mm(out, lhsT, rhs, start, stop, reads, writes, sig, sgc=False, fence=False):
            if fence:
                S.op("pe", lambda e: e.matmul(out, lhsT, rhs, start=start, stop=stop), reads=reads, writes=writes, sig=sig, fence=True)
            elif sgc:
                S.op("pe", lambda e: e.matmul(out, lhsT, rhs, start=start, stop=stop, skip_group_check=True), reads=reads, writes=writes, sig=sig)
            else:
                S.op("pe", lambda e: e.matmul(out, lhsT, rhs, start=start, stop=stop), reads=reads, writes=writes, sig=sig)

        def act(out, in_, func, reads, writes, bias=None, scale=None):
            kw = {}
            if bias is not None:
                kw["bias"] = bias
            if scale is not None:
                kw["scale"] = scale
            S.op("act", lambda e: e.activation(out=out, in_=in_, func=func, **kw), reads=reads, writes=writes)

        def tt(out, in0, in1, op, reads, writes, eng="dve"):
            S.op(eng, lambda e: e.tensor_tensor(out=out, in0=in0, in1=in1, op=op), reads=reads, writes=writes)

        def ts(out, in0, s1, s2, op0, op1, reads, writes, eng="dve"):
            if s2 is None:
                S.op(eng, lambda e: e.tensor_scalar(out=out, in0=in0, scalar1=s1, scalar2=None, op0=op0), reads=reads, writes=writes)
            else:
                S.op(eng, lambda e: e.tensor_scalar(out=out, in0=in0, scalar1=s1, scalar2=s2, op0=op0, op1=op1), reads=reads, writes=writes)

        def stt(out, in0, scalar, in1, op0, op1, reads, writes, eng="dve"):
            S.op(eng, lambda e: e.scalar_tensor_tensor(out=out, in0=in0, scalar=scalar, in1=in1, op0=op0, op1=op1), reads=reads, writes=writes)

        def cp(out, in_, reads, writes, eng="dve"):
            if eng == "act":
                act(out, in_, AF.Copy, reads, writes)
            else:
                S.op(eng, lambda e: e.tensor_copy(out, in_), reads=reads, writes=writes)

        ring = {"p": 0}

        def walloc(n):
            if ring["p"] + n > NBLK:
                ring["p"] = 0
            s = ring["p"]
            ring["p"] += n
            return s, WB[s:s + n]

        def wstage(nblocks):
            if ring["p"] + nblocks > NBLK:
                ring["p"] = 0

        wsemB = [Buf("wsem%d" % i) for i in range(16)]
        wcnt = {"k": 0}

        def wprim():
            b = wsemB[wcnt["k"] % 16]
            wcnt["k"] += 1
            return b

        def wload_cols(src2d, col0, ncols):
            nb = (8 * ncols + 1023) // 1024
            s, bufs = walloc(nb)
            view = wa[:, s * 1024: s * 1024 + 8 * ncols].rearrange("p (c n) -> p c n", c=8)
            pb_ = wprim()
            S.dma("pool", view, src2d[:, col0:col0 + ncols].rearrange("(c p) n -> p c n", p=128), writes=bufs + [pb_], prim=pb_)
            return view, bufs

        def wload_rows(src2d, row0, nrows):
            f = nrows // 128
            s, bufs = walloc(f)
            view = wa[:, s * 1024: (s + f) * 1024].rearrange("p (f n) -> p f n", f=f)
            pb_ = wprim()
            S.dma("pool", view, src2d[row0:row0 + nrows, :].rearrange("(f p) n -> p f n", p=128), writes=bufs + [pb_], prim=pb_)
            return view, bufs

        S.dma("sp", vecs[:], vecs_in, writes=[Bvecs])
        S.dma("sp", cvec[:], cvec_in, writes=[Bcvec])
        S.op("pool", lambda e: e.memset(identf[:], 0.0), writes=[Bidf])
        S.op("pool", lambda e: e.affine_select(out=identf[:], in_=identf[:], pattern=[[-1, 128]], compare_op=ALU.not_equal,
                                               fill=1.0, base=0, channel_multiplier=1), reads=[Bidf], writes=[Bidf])
        cp(identb[:], identf[:], [Bidf], [Bidb], eng="pool")
        S.op("pool", lambda e: e.memset(ones_d[:], 1.0 / 1024.0), writes=[Bones])
        S.op("pool", lambda e: e.memset(ones_h[:], 1.0 / 128.0), writes=[Bones])
        S.op("pool", lambda e: e.memset(Mf[:], 1.0), writes=[BM])
        S.op("pool", lambda e: e.affine_select(out=Mf[:], in_=Mf[:], pattern=[[1, 128]], compare_op=ALU.is_ge,
                                               fill=0.0, base=0, channel_multiplier=-1), reads=[BM], writes=[BM])
        S.op("pool", lambda e: e.memset(Mf[0:64, 64:128], 0.0), reads=[BM], writes=[BM])
        S.op("pool", lambda e: e.memset(Mb[:], 1.0), reads=[BM], writes=[BM])
        S.op("pool", lambda e: e.affine_select(out=Mb[:], in_=Mb[:], pattern=[[-1, 128]], compare_op=ALU.is_ge,
                                               fill=0.0, base=0, channel_multiplier=1), reads=[BM], writes=[BM])
        S.op("pool", lambda e: e.memset(Mb[64:128, 0:64], 0.0), reads=[BM], writes=[BM])
        S.op("pool", lambda e: e.memset(rmask[:], 1.0), writes=[Brm])
        S.op("pool", lambda e: e.memset(rmask[:].rearrange("p (c j) -> p c j", j=64)[:, :, 0:1], 0.0), reads=[Brm], writes=[Brm])
        S.op("pool", lambda e: e.memset(ones_t[:], 1.0), reads=[Brm], writes=[Brm])
        epsc = sb("epsc", [128, 1]); Beps = Buf("epsc")
        S.op("pool", lambda e: e.memset(epsc[:], EPS), writes=[Beps])
        S.barrier()
        hl = vecs[:, V_HLB:V_HLB + 32].rearrange("p (d l h) -> p d l h", d=2, l=2)
        tt(lbv[:], hl[:, :, 1, :], hl[:, :, 0, :], ALU.subtract, [Bvecs], [Blb])
        act(lbv[:], lbv[:], AF.Sigmoid, [Blb], [Blb])
        ts(oml[:], lbv[:], -1.0, 1.0, ALU.mult, ALU.add, [Blb], [Blb])
        ts(noml[:], oml[:], -1.0, None, ALU.mult, None, [Blb], [Blb])
        homl = sb("homl", [128, 2, 8]); c1v = sb("c1v", [128, 2, 8])
        ts(homl[:], oml[:], 0.5, None, ALU.mult, None, [Blb], [Blb])
        tt(c1v[:], homl[:], lbv[:], ALU.add, [Blb], [Blb])

        def rsqrt_eps(out, in_, reads, Bout):
            act(out, in_, AF.Sqrt, list(reads) + [Beps], [Bout], bias=epsc[:, 0:1])
            S.op("dve", lambda e: e.reciprocal(out, out), reads=[Bout], writes=[Bout])

        nstat = {"i": 0}

        def rmsnorm(xt, Bx, N, gcol, h, Bhh):
            i = nstat["i"] % 2
            nstat["i"] += 1
            st, Bst = ph[14 + i][:, :N], PB[14 + i]
            r, Br = rstd[i][:, :N], Brstd[i]
            tt(sqb[:, :, :N], xt[:, :, :N], xt[:, :, :N], ALU.mult, [Bx], [Bsq])
            for c in range(8):
                mm(st, ones_d[:], sqb[:, c, :N], c == 0, c == 7, [Bones, Bsq], [Bst], c == 7)
            rsqrt_eps(r, st, [Bst], Br)
            for c in range(8):
                stt(h[:, c, :N], xt[:, c, :N], vecs[:, gcol + c:gcol + c + 1], r, ALU.mult, ALU.mult, [Bx, Br, Bvecs], [Bhh])

        xcnt = {"i": 0}

        def next_xt():
            i = xcnt["i"] % 2
            xcnt["i"] += 1
            return xts[i], Bxt[i], hbuf[i], Bh[i]

        def dview(dr, off, N):
            return dr[:, off:off + N].rearrange("(c p) n -> p c n", p=128)

        Bx = {}

        def dbuf(name, t):
            k = (name, 0)
            if k not in Bx:
                Bx[k] = Buf("%s_%s" % (name, t))
            return Bx[k]

        def stage_in():
            with nc.sbuf_tensor(U("xin"), [128, 2, D], F32) as xin:
                Bxin = Buf("xin")
                jobs = [(x_in, t * TT, TT, xsA, t * TT, ("A", t)) for t in range(NTILE)] + [(xh_in, 0, 2 * HALO, xhA, 0, ("hA", 0))]
                for src, soff, N, dst, doff, key in jobs:
                    nb = (N + 127) // 128
                    rows = min(N, 128)
                    S.dma("sp", xin[:rows, :nb, :], src[soff:soff + N, :].rearrange("(b p) d -> p b d", p=rows), writes=[Bxin])
                    xt, Bxx, _, _ = next_xt()
                    for c2 in range(4):
                        for ci in range(2):
                            c = 2 * c2 + ci
                            for b in range(nb):
                                S.op("pe", lambda e, c=c, b=b, rows=rows: e.transpose(ph[2 + c][:, b * 128:b * 128 + rows], xin[:rows, b, c * 128:(c + 1) * 128], identf[:rows, :rows]),
                                     reads=[Bxin, Bidf], writes=[PB[2 + c]], sig=(b == nb - 1 and ci == 1))
                        cp(xt[:, 2 * c2:2 * c2 + 2, :N], pbank[1 + c2][:, :].rearrange("p (h n) -> p h n", h=2)[:, :, :N], [PB[2 + 2 * c2]], [Bxx], eng=("act" if c2 % 2 else "dve"))
                    S.dma("sp", dview(dst, doff, N), xt[:, :, :N], reads=[Bxx], writes=[dbuf(*key)])

        def stage_ffn(l, i, src, sname, dst, dname, halo=None):
            gcol = V_NG + (l * 3 + (0 if i == 0 else 2)) * 8
            S.barrier()
            w13 = w13_d[(l, i)]
            w2 = w2_d[(l, i)]
            units = {}
            wstage(66)

            def unit(g):
                if g not in units:
                    G = 4 if g < 5 else 2
                    gv, gb = wload_cols(w13, g * 512, G * 128)
                    uv, ub = wload_cols(w13, FF + g * 512, G * 128)
                    dv, db = wload_rows(w2, g * 512, G * 128)
                    units[g] = (gv, gb, uv, ub, dv, db)
                return units[g]

            with nc.sbuf_tensor(U("sg"), [128, 2, TT], F32) as sg, nc.sbuf_tensor(U("actb"), [128, 3, TT], BF16) as actb:
                Bsg = [Buf("sg0"), Buf("sg1")]
                Bact = [Buf("act%d" % k) for k in range(3)]
                jobs = [(src, sname, dst, dname, t, t * TT, TT) for t in range(NTILE)]
                if halo is not None:
                    jobs.append((halo[0], halo[1], halo[2], halo[3], 0, 0, 2 * HALO))
                def prep(job):
                    (s_d, s_n, d_d, d_n, t, off, N) = job
                    xt, Bxx, h, Bhh = next_xt()
                    S.dma("sp", xt[:, :, :N], dview(s_d, off, N), reads=[dbuf(s_n, t)], writes=[Bxx])
                    rmsnorm(xt, Bxx, N, gcol, h, Bhh)
                    return xt, Bxx, h, Bhh

                cur = prep(jobs[0])
                for idx, (s_d, s_n, d_d, d_n, t, off, N) in enumerate(jobs):
                    xt, Bxx, h, Bhh = cur
                    LAG = 1

                    def ymm(j):
                        g, jj = j // 4, j % 4
                        dv, db = unit(g)[4], unit(g)[5]
                        a = j % 3
                        for dc in range(8):
                            mm(ph[dc][:, :N], dv[:, jj, dc * 128:(dc + 1) * 128], actb[:, a, :N], (j == 0 and dc % 2 == 0), j == 21,
                               [Bact[a]] + db, [PB[dc]], (dc == 7) or (j == 21), sgc=True)

                    for j in range(22 + LAG):
                        if j == 8 and idx + 1 < len(jobs):
                            cur = prep(jobs[idx + 1])
                        if j < 22:
                            g, jj = j // 4, j % 4
                            gv, gb, uv, ub = unit(g)[:4]
                            s3 = j % 3
                            pg, pu = ph[8 + 2 * s3][:, :N], ph[9 + 2 * s3][:, :N]
                            Bpg, Bpu = PB[8 + 2 * s3], PB[9 + 2 * s3]
                            for c in range(8):
                                mm(pg, gv[:, c, jj * 128:(jj + 1) * 128], h[:, c, :N], c == 0, c == 7, [Bhh] + gb, [Bpg], c == 7)
                            for c in range(8):
                                mm(pu, uv[:, c, jj * 128:(jj + 1) * 128], h[:, c, :N], c == 0, c == 7, [Bhh] + ub, [Bpu], c == 7)
                            act(sg[:, j % 2, :N], pg, AF.Silu, [Bpg], [Bsg[j % 2]])
                            tt(actb[:, j % 3, :N], sg[:, j % 2, :N], pu, ALU.mult, [Bsg[j % 2], Bpu], [Bact[j % 3]])
                        if j >= LAG:
                            ymm(j - LAG)
                    for dc in range(8):
                        stt(xt[:, dc, :N], ph[dc][:, :N], 0.5, xt[:, dc, :N], ALU.mult, ALU.add, [PB[dc], Bxx], [Bxx])
                    S.dma("sp", dview(d_d, off, N), xt[:, :, :N], reads=[Bxx], writes=[dbuf(d_n, t)])

        def stage_conv(src, sname, hsrc, hname, dst, dname):
            gcol = V_NG + 1 * 8
            S.barrier()
            W = TT + 32
            with nc.sbuf_tensor(U("ubuf"), [128, 8, W], BF16) as ubuf, \
                    nc.sbuf_tensor(U("sgc"), [128, 2, TT], F32) as sgc, \
                    nc.sbuf_tensor(U("vbuf"), [128, 8, TT], F32) as vbuf, \
                    nc.sbuf_tensor(U("v16"), [128, 8, TT], BF16) as v16, \
                    nc.sbuf_tensor(U("sil"), [128, 8, TT], BF16) as sil, \
                    nc.sbuf_tensor(U("lnst"), [128, 4, TT], F32) as lnst, \
                    nc.sbuf_tensor(U("xw"), [128, 8, TT], F32) as xw:
                Bub, Bsgc = Buf("ubuf"), [Buf("sgc0"), Buf("sgc1")]
                Bvb, Bv16, Bq16, Bsil, Bln, Bxw = Buf("vbuf"), Buf("v16"), Bsq, Buf("sil"), Buf("lnst"), Buf("xw")
                q16 = sqb
                wstage(55)
                pw1 = [wload_cols(wpw1_in, g * 512, 512) for g in range(4)]
                pw2 = [wload_cols(wpw2_in, g * 512, 512) for g in range(2)]
                ds_, dbufs = walloc(31)
                diag = wa[:, ds_ * 1024:(ds_ + 31) * 1024]
                for c in range(8):
                    for j in range(31):
                        m = c * 31 + j
                        ts(diag[:, m * 128:(m + 1) * 128], identb[:], vecs[:, V_WDW + m:V_WDW + m + 1], None, ALU.mult, None,
                           [Bidb, Bvecs], dbufs)
                S.op("dve", lambda e: e.memset(ubuf[:], 0.0), writes=[Bub])

                def make_u(s_d, s_n, t, off, N, ucol, hmask_col=None):
                    xt, Bxx, h, Bhh = next_xt()
                    S.dma("sp", xt[:, :, :N], dview(s_d, off, N), reads=[dbuf(s_n, t)], writes=[Bxx])
                    rmsnorm(xt, Bxx, N, gcol, h, Bhh)
                    for j in range(8):
                        s3 = j % 2
                        pa, pg = ph[2 + 2 * s3][:, :N], ph[3 + 2 * s3][:, :N]
                        Bpa, Bpg = PB[2 + 2 * s3], PB[3 + 2 * s3]
                        av, ab = pw1[j // 4]
                        gv, gb = pw1[2 + j // 4]
                        jj = j % 4
                        for c in range(8):
                            mm(pa, av[:, c, jj * 128:(jj + 1) * 128], h[:, c, :N], c == 0, c == 7, [Bhh] + ab, [Bpa], c == 7)
                        for c in range(8):
                            mm(pg, gv[:, c, jj * 128:(jj + 1) * 128], h[:, c, :N], c == 0, c == 7, [Bhh] + gb, [Bpg], c == 7)
                        act(sgc[:, j % 2, :N], pg, AF.Sigmoid, [Bpg, Bvecs], [Bsgc[j % 2]], bias=vecs[:, V_BPW1 + 8 + j:V_BPW1 + 9 + j])
                        stt(ubuf[:, j, ucol:ucol + N], pa, vecs[:, V_BPW1 + j:V_BPW1 + j + 1], sgc[:, j % 2, :N], ALU.add, ALU.mult,
                            [Bpa, Bsgc[j % 2], Bvecs], [Bub])
                    if hmask_col is not None:
                        for j in range(8):
                            tt(ubuf[:, j, ucol:ucol + N], ubuf[:, j, ucol:ucol + N], cvec[:, hmask_col:hmask_col + N], ALU.mult, [Bub, Bcvec], [Bub])

                def window1(tok0, Wn, col0):
                    S.dma("sp", xw[:, :, :Wn], dview(src, tok0, Wn), reads=[dbuf(sname, t) for t in range(NTILE)], writes=[Bxw])
                    for c in range(8):
                        pv, Bpv = ph[6 + 2 * (c % 2)][:, :Wn], PB[6 + 2 * (c % 2)]
                        for j in range(31):
                            m = c * 31 + j
                            mm(pv, diag[:, m * 128:(m + 1) * 128], ubuf[:, c, col0 - 15 + j: col0 - 15 + j + Wn], j == 0, j == 30,
                               [Bub] + dbufs, [Bpv], j == 30)
                        act(vbuf[:, c, :Wn], pv, AF.Identity, [Bpv, Bvecs], [Bvb], bias=vecs[:, V_BDW + c:V_BDW + c + 1])

                def window2(tok0, Wn, col0):
                    cp(v16[:, :, :Wn], vbuf[:, :, :Wn], [Bvb], [Bv16])
                    tt(q16[:, :, :Wn], vbuf[:, :, :Wn], vbuf[:, :, :Wn], ALU.mult, [Bvb], [Bq16])
                    p1, p2 = ph[10][:, :Wn], ph[11][:, :Wn]
                    for c in range(8):
                        mm(p1, ones_d[:], v16[:, c, :Wn], c == 0, c == 7, [Bones, Bv16], [PB[10]], c == 7)
                    for c in range(8):
                        mm(p2, ones_d[:], q16[:, c, :Wn], c == 0, c == 7, [Bones, Bq16], [PB[11]], c == 7)
                    mean, msq, var, rs = lnst[:, 0, :Wn], lnst[:, 1, :Wn], lnst[:, 2, :Wn], lnst[:, 3, :Wn]
                    cp(mean, p1, [PB[10]], [Bln])
                    tt(msq, mean, mean, ALU.mult, [Bln], [Bln])
                    tt(var, p2, msq, ALU.subtract, [PB[11], Bln], [Bln])
                    rsqrt_eps(rs, var, [Bln], Bln)
                    for c in range(8):
                        tt(vbuf[:, c, :Wn], vbuf[:, c, :Wn], mean, ALU.subtract, [Bvb, Bln], [Bvb])
                        tt(vbuf[:, c, :Wn], vbuf[:, c, :Wn], rs, ALU.mult, [Bvb, Bln], [Bvb])
                        act(sil[:, c, :Wn], vbuf[:, c, :Wn], AF.Silu, [Bvb, Bvecs], [Bsil],
                            bias=vecs[:, V_LNB + c:V_LNB + c + 1], scale=vecs[:, V_LNG + c:V_LNG + c + 1])
                    for dc in range(8):
                        po, Bpo = (ph[12][:, :Wn], PB[12]) if dc % 2 == 0 else (ph[0][:, :Wn], PB[0])
                        wv, wb_ = pw2[dc // 4]
                        for c in range(8):
                            mm(po, wv[:, c, (dc % 4) * 128:(dc % 4 + 1) * 128], sil[:, c, :Wn], c == 0, c == 7, [Bsil] + wb_, [Bpo], c == 7)
                        stt(xw[:, dc, :Wn], po, vecs[:, V_BPW2 + dc:V_BPW2 + dc + 1], xw[:, dc, :Wn], ALU.add, ALU.add, [Bpo, Bxw, Bvecs], [Bxw])
                    S.dma("sp", dview(dst, tok0, Wn), xw[:, :, :Wn], reads=[Bxw], writes=[dbuf(dname, t) for t in range(NTILE)])

                make_u(hsrc, hname, 0, 0, HALO, 16, hmask_col=C_HM)
                make_u(src, sname, 0, 0, TT, 32)
                for t in range(NTILE):
                    wargs = (0, TT - 16, 32) if t == 0 else (t * TT - 16, TT, 16)
                    window1(*wargs)
                    cp(ubuf[:, :, 0:32], ubuf[:, :, TT:TT + 32], [Bub], [Bub], eng="act")
                    if t + 1 < NTILE:
                        make_u(src, sname, t + 1, (t + 1) * TT, TT, 32)
                    else:
                        make_u(hsrc, hname, 0, HALO, HALO, 32, hmask_col=C_HM + HALO)
                    window2(*wargs)
                window1(NT - 16, 16, 16)
                window2(NT - 16, 16, 16)

        def stage_hgrn(src, sname, dst, dname):
            gcol = V_NG + (1 * 3 + 1) * 8
            NCH = TT // 64
            NSB = TT // 128
            TAIL0 = 48
            S.barrier()

            def run_lanes(gens):
                gens = list(gens)
                while gens:
                    nxt = []
                    for g in gens:
                        try:
                            next(g)
                            nxt.append(g)
                        except StopIteration:
                            pass
                    gens = nxt

            with contextlib.ExitStack() as hs_:
                def hsb(name, shape, dt=F32):
                    return hs_.enter_context(nc.sbuf_tensor(U(name), list(shape), dt))
                wstage(48)
                S.op("dve", lambda e: e.memset(wa[:, TAIL0 * 1024:TAIL0 * 1024 + 16], 0.0), writes=WB[TAIL0:NBLK])
                tail_tok = WB[TAIL0].w
                lane_bufs = []

                def LB(name):
                    b = Buf(name)
                    b.w = tail_tok
                    lane_bufs.append(b)
                    return b
                tail = {"o": TAIL0 * 1024}

                def tail_f32(n):
                    o = tail["o"]
                    tail["o"] += 2 * n
                    assert tail["o"] <= NBLK * 1024
                    return wa[:, o:o + 2 * n].bitcast(F32)

                def tail_bf(n):
                    o = tail["o"]
                    tail["o"] += n
                    assert tail["o"] <= NBLK * 1024
                    return wa[:, o:o + n]
                gA = [tail_f32(TT) for _ in range(4)]; BgA = [LB("gA%d" % i) for i in range(4)]
                gB = [tail_f32(TT) for _ in range(4)]; BgB = [LB("gB%d" % i) for i in range(4)]
                gC = [tail_f32(TT) for _ in range(4)]; BgC = [LB("gC%d" % i) for i in range(4)]
                gD = [tail_f32(TT) for _ in range(4)]; BgD_ = [LB("gD%d" % i) for i in range(4)]
                Qt = [tail_bf(TT) for _ in range(8)]; BQt = [LB("Qt%d" % i) for i in range(8)]
                Kt = [tail_bf(TT) for _ in range(8)]; BKt = [LB("Kt%d" % i) for i in range(8)]
                Kh = [tail_bf(TT) for _ in range(8)]; BKh = [LB("Kh%d" % i) for i in range(8)]
                Khtm_ = [tail_bf(TT) for _ in range(8)]; BKhtm = [LB("Khtm%d" % i) for i in range(8)]
                Khtm = [k.rearrange("p (s n) -> p s n", s=NSB) for k in Khtm_]
                erT = hsb("erT", [128, 8, NCH]); Ber = [Buf("er%d" % i) for i in range(8)]
                qbT = hsb("qbT", [128, 4, TT]); Bqb = [Buf("qb%d" % i) for i in range(4)]
                gsT = hsb("gsT", [128, 4, TT]); Bgs = [Buf("gs%d" % i) for i in range(4)]
                osbT = tail_f32(2 * TT).rearrange("p (s n) -> p s n", s=2); Bosb = [LB("osb%d" % i) for i in range(2)]
                osqT = hsb("osqT", [128, 2, TT], BF16); Bosq = [Buf("osq%d" % i) for i in range(2)]
                Vtm = hsb("Vtm", [128, NSB, D], BF16); BV = Buf("Vtm")
                Pm = hsb("Pm", [128, 2, 2, 128], BF16); BPm = [Buf("Pm0"), Buf("Pm1")]
                Sst = hsb("Sst", [128, 2, 128]); BSst = [Buf("Sst0"), Buf("Sst1")]
                Sbf = hsb("Sbf", [128, 2, 2, NCH + 1, 128], BF16); BSbf = [Buf("Sbf0"), Buf("Sbf1")]; BSbf2 = [Buf("Sbf2"), Buf("Sbf3")]
                Sfr = hsb("Sfr", [128, 8, 128]); BSfr = [Buf("Sfr%d" % i) for i in range(8)]
                Sbt = hsb("Sbt", [128, 8, 128]); BSbt = [Buf("Sbt%d" % i) for i in range(8)]
                Cb = hsb("Cb", [128, 8]); BCb = Buf("Cb")
                Cf = hsb("Cf", [128, 8]); BCf = Buf("Cf")
                Usb = hsb("Usb", [128, 4, 128]); BUsb = [Buf("Usb%d" % i) for i in range(4)]
                on = hsb("on", [128, 8, TT], BF16); Bon = Buf("on")
                gU = hsb("gU", [128, 2, 128]); BgU = [Buf("gU0"), Buf("gU1")]
                gDD = hsb("gDD", [128, 128]); BgDD = Buf("gDD")
                rsl = tail_f32(2 * TT).rearrange("p (s n) -> p s n", s=2); Brsl = [LB("rsl0"), LB("rsl1")]
                hx = hsb("hx", [128, 2, TT]); Bhx = [Buf("hx0"), Buf("hx1")]

                win_u = {}

                def win(gi):
                    if gi not in win_u:
                        win_u[gi] = wload_cols(win_in, gi * 512, 512)
                    return win_u[gi]

                def rsqrt_le(out, in_, reads, Bout):
                    act(out, in_, AF.Ln, list(reads) + [Beps], [Bout], bias=epsc[:, 0:1])
                    act(out, out, AF.Exp, [Bout], [Bout], scale=-0.5)

                def load_norm(t):
                    xt, Bxx, h, Bhh = next_xt()
                    S.dma("sp", xt[:], dview(src, t * TT, TT), reads=[dbuf(sname, t)], writes=[Bxx])
                    i = nstat["i"] % 2
                    nstat["i"] += 1
                    st, Bst = ph[14 + i], PB[14 + i]
                    tt(sqb[:], xt[:], xt[:], ALU.mult, [Bxx], [Bsq])
                    for c in range(8):
                        mm(st, ones_d[:], sqb[:, c, :], c == 0, c == 7, [Bones, Bsq], [Bst], c == 7)
                    rsqrt_le(rstd[i][:], st, [Bst], Brstd[i])
                    for c in range(8):
                        stt(h[:, c, :], xt[:, c, :], vecs[:, gcol + c:gcol + c + 1], rstd[i][:], ALU.mult, ALU.mult, [Bxx, Brstd[i], Bvecs], [Bhh])
                    for sbk in range(NSB):
                        for cg in range(2):
                            wv, wb_ = win(2 + cg)
                            pv = pbank[1][:, :]
                            for c in range(8):
                                mm(pv, h[:, c, sbk * 128:(sbk + 1) * 128], wv[:, c, :], c == 0, c == 7, [Bhh] + wb_, [PB[2]], c == 7)
                            cp(Vtm[:, sbk, cg * 512:(cg + 1) * 512], pv, [PB[2]], [BV], eng=("act" if cg else "dve"))
                    return xt, Bxx, h, Bhh

                pslot = {"i": 0}

                def proj(sec, hd, h, Bhh):
                    s3 = pslot["i"] % 3
                    pslot["i"] += 1
                    pz, Bpz = ph[4 + 2 * s3], PB[4 + 2 * s3]
                    wv, wb_ = win(2 * sec + hd // 4)
                    for c in range(8):
                        mm(pz, wv[:, c, (hd % 4) * 128:(hd % 4 + 1) * 128], h[:, c, :], c == 0, c == 7, [Bhh] + wb_, [Bpz], c == 7)
                    return pz, Bpz

                def gates_gen(dr, hd, h, Bhh, li):
                    A, B_ = gA[li], gB[li]
                    pz, Bpz = proj(2 + dr, hd, h, Bhh)
                    act(A, pz, AF.Tanh, [Bpz], [BgA[li]], scale=0.5)
                    yield
                    ts(B_, A, homl[:, dr, hd:hd + 1], c1v[:, dr, hd:hd + 1], ALU.mult, ALU.add, [BgA[li], Blb], [BgB[li]])
                    yield
                    ts(A, B_, -1.0, 1.0, ALU.mult, ALU.add, [BgB[li]], [BgA[li]])
                    yield
                    act(B_, B_, AF.Ln, [BgB[li]], [BgB[li]])
                    yield

                def transp_gen(pi, li):
                    for sbk in range(NSB):
                        ptr = pbank[5][:, :].bitcast(BF16)[:, (li * NSB + sbk) * 128:(li * NSB + sbk + 1) * 128]
                        S.op("pe", lambda e, ptr=ptr, sbk=sbk: e.transpose(ptr, Kh[pi][:, sbk * 128:(sbk + 1) * 128], identb[:]),
                             reads=[BKh[pi], Bidb], writes=[PB[10]])
                        yield
                        cp(Khtm[pi][:, sbk, :], ptr, [PB[10]], [BKhtm[pi]], eng="act")
                        yield

                S.op("dve", lambda e: e.memset(Sfr[:], 0.0), writes=BSfr)
                S.op("dve", lambda e: e.memset(Sbt[:], 0.0), writes=BSbt)
                S.op("dve", lambda e: e.memset(Cb[:], 1.0), writes=[BCb])
                S.op("dve", lambda e: e.memset(Cf[:], 1.0), writes=[BCf])

                def laneA(t, hd, dr, li, pi, h, Bhh):
                    A, B_, C_, D_ = gA[li], gB[li], gC[li], gD[li]
                    yield from gates_gen(dr, hd, h, Bhh, li)
                    S.op("dve", lambda e: e.tensor_tensor_scan(out=C_, data0=ones_t[:], data1=B_, initial=0.0, op0=ALU.mult, op1=ALU.add),
                         reads=[BgB[li], Brm], writes=[BgC[li]])
                    yield
                    if dr == 0:
                        act(D_, C_, AF.Exp, [BgC[li]], [BgD_[li]], bias=C_[:, TT - 1:TT], scale=-1.0)
                        dcol, Bd = erT[:, pi, 0:1], Ber[pi]
                    else:
                        tt(D_, C_, B_, ALU.subtract, [BgC[li], BgB[li]], [BgD_[li]])
                        yield
                        act(D_, D_, AF.Exp, [BgD_[li]], [BgD_[li]])
                        dcol, Bd = Dall[:, hd, t:t + 1], BDall
                    yield
                    act(dcol, C_[:, TT - 1:TT], AF.Exp, [BgC[li]], [Bd])
                    yield
                    tt(Kh[pi], A, D_, ALU.mult, [BgA[li], BgD_[li]], [BKh[pi]])
                    yield
                    yield from transp_gen(pi, li)
                    pu, Bpu = ph[12 + li // 2][:, (li % 2) * 128:(li % 2 + 1) * 128], PB[12]
                    for sbk in range(NSB):
                        mm(pu, Khtm[pi][:, sbk, :], Vtm[:, sbk, hd * 128:(hd + 1) * 128], sbk == 0, sbk == NSB - 1, [BKhtm[pi], BV], [Bpu], sbk == NSB - 1)
                    yield
                    if dr == 0:
                        stt(Sfr[:, hd, :], Sfr[:, hd, :], erT[:, pi, 0:1], pu, ALU.mult, ALU.add, [BSfr[hd], Ber[pi], Bpu], [BSfr[hd]])
                        yield
                        tt(Cf[:, hd:hd + 1], Cf[:, hd:hd + 1], erT[:, pi, 0:1], ALU.mult, [BCf, Ber[pi]], [BCf])
                    else:
                        cp(Usb[:, li, :], pu, [Bpu], [BUsb[li]])
                        yield
                        S.dma("sp", ub_d[hd, t], Usb[:, li, :], reads=[BUsb[li]], writes=[dbuf("ub%d" % hd, 0)])
                        stt(Sbt[:, hd, :], pu, Cb[:, hd:hd + 1], Sbt[:, hd, :], ALU.mult, ALU.add, [Bpu, BCb, BSbt[hd]], [BSbt[hd]])
                        yield
                        tt(Cb[:, hd:hd + 1], Cb[:, hd:hd + 1], Dall[:, hd, t:t + 1], ALU.mult, [BCb, BDall], [BCb])
                    yield

                for t in range(NTILE):
                    xt, Bxx, h, Bhh = load_norm(t)
                    for gi in range(4):
                        lanes = []
                        for hh in range(2):
                            for dr in range(2):
                                li = hh * 2 + dr
                                lanes.append(laneA(t, 2 * gi + hh, dr, li, (gi % 2) * 4 + li, h, Bhh))
                        run_lanes(lanes)

                Bcc_in, Bcc_out = Buf("cc_in"), Buf("cc_out")
                ccv = cc_in.rearrange("(m p) v -> p m v", p=128)
                S.dma("sp", ccv[:, 0:8, :], Sfr[:], reads=BSfr, writes=[Bcc_in])
                S.dma("sp", ccv[:, 8:16, :], Sbt[:], reads=BSbt, writes=[Bcc_in])
                S.op("dve", lambda e: e.memset(gDD[:], 0.0), writes=[BgDD])
                cp(gDD[:, 0:8], Cf[:], [BCf, BgDD], [BgDD])
                cp(gDD[:, 8:16], Cb[:], [BCb, BgDD], [BgDD])
                S.dma("sp", ccv[:, 16, :], gDD[:], reads=[BgDD], writes=[Bcc_in])
                if cc_mode == "produce":
                    S.wait_tok("sp", Bcc_in.w)
                    return
                waits = S._waits("pool", [Bcc_in], [Bcc_out], is_dma=True)
                csem = S.new_sem("cc")
                tokc = (csem, 1)
                S._mark(tokc, [Bcc_in], [Bcc_out])
                if cc_mode == "fused":
                    S.prog["pool"].append((waits, lambda e: e.collective_compute("AllGather", ALU.bypass, replica_groups=[list(range(NCORES))],
                                                                                 ins=[cc_in], outs=[cc_out]), (csem, 1)))
                else:
                    Bcc_out.w = None
                S.op("dve", lambda e: e.memset(Sfr[:], 0.0), reads=BSfr, writes=BSfr)
                S.op("dve", lambda e: e.memset(Sbt[:], 0.0), reads=BSbt, writes=BSbt)
                fold_buf = [wa[:, TAIL0 * 1024 + k * 8 * TT: TAIL0 * 1024 + (k + 1) * 8 * TT].bitcast(F32).rearrange("p (m v) -> p m v", m=8) for k in range(2)]
                fold_B = [BgA, BgB]
                gD2 = [gDD[:, 0:32], gDD[:, 32:64]]
                BgD2 = [Buf("gD2a"), Buf("gD2b")]
                kk_ = 0
                for dr, order in ((0, range(NCORES)), (1, range(NCORES - 1, -1, -1))):
                    Sx, BSx = (Sfr, BSfr) if dr == 0 else (Sbt, BSbt)
                    for r in order:
                        base = r * 17 * 128
                        fb, fB, gd, Bgd = fold_buf[kk_ % 2], fold_B[kk_ % 2], gD2[kk_ % 2], BgD2[kk_ % 2]
                        kk_ += 1
                        S.dma("sp", gd[:, 0:16], cc_out[base + 2048: base + 2176, 0:16], reads=[Bcc_out], writes=[Bgd])
                        S.dma("sp", fb, cc_out[base + dr * 1024: base + dr * 1024 + 1024, :].rearrange("(m p) v -> p m v", p=128),
                              reads=[Bcc_out], writes=list(fB))
                        mcol = cvec[:, C_RM + dr * 8 + r: C_RM + dr * 8 + r + 1]
                        omcol = cvec[:, C_RM1 + dr * 8 + r: C_RM1 + dr * 8 + r + 1]
                        ts(gd[:, 16:24], gd[:, dr * 8: dr * 8 + 8], mcol, omcol, ALU.mult, ALU.add, [Bgd, Bcvec], [Bgd])
                        for hd in range(8):
                            ts(fb[:, hd, :], fb[:, hd, :], mcol, None, ALU.mult, None, list(fB) + [Bcvec], list(fB))
                            stt(Sx[:, hd, :], Sx[:, hd, :], gd[:, 16 + hd:17 + hd], fb[:, hd, :], ALU.mult, ALU.add, [BSx[hd], Bgd] + list(fB), [BSx[hd]])
                P1 = wa[:, TAIL0 * 1024 + 2 * 8 * TT: TAIL0 * 1024 + 3 * 8 * TT].bitcast(F32).rearrange("p (m v) -> p m v", m=8)
                Pst = [Sbt, P1]
                BPst = [BSbt, BgC]
                sbv = sb_d.rearrange("h t p v -> t p h v")
                ubv = ub_d.rearrange("h t p v -> t p h v")
                Bsbd = dbuf("sb0", 0)
                Bubd = [dbuf("ub%d" % hd, 0) for hd in range(8)]
                cur = 0
                S.dma("sp", sbv[NTILE - 1], Pst[cur][:], reads=list(BPst[cur]), writes=[Bsbd])
                for t in range(NTILE - 2, -1, -1):
                    k2 = t % 2
                    fb, fB = fold_buf[k2], fold_B[k2]
                    S.dma("sp", fb, ubv[t + 1], reads=Bubd, writes=list(fB))
                    nxt = 1 - cur
                    for hd in range(8):
                        Bn = BPst[nxt][hd] if nxt == 0 else BPst[nxt][hd % 4]
                        Bc = BPst[cur][hd] if cur == 0 else BPst[cur][hd % 4]
                        stt(Pst[nxt][:, hd, :], Pst[cur][:, hd, :], Dall[:, hd, t + 1:t + 2], fb[:, hd, :], ALU.mult, ALU.add,
                            [Bc, BDall] + list(fB), [Bn])
                    cur = nxt
                    S.dma("sp", sbv[t], Pst[cur][:], reads=list(BPst[cur]), writes=[Bsbd])
                wout_u = {}

                def wout(g):
                    if g not in wout_u:
                        wout_u[g] = wload_cols(wout_in, g * 512, 512)
                    return wout_u[g]

                Mm = (Mf, Mb)

                def headB(hd, hs, h, Bhh):
                    for sec, dstT, Bd in ((0, qbT, Bqb), (4, gsT, Bgs)):
                        pq, Bpq = proj(sec, hd, h, Bhh)
                        act(dstT[:, hs, :], pq, AF.Tanh, [Bpq], [Bd[hs]], scale=0.5)
                        act(hx[:, hs % 2, :], pq, AF.Copy, [Bpq], [Bhx[hs % 2]], scale=0.5)
                        yield
                        stt(dstT[:, hs, :], dstT[:, hs, :], 1.0, hx[:, hs % 2, :], ALU.add, ALU.mult, [Bd[hs], Bhx[hs % 2]], [Bd[hs]])
                        yield

                def laneB(t, hd, dr, li, pi, hs, h, Bhh):
                    A, B_, C_, D_ = gA[li], gB[li], gC[li], gD[li]
                    yield from gates_gen(dr, hd, h, Bhh, li)
                    S.op("dve", lambda e: e.tensor_tensor_scan(out=C_, data0=rmask[:], data1=B_, initial=0.0, op0=ALU.mult, op1=ALU.add),
                         reads=[BgB[li], Brm], writes=[BgC[li]])
                    yield
                    C3 = C_.rearrange("p (c j) -> p c j", j=64)
                    if dr == 1:
                        D3 = D_.rearrange("p (c j) -> p c j", j=64)
                        tt(D3, C3[:, :, 63:64].to_broadcast([128, NCH, 64]), C3, ALU.subtract, [BgC[li]], [BgD_[li]])
                        yield
                        tt(C_, D_, B_, ALU.add, [BgD_[li], BgB[li]], [BgC[li]])
                        yield
                        rcol = C3[:, :, 0]
                    else:
                        rcol = C3[:, :, 63]
                    act(erT[:, pi, :], rcol, AF.Exp, [BgC[li]], [Ber[pi]])
                    yield
                    act(D_, C_, AF.Exp, [BgC[li]], [BgD_[li]])
                    yield
                    tt(Qt[pi], qbT[:, hs, :], D_, ALU.mult, [Bqb[hs], BgD_[li]], [BQt[pi]])
                    yield
                    act(D_, C_, AF.Exp, [BgC[li]], [BgD_[li]], scale=-1.0)
                    yield
                    tt(A, A, D_, ALU.mult, [BgA[li], BgD_[li]], [BgA[li]])
                    yield
                    cp(Kt[pi], A, [BgA[li]], [BKt[pi]], eng="act")
                    yield
                    tt(Kh[pi].rearrange("p (c j) -> p c j", j=64), A.rearrange("p (c j) -> p c j", j=64),
                       erT[:, pi, :].unsqueeze(2).to_broadcast([128, NCH, 64]), ALU.mult, [BgA[li], Ber[pi]], [BKh[pi]])
                    yield
                    yield from transp_gen(pi, li)

                def par_gens(*gens):
                    gens = list(gens)
                    while gens:
                        nxt = []
                        for g in gens:
                            try:
                                next(g)
                                nxt.append(g)
                            except StopIteration:
                                pass
                        gens = nxt
                        yield

                Pm5 = sqb[:, 0:4, :].rearrange("p a n -> p (a n)").rearrange("p (a s d n) -> p a s d n", a=2, s=2, d=2)

                def fwd_chain(t, hd, pf, par):
                    cp(Sbf[:, par, 0, 0, :], Sfr[:, hd, :], [BSfr[hd]], [BSbf[par]], eng="act")
                    yield
                    for ch in range(NCH):
                        sbk, rows = ch // 2, slice((ch % 2) * 64, (ch % 2) * 64 + 64)
                        pu, Bpu = (ph[12][:, 0:128], PB[12]) if par == 0 else (ph[7][:, 0:128], PB[7])
                        mm(pu, Khtm[pf][rows, sbk, :], Vtm[rows, sbk, hd * 128:(hd + 1) * 128], True, True, [BKhtm[pf], BV], [Bpu], True, fence=True)
                        yield
                        stt(Sfr[:, hd, :], Sfr[:, hd, :], erT[:, pf, ch:ch + 1], pu, ALU.mult, ALU.add, [BSfr[hd], Ber[pf], Bpu], [BSfr[hd]])
                        yield
                        if ch < NCH - 1:
                            cp(Sbf[:, par, 0, ch + 1, :], Sfr[:, hd, :], [BSfr[hd]], [BSbf[par]], eng="act")
                            yield

                def bwd_chain(t, hd, pb, par):
                    S.dma("sp", Sst[:, par, :], sb_d[hd, t], reads=[dbuf("sb0", 0)], writes=[BSst[par]])
                    cp(Sbf[:, par, 1, NCH, :], Sst[:, par, :], [BSst[par]], [BSbf2[par]], eng="act")
                    yield
                    for ch in range(NCH - 1, 0, -1):
                        sbk, rows = ch // 2, slice((ch % 2) * 64, (ch % 2) * 64 + 64)
                        pu, Bpu = (ph[5][:, 0:128], PB[5]) if par == 0 else (ph[9][:, 0:128], PB[9])
                        mm(pu, Khtm[pb][rows, sbk, :], Vtm[rows, sbk, hd * 128:(hd + 1) * 128], True, True, [BKhtm[pb], BV], [Bpu], True, fence=True)
                        yield
                        stt(Sst[:, par, :], Sst[:, par, :], erT[:, pb, ch:ch + 1], pu, ALU.mult, ALU.add, [BSst[par], Ber[pb], Bpu], [BSst[par]])
                        yield
                        cp(Sbf[:, par, 1, ch, :], Sst[:, par, :], [BSst[par]], [BSbf2[par]], eng="act")
                        yield

                def scores_gen(hd, pf, pb, par):
                    pset = (pf, pb)
                    for sbk in range(NSB):
                        cols = slice(sbk * 128, (sbk + 1) * 128)
                        for dr in range(2):
                            psc, Bps = ph[3 - par][:, dr * 128:(dr + 1) * 128], PB[3]
                            mm(psc, Kt[pset[dr]][:, cols], Qt[pset[dr]][:, cols], True, True, [BKt[pset[dr]], BQt[pset[dr]]], [Bps], True)
                            yield
                            tt(Pm5[:, par, sbk, dr, :], psc, Mm[dr][:], ALU.mult, [Bps, BM], [Bsq])
                            yield

                def postB(t, hd, pf, pb, hs, par, h, Bhh):
                    yield from par_gens(fwd_chain(t, hd, pf, par), bwd_chain(t, hd, pb, par), scores_gen(hd, pf, pb, par))
                    po, Bpo = ph[par], PB[0]
                    for sbk in range(NSB):
                        cols = slice(sbk * 128, (sbk + 1) * 128)
                        mm(po[:, cols], Vtm[:, sbk, hd * 128:(hd + 1) * 128], Pm5[:, par, sbk, 0, :], True, False, [BV, Bsq], [Bpo], False)
                        mm(po[:, cols], Vtm[:, sbk, hd * 128:(hd + 1) * 128], Pm5[:, par, sbk, 1, :], False, False, [BV, Bsq], [Bpo], False)
                        for cc in (2 * sbk, 2 * sbk + 1):
                            c64 = slice(cc * 64, cc * 64 + 64)
                            mm(po[:, c64], Sbf[:, par, 0, cc, :], Qt[pf][:, c64], False, False, [BSbf[par], BQt[pf]], [Bpo], False)
                            mm(po[:, c64], Sbf[:, par, 1, cc + 1, :], Qt[pb][:, c64], False, True, [BSbf2[par], BQt[pb]], [Bpo], True)
                        yield
                    cp(osbT[:, par, :], po, [Bpo], [Bosb[par]], eng="act")
                    yield
                    tt(osqT[:, par, :], osbT[:, par, :], osbT[:, par, :], ALU.mult, [Bosb[par]], [Bosq[par]])
                    yield
                    pst, Bpst = ph[14 + par], PB[15]
                    mm(pst, ones_h[:], osqT[:, par, :], True, True, [Bones, Bosq[par]], [Bpst], True)
                    yield
                    act(rsl[:, par, :], pst, AF.Ln, [Bpst, Beps], [Brsl[par]], bias=epsc[:, 0:1])
                    yield
                    act(rsl[:, par, :], rsl[:, par, :], AF.Exp, [Brsl[par]], [Brsl[par]], scale=-0.5)
                    yield
                    stt(osbT[:, par, :], osbT[:, par, :], vecs[:, V_GNG + hd:V_GNG + hd + 1], rsl[:, par, :], ALU.mult, ALU.mult, [Bosb[par], Bvecs, Brsl[par]], [Bosb[par]])
                    yield
                    tt(on[:, hd, :], osbT[:, par, :], gsT[:, hs, :], ALU.mult, [Bosb[par], Bgs[hs]], [Bon])
                    yield

                def postpair(t, gi, h, Bhh):
                    gens = []
                    for hh in range(2):
                        hd = 2 * gi + hh
                        base = (gi % 2) * 4 + hh * 2
                        gens.append(postB(t, hd, base, base + 1, (gi % 2) * 2 + hh, hh, h, Bhh))
                    yield from par_gens(*gens)

                for t in range(NTILE):
                    xt, Bxx, h, Bhh = load_norm(t)
                    pending = None
                    for gi in range(4):
                        gens = []
                        for hh in range(2):
                            gens.append(headB(2 * gi + hh, (gi % 2) * 2 + hh, h, Bhh))
                        for hh in range(2):
                            for dr in range(2):
                                li = hh * 2 + dr
                                gens.append(laneB(t, 2 * gi + hh, dr, li, (gi % 2) * 4 + li, (gi % 2) * 2 + hh, h, Bhh))
                        if pending is not None:
                            gens.append(pending)
                        run_lanes(gens)
                        pending = postpair(t, gi, h, Bhh)
                    run_lanes([pending])
                    for dc in range(8):
                        pw, Bpw = (ph[2], PB[2]) if dc % 2 == 0 else (ph[4], PB[4])
                        wv, wb_ = wout(dc // 4)
                        for c in range(8):
                            mm(pw, wv[:, c, (dc % 4) * 128:(dc % 4 + 1) * 128], on[:, c, :], c == 0, c == 7, [Bon] + wb_, [Bpw], c == 7)
                        tt(xt[:, dc, :], xt[:, dc, :], pw, ALU.add, [Bxx, Bpw], [Bxx])
                    S.dma("sp", dview(dst, t * TT, TT), xt[:], reads=[Bxx], writes=[dbuf(dname, t)])
                S.op("dve", lambda e: e.memset(wa[:, TAIL0 * 1024:TAIL0 * 1024 + 16], 0.0), reads=lane_bufs, writes=WB[TAIL0:NBLK])

        def stage_out(src, sname, do_norm=True):
            S.barrier()
            with nc.sbuf_tensor(U("xo"), [128, 8, TT], F32) as xo, nc.sbuf_tensor(U("ot"), [128, 2, D], F32) as ot:
                Bxo, Bot = Buf("xo"), Buf("ot")
                toks = []
                for t in range(NTILE):
                    xt, Bxx, h, Bhh = next_xt()
                    S.dma("sp", xt[:], dview(src, t * TT, TT), reads=[dbuf(sname, t)], writes=[Bxx])
                    if do_norm:
                        i = nstat["i"] % 2
                        nstat["i"] += 1
                        tt(sqb[:], xt[:], xt[:], ALU.mult, [Bxx], [Bsq])
                        for c in range(8):
                            mm(ph[14 + i], ones_d[:], sqb[:, c, :], c == 0, c == 7, [Bones, Bsq], [PB[14 + i]], c == 7)
                        rsqrt_eps(rstd[i][:], ph[14 + i], [PB[14 + i]], Brstd[i])
                        for c in range(8):
                            stt(xo[:, c, :], xt[:, c, :], vecs[:, V_FG + c:V_FG + c + 1], rstd[i][:], ALU.mult, ALU.mult, [Bxx, Brstd[i], Bvecs], [Bxo])
                        srcx, Bsrc = xo, Bxo
                    else:
                        srcx, Bsrc = xt, Bxx
                    for b in range(2):
                        for half in range(2):
                            pbk = pbank[1 + half]
                            for c4 in range(4):
                                c = half * 4 + c4
                                S.op("pe", lambda e, pbk=pbk, c4=c4, c=c, b=b, srcx=srcx: e.transpose(pbk[:, c4 * 128:(c4 + 1) * 128], srcx[:, c, b * 128:(b + 1) * 128], identf[:]),
                                     reads=[Bsrc, Bidf], writes=[PB[2 + 2 * half], PB[3 + 2 * half]], sig=(c4 == 3))
                            cp(ot[:, b, half * 512:(half + 1) * 512], pbk[:, :], [PB[2 + 2 * half], PB[3 + 2 * half]], [Bot], eng=("act" if half else "dve"))
                    toks.append(S.dma("sp", out_d[t * TT:(t + 1) * TT, :].rearrange("(b p) d -> p b d", p=128), ot[:], reads=[Bot], writes=[dbuf("out", t)]))
                for tk in toks:
                    S.wait_tok("sp", tk)

        stage_in()
        cur, cname, oth, oname = xsA, "A", xsB, "B"
        if upto >= 1:
            stage_ffn(0, 0, xsA, "A", xsB, "B", halo=(xhA, "hA", xhB, "hB"))
            cur, cname, oth, oname = xsB, "B", xsA, "A"
        if upto >= 2:
            stage_conv(xsB, "B", xhB, "hB", xsA, "A")
            cur, cname, oth, oname = xsA, "A", xsB, "B"
        if upto >= 3:
            stage_ffn(0, 1, xsA, "A", xsB, "B")
            cur, cname = xsB, "B"
        if upto >= 4:
            stage_ffn(1, 0, xsB, "B", xsA, "A")
            cur, cname = xsA, "A"
        if upto >= 5:
            stage_hgrn(xsA, "A", xsB, "B")
            cur, cname = xsB, "B"
        if cc_mode != "produce":
            if upto >= 6:
                stage_ffn(1, 1, xsB, "B", xsA, "A")
                cur, cname = xsA, "A"
            stage_out(cur, cname, do_norm=(upto >= 7))

        with nc.Block() as block:
            S.emit(block)
    return nc


def colpack(v):
    v = np.asarray(v, dtype=np.float32).reshape(-1, 128)
    return np.ascontiguousarray(v.T)


def needed_weights(upto):
    need = []
    if upto >= 1:
        need += ["w13_00", "w2_00"]
    if upto >= 2:
        need += ["conv_w_pw1", "conv_w_pw2"]
    if upto >= 3:
        need += ["w13_01", "w2_01"]
    if upto >= 4:
        need += ["w13_10", "w2_10"]
    if upto >= 5:
        need += ["hgrn_w_in", "hgrn_w_out"]
    if upto >= 6:
        need += ["w13_11", "w2_11"]
    return need


def make_inputs(upto, x, norm_g, ffn_w13, ffn_w2, conv_w_pw1, conv_b_pw1, conv_w_dw, conv_b_dw, conv_ln_g, conv_ln_b,
                conv_w_pw2, conv_b_pw2, hgrn_w_in, hgrn_lb, hgrn_gn_g, hgrn_w_out, final_g):
    f = lambda a: np.ascontiguousarray(np.asarray(a, dtype=np.float32))
    vecs = np.zeros((128, NV), np.float32)
    vecs[:, V_NG:V_NG + 48] = colpack(f(norm_g).reshape(-1))
    vecs[:, V_FG:V_FG + 8] = colpack(final_g)
    vecs[:, V_BPW1:V_BPW1 + 16] = colpack(f(conv_b_pw1)[0])
    wdw = f(conv_w_dw)[0]
    for c in range(8):
        vecs[:, V_WDW + c * 31:V_WDW + (c + 1) * 31] = wdw[:, c * 128:(c + 1) * 128].T
    vecs[:, V_BDW:V_BDW + 8] = colpack(f(conv_b_dw)[0])
    vecs[:, V_LNG:V_LNG + 8] = colpack(f(conv_ln_g)[0])
    vecs[:, V_LNB:V_LNB + 8] = colpack(f(conv_ln_b)[0])
    vecs[:, V_BPW2:V_BPW2 + 8] = colpack(f(conv_b_pw2)[0])
    vecs[:, V_HLB:V_HLB + 32] = colpack(f(hgrn_lb).reshape(-1))
    vecs[:, V_GNG:V_GNG + 8] = colpack(f(hgrn_gn_g)[0])
    x = f(x)
    allw = {"conv_w_pw1": f(conv_w_pw1)[0], "conv_w_pw2": f(conv_w_pw2)[0], "hgrn_w_in": f(hgrn_w_in)[0], "hgrn_w_out": f(hgrn_w_out)[0]}
    for l in range(2):
        for i in range(2):
            allw["w13_%d%d" % (l, i)] = f(np.asarray(ffn_w13)[l, i])
            allw["w2_%d%d" % (l, i)] = f(np.asarray(ffn_w2)[l, i])
    shared = {"vecs": vecs}
    for k in needed_weights(upto):
        shared[k] = allw[k]
    in_maps = []
    for c in range(NCORES):
        b, pos = c // 4, c % 4
        t0 = pos * NT
        xh = np.zeros((2 * HALO, D), np.float32)
        cvec = np.zeros((128, NCV), np.float32)
        if pos > 0:
            xh[:HALO] = x[b, t0 - HALO:t0]
            cvec[:, C_HM:C_HM + HALO] = 1.0
        if pos < 3:
            xh[HALO:] = x[b, t0 + NT:t0 + NT + HALO]
            cvec[:, C_HM + HALO:C_HM + 2 * HALO] = 1.0
        for r in range(NCORES):
            if r // 4 == b and r < c:
                cvec[:, C_RM + r] = 1.0
            if r // 4 == b and r > c:
                cvec[:, C_RM + 8 + r] = 1.0
        cvec[:, C_RM1:C_RM1 + 16] = 1.0 - cvec[:, C_RM:C_RM + 16]
        m = dict(shared)
        m["x"] = np.ascontiguousarray(x[b, t0:t0 + NT])
        m["xh"] = xh
        m["cvec"] = cvec
        in_maps.append(m)
    return in_maps


_NC_CACHE = {}
FUSED = True


def run(inputs, upto=99):
    in_maps = make_inputs(upto, **inputs)
    if FUSED or upto < 5:
        if upto not in _NC_CACHE:
            _NC_CACHE[upto] = build(upto)
        res = run_bass_kernel_spmd(_NC_CACHE[upto], in_maps, core_ids=list(range(NCORES)))
    else:
        ncA = build(upto, "produce")
        resA = run_bass_kernel_spmd(ncA, in_maps, core_ids=list(range(NCORES)))
        allcc = np.concatenate([resA.results[c]["cc_in"] for c in range(NCORES)], axis=0)
        for m in in_maps:
            m["cc_out"] = allcc
        ncB = build(upto, "consume")
        res = run_bass_kernel_spmd(ncB, in_maps, core_ids=list(range(NCORES)))
    out = np.zeros((2, 4 * NT, D), np.float32)
    for c in range(NCORES):
        out[c // 4, (c % 4) * NT:(c % 4 + 1) * NT] = res.results[c]["out"]
    return out


def kernel(**inputs):
    return run(inputs, 99)
```
